# Optimizing a Trainium2 kernel written in Bass

```python
import math
import jax, jax.numpy as jnp
from jax import lax
import numpy as np

D_MODEL = 1024
BATCH = 2
SEQ = 16384
DEPTH = 1
DEC_BATCH = 16
DEC_SEQ = 64
PAST_LEN = 2048

CHUNK = 64
N_HEADS = 8
QK_NOPE = 64
ROPE_DIM = 32
HEAD_DIM = QK_NOPE + ROPE_DIM
V_DIM = 64
Q_LORA = 384
KV_LORA = 256
ATT_WIDTH = N_HEADS * V_DIM
CONV_CH = D_MODEL // 2
CONV_K = 31
MIX_WIDTH = ATT_WIDTH + CONV_CH
IN_WIDTH = Q_LORA + KV_LORA + ROPE_DIM + 2 * CONV_CH
D_FF = 2816
FFN_K = 3
Q_BLOCK = 128
ROPE_BASE = 10000.0
RMS_EPS = 1e-6
NEG_INF = -1e30
SCALE = HEAD_DIM ** -0.5

kernel_name = "mla_conformer_conv_hybrid_stream_step"


def _rms(x, g):
    xf = x.astype(jnp.float32)
    y = xf * lax.rsqrt(jnp.mean(xf * xf, axis=-1, keepdims=True) + RMS_EPS)
    return (y * g.astype(jnp.float32)).astype(x.dtype)


def _rope_tail(x, pos):
    nope, pe = x[..., :QK_NOPE], x[..., QK_NOPE:]
    inv = 1.0 / (ROPE_BASE ** (jnp.arange(0, ROPE_DIM, 2, dtype=jnp.float32) / ROPE_DIM))
    ang = pos.astype(jnp.float32)[:, None] * inv[None, :]
    cos = jnp.cos(ang)[None, :, None, :].astype(x.dtype)
    sin = jnp.sin(ang)[None, :, None, :].astype(x.dtype)
    p1, p2 = pe[..., :ROPE_DIM // 2], pe[..., ROPE_DIM // 2:]
    return jnp.concatenate([nope, p1 * cos - p2 * sin, p2 * cos + p1 * sin], axis=-1)


def _attend_block(q, k, v, q_pos, k_pos):
    s = jnp.einsum('bqhd,bkhd->bhqk', q, k).astype(jnp.float32) * SCALE
    allowed = (k_pos // CHUNK)[None, :] <= (q_pos // CHUNK)[:, None]
    s = jnp.where(allowed[None, None], s, NEG_INF)
    p = jax.nn.softmax(s, axis=-1).astype(v.dtype)
    return jnp.einsum('bhqk,bkhd->bqhd', p, v)


def _chunk_causal_attention(q, k, v, q_pos, k_pos):
    B, T, H, dh = q.shape
    if T % Q_BLOCK == 0:
        nb = T // Q_BLOCK
        qb = q.reshape(B, nb, Q_BLOCK, H, dh).transpose(1, 0, 2, 3, 4)
        pb = q_pos.reshape(nb, Q_BLOCK)
        out = lax.map(lambda a: _attend_block(a[0], k, v, a[1], k_pos), (qb, pb))
        out = out.transpose(1, 0, 2, 3, 4)
    else:
        out = _attend_block(q, k, v, q_pos, k_pos)
    return out.reshape(B, T, H * V_DIM)


def _causal_dwconv(x_all, w, b):
    C = x_all.shape[-1]
    y = lax.conv_general_dilated(
        x_all, w[:, None, :], window_strides=(1,), padding='VALID',
        dimension_numbers=('NWC', 'WIO', 'NWC'), feature_group_count=C)
    return y + b


def _layer(x, ckv_past, kpe_past, conv_past, ffn_past,
           attn_norm, w_in, q_norm, w_uq, kv_norm, w_ukv, qk_norm_q, qk_norm_k,
           conv_w, conv_b, conv_norm, w_out, ffn_norm, w_up, ffn_conv_w, ffn_conv_b, w_down):
    B, T, _ = x.shape
    pos0 = ckv_past.shape[1]
    h = _rms(x, attn_norm)
    proj = h @ w_in
    c_q = proj[..., :Q_LORA]
    c_kv = _rms(proj[..., Q_LORA:Q_LORA + KV_LORA], kv_norm)
    k_pe = proj[..., Q_LORA + KV_LORA:Q_LORA + KV_LORA + ROPE_DIM]
    glu = proj[..., Q_LORA + KV_LORA + ROPE_DIM:]

    q = (_rms(c_q, q_norm) @ w_uq).reshape(B, T, N_HEADS, HEAD_DIM)
    ckv_all = jnp.concatenate([ckv_past, c_kv], axis=1)
    kpe_all = jnp.concatenate([kpe_past, k_pe], axis=1)
    Tk = ckv_all.shape[1]
    kv = (ckv_all @ w_ukv).reshape(B, Tk, N_HEADS, QK_NOPE + V_DIM)
    k_nope, v = kv[..., :QK_NOPE], kv[..., QK_NOPE:]
    k = jnp.concatenate(
        [k_nope, jnp.broadcast_to(kpe_all[:, :, None, :], (B, Tk, N_HEADS, ROPE_DIM))], axis=-1)
    q_pos = pos0 + jnp.arange(T, dtype=jnp.int32)
    k_pos = jnp.arange(Tk, dtype=jnp.int32)
    q = _rope_tail(_rms(q, qk_norm_q), q_pos)
    k = _rope_tail(_rms(k, qk_norm_k), k_pos)
    att = _chunk_causal_attention(q, k, v, q_pos, k_pos)

    u = glu[..., :CONV_CH] * jax.nn.sigmoid(glu[..., CONV_CH:])
    u_all = jnp.concatenate([conv_past, u], axis=1)
    c = jax.nn.silu(_rms(_causal_dwconv(u_all, conv_w, conv_b), conv_norm))

    x = x + jnp.concatenate([att, c], axis=-1) @ w_out

    up = _rms(x, ffn_norm) @ w_up
    a, gate = up[..., :D_FF], up[..., D_FF:]
    a_all = jnp.concatenate([ffn_past, a], axis=1)
    a = _causal_dwconv(a_all, ffn_conv_w, ffn_conv_b)
    y = x + (jax.nn.silu(a) * gate) @ w_down

    return y, c_kv, k_pe, u_all[:, -(CONV_K - 1):], a_all[:, -(FFN_K - 1):]


def setup_inputs(seed: int = 0) -> dict:
    key = jax.random.key(seed)
    ks = jax.random.split(key, 24)
    f32 = jnp.float32
    nrm = lambda k, shape, s: jax.random.normal(k, shape, f32) * s
    gain = lambda k, n: 1.0 + 0.01 * jax.random.normal(k, (DEPTH, n), f32)
    return {
        "x_prompt": nrm(ks[0], (BATCH, SEQ, D_MODEL), 1.0),
        "x_sample": nrm(ks[1], (DEC_BATCH, DEC_SEQ, D_MODEL), 1.0),
        "cache_ckv": nrm(ks[2], (DEPTH, DEC_BATCH, PAST_LEN, KV_LORA), 1.0),
        "cache_kpe": nrm(ks[3], (DEPTH, DEC_BATCH, PAST_LEN, ROPE_DIM), 0.5),
        "state_conv": nrm(ks[4], (DEPTH, DEC_BATCH, CONV_K - 1, CONV_CH), 0.5),
        "state_ffn_conv": nrm(ks[5], (DEPTH, DEC_BATCH, FFN_K - 1, D_FF), 0.5),
        "attn_norm": gain(ks[6], D_MODEL),
        "w_in": nrm(ks[7], (DEPTH, D_MODEL, IN_WIDTH), D_MODEL ** -0.5),
        "q_norm": gain(ks[8], Q_LORA),
        "w_uq": nrm(ks[9], (DEPTH, Q_LORA, N_HEADS * HEAD_DIM), Q_LORA ** -0.5),
        "kv_norm": gain(ks[10], KV_LORA),
        "w_ukv": nrm(ks[11], (DEPTH, KV_LORA, N_HEADS * (QK_NOPE + V_DIM)), KV_LORA ** -0.5),
        "qk_norm_q": gain(ks[12], HEAD_DIM),
        "qk_norm_k": gain(ks[13], HEAD_DIM),
        "conv_w": nrm(ks[14], (DEPTH, CONV_K, CONV_CH), CONV_K ** -0.5),
        "conv_b": nrm(ks[15], (DEPTH, CONV_CH), 0.01),
        "conv_norm": gain(ks[16], CONV_CH),
        "w_out": nrm(ks[17], (DEPTH, MIX_WIDTH, D_MODEL), MIX_WIDTH ** -0.5),
        "ffn_norm": gain(ks[18], D_MODEL),
        "w_up": nrm(ks[19], (DEPTH, D_MODEL, 2 * D_FF), D_MODEL ** -0.5),
        "ffn_conv_w": nrm(ks[20], (DEPTH, FFN_K, D_FF), FFN_K ** -0.5),
        "ffn_conv_b": nrm(ks[21], (DEPTH, D_FF), 0.01),
        "w_down": nrm(ks[22], (DEPTH, D_FF, D_MODEL), D_FF ** -0.5),
    }


def reference(x_prompt, x_sample, cache_ckv, cache_kpe, state_conv, state_ffn_conv,
              attn_norm, w_in, q_norm, w_uq, kv_norm, w_ukv, qk_norm_q, qk_norm_k,
              conv_w, conv_b, conv_norm, w_out, ffn_norm, w_up, ffn_conv_w, ffn_conv_b, w_down):
    B = x_prompt.shape[0]
    dt = x_prompt.dtype
    yp, ys = x_prompt, x_sample
    p_ckv, p_kpe, p_conv, p_ffn = [], [], [], []
    s_ckv, s_kpe, s_conv, s_ffn = [], [], [], []
    for l in range(DEPTH):
        w = (attn_norm[l], w_in[l], q_norm[l], w_uq[l], kv_norm[l], w_ukv[l], qk_norm_q[l],
             qk_norm_k[l], conv_w[l], conv_b[l], conv_norm[l], w_out[l], ffn_norm[l], w_up[l],
             ffn_conv_w[l], ffn_conv_b[l], w_down[l])
        yp, c1, k1, cv1, f1 = _layer(
            yp, jnp.zeros((B, 0, KV_LORA), dt), jnp.zeros((B, 0, ROPE_DIM), dt),
            jnp.zeros((B, CONV_K - 1, CONV_CH), dt), jnp.zeros((B, FFN_K - 1, D_FF), dt), *w)
        ys, c2, k2, cv2, f2 = _layer(
            ys, cache_ckv[l], cache_kpe[l], state_conv[l], state_ffn_conv[l], *w)
        p_ckv.append(c1); p_kpe.append(k1); p_conv.append(cv1); p_ffn.append(f1)
        s_ckv.append(c2); s_kpe.append(k2); s_conv.append(cv2); s_ffn.append(f2)
    return (yp, ys,
            jnp.stack(p_ckv), jnp.stack(p_kpe), jnp.stack(p_conv), jnp.stack(p_ffn),
            jnp.stack(s_ckv), jnp.stack(s_kpe), jnp.stack(s_conv), jnp.stack(s_ffn))
```

```python
import math
from contextlib import ExitStack

import numpy as np
import ml_dtypes

import concourse.bass as bass
import concourse.mybir as mybir
from concourse.bass_utils import run_bass_kernel_spmd

F32 = mybir.dt.float32
BF16 = mybir.dt.bfloat16
ALU = mybir.AluOpType
AF = mybir.ActivationFunctionType
AX = mybir.AxisListType

D = 1024
QL, KVL, RD, CC = 384, 256, 32, 512
H, HD, NOPE, VD = 8, 96, 64, 64
INW = QL + KVL + RD + 2 * CC
DFF = 2816
NFC = DFF // 128
CK = 31
EPS = 1e-6
SCALE = HD ** -0.5
NEG = -30000.0
NCPB = 4
KC = 16

CFG_FULL = dict(SEQ=16384, PAST=2048, RT=8)


class T:
    __slots__ = ("name", "w", "r", "dsem", "dcnt", "excl")

    def __init__(self, name, excl=False):
        self.name = name
        self.excl = excl
        self.w = None
        self.r = {}
        self.dsem = None
        self.dcnt = 0


class Eng:
    ROT = 30000

    def __init__(self, trk, eng, name):
        self.trk, self.eng, self.name = trk, eng, name
        self.sem = trk.new_sem(name)
        self.cnt = 0
        self.seen = {}

    def _wait(self, sem, val):
        if self.seen.get(sem, 0) >= val:
            return
        self.eng.wait_ge(sem, val)
        self.seen[sem] = val
        self.trk.nwaits += 1

    def _deps(self, reads, writes):
        need = {}

        def add(p, same_ok):
            if p is None:
                return
            sem, val = p
            if sem is self.sem and same_ok and self.name == "pe":
                return
            if need.get(sem, 0) < val:
                need[sem] = val
        for t in reads:
            add(t.w, False)
        for t in writes:
            add(t.w, True)
            for sem, val in t.r.items():
                add((sem, val), True)
        for sem, val in need.items():
            self._wait(sem, val)

    def op(self, fn, reads=(), writes=()):
        ex = [t for t in reads if t.excl and t not in writes]
        if ex:
            reads = [t for t in reads if not t.excl or t in writes]
            writes = list(writes) + ex
        self._deps(reads, writes)
        if self.cnt >= self.ROT:
            self.sem = self.trk.new_sem(self.name)
            self.cnt = 0
        inst = fn(self.eng)
        self.cnt += 1
        inst.then_inc(self.sem, 1)
        self.trk.ninst += 1
        for t in reads:
            if t.r.get(self.sem, 0) < self.cnt:
                t.r[self.sem] = self.cnt
        for t in writes:
            t.w = (self.sem, self.cnt)
            t.r = {}
        return inst

    def dma(self, out, in_, reads=(), writes=()):
        self._deps(reads, writes)
        tw = writes[0]
        if tw.dsem is None:
            tw.dsem = self.trk.new_sem("d_" + tw.name)
            self.trk.dts.append(tw)
        inst = self.eng.dma_start(out=out, in_=in_)
        inst.then_inc(tw.dsem, 16)
        tw.dcnt += 16
        self.trk.ninst += 1
        for t in reads:
            if t.r.get(tw.dsem, 0) < tw.dcnt:
                t.r[tw.dsem] = tw.dcnt
        tw.w = (tw.dsem, tw.dcnt)
        tw.r = {}
        return inst

    def wait_for(self, t):
        if t.w is not None:
            self._wait(*t.w)


class Tracker:
    def __init__(self, nc, stack):
        self.nc, self.stack = nc, stack
        self.nsem = 0
        self.nwaits = 0
        self.ninst = 0
        self.dts = []
        self.pe = Eng(self, nc.tensor, "pe")
        self.act = Eng(self, nc.scalar, "act")
        self.dve = Eng(self, nc.vector, "dve")
        self.pool = Eng(self, nc.gpsimd, "pool")
        self.sp = Eng(self, nc.sync, "sp")
        self.engs = [self.pe, self.act, self.dve, self.pool, self.sp]

    def new_sem(self, name):
        self.nsem += 1
        return self.stack.enter_context(self.nc.semaphore(f"s{self.nsem}_{name}"))

    def barrier(self):
        pts = [(e.sem, e.cnt) for e in self.engs if e.cnt > 0]
        pts += [(t.dsem, t.dcnt) for t in self.dts if t.dcnt > 0]
        for e in self.engs:
            for sem, val in pts:
                e._wait(sem, val)


def build(cfg):
    SEQ, PAST, RT = cfg["SEQ"], cfg["PAST"], cfg["RT"]
    NT = SEQ // 128
    G = NCPB * RT
    NSLOT = NT // G
    assert NSLOT * G == NT
    QG = min(4, RT)
    assert RT % QG == 0
    NOWN = NSLOT * RT
    PT = PAST // 128
    NX1 = NSLOT * (RT + 1) + 1

    nc = bass.Bass("TRN2", target_bir_lowering=False)

    def din(name, shape, dt=F32):
        return nc.dram_tensor(name, list(shape), dt, kind="ExternalInput").ap()

    def dout(name, shape, dt=F32):
        return nc.dram_tensor(name, list(shape), dt, kind="ExternalOutput").ap()

    def dscr(name, shape, dt):
        return nc.dram_tensor(name, list(shape), dt, kind="Internal").ap()

    x_all = din("x_all", [NT * 128, D])
    x_halo = din("x_halo", [NSLOT * 128, D])
    x_s = din("x_s", [128, D])
    cckv = din("cckv", [2 * PAST, KVL])
    ckpe = din("ckpe", [2 * PAST, RD])
    sconvT = din("sconvT", [128, 2, 4, CK - 1])
    sffnT = din("sffnT", [128, 2, NFC, 2])
    rope_k = din("rope_k", [128, NT, 32])
    rope_h = din("rope_h", [128, NSLOT, 32])
    rope_sp = din("rope_sp", [128, max(PT, 1), 32])
    rope_sn = din("rope_sn", [128, 32])
    hflag = din("hflag", [128, NSLOT])
    biast = din("biast", [128, NCPB])
    w_in = din("w_in", [D, INW])
    w_uq = din("w_uq", [QL, H * HD])
    w_ukv = din("w_ukv", [KVL, H * 128])
    w_out = din("w_out", [D, D])
    w_up = din("w_up", [D, 2 * DFF])
    w_down = din("w_down", [DFF, D])
    g_attn = din("g_attn", [128, 8])
    g_q = din("g_q", [128, 3])
    g_ffn = din("g_ffn", [128, 8])
    g_kv = din("g_kv", [128, KVL])
    g_hq = din("g_hq", [128, HD])
    g_hk = din("g_hk", [128, HD])
    cw = din("cw", [128, 4, CK])
    cb = din("cb", [128, 4])
    cg = din("cg", [128, 4])
    fw = din("fw", [128, NFC, 3])
    fb = din("fb", [128, NFC])
    identb = din("identb", [128, 128], BF16)
    identf = din("identf", [128, 128])
    onesb = din("onesb", [128, 128], BF16)
    khot = din("khot", [32, QG * 128], BF16)
    qmask = din("qmask", [32, QG * 128], BF16)

    o_y = dout("o_y", [NOWN * 128, D])
    o_ckv = dout("o_ckv", [NOWN * 128, KVL])
    o_kpe = dout("o_kpe", [NOWN * 128, RD])
    o_conv = dout("o_conv", [32, CC])
    o_ffn = dout("o_ffn", [32, DFF])
    o_ys = dout("o_ys", [128, D])
    o_ckvs = dout("o_ckvs", [128, KVL])
    o_kpes = dout("o_kpes", [128, RD])
    o_convs = dout("o_convs", [2, 32, CC])
    o_ffns = dout("o_ffns", [2, 32, DFF])

    KT = dscr("KT", [H, HD, NT * 128], BF16)
    VV = dscr("VV", [H, 128, NT, 128], BF16)
    KTs = dscr("KTs", [2, H, HD, (PT + 1) * 128], BF16)
    VVs = dscr("VVs", [2, H, 128, PT + 1, 128], BF16)
    X1 = dscr("X1", [NX1 * 128, D], F32)
    NCI = NSLOT * (RT + 1)
    QT = dscr("QT", [HD, H, NCI * 128], BF16)
    UT = dscr("UT", [128, 4, 32 + NCI * 128], F32)

    class _Stop(Exception):
        pass

    def ckpt(name):
        if cfg.get("STOP") == name:
            tk.barrier()
            raise _Stop()
    try:
      with ExitStack() as top:
          tk = Tracker(nc, top)
          pe, act, dve, pool, sp = tk.pe, tk.act, tk.dve, tk.pool, tk.sp

          def sb(st, name, shape, dt):
              return st.enter_context(nc.sbuf_tensor(name, list(shape), dt))

          def ps(st, name, shape, dt):
              return st.enter_context(nc.psum_tensor(name, list(shape), dt))

          pSS = ps(top, "pSS", [128, 2048], F32)
          pS = [pSS[:, i * 512:(i + 1) * 512] for i in range(4)]
          TpS = [T(f"pS{i}", excl=True) for i in range(4)]
          TpSS = [T(f"pSS{i}", excl=True) for i in range(2)]
          pOO = ps(top, "pOO", [128, 1024], F32)
          pO = [pOO[:, i * 512:(i + 1) * 512] for i in range(2)]
          TpO = [T(f"pO{i}", excl=True) for i in range(2)]
          pM0 = ps(top, "pM0", [128, 512], F32)
          pM = [pM0[:, :], pO[0], pO[1]]
          TpM = [T("pM0", excl=True), TpO[0], TpO[1]]
          pT = ps(top, "pT", [128, 1024], BF16)
          TpT = T("pT", excl=True)

          cst = {}
          Tc = T("consts")

          def cload(name, src, shape, dt=F32):
              t = sb(top, "c_" + name, shape, dt)
              sp.dma(t[:], src, writes=[Tc])
              cst[name] = t
              return t
          c_idb = cload("idb", identb[:, :], [128, 128], BF16)
          c_idf = cload("idf", identf[:, :], [128, 128])
          c_ones = cload("ones", onesb[:, :], [128, 128], BF16)
          c_gkv = cload("gkv", g_kv[:, :], [128, KVL])
          c_ghq = cload("ghq", g_hq[:, :], [128, HD])
          c_ghk = cload("ghk", g_hk[:, :], [128, HD])
          c_cw = cload("cw", cw[:, :, :], [128, 4, CK])
          c_cb = cload("cb", cb[:, :], [128, 4])
          c_cg = cload("cg", cg[:, :], [128, 4])
          c_fw = cload("fw", fw[:, :, :], [128, NFC, 3])
          c_fb = cload("fb", fb[:, :], [128, NFC])
          c_hflag = cload("hflag", hflag[:, :], [128, NSLOT])
          c_bias = cload("bias", biast[:, :], [128, NCPB])
          c_gattn = cload("gattn", g_attn[:, :], [128, 8])
          c_gq = cload("gq", g_q[:, :], [128, 3])
          c_gffn = cload("gffn", g_ffn[:, :], [128, 8])
          c_ropeh = cload("ropeh", rope_h[:, :, :], [128, NSLOT, 32])
          c_ropesn = cload("ropesn", rope_sn[:, :], [128, 32])
          c_zero = sb(top, "c_zero", [128, 1], F32)
          dve.op(lambda e: e.memset(c_zero[:], 0.0), writes=[Tc])
          c_eps = sb(top, "c_eps", [128, 1], F32)
          dve.op(lambda e: e.memset(c_eps[:], EPS), writes=[Tc])

          def load_weight(st, dst, src2d, nk, ncols, gain, kp=128, name="w"):
              CH = 2048
              stg = [sb(st, f"stg_{name}{i}", [128, CH], F32) for i in range(2)]
              Tst = [T(f"stg_{name}{i}") for i in range(2)]
              n = 0
              for k in range(nk):
                  for c0 in range(0, ncols, CH):
                      cwid = min(CH, ncols - c0)
                      b = n % 2
                      n += 1
                      sp.dma(stg[b][0:kp, 0:cwid], src2d[k * kp:(k + 1) * kp, c0:c0 + cwid], writes=[Tst[b]])
                      eng = dve if (n % 2) else pool
                      if gain is not None:
                          eng.op(lambda e, b=b, k=k, c0=c0, cwid=cwid: e.tensor_scalar(
                              out=dst[0:kp, k, c0:c0 + cwid], in0=stg[b][0:kp, 0:cwid],
                              scalar1=gain[0:kp, k:k + 1], scalar2=None, op0=ALU.mult),
                              reads=[Tst[b], Tc], writes=[Tw])
                      else:
                          eng.op(lambda e, b=b, k=k, c0=c0, cwid=cwid: e.tensor_copy(
                              out=dst[0:kp, k, c0:c0 + cwid], in_=stg[b][0:kp, 0:cwid]),
                              reads=[Tst[b]], writes=[Tw])

          Tw = T("weights")

          def rstd_from_msq(st_bufs, msq, n):
              ap, Tm = msq
              dve.op(lambda e: e.tensor_scalar(out=ap, in0=ap, scalar1=EPS, scalar2=None, op0=ALU.add),
                     reads=[Tm], writes=[Tm])
              act.op(lambda e: e.activation(out=ap, in_=ap, func=AF.Sqrt), reads=[Tm], writes=[Tm])
              dve.op(lambda e: e.reciprocal(out=ap, in_=ap), reads=[Tm], writes=[Tm])

          class TileBufs:
              def __init__(self, st, tag, nx=2):
                  self.xt = [sb(st, f"xt{tag}{i}", [128, D], F32) for i in range(nx)]
                  self.Txt = [T(f"xt{tag}{i}") for i in range(nx)]
                  self.junk = sb(st, f"junk{tag}", [128, D], BF16)
                  self.Tjunk = T("junk" + tag)
                  self.st = sb(st, f"stat{tag}", [128, 8], F32)
                  self.Tst = [T(f"stat{tag}{i}") for i in range(8)]
                  self.xn = sb(st, f"xn{tag}", [128, D], BF16)
                  self.Txn = T("xn" + tag)
                  self.xnT = sb(st, f"xnT{tag}", [128, 8, 128], BF16)
                  self.TxnT = T("xnT" + tag)
                  self.n = 0

          def front_end(tb, src_rows, w_reads=()):
              b = tb.n % len(tb.xt)
              tb.n += 1
              xt, Txt = tb.xt[b], tb.Txt[b]
              sp.dma(xt[:], src_rows, reads=list(w_reads), writes=[Txt])
              ms, Tms = tb.st[:, 0:1], tb.Tst[0]
              act.op(lambda e: e.activation(out=tb.junk[:], in_=xt[:], func=AF.Square, scale=1.0 / math.sqrt(D),
                                            accum_out=ms), reads=[Txt], writes=[tb.Tjunk, Tms])
              rstd_from_msq(None, (ms, Tms), 1)
              dve.op(lambda e: e.tensor_scalar(out=tb.xn[:], in0=xt[:], scalar1=ms, scalar2=None, op0=ALU.mult),
                     reads=[Txt, Tms], writes=[tb.Txn])
              for k in range(8):
                  pe.op(lambda e, k=k: e.transpose(out=pT[:, k * 128:(k + 1) * 128], in_=tb.xn[:, k * 128:(k + 1) * 128],
                                                   identity=c_idb[:]), reads=[tb.Txn, Tc], writes=[TpT])
              act.op(lambda e: e.activation(out=tb.xnT[:].rearrange("p k t -> p (k t)"), in_=pT[:], func=AF.Copy),
                     reads=[TpT], writes=[tb.TxnT])
              return b

          class HeadBufs:
              def __init__(self, st, tag):
                  self.raw = sb(st, f"hraw{tag}", [128, H, HD], F32)
                  self.Traw = T("hraw" + tag)
                  self.sq = sb(st, f"hsq{tag}", [128, H, HD], F32)
                  self.Tsq = T("hsq" + tag)
                  self.rs = sb(st, f"hrs{tag}", [128, H], F32)
                  self.Trs = T("hrs" + tag)
                  self.t1, self.Tt1 = self.sq, self.Tsq
                  self.ra = sb(st, f"hra{tag}", [128, H, 16], F32)
                  self.rb = sb(st, f"hrb{tag}", [128, H, 16], F32)
                  self.Tra, self.Trb = T("hra" + tag), T("hrb" + tag)
                  self.fin = sb(st, f"hfin{tag}", [128, H, HD], BF16)
                  self.Tfin = T("hfin" + tag)

          def head_norm_rope(hb, gain, cs, Tcs):
              raw, sq, rs, t1, fin = hb.raw, hb.sq, hb.rs, hb.t1, hb.fin
              act.op(lambda e: e.activation(out=sq[:], in_=raw[:], func=AF.Square, scale=1.0 / math.sqrt(HD)),
                     reads=[hb.Traw], writes=[hb.Tsq])
              dve.op(lambda e: e.tensor_reduce(out=rs[:], in_=sq[:], axis=AX.X, op=ALU.add),
                     reads=[hb.Tsq], writes=[hb.Trs])
              rstd_from_msq(None, (rs[:], hb.Trs), H)
              dve.op(lambda e: e.tensor_tensor(out=t1[:], in0=raw[:], in1=rs[:].unsqueeze(2).to_broadcast([128, H, HD]),
                                               op=ALU.mult), reads=[hb.Traw, hb.Trs], writes=[hb.Tt1])
              pool.op(lambda e: e.tensor_tensor(out=t1[:], in0=t1[:], in1=gain[:].unsqueeze(1).to_broadcast([128, H, HD]),
                                                op=ALU.mult), reads=[hb.Tt1, Tc], writes=[hb.Tt1])
              cosb = cs[:, 0:16].unsqueeze(1).to_broadcast([128, H, 16])
              sinb = cs[:, 16:32].unsqueeze(1).to_broadcast([128, H, 16])
              p1, p2 = t1[:, :, 64:80], t1[:, :, 80:96]
              act.op(lambda e: e.activation(out=fin[:, :, 0:64], in_=t1[:, :, 0:64], func=AF.Copy),
                     reads=[hb.Tt1], writes=[hb.Tfin])
              dve.op(lambda e: e.tensor_tensor(out=hb.ra[:], in0=p1, in1=cosb, op=ALU.mult),
                     reads=[hb.Tt1, Tcs], writes=[hb.Tra])
              dve.op(lambda e: e.tensor_tensor(out=hb.rb[:], in0=p2, in1=sinb, op=ALU.mult),
                     reads=[hb.Tt1, Tcs], writes=[hb.Trb])
              dve.op(lambda e: e.tensor_tensor(out=fin[:, :, 64:80], in0=hb.ra[:], in1=hb.rb[:], op=ALU.subtract),
                     reads=[hb.Tra, hb.Trb], writes=[hb.Tfin])
              dve.op(lambda e: e.tensor_tensor(out=hb.ra[:], in0=p2, in1=cosb, op=ALU.mult),
                     reads=[hb.Tt1, Tcs], writes=[hb.Tra])
              dve.op(lambda e: e.tensor_tensor(out=hb.rb[:], in0=p1, in1=sinb, op=ALU.mult),
                     reads=[hb.Tt1, Tcs], writes=[hb.Trb])
              dve.op(lambda e: e.tensor_tensor(out=fin[:, :, 80:96], in0=hb.ra[:], in1=hb.rb[:], op=ALU.add),
                     reads=[hb.Tra, hb.Trb], writes=[hb.Tfin])

          NWAYS = 1
          NWAYS_P1 = 4

          def run_ways(tasks, make_gen, nways=None):
              nways = len(ways) if nways is None else nways
              it = iter(tasks)
              free = list(range(nways))
              active = []
              more = True
              while True:
                  while free and more:
                      try:
                          tsk = next(it)
                      except StopIteration:
                          more = False
                          break
                      w = free.pop(0)
                      active.append((make_gen(tsk, ways[w]), w))
                  if not active:
                      break
                  for gw in list(active):
                      try:
                          next(gw[0])
                      except StopIteration:
                          active.remove(gw)
                          free.append(gw[1])

          def rstd_g(ap, Tm):
              dve.op(lambda e: e.tensor_scalar(out=ap, in0=ap, scalar1=EPS, scalar2=None, op0=ALU.add),
                     reads=[Tm], writes=[Tm])
              yield
              act.op(lambda e: e.activation(out=ap, in_=ap, func=AF.Sqrt), reads=[Tm], writes=[Tm])
              yield
              dve.op(lambda e: e.reciprocal(out=ap, in_=ap), reads=[Tm], writes=[Tm])
              yield

          pS2b = pS[2].bitcast(BF16)
          Tbanks = [(pT[:, :], TpT), (pS2b, TpS[2])]
          Pbanks = [(pO[1], TpO[1]), (pO[0], TpO[0])]
          Kbanks = [((pM[0], pM[1]), (TpM[0], TpM[1])), ((pS[0], pS[1]), (TpS[0], TpS[1]))]

          class Way:
              def __init__(self, st, w):
                  tag = f"W{w}"
                  self.w = w
                  self.xt = sb(st, "xt" + tag, [128, D], F32)
                  self.Txt = T("xt" + tag)
                  self.st = sb(st, "stat" + tag, [128, 8], F32)
                  self.Tst = [T(f"stat{tag}{i}") for i in range(8)]
                  self.xn = sb(st, "xn" + tag, [128, D], BF16)
                  self.Txn = T("xn" + tag)
                  self.junk, self.Tjunk = self.xn, self.Txn
                  self.xnT = sb(st, "xnT" + tag, [128, 8, 128], BF16)
                  self.TxnT = T("xnT" + tag)
                  self.hb = HeadBufs(st, tag)
                  self.rk = sb(st, "rk" + tag, [128, 32], F32)
                  self.Trk = T("rk" + tag)
                  self.cqb = sb(st, "cqb" + tag, [128, QL], BF16)
                  self.Tcqb = T("cqb" + tag)
                  self.cqf = self.hb.sq[:].rearrange("p h d -> p (h d)")[:, 0:QL]
                  self.Tcqf = self.hb.Tsq
                  self.cqT = sb(st, "cqT" + tag, [128, 3, 128], BF16)
                  self.TcqT = T("cqT" + tag)
                  self.sig = sb(st, "sig" + tag, [128, 4, 128], F32)
                  self.Tsig = T("sig" + tag)
                  self.pT, self.TpT = Tbanks[w % 2]
                  self.pP, self.TpP = Pbanks[w % 2]
                  self.pK, self.TpK = Kbanks[w % 2]

              def alloc_p1(self, st):
                  tag = f"W{self.w}"
                  self.ckv = sb(st, "ckv" + tag, [128, KVL], F32)
                  self.Tckv = T("ckv" + tag)
                  self.kpe = sb(st, "kpe" + tag, [128, RD], F32)
                  self.Tkpe = T("kpe" + tag)
                  self.ckvb = sb(st, "ckvb" + tag, [128, KVL], BF16)
                  self.Tckvb = T("ckvb" + tag)
                  self.ckvT = sb(st, "ckvT" + tag, [128, 2, 128], BF16)
                  self.TckvT = T("ckvT" + tag)
                  self.qst = sb(st, "qst" + tag, [HD, H, 128], BF16)
                  self.Tqst = T("qst" + tag)
                  self.ust, self.Tust = self.sig, self.Tsig
                  self.prj = sb(st, "prj" + tag, [128, KVL + RD], F32)
                  self.Tprj = T("prj" + tag)
                  self.cin = sb(st, "cin" + tag, [128, KVL], F32)
                  self.Tcin = T("cin" + tag)
                  self.kin = sb(st, "kin" + tag, [128, RD], F32)
                  self.Tkin = T("kin" + tag)

          def front_end_g(W, src_rows):
              sp.dma(W.xt[:], src_rows, writes=[W.Txt])
              ms, Tms = W.st[:, 0:1], W.Tst[0]
              act.op(lambda e: e.activation(out=W.junk[:], in_=W.xt[:], func=AF.Square, scale=1.0 / math.sqrt(D),
                                            accum_out=ms), reads=[W.Txt], writes=[W.Tjunk, Tms])
              yield
              yield from rstd_g(ms, Tms)
              dve.op(lambda e: e.tensor_scalar(out=W.xn[:], in0=W.xt[:], scalar1=ms, scalar2=None, op0=ALU.mult),
                     reads=[W.Txt, Tms], writes=[W.Txn])
              yield
              for k in range(8):
                  pe.op(lambda e, k=k: e.transpose(out=W.pT[:, k * 128:(k + 1) * 128], in_=W.xn[:, k * 128:(k + 1) * 128],
                                                   identity=c_idb[:]), reads=[W.Txn, Tc], writes=[W.TpT])
              act.op(lambda e: e.activation(out=W.xnT[:].rearrange("p k t -> p (k t)"), in_=W.pT, func=AF.Copy),
                     reads=[W.TpT], writes=[W.TxnT])
              yield

          def head_norm_rope_g(hb, gain, cs, Tcs):
              raw, sq, rs, t1, fin = hb.raw, hb.sq, hb.rs, hb.t1, hb.fin
              act.op(lambda e: e.activation(out=sq[:], in_=raw[:], func=AF.Square, scale=1.0 / math.sqrt(HD)),
                     reads=[hb.Traw], writes=[hb.Tsq])
              yield
              dve.op(lambda e: e.tensor_reduce(out=rs[:], in_=sq[:], axis=AX.X, op=ALU.add),
                     reads=[hb.Tsq], writes=[hb.Trs])
              yield
              yield from rstd_g(rs[:], hb.Trs)
              dve.op(lambda e: e.tensor_tensor(out=t1[:], in0=raw[:], in1=rs[:].unsqueeze(2).to_broadcast([128, H, HD]),
                                               op=ALU.mult), reads=[hb.Traw, hb.Trs], writes=[hb.Tt1])
              yield
              pool.op(lambda e: e.tensor_tensor(out=t1[:], in0=t1[:], in1=gain[:].unsqueeze(1).to_broadcast([128, H, HD]),
                                                op=ALU.mult), reads=[hb.Tt1, Tc], writes=[hb.Tt1])
              yield
              cosb = cs[:, 0:16].unsqueeze(1).to_broadcast([128, H, 16])
              sinb = cs[:, 16:32].unsqueeze(1).to_broadcast([128, H, 16])
              p1, p2 = t1[:, :, 64:80], t1[:, :, 80:96]
              act.op(lambda e: e.activation(out=fin[:, :, 0:64], in_=t1[:, :, 0:64], func=AF.Copy),
                     reads=[hb.Tt1], writes=[hb.Tfin])
              dve.op(lambda e: e.tensor_tensor(out=hb.ra[:], in0=p1, in1=cosb, op=ALU.mult),
                     reads=[hb.Tt1, Tcs], writes=[hb.Tra])
              pool.op(lambda e: e.tensor_tensor(out=hb.rb[:], in0=p2, in1=sinb, op=ALU.mult),
                      reads=[hb.Tt1, Tcs], writes=[hb.Trb])
              yield
              dve.op(lambda e: e.tensor_tensor(out=fin[:, :, 64:80], in0=hb.ra[:], in1=hb.rb[:], op=ALU.subtract),
                     reads=[hb.Tra, hb.Trb], writes=[hb.Tfin])
              yield
              dve.op(lambda e: e.tensor_tensor(out=hb.ra[:], in0=p2, in1=cosb, op=ALU.mult),
                     reads=[hb.Tt1, Tcs], writes=[hb.Tra])
              pool.op(lambda e: e.tensor_tensor(out=hb.rb[:], in0=p1, in1=sinb, op=ALU.mult),
                      reads=[hb.Tt1, Tcs], writes=[hb.Trb])
              yield
              dve.op(lambda e: e.tensor_tensor(out=fin[:, :, 80:96], in0=hb.ra[:], in1=hb.rb[:], op=ALU.add),
                     reads=[hb.Tra, hb.Trb], writes=[hb.Tfin])
              yield

          with ExitStack() as stA:
              wA_in = sb(stA, "wA_in", [128, 8, INW], BF16)
              wA_uq = sb(stA, "wA_uq", [128, 3, H * HD], BF16)
              wA_ukv = sb(stA, "wA_ukv", [128, 2, H * 128], BF16)
              wA_oa = sb(stA, "wA_oa", [64, 8, D], BF16)
              wA_oc = sb(stA, "wA_oc", [128, 4, D], BF16)
              with ExitStack() as stW:
                  load_weight(stW, wA_in, w_in, 8, INW, c_gattn, name="in")
                  load_weight(stW, wA_uq, w_uq, 3, H * HD, c_gq, name="uq")
                  load_weight(stW, wA_ukv, w_ukv, 2, H * 128, None, name="ukv")
                  load_weight(stW, wA_oa, w_out, 8, D, None, kp=64, name="oa")
                  load_weight(stW, wA_oc, w_out[512:1024, :], 4, D, None, name="oc")
                  tk.barrier()
              ckpt("w")

              def tile_front_A_g(W, src_rows, cs, Tcs, qcol, ntok_groups):
                  yield from front_end_g(W, src_rows)
                  yield from qglu_g(W, cs, Tcs, qTa[0:HD, :, qcol:qcol + 128], TqTa, ntok_groups)

              def qglu_g(W, cs, Tcs, qdst, Tqdst, ntok_groups):
                  hb = W.hb
                  for k in range(8):
                      pe.op(lambda e, k=k: e.matmul(W.pP[:, 0:QL], lhsT=W.xnT[:, k, :], rhs=wA_in[:, k, 0:QL],
                                                    start=(k == 0), stop=(k == 7)), reads=[W.TxnT, Tw], writes=[W.TpP])
                  dve.op(lambda e: e.tensor_copy(out=W.cqf, in_=W.pP[:, 0:QL]), reads=[W.TpP], writes=[W.Tcqf])
                  yield
                  ms, Tms = W.st[:, 2:3], W.Tst[2]
                  act.op(lambda e: e.activation(out=W.junk[:, 0:QL], in_=W.cqf, func=AF.Square,
                                                scale=1.0 / math.sqrt(QL), accum_out=ms),
                         reads=[W.Tcqf], writes=[W.Tjunk, Tms])
                  yield
                  yield from rstd_g(ms, Tms)
                  dve.op(lambda e: e.tensor_scalar(out=W.cqb[:], in0=W.cqf, scalar1=ms, scalar2=None, op0=ALU.mult),
                         reads=[W.Tcqf, Tms], writes=[W.Tcqb])
                  yield
                  for k in range(3):
                      pe.op(lambda e, k=k: e.transpose(out=W.pT[:, k * 128:(k + 1) * 128], in_=W.cqb[:, k * 128:(k + 1) * 128],
                                                       identity=c_idb[:]), reads=[W.Tcqb, Tc], writes=[W.TpT])
                  dve.op(lambda e: e.tensor_copy(out=W.cqT[:].rearrange("p k t -> p (k t)"), in_=W.pT[:, 0:384]),
                         reads=[W.TpT], writes=[W.TcqT])
                  yield
                  for nb, (c0, cw_) in enumerate(((0, 512), (512, 256))):
                      for k in range(3):
                          pe.op(lambda e, nb=nb, k=k, c0=c0, cw_=cw_: e.matmul(
                              W.pK[nb][:, 0:cw_], lhsT=W.cqT[:, k, :], rhs=wA_uq[:, k, c0:c0 + cw_],
                              start=(k == 0), stop=(k == 2)), reads=[W.TcqT, Tw], writes=[W.TpK[nb]])
                  rawf = hb.raw[:].rearrange("p h d -> p (h d)")
                  act.op(lambda e: e.activation(out=rawf[:, 0:512], in_=W.pK[0][:, 0:512], func=AF.Copy),
                         reads=[W.TpK[0]], writes=[hb.Traw])
                  dve.op(lambda e: e.tensor_copy(out=rawf[:, 512:768], in_=W.pK[1][:, 0:256]),
                         reads=[W.TpK[1]], writes=[hb.Traw])
                  yield
                  yield from head_norm_rope_g(hb, c_ghq, cs, Tcs)
                  for h in range(H):
                      pe.op(lambda e, h=h: e.transpose(out=W.pT[0:HD, h * 128:(h + 1) * 128], in_=hb.fin[:, h, :],
                                                       identity=c_idb[:]), reads=[hb.Tfin, Tc], writes=[W.TpT])
                  act.op(lambda e: e.activation(out=qdst,
                                                in_=W.pT[0:HD, :].rearrange("p (h t) -> p h t", h=H), func=AF.Copy),
                         reads=[W.TpT], writes=[Tqdst])
                  yield
                  for half in (1, 0):
                      for c in range(4):
                          col = QL + KVL + RD + half * CC + c * 128
                          for k in range(8):
                              pe.op(lambda e, half=half, c=c, k=k, col=col: e.matmul(
                                  W.pK[half][:, c * 128:(c + 1) * 128], lhsT=wA_in[:, k, col:col + 128], rhs=W.xnT[:, k, :],
                                  start=(k == 0), stop=(k == 7)), reads=[W.TxnT, Tw], writes=[W.TpK[half]])
                      if half == 1:
                          act.op(lambda e: e.activation(out=W.sig[:].rearrange("p c t -> p (c t)"), in_=W.pK[1][:, :],
                                                        func=AF.Sigmoid), reads=[W.TpK[1]], writes=[W.Tsig])
                  for (tok0, ntok, ucol_) in ntok_groups:
                      tgt = ucol_ if isinstance(ucol_, tuple) else (uT, TuT, ucol_)
                      ub, Tub, uc = tgt
                      dve.op(lambda e, tok0=tok0, ntok=ntok, ub=ub, uc=uc: e.tensor_tensor(
                          out=ub[:, :, uc:uc + ntok],
                          in0=W.pK[0][:, :].rearrange("p (c t) -> p c t", c=4)[:, :, tok0:tok0 + ntok],
                          in1=W.sig[:, :, tok0:tok0 + ntok], op=ALU.mult), reads=[W.TpK[0], W.Tsig], writes=[Tub])
                  yield

              ways = [Way(stA, w) for w in range(NWAYS)]
              TKT, TVV = T("KT"), T("VV")
              TKTs, TVVs = T("KTs"), T("VVs")
              TX1 = T("X1")
              Tout = T("outs")
              with ExitStack() as stP1:
                  for w in range(NWAYS, NWAYS_P1):
                      ways.append(Way(stP1, w))
                  for W_ in ways:
                      W_.alloc_p1(stP1)
                  KS = 4
                  kst = [sb(stP1, f"kst{i}", [HD, H, KS * 128], BF16) for i in range(2)]
                  Tkst = [T(f"kst{i}") for i in range(2)]
                  vst = [sb(stP1, f"vst{i}", [128, H, KS, 128], BF16) for i in range(2)]
                  Tvst = [T(f"vst{i}") for i in range(2)]
                  for i in range(2):
                      pool.op(lambda e, i=i: e.memset(vst[i][:], 1.0), writes=[Tvst[i]])

                  def kv_from_ckv_g(W, ckv_ap, Tck, kpe_ap, Tkp, cs, Tcs, stage_slot):
                      sbuf, slot = stage_slot
                      hb = W.hb
                      act.op(lambda e: e.activation(out=W.ckvb[:], in_=ckv_ap, func=AF.Copy), reads=[Tck], writes=[W.Tckvb])
                      yield
                      for k in range(2):
                          pe.op(lambda e, k=k: e.transpose(out=W.pT[:, k * 128:(k + 1) * 128],
                                                           in_=W.ckvb[:, k * 128:(k + 1) * 128], identity=c_idb[:]),
                                reads=[W.Tckvb, Tc], writes=[W.TpT])
                      dve.op(lambda e: e.tensor_copy(out=W.ckvT[:].rearrange("p k t -> p (k t)"), in_=W.pT[:, 0:256]),
                             reads=[W.TpT], writes=[W.TckvT])
                      yield
                      for nb in range(2):
                          for k in range(2):
                              pe.op(lambda e, nb=nb, k=k: e.matmul(W.pK[nb][:, :], lhsT=W.ckvT[:, k, :],
                                                                   rhs=wA_ukv[:, k, nb * 512:(nb + 1) * 512],
                                                                   start=(k == 0), stop=(k == 1)),
                                    reads=[W.TckvT, Tw], writes=[W.TpK[nb]])
                      for nb in range(2):
                          src = W.pK[nb][:, :].rearrange("p (h c) -> p h c", h=4)
                          act.op(lambda e, nb=nb, src=src: e.activation(out=hb.raw[:, nb * 4:(nb + 1) * 4, 0:64],
                                                                        in_=src[:, :, 0:64], func=AF.Copy),
                                 reads=[W.TpK[nb]], writes=[hb.Traw])
                          dve.op(lambda e, nb=nb, src=src: e.tensor_copy(out=vst[sbuf][:, nb * 4:(nb + 1) * 4, slot, 0:64],
                                                                         in_=src[:, :, 64:128]),
                                 reads=[W.TpK[nb]], writes=[Tvst[sbuf]])
                      yield
                      pool.op(lambda e: e.tensor_copy(out=hb.raw[:, :, 64:96],
                                                      in_=kpe_ap.unsqueeze(1).to_broadcast([128, H, RD])),
                              reads=[Tkp], writes=[hb.Traw])
                      yield
                      yield from head_norm_rope_g(hb, c_ghk, cs, Tcs)
                      for h in range(H):
                          pe.op(lambda e, h=h: e.transpose(out=W.pT[0:HD, h * 128:(h + 1) * 128], in_=hb.fin[:, h, :],
                                                           identity=c_idb[:]), reads=[hb.Tfin, Tc], writes=[W.TpT])
                      act.op(lambda e: e.activation(out=kst[sbuf][:, :, slot * 128:(slot + 1) * 128],
                                                    in_=W.pT[0:HD, :].rearrange("p (h t) -> p h t", h=H), func=AF.Copy),
                             reads=[W.TpT], writes=[Tkst[sbuf]])
                      yield

                  def ckv_from_x_g(W, own_row=None, o_ck=None, o_kp=None):
                      for k in range(8):
                          pe.op(lambda e, k=k: e.matmul(W.pP[:, 0:KVL + RD], lhsT=W.xnT[:, k, :],
                                                        rhs=wA_in[:, k, QL:QL + KVL + RD], start=(k == 0), stop=(k == 7)),
                                reads=[W.TxnT, Tw], writes=[W.TpP])
                      dve.op(lambda e: e.tensor_copy(out=W.prj[:], in_=W.pP[:, 0:KVL + RD]), reads=[W.TpP], writes=[W.Tprj])
                      yield
                      ms, Tms = W.st[:, 1:2], W.Tst[1]
                      act.op(lambda e: e.activation(out=W.junk[:, 0:KVL], in_=W.prj[:, 0:KVL], func=AF.Square,
                                                    scale=1.0 / math.sqrt(KVL), accum_out=ms),
                             reads=[W.Tprj], writes=[W.Tjunk, Tms])
                      pool.op(lambda e: e.tensor_copy(out=W.kpe[:], in_=W.prj[:, KVL:KVL + RD]),
                              reads=[W.Tprj], writes=[W.Tkpe])
                      yield
                      yield from rstd_g(ms, Tms)
                      dve.op(lambda e: e.scalar_tensor_tensor(out=W.ckv[:], in0=W.prj[:, 0:KVL], scalar=ms, in1=c_gkv[:],
                                                              op0=ALU.mult, op1=ALU.mult),
                             reads=[W.Tprj, Tms, Tc], writes=[W.Tckv])
                      yield
                      if own_row is not None:
                          sp.dma(o_ck[own_row:own_row + 128, :], W.ckv[:], reads=[W.Tckv], writes=[Tout])
                          sp.dma(o_kp[own_row:own_row + 128, :], W.kpe[:], reads=[W.Tkpe], writes=[Tout])

                  gdone = {}

                  TQT, TUT = T("QT"), T("UT")
                  zpad = sb(stP1, "zpad", [128, 4, 32], F32)
                  Tzpad = T("zpad")
                  dve.op(lambda e: e.memset(zpad[:], 0.0), writes=[Tzpad])
                  sp.dma(UT[:, :, 0:32], zpad[:], reads=[Tzpad], writes=[TUT])

                  def qu_to_scratch_g(W, cs, Tcs, ci):
                      yield from qglu_g(W, cs, Tcs, W.qst[:], W.Tqst, [(0, 128, (W.ust, W.Tust, 0))])
                      sp.dma(QT[:, :, ci * 128:(ci + 1) * 128], W.qst[:], reads=[W.Tqst], writes=[TQT])
                      sp.dma(UT[:, :, 32 + ci * 128:32 + (ci + 1) * 128], W.ust[:], reads=[W.Tust], writes=[TUT])

                  def p1_tile_g(lt, W):
                      if isinstance(lt, tuple):
                          t = lt[1]
                          yield from front_end_g(W, x_halo[t * 128:(t + 1) * 128, :])
                          yield from qu_to_scratch_g(W, c_ropeh[:, t, :], Tc, t * (RT + 1))
                          return
                      gi = lt // KS
                      sbuf, slot = gi % 2, lt % KS
                      sp.dma(W.rk[:], rope_k[:, lt, :], writes=[W.Trk])
                      yield from front_end_g(W, x_all[lt * 128:(lt + 1) * 128, :])
                      t, rem = divmod(lt, G)
                      own = rem < RT
                      yield from ckv_from_x_g(W, own_row=(t * RT + rem) * 128 if own else None, o_ck=o_ckv, o_kp=o_kpe)
                      yield from kv_from_ckv_g(W, W.ckv[:], W.Tckv, W.kpe[:], W.Tkpe, W.rk, W.Trk, (sbuf, slot))
                      if own:
                          yield from qu_to_scratch_g(W, W.rk, W.Trk, t * (RT + 1) + 1 + rem)
                      gdone[gi] = gdone.get(gi, 0) + 1
                      if gdone[gi] == KS:
                          lt0 = gi * KS
                          sp.dma(KT[:, :, lt0 * 128:(lt0 + KS) * 128].rearrange("h d t -> d h t"), kst[sbuf][:],
                                 reads=[Tkst[sbuf]], writes=[TKT])
                          sp.dma(VV[:, :, lt0:lt0 + KS, :].rearrange("h p s c -> p h s c"), vst[sbuf][:],
                                 reads=[Tvst[sbuf]], writes=[TVV])
                  run_ways(list(range(NT)) + [("h", t) for t in range(NSLOT)], p1_tile_g)

                  ckpt("p1")
                  NG0 = NT // KS
                  PG = (PT + KS - 1) // KS
                  sdone = {}

                  def p1s_tile_g(ep, W):
                      e_, p = ep
                      gi = NG0 + e_ * PG + p // KS
                      sbuf, slot = gi % 2, p % KS
                      r0 = e_ * PAST + p * 128
                      sp.dma(W.cin[:], cckv[r0:r0 + 128, :], writes=[W.Tcin])
                      sp.dma(W.kin[:], ckpe[r0:r0 + 128, :], writes=[W.Tkin])
                      sp.dma(W.rk[:], rope_sp[:, p, :], writes=[W.Trk])
                      yield from kv_from_ckv_g(W, W.cin[:], W.Tcin, W.kin[:], W.Tkin, W.rk, W.Trk, (sbuf, slot))
                      sdone[gi] = sdone.get(gi, 0) + 1
                      p0 = (p // KS) * KS
                      ns = min(KS, PT - p0)
                      if sdone[gi] == ns:
                          sp.dma(KTs[e_, :, :, p0 * 128:(p0 + ns) * 128].rearrange("h d t -> d h t"),
                                 kst[sbuf][:, :, 0:ns * 128], reads=[Tkst[sbuf]], writes=[TKTs])
                          sp.dma(VVs[e_, :, :, p0:p0 + ns, :].rearrange("h p s c -> p h s c"),
                                 vst[sbuf][:, :, 0:ns, :], reads=[Tvst[sbuf]], writes=[TVVs])
                  run_ways([(e_, p) for e_ in range(2) for p in range(PT)], p1s_tile_g)
                  sbuf = (NG0 + 2 * PG) % 2

                  def p1n_g(_, W):
                      yield from front_end_g(W, x_s[:, :])
                      yield from ckv_from_x_g(W, own_row=0, o_ck=o_ckvs, o_kp=o_kpes)
                      yield from kv_from_ckv_g(W, W.ckv[:], W.Tckv, W.kpe[:], W.Tkpe, c_ropesn, Tc, (sbuf, 0))
                  run_ways([0], p1n_g)
                  for e_ in range(2):
                      sp.dma(KTs[e_, :, :, PT * 128:PT * 128 + 64].rearrange("h d t -> d h t"),
                             kst[sbuf][:, :, e_ * 64:(e_ + 1) * 64], reads=[Tkst[sbuf]], writes=[TKTs])
                      sp.dma(VVs[e_, :, 0:64, PT:PT + 1, :].rearrange("h p s c -> p h s c"),
                             vst[sbuf][e_ * 64:(e_ + 1) * 64, :, 0:1, :], reads=[Tvst[sbuf]], writes=[TVVs])

                  tk.barrier()
                  del ways[NWAYS:]
              ckpt("p1s")
              qTas = [sb(stA, f"qTa{i}", [128, H, QG * 128], BF16) for i in range(2)]
              TqTas = [T(f"qTa{i}") for i in range(2)]
              for i in range(2):
                  for h in range(H):
                      sp.dma(qTas[i][96:128, h, :], qmask[:, :], writes=[TqTas[i]])
              qTa, TqTa = qTas[0], TqTas[0]
              cTgs = [sb(stA, f"cTg{i}", [128, 4, QG * 128], BF16) for i in range(2)]
              TcTgs = [T(f"cTg{i}") for i in range(2)]
              cTg, TcTg = cTgs[0], TcTgs[0]
              attTs = [sb(stA, f"attT{i}", [64, H, QG * 128], BF16) for i in range(2)]
              TattTs = [T(f"attT{i}") for i in range(2)]
              attT, TattT = attTs[0], TattTs[0]
              for i in range(2):
                  pool.op(lambda e, i=i: e.memset(attTs[i][:], 0.0), writes=[TattTs[i]])
              UW = CK - 1 + QG * 128
              uTs = [sb(stA, f"uT{i}", [128, 4, UW], F32) for i in range(2)]
              TuTs = [T(f"uT{i}") for i in range(2)]
              uT, TuT = uTs[0], TuTs[0]
              acc = sb(stA, "acc", [128, 4, QG * 128], F32)
              Tacc = T("acc")
              sqc = sb(stA, "sqc", [128, 4, QG * 128], BF16)
              Tsqc = T("sqc")
              rsc = sb(stA, "rsc", [128, QG * 128], F32)
              Trsc = T("rsc")
              cpre, Tcpre = acc, Tacc
              NKB = 3
              kb = [sb(stA, f"kb{i}", [HD, KC * 128], BF16) for i in range(NKB)]
              Tkb = [T(f"kb{i}") for i in range(NKB)]
              vb = [sb(stA, f"vb{i}", [128, KC, 128], BF16) for i in range(NKB)]
              Tvb = [T(f"vb{i}") for i in range(NKB)]
              kd = [sb(stA, f"kd{i}", [128, QG * 128], BF16) for i in range(2)]
              Tkd = [T(f"kd{i}") for i in range(2)]
              for i in range(2):
                  sp.dma(kd[i][96:128, :], khot[:, :], writes=[Tkd[i]])
              vd = [sb(stA, f"vd{i}", [128, QG, 128], BF16) for i in range(2)]
              Tvd = [T(f"vd{i}") for i in range(2)]
              pb = [sb(stA, f"pb{i}", [128, 1024], BF16) for i in range(2)]
              Tpb = [T(f"pb{i}") for i in range(2)]
              rcp = sb(stA, "rcp", [64, 512], F32)
              Trcp = T("rcp")
              rcs = sb(stA, "rcs", [128, 512], F32)
              Trcs = T("rcs")
              xr = [sb(stA, f"xr{i}", [128, D], F32) for i in range(2)]
              Txr = [T(f"xr{i}") for i in range(2)]
              x1o, Tx1o = xr, Txr
              cvo = sb(stA, "cvo", [32, CC], F32)
              Tcvo = T("cvo")
              cnt = dict(kb=0, pb=0, x1=0, po=0, kd=0)

              def conv_module_g(uT_, TuT_, cT_, TcT_, ucol, ncol, ccol, bank=0):
                  PB_, TPB_ = pM[bank], TpM[bank]
                  for c in range(4):
                      dve.op(lambda e, c=c: e.tensor_scalar(out=acc[:, c, 0:ncol], in0=uT_[:, c, ucol - 30:ucol - 30 + ncol],
                                                            scalar1=c_cw[:, c, 0:1], scalar2=c_cb[:, c:c + 1],
                                                            op0=ALU.mult, op1=ALU.add),
                             reads=[TuT_, Tc], writes=[Tacc])
                      yield
                      for k in range(1, CK):
                          dve.op(lambda e, c=c, k=k: e.scalar_tensor_tensor(
                              out=acc[:, c, 0:ncol], in0=uT_[:, c, ucol - 30 + k:ucol - 30 + k + ncol],
                              scalar=c_cw[:, c, k:k + 1], in1=acc[:, c, 0:ncol], op0=ALU.mult, op1=ALU.add),
                              reads=[TuT_, Tc, Tacc], writes=[Tacc])
                          yield
                  act.op(lambda e: e.activation(out=sqc[:, :, 0:ncol], in_=acc[:, :, 0:ncol], func=AF.Square,
                                                scale=1.0 / math.sqrt(CC)), reads=[Tacc], writes=[Tsqc])
                  yield
                  for c in range(4):
                      pe.op(lambda e, c=c: e.matmul(PB_[:, 0:ncol], lhsT=c_ones[:], rhs=sqc[:, c, 0:ncol],
                                                    start=(c == 0), stop=(c == 3)), reads=[Tsqc, Tc], writes=[TPB_])
                  dve.op(lambda e: e.tensor_scalar(out=rsc[:, 0:ncol], in0=PB_[:, 0:ncol], scalar1=EPS, scalar2=None,
                                                   op0=ALU.add), reads=[TPB_], writes=[Trsc])
                  yield
                  act.op(lambda e: e.activation(out=rsc[:, 0:ncol], in_=rsc[:, 0:ncol], func=AF.Sqrt),
                         reads=[Trsc], writes=[Trsc])
                  yield
                  dve.op(lambda e: e.reciprocal(out=rsc[:, 0:ncol], in_=rsc[:, 0:ncol]), reads=[Trsc], writes=[Trsc])
                  yield
                  for c in range(4):
                      dve.op(lambda e, c=c: e.scalar_tensor_tensor(out=cpre[:, c, 0:ncol], in0=acc[:, c, 0:ncol],
                                                                   scalar=c_cg[:, c:c + 1], in1=rsc[:, 0:ncol],
                                                                   op0=ALU.mult, op1=ALU.mult),
                             reads=[Tacc, Trsc, Tc], writes=[Tcpre])
                      yield
                  act.op(lambda e: e.activation(out=cT_[:, :, ccol:ccol + ncol], in_=cpre[:, :, 0:ncol], func=AF.Silu),
                         reads=[Tcpre], writes=[TcT_])
                  yield

              def conv_module(ucol, ncol, ccol):
                  for _ in conv_module_g(uT, TuT, cTg, TcTg, ucol, ncol, ccol, bank=2):
                      pass

              def attention(ncols, segs, kt_src, vv_src, Tk, Tv, qc0=0, ksz_last=128, side=None):
                  items = []
                  for h in range(H):
                      po_i = cnt["po"] % 2
                      cnt["po"] += 1
                      PO, TPO = pO[po_i], TpO[po_i]
                      first = True
                      nseg = len(segs)
                      for si, (kind, t0, ntl, bcol, ksz) in enumerate(segs):
                          last_seg = si == nseg - 1
                          if kind == "d":
                              di = cnt["kd"] % 2
                              cnt["kd"] += 1
                              KD, TKD, VD, TVD = kd[di], Tkd[di], vd[di], Tvd[di]
                              loads = [(KD[0:HD, 0:ntl * 128], kt_src(h, t0, ntl), Tk, TKD),
                                       (VD[:, 0:ntl, :], vv_src(h, t0, ntl), Tv, TVD)]
                              chunks = [(t0, ntl, KD, TKD, VD, TVD, 128, loads)]
                          else:
                              chunks = []
                              for c0 in range(t0, t0 + ntl, KC):
                                  cn = min(KC, t0 + ntl - c0)
                                  bi = cnt["kb"] % NKB
                                  cnt["kb"] += 1
                                  KB, TKB, VB, TVB = kb[bi], Tkb[bi], vb[bi], Tvb[bi]
                                  lastc = (c0 + cn == t0 + ntl)
                                  kz_l = ksz if lastc else 128
                                  loads = [(KB[0:HD, 0:(cn - 1) * 128 + kz_l], kt_src(h, c0, cn, kz_l), Tk, TKB)]
                                  if kz_l == 128:
                                      loads.append((VB[:, 0:cn, :], vv_src(h, c0, cn), Tv, TVB))
                                  else:
                                      if cn > 1:
                                          loads.append((VB[:, 0:cn - 1, :], vv_src(h, c0, cn - 1), Tv, TVB))
                                      loads.append((VB[0:kz_l, cn - 1:cn, :], vv_src(h, c0 + cn - 1, 1, kz_l), Tv, TVB))
                                  chunks.append((c0, cn, KB, TKB, VB, TVB, HD, loads))
                          for ci_, (c0, cn, KB, TKB, VB, TVB, KR, loads) in enumerate(chunks):
                              for j in range(cn):
                                  lastt = (c0 + j == t0 + ntl - 1)
                                  kz = ksz if lastt else 128
                                  cs0 = j * 128 if kind == "d" else 0
                                  items.append(dict(h=h, PO=PO, TPO=TPO, KB=KB, TKB=TKB, VB=VB, TVB=TVB, KR=KR, j=j, kz=kz,
                                                    cs0=cs0, bcol=bcol, first=first, last=(last_seg and lastt),
                                                    loads=(loads if j == 0 else None), key=(h, si, ci_, kind)))
                                  first = False
                  units = []
                  i = 0
                  while i < len(items):
                      a = items[i]
                      if (i + 1 < len(items) and a["key"][3] == "n" and items[i + 1]["key"] == a["key"]
                              and a["kz"] == 128 and items[i + 1]["kz"] == 128):
                          units.append([a, items[i + 1]])
                          i += 2
                      else:
                          units.append([a])
                          i += 1

                  def emit_qk(u):
                      pi = cnt["pb"] % 2
                      cnt["pb"] += 1
                      PSp = pSS[:, pi * 1024:(pi + 1) * 1024].rearrange("p (i c) -> p i c", i=2)
                      PBp = pb[pi][:, :].rearrange("p (i c) -> p i c", i=2)
                      for i, it in enumerate(u):
                          if it["loads"]:
                              for (dst, src, Tsrc, Tdst) in it["loads"]:
                                  sp.dma(dst, src, reads=[Tsrc], writes=[Tdst])
                          it["PS"], it["PB"], it["TPS"], it["TPB"] = PSp, PBp, TpSS[pi], Tpb[pi]
                          pe.op(lambda e, it=it, i=i: e.matmul(
                              PSp[0:it["kz"], i, it["cs0"]:ncols],
                              lhsT=it["KB"][0:it["KR"], it["j"] * 128:it["j"] * 128 + it["kz"]],
                              rhs=qTa[0:it["KR"], it["h"], qc0 + it["cs0"]:qc0 + ncols], start=True, stop=True),
                              reads=[it["TKB"], TqTa], writes=[TpSS[pi]])

                  def emit_rest(u):
                      a = u[0]
                      kz, cs0, n = a["kz"], a["cs0"], len(u)
                      PSp, PBp, TPS, TPB = a["PS"], a["PB"], a["TPS"], a["TPB"]
                      bias_ap = c_zero[0:kz, 0:1] if a["bcol"] is None else c_bias[0:kz, a["bcol"]:a["bcol"] + 1]
                      act.op(lambda e: e.activation(out=PBp[0:kz, 0:n, cs0:ncols], in_=PSp[0:kz, 0:n, cs0:ncols],
                                                    func=AF.Exp, bias=bias_ap, scale=SCALE),
                             reads=[TPS, Tc], writes=[TPB])
                      for i, it in enumerate(u):
                          pe.op(lambda e, it=it, i=i: e.matmul(it["PO"][:, cs0:ncols], lhsT=it["VB"][0:kz, it["j"], :],
                                                               rhs=PBp[0:kz, i, cs0:ncols], start=it["first"], stop=it["last"]),
                                reads=[it["TVB"], TPB], writes=[it["TPO"]])
                          if it["last"]:
                              PO, TPO, h = it["PO"], it["TPO"], it["h"]
                              dve.op(lambda e, PO=PO: e.tensor_scalar(out=rcs[64:128, 0:ncols], in0=PO[64:128, 0:ncols],
                                                                      scalar1=1e-30, scalar2=None, op0=ALU.add),
                                     reads=[TPO], writes=[Trcs])
                              dve.op(lambda e: e.reciprocal(out=rcs[64:128, 0:ncols], in_=rcs[64:128, 0:ncols]),
                                     reads=[Trcs], writes=[Trcs])
                              dve.op(lambda e: e.tensor_copy(out=rcp[0:64, 0:ncols], in_=rcs[64:128, 0:ncols]),
                                     reads=[Trcs], writes=[Trcp])
                              dve.op(lambda e, PO=PO, h=h: e.tensor_tensor(out=attT[:, h, qc0:qc0 + ncols], in0=PO[0:64, 0:ncols],
                                                                           in1=rcp[0:64, 0:ncols], op=ALU.mult),
                                     reads=[TPO, Trcp], writes=[TattT])
                  nu = len(units)
                  if nu:
                      emit_qk(units[0])
                  for i in range(nu):
                      if i + 1 < nu:
                          emit_qk(units[i + 1])
                      emit_rest(units[i])
                      if side is not None:
                          for _ in range(max(1, -(-160 // nu))):
                              next(side, None)
                  if side is not None:
                      for _ in side:
                          pass

              def attention_halo(segs, side=None):
                  tiles = []
                  for (kind, t0, ntl, bcol, ksz) in segs:
                      for c0 in range(t0, t0 + ntl, 2):
                          cn = min(2, t0 + ntl - c0)
                          for j in range(cn):
                              tiles.append((c0, cn, j, bcol))
                  nt_ = len(tiles)
                  st_ = {}

                  def emit_qk(n):
                      c0, cn, j, bcol = tiles[n]
                      if j == 0:
                          bi = cnt["kb"] % NKB
                          cnt["kb"] += 1
                          KB3 = kb[bi][0:HD, 0:H * 256].rearrange("p (h t) -> p h t", h=H)
                          VB4 = vb[bi][:, 0:H * 2, :].rearrange("p (h s) c -> p h s c", h=H)
                          sp.dma(KB3[:, :, 0:cn * 128], KT[:, :, c0 * 128:(c0 + cn) * 128].rearrange("h d t -> d h t"),
                                 reads=[TKT], writes=[Tkb[bi]])
                          sp.dma(VB4[:, :, 0:cn, :], VV[:, :, c0:c0 + cn, :].rearrange("h p s c -> p h s c"),
                                 reads=[TVV], writes=[Tvb[bi]])
                          st_["cur"] = (KB3, VB4, Tkb[bi], Tvb[bi])
                      KB3, VB4, TKB, TVB = st_["cur"]
                      pi = cnt["pb"] % 2
                      cnt["pb"] += 1
                      PSp = pSS[:, pi * 1024:(pi + 1) * 1024]
                      for h in range(H):
                          pe.op(lambda e, h=h: e.matmul(PSp[:, h * 64:(h + 1) * 64], lhsT=KB3[:, h, j * 128:(j + 1) * 128],
                                                        rhs=qTa[0:HD, h, 64:128], start=True, stop=True,
                                                        skip_group_check=True),
                                reads=[TKB, TqTa], writes=[TpSS[pi]])
                      st_[n] = (PSp, pb[pi], TpSS[pi], Tpb[pi], VB4, TVB, j, bcol)

                  def emit_rest(n):
                      PSp, PB, TPS, TPB, VB4, TVB, j, bcol = st_.pop(n)
                      bias_ap = c_zero[:, 0:1] if bcol is None else c_bias[:, bcol:bcol + 1]
                      act.op(lambda e: e.activation(out=PB[:, 0:512], in_=PSp[:, 0:512], func=AF.Exp, bias=bias_ap, scale=SCALE),
                             reads=[TPS, Tc], writes=[TPB])
                      for h in range(H):
                          pe.op(lambda e, h=h: e.matmul(pO[0][:, h * 64:(h + 1) * 64], lhsT=VB4[:, h, j, :],
                                                        rhs=PB[:, h * 64:(h + 1) * 64],
                                                        start=(n == 0 and h == 0), stop=(n == nt_ - 1),
                                                        skip_group_check=True),
                                reads=[TVB, TPB], writes=[TpO[0]])
                  if nt_:
                      emit_qk(0)
                  for n in range(nt_):
                      if n + 1 < nt_:
                          emit_qk(n + 1)
                      emit_rest(n)
                      if side is not None:
                          for _ in range(max(2, -(-160 // nt_))):
                              next(side, None)
                  if side is not None:
                      for _ in side:
                          pass
                  dve.op(lambda e: e.tensor_scalar(out=rcs[64:128, 0:512], in0=pO[0][64:128, :], scalar1=1e-30,
                                                   scalar2=None, op0=ALU.add), reads=[TpO[0]], writes=[Trcs])
                  dve.op(lambda e: e.reciprocal(out=rcs[64:128, 0:512], in_=rcs[64:128, 0:512]), reads=[Trcs], writes=[Trcs])
                  dve.op(lambda e: e.tensor_copy(out=rcp[0:64, 0:512], in_=rcs[64:128, 0:512]), reads=[Trcs], writes=[Trcp])
                  dve.op(lambda e: e.tensor_tensor(
                      out=attT[:, :, 64:128], in0=pO[0][0:64, :].rearrange("p (h t) -> p h t", h=H),
                      in1=rcp[0:64, 0:512].rearrange("p (h t) -> p h t", h=H), op=ALU.mult),
                      reads=[TpO[0], Trcp], writes=[TattT])

              def out_proj_g(aT_, TaT_, cT_, TcT_, src_rows, col, x1_row, bank=0):
                  bi = cnt["x1"] % 2
                  cnt["x1"] += 1
                  PB_, TPB_ = pM[bank], TpM[bank]
                  sp.dma(xr[bi][:], src_rows, writes=[Txr[bi]])
                  for nb in range(2):
                      for h in range(H):
                          pe.op(lambda e, nb=nb, h=h: e.matmul(PB_[:, :], lhsT=aT_[:, h, col:col + 128],
                                                               rhs=wA_oa[:, h, nb * 512:(nb + 1) * 512],
                                                               start=(h == 0), stop=False),
                                reads=[TaT_, Tw], writes=[TPB_])
                      for c in range(4):
                          pe.op(lambda e, nb=nb, c=c: e.matmul(PB_[:, :], lhsT=cT_[:, c, col:col + 128],
                                                               rhs=wA_oc[:, c, nb * 512:(nb + 1) * 512],
                                                               start=False, stop=(c == 3)),
                                reads=[TcT_, Tw], writes=[TPB_])
                      dve.op(lambda e, nb=nb: e.tensor_tensor(out=x1o[bi][:, nb * 512:(nb + 1) * 512], in0=PB_[:, :],
                                                              in1=xr[bi][:, nb * 512:(nb + 1) * 512], op=ALU.add),
                             reads=[TPB_, Txr[bi]], writes=[Tx1o[bi]])
                      yield
                  sp.dma(X1[x1_row * 128:(x1_row + 1) * 128, :], x1o[bi][:], reads=[Tx1o[bi]], writes=[TX1])
                  yield

              def out_proj(src_rows, col, x1_row):
                  for _ in out_proj_g(attT, TattT, cTg, TcTg, src_rows, col, x1_row, bank=0):
                      pass

              def emit_conv_state(ub, Tub, col0, dst):
                  for c in range(4):
                      pe.op(lambda e, c=c: e.transpose(out=pM[2][0:32, c * 128:(c + 1) * 128], in_=ub[:, c, col0:col0 + 32],
                                                       identity=c_idf[:]), reads=[Tub, Tc], writes=[TpM[2]])
                  dve.op(lambda e: e.tensor_copy(out=cvo[:], in_=pM[2][0:32, :]), reads=[TpM[2]], writes=[Tcvo])
                  sp.dma(dst, cvo[:], reads=[Tcvo], writes=[Tout])

              def kt_p(h, t0, n, ksz=128):
                  return KT[h, :, t0 * 128:(t0 + n - 1) * 128 + ksz]

              def vv_p(h, t0, n, ksz=128):
                  return VV[h, 0:ksz, t0:t0 + n, :]


              groups = []
              for t in range(NSLOT):
                  groups.append((t, None))
                  for g in range(RT // QG):
                      groups.append((t, g))

              def load_group(n):
                  t, g = groups[n]
                  b = n % 2
                  ci0 = t * (RT + 1) + (0 if g is None else 1 + g * QG)
                  ntl = 1 if g is None else QG
                  sp.dma(qTas[b][0:HD, :, 0:ntl * 128], QT[:, :, ci0 * 128:(ci0 + ntl) * 128], reads=[TQT], writes=[TqTas[b]])
                  sp.dma(uTs[b][:, :, 0:CK - 1 + ntl * 128],
                         UT[:, :, 32 + ci0 * 128 - (CK - 1):32 + (ci0 + ntl) * 128], reads=[TUT], writes=[TuTs[b]])
              def conv_of(n):
                  ncol_ = 128 if groups[n][1] is None else QG * 128
                  return conv_module_g(uTs[n % 2], TuTs[n % 2], cTgs[n % 2], TcTgs[n % 2], CK - 1, ncol_, 0, bank=0)

              def outproj_of(n):
                  t, g = groups[n]
                  b = n % 2
                  if g is None:
                      yield from out_proj_g(attTs[b], TattTs[b], cTgs[b], TcTgs[b], x_halo[t * 128:(t + 1) * 128, :], 0,
                                            t * (RT + 1), bank=0)
                  else:
                      for i in range(QG):
                          lt = t * G + g * QG + i
                          yield from out_proj_g(attTs[b], TattTs[b], cTgs[b], TcTgs[b], x_all[lt * 128:(lt + 1) * 128, :],
                                                i * 128, t * (RT + 1) + 1 + g * QG + i, bank=0)

              def chain(*gens):
                  for g_ in gens:
                      if g_ is not None:
                          yield from g_
              load_group(0)
              for _ in conv_of(0):
                  pass
              for n, (t, g) in enumerate(groups):
                  base = t * G
                  qTa, TqTa, uT, TuT = qTas[n % 2], TqTas[n % 2], uTs[n % 2], TuTs[n % 2]
                  attT, TattT = attTs[n % 2], TattTs[n % 2]
                  nxt = None
                  if n + 1 < len(groups):
                      load_group(n + 1)
                      nxt = conv_of(n + 1)
                  side = chain(outproj_of(n - 1) if n > 0 else None, nxt)
                  if g is None:
                      segs = []
                      if t > 0:
                          segs.append(("n", 0, base, None, 128))
                      for r in range(1, NCPB):
                          segs.append(("n", base + r * RT, RT, r, 128))
                      attention_halo(segs, side=side)
                  else:
                      i0 = g * QG
                      segs = []
                      if base + i0 > 0:
                          segs.append(("n", 0, base + i0, None, 128))
                      segs.append(("d", base + i0, QG, None, 128))
                      for r in range(1, NCPB):
                          segs.append(("n", base + r * RT, RT, r, 128))
                      attention(QG * 128, segs, kt_p, vv_p, TKT, TVV, side=side)
                      if t == NSLOT - 1 and g == RT // QG - 1:
                          emit_conv_state(uT, TuT, UW - 32, o_conv[:, :])
              for _ in outproj_of(len(groups) - 1):
                  pass
              attT, TattT = attTs[0], TattTs[0]
              qTa, TqTa, uT, TuT = qTas[0], TqTas[0], uTs[0], TuTs[0]
              cTg, TcTg = cTgs[0], TcTgs[0]

              ckpt("p2")
              uS = [sb(stA, f"uS{i}", [128, 4, CK - 1 + 64], F32) for i in range(2)]
              TuS = [T(f"uS{i}") for i in range(2)]
              for e_ in range(2):
                  sp.dma(uS[e_][:, :, 0:CK - 1], sconvT[:, e_, :, :], writes=[TuS[e_]])
              run_ways([0], lambda _, W: tile_front_A_g(W, x_s[:, :], c_ropesn, Tc, 0,
                                                        [(0, 64, (uS[0], TuS[0], CK - 1)), (64, 64, (uS[1], TuS[1], CK - 1))]))
              for e_ in range(2):
                  dve.op(lambda e, e_=e_: e.tensor_copy(out=uT[:, :, 0:CK - 1 + 64], in_=uS[e_][:, :, :]),
                         reads=[TuS[e_]], writes=[TuT])
                  conv_module(CK - 1, 64, e_ * 64)
                  emit_conv_state(uS[e_], TuS[e_], CK - 1 + 64 - 32, o_convs[e_, :, :])

                  def kt_s(h, t0, n, ksz=128, e_=e_):
                      return KTs[e_, h, :, t0 * 128:(t0 + n - 1) * 128 + ksz]

                  def vv_s(h, t0, n, ksz=128, e_=e_):
                      return VVs[e_, h, 0:ksz, t0:t0 + n, :]
                  attention(64, [("n", 0, PT + 1, None, 64)], kt_s, vv_s, TKTs, TVVs, qc0=e_ * 64)
              out_proj(x_s[:, :], 0, NX1 - 1)
              tk.barrier()
              ckpt("p2s")

          with ExitStack() as stB:
              wB_up = sb(stB, "wB_up", [128, 8, 2 * DFF], BF16)
              wB_dn = sb(stB, "wB_dn", [128, NFC, D], BF16)
              with ExitStack() as stW:
                  load_weight(stW, wB_up, w_up, 8, 2 * DFF, c_gffn, name="up")
                  load_weight(stW, wB_dn, w_down, NFC, D, None, name="dn")
                  tk.barrier()
              QB = QG
              NB_ = QB * 128
              xtB = sb(stB, "xtB", [128, QB, D], F32)
              TxtB = [T(f"xtB{j}") for j in range(QB)]
              msB = sb(stB, "msB", [128, QB], F32)
              TmsB = T("msB")
              xnB = sb(stB, "xnB", [128, D], BF16)
              TxnB = T("xnB")
              xnTB = sb(stB, "xnTB", [128, 8, NB_], BF16)
              TxnTB = T("xnTB")
              hT = sb(stB, "hT", [128, NFC, NB_], BF16)
              ThT = T("hT")
              AW = NB_ + 8
              aTc = [sb(stB, f"aTc{i}", [128, AW], F32) for i in range(2)]
              TaTc = [T(f"aTc{i}") for i in range(2)]
              accB = [sb(stB, f"accB{i}", [128, AW], F32) for i in range(2)]
              TaccB = [T(f"accB{i}") for i in range(2)]
              yo = [sb(stB, f"yo{i}", [128, D], F32) for i in range(2)]
              Tyo = [T(f"yo{i}") for i in range(2)]
              arow = [sb(stB, f"arow{i}", [128, 512], F32) for i in range(2)]
              Tarow = [T(f"arow{i}") for i in range(2)]
              hist = sb(stB, "hist", [128, NFC, 2], F32)
              Thist = T("hist")
              hnew = sb(stB, "hnew", [128, NFC, 2], F32)
              Thnew = T("hnew")
              Toutb = T("outsB")
              Abank = [(pM[0], TpM[0]), (pS[3], TpS[3])]
              Gbank = [(pS[0], TpS[0]), (pS[1], TpS[1])]
              Dbank = [(pO[0], TpO[0]), (pO[1], TpO[1])]
              cb_ = dict(ab=0, gb=0, db=0, yo=0, ar=0)

              def ffn_batch(x1_rows, subs, outs, a_rows_dst=None):
                  nt = len(x1_rows)
                  N = nt * 128
                  halo = outs is None
                  for j, r in enumerate(x1_rows):
                      sp.dma(xtB[:, j, :], X1[r * 128:(r + 1) * 128, :], writes=[TxtB[j]])
                      act.op(lambda e, j=j: e.activation(out=xnB[:], in_=xtB[:, j, :], func=AF.Square, scale=1.0 / math.sqrt(D),
                                                         accum_out=msB[:, j:j + 1]), reads=[TxtB[j]], writes=[TxnB, TmsB])
                  rstd_from_msq(None, (msB[:, 0:nt], TmsB), nt)
                  for j in range(nt):
                      dve.op(lambda e, j=j: e.tensor_scalar(out=xnB[:], in0=xtB[:, j, :], scalar1=msB[:, j:j + 1], scalar2=None,
                                                            op0=ALU.mult), reads=[TxtB[j], TmsB], writes=[TxnB])
                      for k in range(8):
                          pe.op(lambda e, k=k: e.transpose(out=pT[:, k * 128:(k + 1) * 128], in_=xnB[:, k * 128:(k + 1) * 128],
                                                           identity=c_idb[:]), reads=[TxnB, Tc], writes=[TpT])
                      act.op(lambda e, j=j: e.activation(out=xnTB[:, :, j * 128:(j + 1) * 128],
                                                         in_=pT[:, :].rearrange("p (k t) -> p k t", k=8), func=AF.Copy),
                             reads=[TpT], writes=[TxnTB])
                  W_ = N + 2 * len(subs)
                  state = {}

                  def stage1(c):
                      ai = cb_["ab"] % 2
                      cb_["ab"] += 1
                      PA, TPA = Abank[ai]
                      AT, TAT, AC, TAC = aTc[ai], TaTc[ai], accB[ai], TaccB[ai]
                      for k in range(8):
                          pe.op(lambda e, k=k: e.matmul(PA[:, 0:N], lhsT=wB_up[:, k, c * 128:(c + 1) * 128], rhs=xnTB[:, k, 0:N],
                                                        start=(k == 0), stop=(k == 7)), reads=[TxnTB, Tw], writes=[TPA])
                      if not halo:
                          gi = cb_["gb"] % 2
                          cb_["gb"] += 1
                          PG, TPG = Gbank[gi]
                          col = DFF + c * 128
                          for k in range(8):
                              pe.op(lambda e, k=k: e.matmul(PG[:, 0:N], lhsT=wB_up[:, k, col:col + 128], rhs=xnTB[:, k, 0:N],
                                                            start=(k == 0), stop=(k == 7)), reads=[TxnTB, Tw], writes=[TPG])
                      else:
                          PG = TPG = None
                      for i, (tok0, ntok, hap, Th) in enumerate(subs):
                          w0 = tok0 + 2 * i
                          act.op(lambda e, w0=w0, tok0=tok0, ntok=ntok: e.activation(
                              out=AT[:, w0 + 2:w0 + 2 + ntok], in_=PA[:, tok0:tok0 + ntok], func=AF.Copy),
                              reads=[TPA], writes=[TAT])
                          if not halo:
                              act.op(lambda e, w0=w0, tok0=tok0, ntok=ntok: e.activation(
                                  out=AC[:, w0:w0 + ntok], in_=PA[:, tok0:tok0 + ntok], func=AF.Identity,
                                  scale=c_fw[:, c, 2:3], bias=c_fb[:, c:c + 1]), reads=[TPA, Tc], writes=[TAC])
                              dve.op(lambda e, w0=w0, hap=hap: e.tensor_copy(out=AT[:, w0:w0 + 2], in_=hap[:, c, :]),
                                     reads=[Th], writes=[TAT])
                          if i == len(subs) - 1:
                              dve.op(lambda e, w0=w0, ntok=ntok: e.tensor_copy(out=hnew[:, c, :], in_=AT[:, w0 + ntok:w0 + ntok + 2]),
                                     reads=[TAT], writes=[Thnew])
                      state[c] = (AT, TAT, AC, TAC, PG, TPG)

                  def stage2(c):
                      AT, TAT, AC, TAC, PG, TPG = state.pop(c)
                      si = cb_["ab"] % 2
                      SL, TSL = AC, TAC
                      dve.op(lambda e: e.scalar_tensor_tensor(out=AC[:, 0:W_ - 2], in0=AT[:, 0:W_ - 2], scalar=c_fw[:, c, 0:1],
                                                              in1=AC[:, 0:W_ - 2], op0=ALU.mult, op1=ALU.add),
                             reads=[TAT, TAC, Tc], writes=[TAC])
                      dve.op(lambda e: e.scalar_tensor_tensor(out=AC[:, 0:W_ - 2], in0=AT[:, 1:W_ - 1], scalar=c_fw[:, c, 1:2],
                                                              in1=AC[:, 0:W_ - 2], op0=ALU.mult, op1=ALU.add),
                             reads=[TAT, TAC, Tc], writes=[TAC])
                      act.op(lambda e: e.activation(out=SL[:, 0:W_ - 2], in_=AC[:, 0:W_ - 2], func=AF.Silu),
                             reads=[TAC], writes=[TSL])
                      for i, (tok0, ntok, hap, Th) in enumerate(subs):
                          w0 = tok0 + 2 * i
                          dve.op(lambda e, w0=w0, tok0=tok0, ntok=ntok: e.tensor_tensor(
                              out=hT[:, c, tok0:tok0 + ntok], in0=PG[:, tok0:tok0 + ntok], in1=SL[:, w0:w0 + ntok], op=ALU.mult),
                              reads=[TPG, TSL], writes=[ThT])

                  if len(subs) > 1:
                      for i in range(2):
                          dve.op(lambda e, i=i: e.memset(accB[i][:], 0.0), writes=[TaccB[i]])
                  for c in range(NFC + 1):
                      if c < NFC:
                          stage1(c)
                      if c >= 1 and not halo:
                          stage2(c - 1)
                  dve.op(lambda e: e.tensor_copy(out=hist[:], in_=hnew[:]), reads=[Thnew], writes=[Thist])
                  if halo:
                      return
                  if a_rows_dst is not None:
                      jl = nt - 1
                      for n0 in range(0, DFF, 512):
                          nw = min(512, DFF - n0)
                          PA, TPA = pS[2], TpS[2]
                          ri = cb_["ar"] % 2
                          cb_["ar"] += 1
                          for k in range(8):
                              pe.op(lambda e, k=k, n0=n0, nw=nw: e.matmul(
                                  PA[:, 0:nw], lhsT=xnTB[:, k, jl * 128:(jl + 1) * 128], rhs=wB_up[:, k, n0:n0 + nw],
                                  start=(k == 0), stop=(k == 7)), reads=[TxnTB, Tw], writes=[TPA])
                          dve.op(lambda e, nw=nw, ri=ri: e.tensor_copy(out=arow[ri][:, 0:nw], in_=PA[:, 0:nw]),
                                 reads=[TPA], writes=[Tarow[ri]])
                          for (r0, dst) in a_rows_dst:
                              sp.dma(dst[:, n0:n0 + nw], arow[ri][r0:r0 + 32, 0:nw], reads=[Tarow[ri]], writes=[Toutb])
                  for j in range(nt):
                      yi = cb_["yo"] % 2
                      cb_["yo"] += 1
                      for nb in range(2):
                          PD, TPD = Dbank[cb_["db"] % 2]
                          cb_["db"] += 1
                          for c in range(NFC):
                              pe.op(lambda e, nb=nb, c=c, j=j, PD=PD: e.matmul(
                                  PD[:, :], lhsT=hT[:, c, j * 128:(j + 1) * 128], rhs=wB_dn[:, c, nb * 512:(nb + 1) * 512],
                                  start=(c == 0), stop=(c == NFC - 1)), reads=[ThT, Tw], writes=[TPD])
                          dve.op(lambda e, nb=nb, j=j, PD=PD, yi=yi: e.tensor_tensor(
                              out=yo[yi][:, nb * 512:(nb + 1) * 512], in0=PD[:, :], in1=xtB[:, j, nb * 512:(nb + 1) * 512],
                              op=ALU.add), reads=[TPD, TxtB[j]], writes=[Tyo[yi]])
                      sp.dma(outs[j], yo[yi][:], reads=[Tyo[yi]], writes=[Toutb])

              for t in range(NSLOT):
                  r0 = t * (RT + 1)
                  ffn_batch([r0], [(0, 128, None, None)], None)
                  dve.op(lambda e, t=t: e.tensor_scalar(out=hist[:], in0=hist[:], scalar1=c_hflag[:, t:t + 1],
                                                        scalar2=None, op0=ALU.mult), reads=[Thist, Tc], writes=[Thist])
                  for i0 in range(0, RT, QB):
                      last = (t == NSLOT - 1 and i0 + QB == RT)
                      ffn_batch([r0 + 1 + i0 + i for i in range(QB)], [(0, NB_, hist, Thist)],
                                [o_y[(t * RT + i0 + i) * 128:(t * RT + i0 + i + 1) * 128, :] for i in range(QB)],
                                a_rows_dst=[(96, o_ffn)] if last else None)
              hs = [sb(stB, f"hs{i}", [128, NFC, 2], F32) for i in range(2)]
              Ths = [T(f"hs{i}") for i in range(2)]
              for e_ in range(2):
                  sp.dma(hs[e_][:], sffnT[:, e_, :, :], writes=[Ths[e_]])
              ffn_batch([NX1 - 1], [(0, 64, hs[0], Ths[0]), (64, 64, hs[1], Ths[1])], [o_ys[:, :]],
                        a_rows_dst=[(32, o_ffns[0]), (96, o_ffns[1])])
              tk.barrier()
          print(f"[build] sems={tk.nsem} waits={tk.nwaits} insts={tk.ninst}", flush=True)
    except _Stop:
        pass
    return nc


def _rope_tab(pos):
    inv = (1.0 / (10000.0 ** (np.arange(0, RD, 2, dtype=np.float32) / np.float32(RD)))).astype(np.float32)
    ang = pos.astype(np.float32)[:, None] * inv[None, :]
    return np.concatenate([np.cos(ang.astype(np.float64)), np.sin(ang.astype(np.float64))], axis=1).astype(np.float32)


def _run(inputs, cfg):
    SEQ, PAST, RT = cfg["SEQ"], cfg["PAST"], cfg["RT"]
    NT = SEQ // 128
    G = NCPB * RT
    NSLOT = NT // G
    QG = min(4, RT)
    PT = PAST // 128
    f32 = np.float32
    bf = ml_dtypes.bfloat16
    g = {k: np.asarray(v) for k, v in inputs.items()}
    xp, xs = g["x_prompt"], g["x_sample"]
    B = xp.shape[0]
    assert B * NCPB == 8 and xs.shape[0] == 16

    def chunked(v, n):
        return np.ascontiguousarray(v.reshape(n, 128).T).astype(f32)

    def bc(v):
        return np.ascontiguousarray(np.broadcast_to(v[None, :], (128, v.shape[0]))).astype(f32)
    common = {
        "w_in": g["w_in"][0], "w_uq": g["w_uq"][0], "w_ukv": g["w_ukv"][0], "w_out": g["w_out"][0],
        "w_up": g["w_up"][0], "w_down": g["w_down"][0],
        "g_attn": chunked(g["attn_norm"][0], 8), "g_q": chunked(g["q_norm"][0], 3),
        "g_ffn": chunked(g["ffn_norm"][0], 8), "g_kv": bc(g["kv_norm"][0]),
        "g_hq": bc(g["qk_norm_q"][0]), "g_hk": bc(g["qk_norm_k"][0]),
        "cw": np.ascontiguousarray(g["conv_w"][0].T.reshape(4, 128, CK).transpose(1, 0, 2)),
        "cb": chunked(g["conv_b"][0], 4), "cg": chunked(g["conv_norm"][0], 4),
        "fw": np.ascontiguousarray(g["ffn_conv_w"][0].T.reshape(NFC, 128, 3).transpose(1, 0, 2)),
        "fb": chunked(g["ffn_conv_b"][0], NFC),
        "identb": np.eye(128, dtype=f32).astype(bf), "identf": np.eye(128, dtype=f32),
        "onesb": np.ones((128, 128), f32).astype(bf),
    }
    nch = QG * 2
    kh = np.zeros((32, QG * 128), f32)
    qm = np.zeros((32, QG * 128), f32)
    for c in range(nch):
        kh[c, c * 64:(c + 1) * 64] = 1.0
        qm[c, :c * 64] = NEG
    common["khot"] = kh.astype(bf)
    common["qmask"] = qm.astype(bf)
    common["rope_sp"] = np.ascontiguousarray(
        _rope_tab(np.arange(max(PT, 1) * 128)).reshape(max(PT, 1), 128, 32).transpose(1, 0, 2))
    common["rope_sn"] = _rope_tab(PAST + (np.arange(128) % 64))

    in_maps, metas = [], []
    for core in range(8):
        b, j = divmod(core, NCPB)
        others = [r for r in range(NCPB) if r != j]
        order = [j] + others
        gt = np.array([t * G + order[r] * RT + i for t in range(NSLOT) for r in range(NCPB) for i in range(RT)])
        tok = (gt[:, None] * 128 + np.arange(128)[None, :]).reshape(-1)
        m = dict(common)
        m["x_all"] = np.ascontiguousarray(xp[b][tok])
        xh = np.zeros((NSLOT, 128, D), f32)
        hf = np.zeros((128, NSLOT), f32)
        rh = np.zeros((NSLOT, 128, 32), f32)
        for t in range(NSLOT):
            ht = t * G + j * RT - 1
            if ht >= 0:
                xh[t] = xp[b, ht * 128:(ht + 1) * 128]
                hf[:, t] = 1.0
                rh[t] = _rope_tab(ht * 128 + np.arange(128))
        m["x_halo"] = xh.reshape(NSLOT * 128, D)
        m["hflag"] = hf
        m["rope_h"] = np.ascontiguousarray(rh.transpose(1, 0, 2))
        m["rope_k"] = np.ascontiguousarray(_rope_tab(tok).reshape(NT, 128, 32).transpose(1, 0, 2))
        bt = np.zeros((128, NCPB), f32)
        for r in range(1, NCPB):
            bt[:, r] = 0.0 if others[r - 1] < j else NEG
        m["biast"] = bt
        e0 = 2 * core
        m["x_s"] = np.ascontiguousarray(xs[e0:e0 + 2].reshape(128, D))
        m["cckv"] = np.ascontiguousarray(g["cache_ckv"][0, e0:e0 + 2].reshape(2 * PAST, KVL))
        m["ckpe"] = np.ascontiguousarray(g["cache_kpe"][0, e0:e0 + 2].reshape(2 * PAST, RD))
        sc = g["state_conv"][0, e0:e0 + 2]
        m["sconvT"] = np.ascontiguousarray(sc.transpose(2, 0, 1).reshape(4, 128, 2, CK - 1).transpose(1, 2, 0, 3))
        sf = g["state_ffn_conv"][0, e0:e0 + 2]
        m["sffnT"] = np.ascontiguousarray(sf.transpose(2, 0, 1).reshape(NFC, 128, 2, 2).transpose(1, 2, 0, 3))
        in_maps.append(m)
        own_tok = (np.array([t * G + j * RT + i for t in range(NSLOT) for i in range(RT)])[:, None] * 128
                   + np.arange(128)[None, :]).reshape(-1)
        metas.append((b, j, own_tok))

    nc = build(cfg)
    res = run_bass_kernel_spmd(nc, in_maps, core_ids=list(range(8)))
    R = res.results

    y_p = np.zeros((B, SEQ, D), f32)
    ckv_p = np.zeros((1, B, SEQ, KVL), f32)
    kpe_p = np.zeros((1, B, SEQ, RD), f32)
    conv_p = np.zeros((1, B, CK - 1, CC), f32)
    ffn_p = np.zeros((1, B, 2, DFF), f32)
    y_s = np.zeros((16, 64, D), f32)
    ckv_s = np.zeros((1, 16, 64, KVL), f32)
    kpe_s = np.zeros((1, 16, 64, RD), f32)
    conv_s = np.zeros((1, 16, CK - 1, CC), f32)
    ffn_s = np.zeros((1, 16, 2, DFF), f32)
    for core in range(8):
        b, j, own_tok = metas[core]
        r = R[core]
        y_p[b, own_tok] = r["o_y"]
        ckv_p[0, b, own_tok] = r["o_ckv"]
        kpe_p[0, b, own_tok] = r["o_kpe"]
        if j == NCPB - 1:
            conv_p[0, b] = r["o_conv"][2:32]
            ffn_p[0, b] = r["o_ffn"][30:32]
        e0 = 2 * core
        y_s[e0:e0 + 2] = r["o_ys"].reshape(2, 64, D)
        ckv_s[0, e0:e0 + 2] = r["o_ckvs"].reshape(2, 64, KVL)
        kpe_s[0, e0:e0 + 2] = r["o_kpes"].reshape(2, 64, RD)
        conv_s[0, e0:e0 + 2] = r["o_convs"][:, 2:32]
        ffn_s[0, e0:e0 + 2] = r["o_ffns"][:, 30:32]
    return (y_p, y_s, ckv_p, kpe_p, conv_p, ffn_p, ckv_s, kpe_s, conv_s, ffn_s)


def kernel(**inputs):
    return _run(inputs, CFG_FULL)
```

```python
import math
from contextlib import ExitStack

import numpy as np
import ml_dtypes

import concourse.bass as bass
import concourse.mybir as mybir
from concourse.bass_utils import run_bass_kernel_spmd

F32 = mybir.dt.float32
BF16 = mybir.dt.bfloat16
ALU = mybir.AluOpType
AF = mybir.ActivationFunctionType
AX = mybir.AxisListType

D = 1024
QL, KVL, RD, CC = 384, 256, 32, 512
H, HD, NOPE, VD = 8, 96, 64, 64
INW = QL + KVL + RD + 2 * CC
DFF = 2816
NFC = DFF // 128
CK = 31
EPS = 1e-6
SCALE = HD ** -0.5
NEG = -30000.0
NCPB = 4
KC = 16

CFG_FULL = dict(SEQ=16384, PAST=2048, RT=8)


class T:
    __slots__ = ("name", "w", "r", "dsem", "dcnt", "excl")

    def __init__(self, name, excl=False):
        self.name = name
        self.excl = excl
        self.w = None
        self.r = {}
        self.dsem = None
        self.dcnt = 0


class Eng:
    ROT = 30000

    def __init__(self, trk, eng, name):
        self.trk, self.eng, self.name = trk, eng, name
        self.sem = trk.new_sem(name)
        self.cnt = 0
        self.seen = {}

    def _wait(self, sem, val):
        if self.seen.get(sem, 0) >= val:
            return
        self.eng.wait_ge(sem, val)
        self.seen[sem] = val
        self.trk.nwaits += 1

    def _deps(self, reads, writes):
        need = {}

        def add(p, same_ok):
            if p is None:
                return
            sem, val = p
            if sem is self.sem and same_ok and self.name == "pe":
                return
            if need.get(sem, 0) < val:
                need[sem] = val
        for t in reads:
            add(t.w, False)
        for t in writes:
            add(t.w, True)
            for sem, val in t.r.items():
                add((sem, val), True)
        for sem, val in need.items():
            self._wait(sem, val)

    def op(self, fn, reads=(), writes=()):
        ex = [t for t in reads if t.excl and t not in writes]
        if ex:
            reads = [t for t in reads if not t.excl or t in writes]
            writes = list(writes) + ex
        self._deps(reads, writes)
        if self.cnt >= self.ROT:
            self.sem = self.trk.new_sem(self.name)
            self.cnt = 0
        inst = fn(self.eng)
        self.cnt += 1
        inst.then_inc(self.sem, 1)
        self.trk.ninst += 1
        for t in reads:
            if t.r.get(self.sem, 0) < self.cnt:
                t.r[self.sem] = self.cnt
        for t in writes:
            t.w = (self.sem, self.cnt)
            t.r = {}
        return inst

    def dma(self, out, in_, reads=(), writes=()):
        self._deps(reads, writes)
        tw = writes[0]
        if tw.dsem is None:
            tw.dsem = self.trk.new_sem("d_" + tw.name)
            self.trk.dts.append(tw)
        inst = self.eng.dma_start(out=out, in_=in_)
        inst.then_inc(tw.dsem, 16)
        tw.dcnt += 16
        self.trk.ninst += 1
        for t in reads:
            if t.r.get(tw.dsem, 0) < tw.dcnt:
                t.r[tw.dsem] = tw.dcnt
        tw.w = (tw.dsem, tw.dcnt)
        tw.r = {}
        return inst

    def wait_for(self, t):
        if t.w is not None:
            self._wait(*t.w)


class Tracker:
    def __init__(self, nc, stack):
        self.nc, self.stack = nc, stack
        self.nsem = 0
        self.nwaits = 0
        self.ninst = 0
        self.dts = []
        self.pe = Eng(self, nc.tensor, "pe")
        self.act = Eng(self, nc.scalar, "act")
        self.dve = Eng(self, nc.vector, "dve")
        self.pool = Eng(self, nc.gpsimd, "pool")
        self.sp = Eng(self, nc.sync, "sp")
        self.engs = [self.pe, self.act, self.dve, self.pool, self.sp]

    def new_sem(self, name):
        self.nsem += 1
        return self.stack.enter_context(self.nc.semaphore(f"s{self.nsem}_{name}"))

    def barrier(self):
        pts = [(e.sem, e.cnt) for e in self.engs if e.cnt > 0]
        pts += [(t.dsem, t.dcnt) for t in self.dts if t.dcnt > 0]
        for e in self.engs:
            for sem, val in pts:
                e._wait(sem, val)


def build(cfg):
    SEQ, PAST, RT = cfg["SEQ"], cfg["PAST"], cfg["RT"]
    NT = SEQ // 128
    G = NCPB * RT
    NSLOT = NT // G
    assert NSLOT * G == NT
    QG = min(4, RT)
    assert RT % QG == 0
    NOWN = NSLOT * RT
    PT = PAST // 128
    NX1 = NSLOT * (RT + 1) + 1

    nc = bass.Bass("TRN2", target_bir_lowering=False)

    def din(name, shape, dt=F32):
        return nc.dram_tensor(name, list(shape), dt, kind="ExternalInput").ap()

    def dout(name, shape, dt=F32):
        return nc.dram_tensor(name, list(shape), dt, kind="ExternalOutput").ap()

    def dscr(name, shape, dt):
        return nc.dram_tensor(name, list(shape), dt, kind="Internal").ap()

    x_all = din("x_all", [NT * 128, D])
    x_halo = din("x_halo", [NSLOT * 128, D])
    x_s = din("x_s", [128, D])
    cckv = din("cckv", [2 * PAST, KVL])
    ckpe = din("ckpe", [2 * PAST, RD])
    sconvT = din("sconvT", [128, 2, 4, CK - 1])
    sffnT = din("sffnT", [128, 2, NFC, 2])
    rope_k = din("rope_k", [128, NT, 32])
    rope_h = din("rope_h", [128, NSLOT, 32])
    rope_sp = din("rope_sp", [128, max(PT, 1), 32])
    rope_sn = din("rope_sn", [128, 32])
    hflag = din("hflag", [128, NSLOT])
    biast = din("biast", [128, NCPB])
    w_in = din("w_in", [D, INW])
    w_uq = din("w_uq", [QL, H * HD])
    w_ukv = din("w_ukv", [KVL, H * 128])
    w_out = din("w_out", [D, D])
    w_up = din("w_up", [D, 2 * DFF])
    w_down = din("w_down", [DFF, D])
    g_attn = din("g_attn", [128, 8])
    g_q = din("g_q", [128, 3])
    g_ffn = din("g_ffn", [128, 8])
    g_kv = din("g_kv", [128, KVL])
    g_hq = din("g_hq", [128, HD])
    g_hk = din("g_hk", [128, HD])
    cw = din("cw", [128, 4, CK])
    cb = din("cb", [128, 4])
    cg = din("cg", [128, 4])
    fw = din("fw", [128, NFC, 3])
    fb = din("fb", [128, NFC])
    identb = din("identb", [128, 128], BF16)
    identf = din("identf", [128, 128])
    onesb = din("onesb", [128, 128], BF16)
    khot = din("khot", [32, QG * 128], BF16)
    qmask = din("qmask", [32, QG * 128], BF16)

    o_y = dout("o_y", [NOWN * 128, D])
    o_ckv = dout("o_ckv", [NOWN * 128, KVL])
    o_kpe = dout("o_kpe", [NOWN * 128, RD])
    o_conv = dout("o_conv", [32, CC])
    o_ffn = dout("o_ffn", [32, DFF])
    o_ys = dout("o_ys", [128, D])
    o_ckvs = dout("o_ckvs", [128, KVL])
    o_kpes = dout("o_kpes", [128, RD])
    o_convs = dout("o_convs", [2, 32, CC])
    o_ffns = dout("o_ffns", [2, 32, DFF])

    KT = dscr("KT", [H, HD, NT * 128], BF16)
    VV = dscr("VV", [H, 128, NT, 128], BF16)
    KTs = dscr("KTs", [2, H, HD, (PT + 1) * 128], BF16)
    VVs = dscr("VVs", [2, H, 128, PT + 1, 128], BF16)
    X1 = dscr("X1", [NX1 * 128, D], F32)
    NCI = NSLOT * (RT + 1)
    QT = dscr("QT", [HD, H, NCI * 128], BF16)
    UT = dscr("UT", [128, 4, 32 + NCI * 128], F32)

    class _Stop(Exception):
        pass

    def ckpt(name):
        if cfg.get("STOP") == name:
            tk.barrier()
            raise _Stop()
    try:
      with ExitStack() as top:
          tk = Tracker(nc, top)
          pe, act, dve, pool, sp = tk.pe, tk.act, tk.dve, tk.pool, tk.sp

          def sb(st, name, shape, dt):
              return st.enter_context(nc.sbuf_tensor(name, list(shape), dt))

          def ps(st, name, shape, dt):
              return st.enter_context(nc.psum_tensor(name, list(shape), dt))

          pSS = ps(top, "pSS", [128, 2048], F32)
          pS = [pSS[:, i * 512:(i + 1) * 512] for i in range(4)]
          TpS = [T(f"pS{i}", excl=True) for i in range(4)]
          TpSS = [T(f"pSS{i}", excl=True) for i in range(2)]
          pOO = ps(top, "pOO", [128, 1024], F32)
          pO = [pOO[:, i * 512:(i + 1) * 512] for i in range(2)]
          TpO = [T(f"pO{i}", excl=True) for i in range(2)]
          pM0 = ps(top, "pM0", [128, 512], F32)
          pM = [pM0[:, :], pO[0], pO[1]]
          TpM = [T("pM0", excl=True), TpO[0], TpO[1]]
          pT = ps(top, "pT", [128, 1024], BF16)
          TpT = T("pT", excl=True)

          cst = {}
          Tc = T("consts")

          def cload(name, src, shape, dt=F32):
              t = sb(top, "c_" + name, shape, dt)
              sp.dma(t[:], src, writes=[Tc])
              cst[name] = t
              return t
          c_idb = cload("idb", identb[:, :], [128, 128], BF16)
          c_idf = cload("idf", identf[:, :], [128, 128])
          c_ones = cload("ones", onesb[:, :], [128, 128], BF16)
          c_gkv = cload("gkv", g_kv[:, :], [128, KVL])
          c_ghq = cload("ghq", g_hq[:, :], [128, HD])
          c_ghk = cload("ghk", g_hk[:, :], [128, HD])
          c_cw = cload("cw", cw[:, :, :], [128, 4, CK])
          c_cb = cload("cb", cb[:, :], [128, 4])
          c_cg = cload("cg", cg[:, :], [128, 4])
          c_fw = cload("fw", fw[:, :, :], [128, NFC, 3])
          c_fb = cload("fb", fb[:, :], [128, NFC])
          c_hflag = cload("hflag", hflag[:, :], [128, NSLOT])
          c_bias = cload("bias", biast[:, :], [128, NCPB])
          c_gattn = cload("gattn", g_attn[:, :], [128, 8])
          c_gq = cload("gq", g_q[:, :], [128, 3])
          c_gffn = cload("gffn", g_ffn[:, :], [128, 8])
          c_ropeh = cload("ropeh", rope_h[:, :, :], [128, NSLOT, 32])
          c_ropesn = cload("ropesn", rope_sn[:, :], [128, 32])
          c_zero = sb(top, "c_zero", [128, 1], F32)
          dve.op(lambda e: e.memset(c_zero[:], 0.0), writes=[Tc])
          c_eps = sb(top, "c_eps", [128, 1], F32)
          dve.op(lambda e: e.memset(c_eps[:], EPS), writes=[Tc])

          def load_weight(st, dst, src2d, nk, ncols, gain, kp=128, name="w"):
              CH = 2048
              stg = [sb(st, f"stg_{name}{i}", [128, CH], F32) for i in range(2)]
              Tst = [T(f"stg_{name}{i}") for i in range(2)]
              n = 0
              for k in range(nk):
                  for c0 in range(0, ncols, CH):
                      cwid = min(CH, ncols - c0)
                      b = n % 2
                      n += 1
                      sp.dma(stg[b][0:kp, 0:cwid], src2d[k * kp:(k + 1) * kp, c0:c0 + cwid], writes=[Tst[b]])
                      if n % 2:
                          if gain is not None:
                              dve.op(lambda e, b=b, k=k, c0=c0, cwid=cwid: e.tensor_scalar(
                                  out=dst[0:kp, k, c0:c0 + cwid], in0=stg[b][0:kp, 0:cwid],
                                  scalar1=gain[0:kp, k:k + 1], scalar2=None, op0=ALU.mult),
                                  reads=[Tst[b], Tc], writes=[Tw])
                          else:
                              dve.op(lambda e, b=b, k=k, c0=c0, cwid=cwid: e.tensor_copy(
                                  out=dst[0:kp, k, c0:c0 + cwid], in_=stg[b][0:kp, 0:cwid]),
                                  reads=[Tst[b]], writes=[Tw])
                      else:
                          if gain is not None:
                              act.op(lambda e, b=b, k=k, c0=c0, cwid=cwid: e.activation(
                                  out=dst[0:kp, k, c0:c0 + cwid], in_=stg[b][0:kp, 0:cwid], func=AF.Copy,
                                  scale=gain[0:kp, k:k + 1]), reads=[Tst[b], Tc], writes=[Tw])
                          else:
                              act.op(lambda e, b=b, k=k, c0=c0, cwid=cwid: e.activation(
                                  out=dst[0:kp, k, c0:c0 + cwid], in_=stg[b][0:kp, 0:cwid], func=AF.Copy),
                                  reads=[Tst[b]], writes=[Tw])

          Tw = T("weights")

          def rstd_from_msq(st_bufs, msq, n):
              ap, Tm = msq
              dve.op(lambda e: e.tensor_scalar(out=ap, in0=ap, scalar1=EPS, scalar2=None, op0=ALU.add),
                     reads=[Tm], writes=[Tm])
              act.op(lambda e: e.activation(out=ap, in_=ap, func=AF.Sqrt), reads=[Tm], writes=[Tm])
              dve.op(lambda e: e.reciprocal(out=ap, in_=ap), reads=[Tm], writes=[Tm])

          class TileBufs:
              def __init__(self, st, tag, nx=2):
                  self.xt = [sb(st, f"xt{tag}{i}", [128, D], F32) for i in range(nx)]
                  self.Txt = [T(f"xt{tag}{i}") for i in range(nx)]
                  self.junk = sb(st, f"junk{tag}", [128, D], BF16)
                  self.Tjunk = T("junk" + tag)
                  self.st = sb(st, f"stat{tag}", [128, 8], F32)
                  self.Tst = [T(f"stat{tag}{i}") for i in range(8)]
                  self.xn = sb(st, f"xn{tag}", [128, D], BF16)
                  self.Txn = T("xn" + tag)
                  self.xnT = sb(st, f"xnT{tag}", [128, 8, 128], BF16)
                  self.TxnT = T("xnT" + tag)
                  self.n = 0

          def front_end(tb, src_rows, w_reads=()):
              b = tb.n % len(tb.xt)
              tb.n += 1
              xt, Txt = tb.xt[b], tb.Txt[b]
              sp.dma(xt[:], src_rows, reads=list(w_reads), writes=[Txt])
              ms, Tms = tb.st[:, 0:1], tb.Tst[0]
              act.op(lambda e: e.activation(out=tb.junk[:], in_=xt[:], func=AF.Square, scale=1.0 / math.sqrt(D),
                                            accum_out=ms), reads=[Txt], writes=[tb.Tjunk, Tms])
              rstd_from_msq(None, (ms, Tms), 1)
              dve.op(lambda e: e.tensor_scalar(out=tb.xn[:], in0=xt[:], scalar1=ms, scalar2=None, op0=ALU.mult),
                     reads=[Txt, Tms], writes=[tb.Txn])
              for k in range(8):
                  pe.op(lambda e, k=k: e.transpose(out=pT[:, k * 128:(k + 1) * 128], in_=tb.xn[:, k * 128:(k + 1) * 128],
                                                   identity=c_idb[:]), reads=[tb.Txn, Tc], writes=[TpT])
              act.op(lambda e: e.activation(out=tb.xnT[:].rearrange("p k t -> p (k t)"), in_=pT[:], func=AF.Copy),
                     reads=[TpT], writes=[tb.TxnT])
              return b

          class HeadBufs:
              def __init__(self, st, tag):
                  self.raw = sb(st, f"hraw{tag}", [128, H, HD], F32)
                  self.Traw = T("hraw" + tag)
                  self.sq = sb(st, f"hsq{tag}", [128, H, HD], F32)
                  self.Tsq = T("hsq" + tag)
                  self.rs = sb(st, f"hrs{tag}", [128, H], F32)
                  self.Trs = T("hrs" + tag)
                  self.t1, self.Tt1 = self.sq, self.Tsq
                  self.ra = sb(st, f"hra{tag}", [128, H, 16], F32)
                  self.rb = sb(st, f"hrb{tag}", [128, H, 16], F32)
                  self.Tra, self.Trb = T("hra" + tag), T("hrb" + tag)
                  self.fin = sb(st, f"hfin{tag}", [128, H, HD], BF16)
                  self.Tfin = T("hfin" + tag)

          def head_norm_rope(hb, gain, cs, Tcs):
              raw, sq, rs, t1, fin = hb.raw, hb.sq, hb.rs, hb.t1, hb.fin
              act.op(lambda e: e.activation(out=sq[:], in_=raw[:], func=AF.Square, scale=1.0 / math.sqrt(HD)),
                     reads=[hb.Traw], writes=[hb.Tsq])
              dve.op(lambda e: e.tensor_reduce(out=rs[:], in_=sq[:], axis=AX.X, op=ALU.add),
                     reads=[hb.Tsq], writes=[hb.Trs])
              rstd_from_msq(None, (rs[:], hb.Trs), H)
              dve.op(lambda e: e.tensor_tensor(out=t1[:], in0=raw[:], in1=rs[:].unsqueeze(2).to_broadcast([128, H, HD]),
                                               op=ALU.mult), reads=[hb.Traw, hb.Trs], writes=[hb.Tt1])
              pool.op(lambda e: e.tensor_tensor(out=t1[:], in0=t1[:], in1=gain[:].unsqueeze(1).to_broadcast([128, H, HD]),
                                                op=ALU.mult), reads=[hb.Tt1, Tc], writes=[hb.Tt1])
              cosb = cs[:, 0:16].unsqueeze(1).to_broadcast([128, H, 16])
              sinb = cs[:, 16:32].unsqueeze(1).to_broadcast([128, H, 16])
              p1, p2 = t1[:, :, 64:80], t1[:, :, 80:96]
              act.op(lambda e: e.activation(out=fin[:, :, 0:64], in_=t1[:, :, 0:64], func=AF.Copy),
                     reads=[hb.Tt1], writes=[hb.Tfin])
              dve.op(lambda e: e.tensor_tensor(out=hb.ra[:], in0=p1, in1=cosb, op=ALU.mult),
                     reads=[hb.Tt1, Tcs], writes=[hb.Tra])
              dve.op(lambda e: e.tensor_tensor(out=hb.rb[:], in0=p2, in1=sinb, op=ALU.mult),
                     reads=[hb.Tt1, Tcs], writes=[hb.Trb])
              dve.op(lambda e: e.tensor_tensor(out=fin[:, :, 64:80], in0=hb.ra[:], in1=hb.rb[:], op=ALU.subtract),
                     reads=[hb.Tra, hb.Trb], writes=[hb.Tfin])
              dve.op(lambda e: e.tensor_tensor(out=hb.ra[:], in0=p2, in1=cosb, op=ALU.mult),
                     reads=[hb.Tt1, Tcs], writes=[hb.Tra])
              dve.op(lambda e: e.tensor_tensor(out=hb.rb[:], in0=p1, in1=sinb, op=ALU.mult),
                     reads=[hb.Tt1, Tcs], writes=[hb.Trb])
              dve.op(lambda e: e.tensor_tensor(out=fin[:, :, 80:96], in0=hb.ra[:], in1=hb.rb[:], op=ALU.add),
                     reads=[hb.Tra, hb.Trb], writes=[hb.Tfin])

          NWAYS = 1
          NWAYS_P1 = 4

          def run_ways(tasks, make_gen, nways=None):
              nways = len(ways) if nways is None else nways
              it = iter(tasks)
              free = list(range(nways))
              active = []
              more = True
              while True:
                  while free and more:
                      try:
                          tsk = next(it)
                      except StopIteration:
                          more = False
                          break
                      w = free.pop(0)
                      active.append((make_gen(tsk, ways[w]), w))
                  if not active:
                      break
                  for gw in list(active):
                      try:
                          next(gw[0])
                      except StopIteration:
                          active.remove(gw)
                          free.append(gw[1])

          def rstd_g(ap, Tm):
              dve.op(lambda e: e.tensor_scalar(out=ap, in0=ap, scalar1=EPS, scalar2=None, op0=ALU.add),
                     reads=[Tm], writes=[Tm])
              yield
              act.op(lambda e: e.activation(out=ap, in_=ap, func=AF.Sqrt), reads=[Tm], writes=[Tm])
              yield
              dve.op(lambda e: e.reciprocal(out=ap, in_=ap), reads=[Tm], writes=[Tm])
              yield

          pS2b = pS[2].bitcast(BF16)
          Tbanks = [(pT[:, :], TpT), (pS2b, TpS[2])]
          Pbanks = [(pO[1], TpO[1]), (pO[0], TpO[0])]
          Kbanks = [((pM[0], pM[1]), (TpM[0], TpM[1])), ((pS[0], pS[1]), (TpS[0], TpS[1]))]

          class Way:
              def __init__(self, st, w):
                  tag = f"W{w}"
                  self.w = w
                  self.xt = sb(st, "xt" + tag, [128, D], F32)
                  self.Txt = T("xt" + tag)
                  self.st = sb(st, "stat" + tag, [128, 8], F32)
                  self.Tst = [T(f"stat{tag}{i}") for i in range(8)]
                  self.xn = sb(st, "xn" + tag, [128, D], BF16)
                  self.Txn = T("xn" + tag)
                  self.junk, self.Tjunk = self.xn, self.Txn
                  self.xnT = sb(st, "xnT" + tag, [128, 8, 128], BF16)
                  self.TxnT = T("xnT" + tag)
                  self.hb = HeadBufs(st, tag)
                  self.rk = sb(st, "rk" + tag, [128, 32], F32)
                  self.Trk = T("rk" + tag)
                  self.cqb = sb(st, "cqb" + tag, [128, QL], BF16)
                  self.Tcqb = T("cqb" + tag)
                  self.cqf = self.hb.sq[:].rearrange("p h d -> p (h d)")[:, 0:QL]
                  self.Tcqf = self.hb.Tsq
                  self.cqT = sb(st, "cqT" + tag, [128, 3, 128], BF16)
                  self.TcqT = T("cqT" + tag)
                  self.sig = sb(st, "sig" + tag, [128, 4, 128], F32)
                  self.Tsig = T("sig" + tag)
                  self.pT, self.TpT = Tbanks[w % 2]
                  self.pP, self.TpP = Pbanks[w % 2]
                  self.pK, self.TpK = Kbanks[w % 2]

              def alloc_p1(self, st):
                  tag = f"W{self.w}"
                  self.ckv = sb(st, "ckv" + tag, [128, KVL], F32)
                  self.Tckv = T("ckv" + tag)
                  self.kpe = sb(st, "kpe" + tag, [128, RD], F32)
                  self.Tkpe = T("kpe" + tag)
                  self.ckvb = sb(st, "ckvb" + tag, [128, KVL], BF16)
                  self.Tckvb = T("ckvb" + tag)
                  self.ckvT = sb(st, "ckvT" + tag, [128, 2, 128], BF16)
                  self.TckvT = T("ckvT" + tag)
                  self.qst = sb(st, "qst" + tag, [HD, H, 128], BF16)
                  self.Tqst = T("qst" + tag)
                  self.ust, self.Tust = self.sig, self.Tsig
                  self.prj = sb(st, "prj" + tag, [128, KVL + RD], F32)
                  self.Tprj = T("prj" + tag)
                  self.cin = sb(st, "cin" + tag, [128, KVL], F32)
                  self.Tcin = T("cin" + tag)
                  self.kin = sb(st, "kin" + tag, [128, RD], F32)
                  self.Tkin = T("kin" + tag)

          def front_end_g(W, src_rows):
              sp.dma(W.xt[:], src_rows, writes=[W.Txt])
              ms, Tms = W.st[:, 0:1], W.Tst[0]
              act.op(lambda e: e.activation(out=W.junk[:], in_=W.xt[:], func=AF.Square, scale=1.0 / math.sqrt(D),
                                            accum_out=ms), reads=[W.Txt], writes=[W.Tjunk, Tms])
              yield
              yield from rstd_g(ms, Tms)
              dve.op(lambda e: e.tensor_scalar(out=W.xn[:], in0=W.xt[:], scalar1=ms, scalar2=None, op0=ALU.mult),
                     reads=[W.Txt, Tms], writes=[W.Txn])
              yield
              for k in range(8):
                  pe.op(lambda e, k=k: e.transpose(out=W.pT[:, k * 128:(k + 1) * 128], in_=W.xn[:, k * 128:(k + 1) * 128],
                                                   identity=c_idb[:]), reads=[W.Txn, Tc], writes=[W.TpT])
              act.op(lambda e: e.activation(out=W.xnT[:].rearrange("p k t -> p (k t)"), in_=W.pT, func=AF.Copy),
                     reads=[W.TpT], writes=[W.TxnT])
              yield

          def head_norm_rope_g(hb, gain, cs, Tcs):
              raw, sq, rs, t1, fin = hb.raw, hb.sq, hb.rs, hb.t1, hb.fin
              act.op(lambda e: e.activation(out=sq[:], in_=raw[:], func=AF.Square, scale=1.0 / math.sqrt(HD)),
                     reads=[hb.Traw], writes=[hb.Tsq])
              yield
              dve.op(lambda e: e.tensor_reduce(out=rs[:], in_=sq[:], axis=AX.X, op=ALU.add),
                     reads=[hb.Tsq], writes=[hb.Trs])
              yield
              yield from rstd_g(rs[:], hb.Trs)
              dve.op(lambda e: e.tensor_tensor(out=t1[:], in0=raw[:], in1=rs[:].unsqueeze(2).to_broadcast([128, H, HD]),
                                               op=ALU.mult), reads=[hb.Traw, hb.Trs], writes=[hb.Tt1])
              yield
              pool.op(lambda e: e.tensor_tensor(out=t1[:], in0=t1[:], in1=gain[:].unsqueeze(1).to_broadcast([128, H, HD]),
                                                op=ALU.mult), reads=[hb.Tt1, Tc], writes=[hb.Tt1])
              yield
              cosb = cs[:, 0:16].unsqueeze(1).to_broadcast([128, H, 16])
              sinb = cs[:, 16:32].unsqueeze(1).to_broadcast([128, H, 16])
              p1, p2 = t1[:, :, 64:80], t1[:, :, 80:96]
              act.op(lambda e: e.activation(out=fin[:, :, 0:64], in_=t1[:, :, 0:64], func=AF.Copy),
                     reads=[hb.Tt1], writes=[hb.Tfin])
              dve.op(lambda e: e.tensor_tensor(out=hb.ra[:], in0=p1, in1=cosb, op=ALU.mult),
                     reads=[hb.Tt1, Tcs], writes=[hb.Tra])
              pool.op(lambda e: e.tensor_tensor(out=hb.rb[:], in0=p2, in1=sinb, op=ALU.mult),
                      reads=[hb.Tt1, Tcs], writes=[hb.Trb])
              yield
              dve.op(lambda e: e.tensor_tensor(out=fin[:, :, 64:80], in0=hb.ra[:], in1=hb.rb[:], op=ALU.subtract),
                     reads=[hb.Tra, hb.Trb], writes=[hb.Tfin])
              yield
              dve.op(lambda e: e.tensor_tensor(out=hb.ra[:], in0=p2, in1=cosb, op=ALU.mult),
                     reads=[hb.Tt1, Tcs], writes=[hb.Tra])
              pool.op(lambda e: e.tensor_tensor(out=hb.rb[:], in0=p1, in1=sinb, op=ALU.mult),
                      reads=[hb.Tt1, Tcs], writes=[hb.Trb])
              yield
              dve.op(lambda e: e.tensor_tensor(out=fin[:, :, 80:96], in0=hb.ra[:], in1=hb.rb[:], op=ALU.add),
                     reads=[hb.Tra, hb.Trb], writes=[hb.Tfin])
              yield

          with ExitStack() as stA:
              wA_in = sb(stA, "wA_in", [128, 8, INW], BF16)
              wA_uq = sb(stA, "wA_uq", [128, 3, H * HD], BF16)
              wA_ukv = sb(stA, "wA_ukv", [128, 2, H * 128], BF16)
              wA_oa = sb(stA, "wA_oa", [64, 8, D], BF16)
              wA_oc = sb(stA, "wA_oc", [128, 4, D], BF16)
              with ExitStack() as stW:
                  load_weight(stW, wA_in, w_in, 8, INW, c_gattn, name="in")
                  load_weight(stW, wA_uq, w_uq, 3, H * HD, c_gq, name="uq")
                  load_weight(stW, wA_ukv, w_ukv, 2, H * 128, None, name="ukv")
                  load_weight(stW, wA_oa, w_out, 8, D, None, kp=64, name="oa")
                  load_weight(stW, wA_oc, w_out[512:1024, :], 4, D, None, name="oc")
                  tk.barrier()
              ckpt("w")

              def tile_front_A_g(W, src_rows, cs, Tcs, qcol, ntok_groups):
                  yield from front_end_g(W, src_rows)
                  yield from qglu_g(W, cs, Tcs, qTa[0:HD, :, qcol:qcol + 128], TqTa, ntok_groups)

              def qglu_g(W, cs, Tcs, qdst, Tqdst, ntok_groups):
                  hb = W.hb
                  for k in range(8):
                      pe.op(lambda e, k=k: e.matmul(W.pP[:, 0:QL], lhsT=W.xnT[:, k, :], rhs=wA_in[:, k, 0:QL],
                                                    start=(k == 0), stop=(k == 7)), reads=[W.TxnT, Tw], writes=[W.TpP])
                  dve.op(lambda e: e.tensor_copy(out=W.cqf, in_=W.pP[:, 0:QL]), reads=[W.TpP], writes=[W.Tcqf])
                  yield
                  ms, Tms = W.st[:, 2:3], W.Tst[2]
                  act.op(lambda e: e.activation(out=W.junk[:, 0:QL], in_=W.cqf, func=AF.Square,
                                                scale=1.0 / math.sqrt(QL), accum_out=ms),
                         reads=[W.Tcqf], writes=[W.Tjunk, Tms])
                  yield
                  yield from rstd_g(ms, Tms)
                  dve.op(lambda e: e.tensor_scalar(out=W.cqb[:], in0=W.cqf, scalar1=ms, scalar2=None, op0=ALU.mult),
                         reads=[W.Tcqf, Tms], writes=[W.Tcqb])
                  yield
                  for k in range(3):
                      pe.op(lambda e, k=k: e.transpose(out=W.pT[:, k * 128:(k + 1) * 128], in_=W.cqb[:, k * 128:(k + 1) * 128],
                                                       identity=c_idb[:]), reads=[W.Tcqb, Tc], writes=[W.TpT])
                  dve.op(lambda e: e.tensor_copy(out=W.cqT[:].rearrange("p k t -> p (k t)"), in_=W.pT[:, 0:384]),
                         reads=[W.TpT], writes=[W.TcqT])
                  yield
                  for nb, (c0, cw_) in enumerate(((0, 512), (512, 256))):
                      for k in range(3):
                          pe.op(lambda e, nb=nb, k=k, c0=c0, cw_=cw_: e.matmul(
                              W.pK[nb][:, 0:cw_], lhsT=W.cqT[:, k, :], rhs=wA_uq[:, k, c0:c0 + cw_],
                              start=(k == 0), stop=(k == 2)), reads=[W.TcqT, Tw], writes=[W.TpK[nb]])
                  rawf = hb.raw[:].rearrange("p h d -> p (h d)")
                  act.op(lambda e: e.activation(out=rawf[:, 0:512], in_=W.pK[0][:, 0:512], func=AF.Copy),
                         reads=[W.TpK[0]], writes=[hb.Traw])
                  dve.op(lambda e: e.tensor_copy(out=rawf[:, 512:768], in_=W.pK[1][:, 0:256]),
                         reads=[W.TpK[1]], writes=[hb.Traw])
                  yield
                  yield from head_norm_rope_g(hb, c_ghq, cs, Tcs)
                  for h in range(H):
                      pe.op(lambda e, h=h: e.transpose(out=W.pT[0:HD, h * 128:(h + 1) * 128], in_=hb.fin[:, h, :],
                                                       identity=c_idb[:]), reads=[hb.Tfin, Tc], writes=[W.TpT])
                  act.op(lambda e: e.activation(out=qdst,
                                                in_=W.pT[0:HD, :].rearrange("p (h t) -> p h t", h=H), func=AF.Copy),
                         reads=[W.TpT], writes=[Tqdst])
                  yield
                  for half in (1, 0):
                      for c in range(4):
                          col = QL + KVL + RD + half * CC + c * 128
                          for k in range(8):
                              pe.op(lambda e, half=half, c=c, k=k, col=col: e.matmul(
                                  W.pK[half][:, c * 128:(c + 1) * 128], lhsT=wA_in[:, k, col:col + 128], rhs=W.xnT[:, k, :],
                                  start=(k == 0), stop=(k == 7)), reads=[W.TxnT, Tw], writes=[W.TpK[half]])
                      if half == 1:
                          act.op(lambda e: e.activation(out=W.sig[:].rearrange("p c t -> p (c t)"), in_=W.pK[1][:, :],
                                                        func=AF.Sigmoid), reads=[W.TpK[1]], writes=[W.Tsig])
                  for (tok0, ntok, ucol_) in ntok_groups:
                      tgt = ucol_ if isinstance(ucol_, tuple) else (uT, TuT, ucol_)
                      ub, Tub, uc = tgt
                      dve.op(lambda e, tok0=tok0, ntok=ntok, ub=ub, uc=uc: e.tensor_tensor(
                          out=ub[:, :, uc:uc + ntok],
                          in0=W.pK[0][:, :].rearrange("p (c t) -> p c t", c=4)[:, :, tok0:tok0 + ntok],
                          in1=W.sig[:, :, tok0:tok0 + ntok], op=ALU.mult), reads=[W.TpK[0], W.Tsig], writes=[Tub])
                  yield

              ways = [Way(stA, w) for w in range(NWAYS)]
              TKT, TVV = T("KT"), T("VV")
              TKTs, TVVs = T("KTs"), T("VVs")
              TX1 = T("X1")
              Tout = T("outs")
              with ExitStack() as stP1:
                  for w in range(NWAYS, NWAYS_P1):
                      ways.append(Way(stP1, w))
                  for W_ in ways:
                      W_.alloc_p1(stP1)
                  KS = 4
                  kst = [sb(stP1, f"kst{i}", [HD, H, KS * 128], BF16) for i in range(2)]
                  Tkst = [T(f"kst{i}") for i in range(2)]
                  vst = [sb(stP1, f"vst{i}", [128, H, KS, 128], BF16) for i in range(2)]
                  Tvst = [T(f"vst{i}") for i in range(2)]
                  for i in range(2):
                      pool.op(lambda e, i=i: e.memset(vst[i][:], 1.0), writes=[Tvst[i]])

                  def kv_from_ckv_g(W, ckv_ap, Tck, kpe_ap, Tkp, cs, Tcs, stage_slot):
                      sbuf, slot = stage_slot
                      hb = W.hb
                      act.op(lambda e: e.activation(out=W.ckvb[:], in_=ckv_ap, func=AF.Copy), reads=[Tck], writes=[W.Tckvb])
                      yield
                      for k in range(2):
                          pe.op(lambda e, k=k: e.transpose(out=W.pT[:, k * 128:(k + 1) * 128],
                                                           in_=W.ckvb[:, k * 128:(k + 1) * 128], identity=c_idb[:]),
                                reads=[W.Tckvb, Tc], writes=[W.TpT])
                      dve.op(lambda e: e.tensor_copy(out=W.ckvT[:].rearrange("p k t -> p (k t)"), in_=W.pT[:, 0:256]),
                             reads=[W.TpT], writes=[W.TckvT])
                      yield
                      for nb in range(2):
                          for k in range(2):
                              pe.op(lambda e, nb=nb, k=k: e.matmul(W.pK[nb][:, :], lhsT=W.ckvT[:, k, :],
                                                                   rhs=wA_ukv[:, k, nb * 512:(nb + 1) * 512],
                                                                   start=(k == 0), stop=(k == 1)),
                                    reads=[W.TckvT, Tw], writes=[W.TpK[nb]])
                      for nb in range(2):
                          src = W.pK[nb][:, :].rearrange("p (h c) -> p h c", h=4)
                          act.op(lambda e, nb=nb, src=src: e.activation(out=hb.raw[:, nb * 4:(nb + 1) * 4, 0:64],
                                                                        in_=src[:, :, 0:64], func=AF.Copy),
                                 reads=[W.TpK[nb]], writes=[hb.Traw])
                          dve.op(lambda e, nb=nb, src=src: e.tensor_copy(out=vst[sbuf][:, nb * 4:(nb + 1) * 4, slot, 0:64],
                                                                         in_=src[:, :, 64:128]),
                                 reads=[W.TpK[nb]], writes=[Tvst[sbuf]])
                      yield
                      pool.op(lambda e: e.tensor_copy(out=hb.raw[:, :, 64:96],
                                                      in_=kpe_ap.unsqueeze(1).to_broadcast([128, H, RD])),
                              reads=[Tkp], writes=[hb.Traw])
                      yield
                      yield from head_norm_rope_g(hb, c_ghk, cs, Tcs)
                      for h in range(H):
                          pe.op(lambda e, h=h: e.transpose(out=W.pT[0:HD, h * 128:(h + 1) * 128], in_=hb.fin[:, h, :],
                                                           identity=c_idb[:]), reads=[hb.Tfin, Tc], writes=[W.TpT])
                      act.op(lambda e: e.activation(out=kst[sbuf][:, :, slot * 128:(slot + 1) * 128],
                                                    in_=W.pT[0:HD, :].rearrange("p (h t) -> p h t", h=H), func=AF.Copy),
                             reads=[W.TpT], writes=[Tkst[sbuf]])
                      yield

                  def ckv_from_x_g(W, own_row=None, o_ck=None, o_kp=None):
                      for k in range(8):
                          pe.op(lambda e, k=k: e.matmul(W.pP[:, 0:KVL + RD], lhsT=W.xnT[:, k, :],
                                                        rhs=wA_in[:, k, QL:QL + KVL + RD], start=(k == 0), stop=(k == 7)),
                                reads=[W.TxnT, Tw], writes=[W.TpP])
                      dve.op(lambda e: e.tensor_copy(out=W.prj[:], in_=W.pP[:, 0:KVL + RD]), reads=[W.TpP], writes=[W.Tprj])
                      yield
                      ms, Tms = W.st[:, 1:2], W.Tst[1]
                      act.op(lambda e: e.activation(out=W.junk[:, 0:KVL], in_=W.prj[:, 0:KVL], func=AF.Square,
                                                    scale=1.0 / math.sqrt(KVL), accum_out=ms),
                             reads=[W.Tprj], writes=[W.Tjunk, Tms])
                      pool.op(lambda e: e.tensor_copy(out=W.kpe[:], in_=W.prj[:, KVL:KVL + RD]),
                              reads=[W.Tprj], writes=[W.Tkpe])
                      yield
                      yield from rstd_g(ms, Tms)
                      dve.op(lambda e: e.scalar_tensor_tensor(out=W.ckv[:], in0=W.prj[:, 0:KVL], scalar=ms, in1=c_gkv[:],
                                                              op0=ALU.mult, op1=ALU.mult),
                             reads=[W.Tprj, Tms, Tc], writes=[W.Tckv])
                      yield
                      if own_row is not None:
                          sp.dma(o_ck[own_row:own_row + 128, :], W.ckv[:], reads=[W.Tckv], writes=[Tout])
                          sp.dma(o_kp[own_row:own_row + 128, :], W.kpe[:], reads=[W.Tkpe], writes=[Tout])

                  gdone = {}

                  TQT, TUT = T("QT"), T("UT")
                  zpad = sb(stP1, "zpad", [128, 4, 32], F32)
                  Tzpad = T("zpad")
                  dve.op(lambda e: e.memset(zpad[:], 0.0), writes=[Tzpad])
                  sp.dma(UT[:, :, 0:32], zpad[:], reads=[Tzpad], writes=[TUT])

                  def qu_to_scratch_g(W, cs, Tcs, ci):
                      yield from qglu_g(W, cs, Tcs, W.qst[:], W.Tqst, [(0, 128, (W.ust, W.Tust, 0))])
                      sp.dma(QT[:, :, ci * 128:(ci + 1) * 128], W.qst[:], reads=[W.Tqst], writes=[TQT])
                      sp.dma(UT[:, :, 32 + ci * 128:32 + (ci + 1) * 128], W.ust[:], reads=[W.Tust], writes=[TUT])

                  def p1_tile_g(lt, W):
                      if isinstance(lt, tuple):
                          t = lt[1]
                          yield from front_end_g(W, x_halo[t * 128:(t + 1) * 128, :])
                          yield from qu_to_scratch_g(W, c_ropeh[:, t, :], Tc, t * (RT + 1))
                          return
                      gi = lt // KS
                      sbuf, slot = gi % 2, lt % KS
                      sp.dma(W.rk[:], rope_k[:, lt, :], writes=[W.Trk])
                      yield from front_end_g(W, x_all[lt * 128:(lt + 1) * 128, :])
                      t, rem = divmod(lt, G)
                      own = rem < RT
                      yield from ckv_from_x_g(W, own_row=(t * RT + rem) * 128 if own else None, o_ck=o_ckv, o_kp=o_kpe)
                      yield from kv_from_ckv_g(W, W.ckv[:], W.Tckv, W.kpe[:], W.Tkpe, W.rk, W.Trk, (sbuf, slot))
                      if own:
                          yield from qu_to_scratch_g(W, W.rk, W.Trk, t * (RT + 1) + 1 + rem)
                      gdone[gi] = gdone.get(gi, 0) + 1
                      if gdone[gi] == KS:
                          lt0 = gi * KS
                          sp.dma(KT[:, :, lt0 * 128:(lt0 + KS) * 128].rearrange("h d t -> d h t"), kst[sbuf][:],
                                 reads=[Tkst[sbuf]], writes=[TKT])
                          sp.dma(VV[:, :, lt0:lt0 + KS, :].rearrange("h p s c -> p h s c"), vst[sbuf][:],
                                 reads=[Tvst[sbuf]], writes=[TVV])
                  run_ways(list(range(NT)) + [("h", t) for t in range(NSLOT)], p1_tile_g)

                  ckpt("p1")
                  NG0 = NT // KS
                  PG = (PT + KS - 1) // KS
                  sdone = {}

                  def p1s_tile_g(ep, W):
                      e_, p = ep
                      gi = NG0 + e_ * PG + p // KS
                      sbuf, slot = gi % 2, p % KS
                      r0 = e_ * PAST + p * 128
                      sp.dma(W.cin[:], cckv[r0:r0 + 128, :], writes=[W.Tcin])
                      sp.dma(W.kin[:], ckpe[r0:r0 + 128, :], writes=[W.Tkin])
                      sp.dma(W.rk[:], rope_sp[:, p, :], writes=[W.Trk])
                      yield from kv_from_ckv_g(W, W.cin[:], W.Tcin, W.kin[:], W.Tkin, W.rk, W.Trk, (sbuf, slot))
                      sdone[gi] = sdone.get(gi, 0) + 1
                      p0 = (p // KS) * KS
                      ns = min(KS, PT - p0)
                      if sdone[gi] == ns:
                          sp.dma(KTs[e_, :, :, p0 * 128:(p0 + ns) * 128].rearrange("h d t -> d h t"),
                                 kst[sbuf][:, :, 0:ns * 128], reads=[Tkst[sbuf]], writes=[TKTs])
                          sp.dma(VVs[e_, :, :, p0:p0 + ns, :].rearrange("h p s c -> p h s c"),
                                 vst[sbuf][:, :, 0:ns, :], reads=[Tvst[sbuf]], writes=[TVVs])
                  run_ways([(e_, p) for e_ in range(2) for p in range(PT)], p1s_tile_g)
                  sbuf = (NG0 + 2 * PG) % 2

                  def p1n_g(_, W):
                      yield from front_end_g(W, x_s[:, :])
                      yield from ckv_from_x_g(W, own_row=0, o_ck=o_ckvs, o_kp=o_kpes)
                      yield from kv_from_ckv_g(W, W.ckv[:], W.Tckv, W.kpe[:], W.Tkpe, c_ropesn, Tc, (sbuf, 0))
                  run_ways([0], p1n_g)
                  for e_ in range(2):
                      sp.dma(KTs[e_, :, :, PT * 128:PT * 128 + 64].rearrange("h d t -> d h t"),
                             kst[sbuf][:, :, e_ * 64:(e_ + 1) * 64], reads=[Tkst[sbuf]], writes=[TKTs])
                      sp.dma(VVs[e_, :, 0:64, PT:PT + 1, :].rearrange("h p s c -> p h s c"),
                             vst[sbuf][e_ * 64:(e_ + 1) * 64, :, 0:1, :], reads=[Tvst[sbuf]], writes=[TVVs])

                  tk.barrier()
                  del ways[NWAYS:]
              ckpt("p1s")
              qTas = [sb(stA, f"qTa{i}", [128, H, QG * 128], BF16) for i in range(2)]
              TqTas = [T(f"qTa{i}") for i in range(2)]
              for i in range(2):
                  for h in range(H):
                      sp.dma(qTas[i][96:128, h, :], qmask[:, :], writes=[TqTas[i]])
              qTa, TqTa = qTas[0], TqTas[0]
              cTgs = [sb(stA, f"cTg{i}", [128, 4, QG * 128], BF16) for i in range(2)]
              TcTgs = [T(f"cTg{i}") for i in range(2)]
              cTg, TcTg = cTgs[0], TcTgs[0]
              attTs = [sb(stA, f"attT{i}", [64, H, QG * 128], BF16) for i in range(2)]
              TattTs = [T(f"attT{i}") for i in range(2)]
              attT, TattT = attTs[0], TattTs[0]
              for i in range(2):
                  pool.op(lambda e, i=i: e.memset(attTs[i][:], 0.0), writes=[TattTs[i]])
              UW = CK - 1 + QG * 128
              uTs = [sb(stA, f"uT{i}", [128, 4, UW], F32) for i in range(2)]
              TuTs = [T(f"uT{i}") for i in range(2)]
              uT, TuT = uTs[0], TuTs[0]
              acc = sb(stA, "acc", [128, 4, QG * 128], F32)
              Tacc = T("acc")
              sqc = sb(stA, "sqc", [128, 4, QG * 128], BF16)
              Tsqc = T("sqc")
              rsc = sb(stA, "rsc", [128, QG * 128], F32)
              Trsc = T("rsc")
              cpre, Tcpre = acc, Tacc
              NKB = 3
              kb = [sb(stA, f"kb{i}", [HD, KC * 128], BF16) for i in range(NKB)]
              Tkb = [T(f"kb{i}") for i in range(NKB)]
              vb = [sb(stA, f"vb{i}", [128, KC, 128], BF16) for i in range(NKB)]
              Tvb = [T(f"vb{i}") for i in range(NKB)]
              kd = [sb(stA, f"kd{i}", [128, QG * 128], BF16) for i in range(2)]
              Tkd = [T(f"kd{i}") for i in range(2)]
              for i in range(2):
                  sp.dma(kd[i][96:128, :], khot[:, :], writes=[Tkd[i]])
              vd = [sb(stA, f"vd{i}", [128, QG, 128], BF16) for i in range(2)]
              Tvd = [T(f"vd{i}") for i in range(2)]
              pb = [sb(stA, f"pb{i}", [128, 1024], BF16) for i in range(2)]
              Tpb = [T(f"pb{i}") for i in range(2)]
              rcp = sb(stA, "rcp", [64, 512], F32)
              Trcp = T("rcp")
              rcs = sb(stA, "rcs", [128, 512], F32)
              Trcs = T("rcs")
              xr = [sb(stA, f"xr{i}", [128, D], F32) for i in range(2)]
              Txr = [T(f"xr{i}") for i in range(2)]
              x1o, Tx1o = xr, Txr
              cvo = sb(stA, "cvo", [32, CC], F32)
              Tcvo = T("cvo")
              cnt = dict(kb=0, pb=0, x1=0, po=0, kd=0)

              def conv_module_g(uT_, TuT_, cT_, TcT_, ucol, ncol, ccol, bank=0):
                  PB_, TPB_ = pM[bank], TpM[bank]
                  for c in range(4):
                      dve.op(lambda e, c=c: e.tensor_scalar(out=acc[:, c, 0:ncol], in0=uT_[:, c, ucol - 30:ucol - 30 + ncol],
                                                            scalar1=c_cw[:, c, 0:1], scalar2=c_cb[:, c:c + 1],
                                                            op0=ALU.mult, op1=ALU.add),
                             reads=[TuT_, Tc], writes=[Tacc])
                      yield
                      for k in range(1, CK):
                          dve.op(lambda e, c=c, k=k: e.scalar_tensor_tensor(
                              out=acc[:, c, 0:ncol], in0=uT_[:, c, ucol - 30 + k:ucol - 30 + k + ncol],
                              scalar=c_cw[:, c, k:k + 1], in1=acc[:, c, 0:ncol], op0=ALU.mult, op1=ALU.add),
                              reads=[TuT_, Tc, Tacc], writes=[Tacc])
                          yield
                  act.op(lambda e: e.activation(out=sqc[:, :, 0:ncol], in_=acc[:, :, 0:ncol], func=AF.Square,
                                                scale=1.0 / math.sqrt(CC)), reads=[Tacc], writes=[Tsqc])
                  yield
                  for c in range(4):
                      pe.op(lambda e, c=c: e.matmul(PB_[:, 0:ncol], lhsT=c_ones[:], rhs=sqc[:, c, 0:ncol],
                                                    start=(c == 0), stop=(c == 3)), reads=[Tsqc, Tc], writes=[TPB_])
                  dve.op(lambda e: e.tensor_scalar(out=rsc[:, 0:ncol], in0=PB_[:, 0:ncol], scalar1=EPS, scalar2=None,
                                                   op0=ALU.add), reads=[TPB_], writes=[Trsc])
                  yield
                  act.op(lambda e: e.activation(out=rsc[:, 0:ncol], in_=rsc[:, 0:ncol], func=AF.Sqrt),
                         reads=[Trsc], writes=[Trsc])
                  yield
                  dve.op(lambda e: e.reciprocal(out=rsc[:, 0:ncol], in_=rsc[:, 0:ncol]), reads=[Trsc], writes=[Trsc])
                  yield
                  for c in range(4):
                      dve.op(lambda e, c=c: e.scalar_tensor_tensor(out=cpre[:, c, 0:ncol], in0=acc[:, c, 0:ncol],
                                                                   scalar=c_cg[:, c:c + 1], in1=rsc[:, 0:ncol],
                                                                   op0=ALU.mult, op1=ALU.mult),
                             reads=[Tacc, Trsc, Tc], writes=[Tcpre])
                      yield
                  act.op(lambda e: e.activation(out=cT_[:, :, ccol:ccol + ncol], in_=cpre[:, :, 0:ncol], func=AF.Silu),
                         reads=[Tcpre], writes=[TcT_])
                  yield

              def conv_module(ucol, ncol, ccol):
                  for _ in conv_module_g(uT, TuT, cTg, TcTg, ucol, ncol, ccol, bank=2):
                      pass

              def attention(ncols, segs, kt_src, vv_src, Tk, Tv, qc0=0, ksz_last=128, side=None):
                  items = []
                  for h in range(H):
                      po_i = cnt["po"] % 2
                      cnt["po"] += 1
                      PO, TPO = pO[po_i], TpO[po_i]
                      first = True
                      nseg = len(segs)
                      for si, (kind, t0, ntl, bcol, ksz) in enumerate(segs):
                          last_seg = si == nseg - 1
                          if kind == "d":
                              di = cnt["kd"] % 2
                              cnt["kd"] += 1
                              KD, TKD, VD, TVD = kd[di], Tkd[di], vd[di], Tvd[di]
                              loads = [(KD[0:HD, 0:ntl * 128], kt_src(h, t0, ntl), Tk, TKD),
                                       (VD[:, 0:ntl, :], vv_src(h, t0, ntl), Tv, TVD)]
                              chunks = [(t0, ntl, KD, TKD, VD, TVD, 128, loads)]
                          else:
                              chunks = []
                              for c0 in range(t0, t0 + ntl, KC):
                                  cn = min(KC, t0 + ntl - c0)
                                  bi = cnt["kb"] % NKB
                                  cnt["kb"] += 1
                                  KB, TKB, VB, TVB = kb[bi], Tkb[bi], vb[bi], Tvb[bi]
                                  lastc = (c0 + cn == t0 + ntl)
                                  kz_l = ksz if lastc else 128
                                  loads = [(KB[0:HD, 0:(cn - 1) * 128 + kz_l], kt_src(h, c0, cn, kz_l), Tk, TKB)]
                                  if kz_l == 128:
                                      loads.append((VB[:, 0:cn, :], vv_src(h, c0, cn), Tv, TVB))
                                  else:
                                      if cn > 1:
                                          loads.append((VB[:, 0:cn - 1, :], vv_src(h, c0, cn - 1), Tv, TVB))
                                      loads.append((VB[0:kz_l, cn - 1:cn, :], vv_src(h, c0 + cn - 1, 1, kz_l), Tv, TVB))
                                  chunks.append((c0, cn, KB, TKB, VB, TVB, HD, loads))
                          for ci_, (c0, cn, KB, TKB, VB, TVB, KR, loads) in enumerate(chunks):
                              for j in range(cn):
                                  lastt = (c0 + j == t0 + ntl - 1)
                                  kz = ksz if lastt else 128
                                  cs0 = j * 128 if kind == "d" else 0
                                  items.append(dict(h=h, PO=PO, TPO=TPO, KB=KB, TKB=TKB, VB=VB, TVB=TVB, KR=KR, j=j, kz=kz,
                                                    cs0=cs0, bcol=bcol, first=first, last=(last_seg and lastt),
                                                    loads=(loads if j == 0 else None), key=(h, si, ci_, kind)))
                                  first = False
                  units = []
                  i = 0
                  while i < len(items):
                      a = items[i]
                      if (i + 1 < len(items) and a["key"][3] == "n" and items[i + 1]["key"] == a["key"]
                              and a["kz"] == 128 and items[i + 1]["kz"] == 128):
                          units.append([a, items[i + 1]])
                          i += 2
                      else:
                          units.append([a])
                          i += 1

                  def emit_qk(u):
                      pi = cnt["pb"] % 2
                      cnt["pb"] += 1
                      PSp = pSS[:, pi * 1024:(pi + 1) * 1024].rearrange("p (i c) -> p i c", i=2)
                      PBp = pb[pi][:, :].rearrange("p (i c) -> p i c", i=2)
                      for i, it in enumerate(u):
                          if it["loads"]:
                              for (dst, src, Tsrc, Tdst) in it["loads"]:
                                  sp.dma(dst, src, reads=[Tsrc], writes=[Tdst])
                          it["PS"], it["PB"], it["TPS"], it["TPB"] = PSp, PBp, TpSS[pi], Tpb[pi]
                          pe.op(lambda e, it=it, i=i: e.matmul(
                              PSp[0:it["kz"], i, it["cs0"]:ncols],
                              lhsT=it["KB"][0:it["KR"], it["j"] * 128:it["j"] * 128 + it["kz"]],
                              rhs=qTa[0:it["KR"], it["h"], qc0 + it["cs0"]:qc0 + ncols], start=True, stop=True),
                              reads=[it["TKB"], TqTa], writes=[TpSS[pi]])

                  def emit_rest(u):
                      a = u[0]
                      kz, cs0, n = a["kz"], a["cs0"], len(u)
                      PSp, PBp, TPS, TPB = a["PS"], a["PB"], a["TPS"], a["TPB"]
                      bias_ap = c_zero[0:kz, 0:1] if a["bcol"] is None else c_bias[0:kz, a["bcol"]:a["bcol"] + 1]
                      act.op(lambda e: e.activation(out=PBp[0:kz, 0:n, cs0:ncols], in_=PSp[0:kz, 0:n, cs0:ncols],
                                                    func=AF.Exp, bias=bias_ap, scale=SCALE),
                             reads=[TPS, Tc], writes=[TPB])
                      for i, it in enumerate(u):
                          pe.op(lambda e, it=it, i=i: e.matmul(it["PO"][:, cs0:ncols], lhsT=it["VB"][0:kz, it["j"], :],
                                                               rhs=PBp[0:kz, i, cs0:ncols], start=it["first"], stop=it["last"]),
                                reads=[it["TVB"], TPB], writes=[it["TPO"]])
                          if it["last"]:
                              PO, TPO, h = it["PO"], it["TPO"], it["h"]
                              dve.op(lambda e, PO=PO: e.tensor_scalar(out=rcs[64:128, 0:ncols], in0=PO[64:128, 0:ncols],
                                                                      scalar1=1e-30, scalar2=None, op0=ALU.add),
                                     reads=[TPO], writes=[Trcs])
                              dve.op(lambda e: e.reciprocal(out=rcs[64:128, 0:ncols], in_=rcs[64:128, 0:ncols]),
                                     reads=[Trcs], writes=[Trcs])
                              dve.op(lambda e: e.tensor_copy(out=rcp[0:64, 0:ncols], in_=rcs[64:128, 0:ncols]),
                                     reads=[Trcs], writes=[Trcp])
                              dve.op(lambda e, PO=PO, h=h: e.tensor_tensor(out=attT[:, h, qc0:qc0 + ncols], in0=PO[0:64, 0:ncols],
                                                                           in1=rcp[0:64, 0:ncols], op=ALU.mult),
                                     reads=[TPO, Trcp], writes=[TattT])
                  nu = len(units)
                  if nu:
                      emit_qk(units[0])
                  for i in range(nu):
                      if i + 1 < nu:
                          emit_qk(units[i + 1])
                      emit_rest(units[i])
                      if side is not None:
                          next(side, None)
                  if side is not None:
                      for _ in side:
                          pass

              def attention_halo(segs, side=None):
                  tiles = []
                  for (kind, t0, ntl, bcol, ksz) in segs:
                      for c0 in range(t0, t0 + ntl, 2):
                          cn = min(2, t0 + ntl - c0)
                          for j in range(cn):
                              tiles.append((c0, cn, j, bcol))
                  nt_ = len(tiles)
                  st_ = {}

                  def emit_qk(n):
                      c0, cn, j, bcol = tiles[n]
                      if j == 0:
                          bi = cnt["kb"] % NKB
                          cnt["kb"] += 1
                          KB3 = kb[bi][0:HD, 0:H * 256].rearrange("p (h t) -> p h t", h=H)
                          VB4 = vb[bi][:, 0:H * 2, :].rearrange("p (h s) c -> p h s c", h=H)
                          sp.dma(KB3[:, :, 0:cn * 128], KT[:, :, c0 * 128:(c0 + cn) * 128].rearrange("h d t -> d h t"),
                                 reads=[TKT], writes=[Tkb[bi]])
                          sp.dma(VB4[:, :, 0:cn, :], VV[:, :, c0:c0 + cn, :].rearrange("h p s c -> p h s c"),
                                 reads=[TVV], writes=[Tvb[bi]])
                          st_["cur"] = (KB3, VB4, Tkb[bi], Tvb[bi])
                      KB3, VB4, TKB, TVB = st_["cur"]
                      pi = cnt["pb"] % 2
                      cnt["pb"] += 1
                      PSp = pSS[:, pi * 1024:(pi + 1) * 1024]
                      for h in range(H):
                          pe.op(lambda e, h=h: e.matmul(PSp[:, h * 64:(h + 1) * 64], lhsT=KB3[:, h, j * 128:(j + 1) * 128],
                                                        rhs=qTa[0:HD, h, 64:128], start=True, stop=True,
                                                        skip_group_check=True),
                                reads=[TKB, TqTa], writes=[TpSS[pi]])
                      st_[n] = (PSp, pb[pi], TpSS[pi], Tpb[pi], VB4, TVB, j, bcol)

                  def emit_rest(n):
                      PSp, PB, TPS, TPB, VB4, TVB, j, bcol = st_.pop(n)
                      bias_ap = c_zero[:, 0:1] if bcol is None else c_bias[:, bcol:bcol + 1]
                      act.op(lambda e: e.activation(out=PB[:, 0:512], in_=PSp[:, 0:512], func=AF.Exp, bias=bias_ap, scale=SCALE),
                             reads=[TPS, Tc], writes=[TPB])
                      for h in range(H):
                          pe.op(lambda e, h=h: e.matmul(pO[0][:, h * 64:(h + 1) * 64], lhsT=VB4[:, h, j, :],
                                                        rhs=PB[:, h * 64:(h + 1) * 64],
                                                        start=(n == 0 and h == 0), stop=(n == nt_ - 1),
                                                        skip_group_check=True),
                                reads=[TVB, TPB], writes=[TpO[0]])
                  if nt_:
                      emit_qk(0)
                  for n in range(nt_):
                      if n + 1 < nt_:
                          emit_qk(n + 1)
                      emit_rest(n)
                      if side is not None:
                          next(side, None)
                          next(side, None)
                  if side is not None:
                      for _ in side:
                          pass
                  dve.op(lambda e: e.tensor_scalar(out=rcs[64:128, 0:512], in0=pO[0][64:128, :], scalar1=1e-30,
                                                   scalar2=None, op0=ALU.add), reads=[TpO[0]], writes=[Trcs])
                  dve.op(lambda e: e.reciprocal(out=rcs[64:128, 0:512], in_=rcs[64:128, 0:512]), reads=[Trcs], writes=[Trcs])
                  dve.op(lambda e: e.tensor_copy(out=rcp[0:64, 0:512], in_=rcs[64:128, 0:512]), reads=[Trcs], writes=[Trcp])
                  dve.op(lambda e: e.tensor_tensor(
                      out=attT[:, :, 64:128], in0=pO[0][0:64, :].rearrange("p (h t) -> p h t", h=H),
                      in1=rcp[0:64, 0:512].rearrange("p (h t) -> p h t", h=H), op=ALU.mult),
                      reads=[TpO[0], Trcp], writes=[TattT])

              def out_proj_g(aT_, TaT_, cT_, TcT_, src_rows, col, x1_row, bank=0):
                  bi = cnt["x1"] % 2
                  cnt["x1"] += 1
                  PB_, TPB_ = pM[bank], TpM[bank]
                  sp.dma(xr[bi][:], src_rows, writes=[Txr[bi]])
                  for nb in range(2):
                      for h in range(H):
                          pe.op(lambda e, nb=nb, h=h: e.matmul(PB_[:, :], lhsT=aT_[:, h, col:col + 128],
                                                               rhs=wA_oa[:, h, nb * 512:(nb + 1) * 512],
                                                               start=(h == 0), stop=False),
                                reads=[TaT_, Tw], writes=[TPB_])
                      for c in range(4):
                          pe.op(lambda e, nb=nb, c=c: e.matmul(PB_[:, :], lhsT=cT_[:, c, col:col + 128],
                                                               rhs=wA_oc[:, c, nb * 512:(nb + 1) * 512],
                                                               start=False, stop=(c == 3)),
                                reads=[TcT_, Tw], writes=[TPB_])
                      dve.op(lambda e, nb=nb: e.tensor_tensor(out=x1o[bi][:, nb * 512:(nb + 1) * 512], in0=PB_[:, :],
                                                              in1=xr[bi][:, nb * 512:(nb + 1) * 512], op=ALU.add),
                             reads=[TPB_, Txr[bi]], writes=[Tx1o[bi]])
                      yield
                  sp.dma(X1[x1_row * 128:(x1_row + 1) * 128, :], x1o[bi][:], reads=[Tx1o[bi]], writes=[TX1])
                  yield

              def out_proj(src_rows, col, x1_row):
                  for _ in out_proj_g(attT, TattT, cTg, TcTg, src_rows, col, x1_row, bank=0):
                      pass

              def emit_conv_state(ub, Tub, col0, dst):
                  for c in range(4):
                      pe.op(lambda e, c=c: e.transpose(out=pM[2][0:32, c * 128:(c + 1) * 128], in_=ub[:, c, col0:col0 + 32],
                                                       identity=c_idf[:]), reads=[Tub, Tc], writes=[TpM[2]])
                  dve.op(lambda e: e.tensor_copy(out=cvo[:], in_=pM[2][0:32, :]), reads=[TpM[2]], writes=[Tcvo])
                  sp.dma(dst, cvo[:], reads=[Tcvo], writes=[Tout])

              def kt_p(h, t0, n, ksz=128):
                  return KT[h, :, t0 * 128:(t0 + n - 1) * 128 + ksz]

              def vv_p(h, t0, n, ksz=128):
                  return VV[h, 0:ksz, t0:t0 + n, :]


              groups = []
              for t in range(NSLOT):
                  groups.append((t, None))
                  for g in range(RT // QG):
                      groups.append((t, g))

              def load_group(n):
                  t, g = groups[n]
                  b = n % 2
                  ci0 = t * (RT + 1) + (0 if g is None else 1 + g * QG)
                  ntl = 1 if g is None else QG
                  sp.dma(qTas[b][0:HD, :, 0:ntl * 128], QT[:, :, ci0 * 128:(ci0 + ntl) * 128], reads=[TQT], writes=[TqTas[b]])
                  sp.dma(uTs[b][:, :, 0:CK - 1 + ntl * 128],
                         UT[:, :, 32 + ci0 * 128 - (CK - 1):32 + (ci0 + ntl) * 128], reads=[TUT], writes=[TuTs[b]])
              def conv_of(n):
                  ncol_ = 128 if groups[n][1] is None else QG * 128
                  return conv_module_g(uTs[n % 2], TuTs[n % 2], cTgs[n % 2], TcTgs[n % 2], CK - 1, ncol_, 0, bank=0)

              def outproj_of(n):
                  t, g = groups[n]
                  b = n % 2
                  if g is None:
                      yield from out_proj_g(attTs[b], TattTs[b], cTgs[b], TcTgs[b], x_halo[t * 128:(t + 1) * 128, :], 0,
                                            t * (RT + 1), bank=0)
                  else:
                      for i in range(QG):
                          lt = t * G + g * QG + i
                          yield from out_proj_g(attTs[b], TattTs[b], cTgs[b], TcTgs[b], x_all[lt * 128:(lt + 1) * 128, :],
                                                i * 128, t * (RT + 1) + 1 + g * QG + i, bank=0)

              def chain(*gens):
                  for g_ in gens:
                      if g_ is not None:
                          yield from g_
              load_group(0)
              for _ in conv_of(0):
                  pass
              for n, (t, g) in enumerate(groups):
                  base = t * G
                  qTa, TqTa, uT, TuT = qTas[n % 2], TqTas[n % 2], uTs[n % 2], TuTs[n % 2]
                  attT, TattT = attTs[n % 2], TattTs[n % 2]
                  nxt = None
                  if n + 1 < len(groups):
                      load_group(n + 1)
                      nxt = conv_of(n + 1)
                  side = chain(outproj_of(n - 1) if n > 0 else None, nxt)
                  if g is None:
                      segs = []
                      if t > 0:
                          segs.append(("n", 0, base, None, 128))
                      for r in range(1, NCPB):
                          segs.append(("n", base + r * RT, RT, r, 128))
                      attention_halo(segs, side=side)
                  else:
                      i0 = g * QG
                      segs = []
                      if base + i0 > 0:
                          segs.append(("n", 0, base + i0, None, 128))
                      segs.append(("d", base + i0, QG, None, 128))
                      for r in range(1, NCPB):
                          segs.append(("n", base + r * RT, RT, r, 128))
                      attention(QG * 128, segs, kt_p, vv_p, TKT, TVV, side=side)
                      if t == NSLOT - 1 and g == RT // QG - 1:
                          emit_conv_state(uT, TuT, UW - 32, o_conv[:, :])
              for _ in outproj_of(len(groups) - 1):
                  pass
              attT, TattT = attTs[0], TattTs[0]
              qTa, TqTa, uT, TuT = qTas[0], TqTas[0], uTs[0], TuTs[0]
              cTg, TcTg = cTgs[0], TcTgs[0]

              ckpt("p2")
              uS = [sb(stA, f"uS{i}", [128, 4, CK - 1 + 64], F32) for i in range(2)]
              TuS = [T(f"uS{i}") for i in range(2)]
              for e_ in range(2):
                  sp.dma(uS[e_][:, :, 0:CK - 1], sconvT[:, e_, :, :], writes=[TuS[e_]])
              run_ways([0], lambda _, W: tile_front_A_g(W, x_s[:, :], c_ropesn, Tc, 0,
                                                        [(0, 64, (uS[0], TuS[0], CK - 1)), (64, 64, (uS[1], TuS[1], CK - 1))]))
              for e_ in range(2):
                  dve.op(lambda e, e_=e_: e.tensor_copy(out=uT[:, :, 0:CK - 1 + 64], in_=uS[e_][:, :, :]),
                         reads=[TuS[e_]], writes=[TuT])
                  conv_module(CK - 1, 64, e_ * 64)
                  emit_conv_state(uS[e_], TuS[e_], CK - 1 + 64 - 32, o_convs[e_, :, :])

                  def kt_s(h, t0, n, ksz=128, e_=e_):
                      return KTs[e_, h, :, t0 * 128:(t0 + n - 1) * 128 + ksz]

                  def vv_s(h, t0, n, ksz=128, e_=e_):
                      return VVs[e_, h, 0:ksz, t0:t0 + n, :]
                  attention(64, [("n", 0, PT + 1, None, 64)], kt_s, vv_s, TKTs, TVVs, qc0=e_ * 64)
              out_proj(x_s[:, :], 0, NX1 - 1)
              tk.barrier()
              ckpt("p2s")

          with ExitStack() as stB:
              wB_up = sb(stB, "wB_up", [128, 8, 2 * DFF], BF16)
              wB_dn = sb(stB, "wB_dn", [128, NFC, D], BF16)
              with ExitStack() as stW:
                  load_weight(stW, wB_up, w_up, 8, 2 * DFF, c_gffn, name="up")
                  load_weight(stW, wB_dn, w_down, NFC, D, None, name="dn")
                  tk.barrier()
              QB = QG
              NB_ = QB * 128
              xtB = sb(stB, "xtB", [128, QB, D], F32)
              TxtB = [T(f"xtB{j}") for j in range(QB)]
              msB = sb(stB, "msB", [128, QB], F32)
              TmsB = T("msB")
              xnB = sb(stB, "xnB", [128, D], BF16)
              TxnB = T("xnB")
              xnTB = sb(stB, "xnTB", [128, 8, NB_], BF16)
              TxnTB = T("xnTB")
              hT = sb(stB, "hT", [128, NFC, NB_], BF16)
              ThT = T("hT")
              AW = NB_ + 8
              aTc = [sb(stB, f"aTc{i}", [128, AW], F32) for i in range(2)]
              TaTc = [T(f"aTc{i}") for i in range(2)]
              accB = [sb(stB, f"accB{i}", [128, AW], F32) for i in range(2)]
              TaccB = [T(f"accB{i}") for i in range(2)]
              yo = [sb(stB, f"yo{i}", [128, D], F32) for i in range(2)]
              Tyo = [T(f"yo{i}") for i in range(2)]
              arow = [sb(stB, f"arow{i}", [128, 512], F32) for i in range(2)]
              Tarow = [T(f"arow{i}") for i in range(2)]
              hist = sb(stB, "hist", [128, NFC, 2], F32)
              Thist = T("hist")
              hnew = sb(stB, "hnew", [128, NFC, 2], F32)
              Thnew = T("hnew")
              Toutb = T("outsB")
              Abank = [(pM[0], TpM[0]), (pS[3], TpS[3])]
              Gbank = [(pS[0], TpS[0]), (pS[1], TpS[1])]
              Dbank = [(pO[0], TpO[0]), (pO[1], TpO[1])]
              cb_ = dict(ab=0, gb=0, db=0, yo=0, ar=0)

              def ffn_batch(x1_rows, subs, outs, a_rows_dst=None):
                  nt = len(x1_rows)
                  N = nt * 128
                  halo = outs is None
                  for j, r in enumerate(x1_rows):
                      sp.dma(xtB[:, j, :], X1[r * 128:(r + 1) * 128, :], writes=[TxtB[j]])
                      act.op(lambda e, j=j: e.activation(out=xnB[:], in_=xtB[:, j, :], func=AF.Square, scale=1.0 / math.sqrt(D),
                                                         accum_out=msB[:, j:j + 1]), reads=[TxtB[j]], writes=[TxnB, TmsB])
                  rstd_from_msq(None, (msB[:, 0:nt], TmsB), nt)
                  for j in range(nt):
                      dve.op(lambda e, j=j: e.tensor_scalar(out=xnB[:], in0=xtB[:, j, :], scalar1=msB[:, j:j + 1], scalar2=None,
                                                            op0=ALU.mult), reads=[TxtB[j], TmsB], writes=[TxnB])
                      for k in range(8):
                          pe.op(lambda e, k=k: e.transpose(out=pT[:, k * 128:(k + 1) * 128], in_=xnB[:, k * 128:(k + 1) * 128],
                                                           identity=c_idb[:]), reads=[TxnB, Tc], writes=[TpT])
                      act.op(lambda e, j=j: e.activation(out=xnTB[:, :, j * 128:(j + 1) * 128],
                                                         in_=pT[:, :].rearrange("p (k t) -> p k t", k=8), func=AF.Copy),
                             reads=[TpT], writes=[TxnTB])
                  W_ = N + 2 * len(subs)
                  state = {}

                  def stage1(c):
                      ai = cb_["ab"] % 2
                      cb_["ab"] += 1
                      PA, TPA = Abank[ai]
                      AT, TAT, AC, TAC = aTc[ai], TaTc[ai], accB[ai], TaccB[ai]
                      for k in range(8):
                          pe.op(lambda e, k=k: e.matmul(PA[:, 0:N], lhsT=wB_up[:, k, c * 128:(c + 1) * 128], rhs=xnTB[:, k, 0:N],
                                                        start=(k == 0), stop=(k == 7)), reads=[TxnTB, Tw], writes=[TPA])
                      if not halo:
                          gi = cb_["gb"] % 2
                          cb_["gb"] += 1
                          PG, TPG = Gbank[gi]
                          col = DFF + c * 128
                          for k in range(8):
                              pe.op(lambda e, k=k: e.matmul(PG[:, 0:N], lhsT=wB_up[:, k, col:col + 128], rhs=xnTB[:, k, 0:N],
                                                            start=(k == 0), stop=(k == 7)), reads=[TxnTB, Tw], writes=[TPG])
                      else:
                          PG = TPG = None
                      for i, (tok0, ntok, hap, Th) in enumerate(subs):
                          w0 = tok0 + 2 * i
                          act.op(lambda e, w0=w0, tok0=tok0, ntok=ntok: e.activation(
                              out=AT[:, w0 + 2:w0 + 2 + ntok], in_=PA[:, tok0:tok0 + ntok], func=AF.Copy),
                              reads=[TPA], writes=[TAT])
                          if not halo:
                              act.op(lambda e, w0=w0, tok0=tok0, ntok=ntok: e.activation(
                                  out=AC[:, w0:w0 + ntok], in_=PA[:, tok0:tok0 + ntok], func=AF.Identity,
                                  scale=c_fw[:, c, 2:3], bias=c_fb[:, c:c + 1]), reads=[TPA, Tc], writes=[TAC])
                              dve.op(lambda e, w0=w0, hap=hap: e.tensor_copy(out=AT[:, w0:w0 + 2], in_=hap[:, c, :]),
                                     reads=[Th], writes=[TAT])
                          if i == len(subs) - 1:
                              dve.op(lambda e, w0=w0, ntok=ntok: e.tensor_copy(out=hnew[:, c, :], in_=AT[:, w0 + ntok:w0 + ntok + 2]),
                                     reads=[TAT], writes=[Thnew])
                      state[c] = (AT, TAT, AC, TAC, PG, TPG)

                  def stage2(c):
                      AT, TAT, AC, TAC, PG, TPG = state.pop(c)
                      si = cb_["ab"] % 2
                      SL, TSL = AC, TAC
                      dve.op(lambda e: e.scalar_tensor_tensor(out=AC[:, 0:W_ - 2], in0=AT[:, 0:W_ - 2], scalar=c_fw[:, c, 0:1],
                                                              in1=AC[:, 0:W_ - 2], op0=ALU.mult, op1=ALU.add),
                             reads=[TAT, TAC, Tc], writes=[TAC])
                      dve.op(lambda e: e.scalar_tensor_tensor(out=AC[:, 0:W_ - 2], in0=AT[:, 1:W_ - 1], scalar=c_fw[:, c, 1:2],
                                                              in1=AC[:, 0:W_ - 2], op0=ALU.mult, op1=ALU.add),
                             reads=[TAT, TAC, Tc], writes=[TAC])
                      act.op(lambda e: e.activation(out=SL[:, 0:W_ - 2], in_=AC[:, 0:W_ - 2], func=AF.Silu),
                             reads=[TAC], writes=[TSL])
                      for i, (tok0, ntok, hap, Th) in enumerate(subs):
                          w0 = tok0 + 2 * i
                          dve.op(lambda e, w0=w0, tok0=tok0, ntok=ntok: e.tensor_tensor(
                              out=hT[:, c, tok0:tok0 + ntok], in0=PG[:, tok0:tok0 + ntok], in1=SL[:, w0:w0 + ntok], op=ALU.mult),
                              reads=[TPG, TSL], writes=[ThT])

                  if len(subs) > 1:
                      for i in range(2):
                          dve.op(lambda e, i=i: e.memset(accB[i][:], 0.0), writes=[TaccB[i]])
                  for c in range(NFC + 1):
                      if c < NFC:
                          stage1(c)
                      if c >= 1 and not halo:
                          stage2(c - 1)
                  dve.op(lambda e: e.tensor_copy(out=hist[:], in_=hnew[:]), reads=[Thnew], writes=[Thist])
                  if halo:
                      return
                  if a_rows_dst is not None:
                      jl = nt - 1
                      for n0 in range(0, DFF, 512):
                          nw = min(512, DFF - n0)
                          PA, TPA = pS[2], TpS[2]
                          ri = cb_["ar"] % 2
                          cb_["ar"] += 1
                          for k in range(8):
                              pe.op(lambda e, k=k, n0=n0, nw=nw: e.matmul(
                                  PA[:, 0:nw], lhsT=xnTB[:, k, jl * 128:(jl + 1) * 128], rhs=wB_up[:, k, n0:n0 + nw],
                                  start=(k == 0), stop=(k == 7)), reads=[TxnTB, Tw], writes=[TPA])
                          dve.op(lambda e, nw=nw, ri=ri: e.tensor_copy(out=arow[ri][:, 0:nw], in_=PA[:, 0:nw]),
                                 reads=[TPA], writes=[Tarow[ri]])
                          for (r0, dst) in a_rows_dst:
                              sp.dma(dst[:, n0:n0 + nw], arow[ri][r0:r0 + 32, 0:nw], reads=[Tarow[ri]], writes=[Toutb])
                  for j in range(nt):
                      yi = cb_["yo"] % 2
                      cb_["yo"] += 1
                      for nb in range(2):
                          PD, TPD = Dbank[cb_["db"] % 2]
                          cb_["db"] += 1
                          for c in range(NFC):
                              pe.op(lambda e, nb=nb, c=c, j=j, PD=PD: e.matmul(
                                  PD[:, :], lhsT=hT[:, c, j * 128:(j + 1) * 128], rhs=wB_dn[:, c, nb * 512:(nb + 1) * 512],
                                  start=(c == 0), stop=(c == NFC - 1)), reads=[ThT, Tw], writes=[TPD])
                          dve.op(lambda e, nb=nb, j=j, PD=PD, yi=yi: e.tensor_tensor(
                              out=yo[yi][:, nb * 512:(nb + 1) * 512], in0=PD[:, :], in1=xtB[:, j, nb * 512:(nb + 1) * 512],
                              op=ALU.add), reads=[TPD, TxtB[j]], writes=[Tyo[yi]])
                      sp.dma(outs[j], yo[yi][:], reads=[Tyo[yi]], writes=[Toutb])

              for t in range(NSLOT):
                  r0 = t * (RT + 1)
                  ffn_batch([r0], [(0, 128, None, None)], None)
                  dve.op(lambda e, t=t: e.tensor_scalar(out=hist[:], in0=hist[:], scalar1=c_hflag[:, t:t + 1],
                                                        scalar2=None, op0=ALU.mult), reads=[Thist, Tc], writes=[Thist])
                  for i0 in range(0, RT, QB):
                      last = (t == NSLOT - 1 and i0 + QB == RT)
                      ffn_batch([r0 + 1 + i0 + i for i in range(QB)], [(0, NB_, hist, Thist)],
                                [o_y[(t * RT + i0 + i) * 128:(t * RT + i0 + i + 1) * 128, :] for i in range(QB)],
                                a_rows_dst=[(96, o_ffn)] if last else None)
              hs = [sb(stB, f"hs{i}", [128, NFC, 2], F32) for i in range(2)]
              Ths = [T(f"hs{i}") for i in range(2)]
              for e_ in range(2):
                  sp.dma(hs[e_][:], sffnT[:, e_, :, :], writes=[Ths[e_]])
              ffn_batch([NX1 - 1], [(0, 64, hs[0], Ths[0]), (64, 64, hs[1], Ths[1])], [o_ys[:, :]],
                        a_rows_dst=[(32, o_ffns[0]), (96, o_ffns[1])])
              tk.barrier()
          print(f"[build] sems={tk.nsem} waits={tk.nwaits} insts={tk.ninst}", flush=True)
    except _Stop:
        pass
    return nc


def _rope_tab(pos):
    inv = (1.0 / (10000.0 ** (np.arange(0, RD, 2, dtype=np.float32) / np.float32(RD)))).astype(np.float32)
    ang = pos.astype(np.float32)[:, None] * inv[None, :]
    return np.concatenate([np.cos(ang.astype(np.float64)), np.sin(ang.astype(np.float64))], axis=1).astype(np.float32)


def _run(inputs, cfg):
    SEQ, PAST, RT = cfg["SEQ"], cfg["PAST"], cfg["RT"]
    NT = SEQ // 128
    G = NCPB * RT
    NSLOT = NT // G
    QG = min(4, RT)
    PT = PAST // 128
    f32 = np.float32
    bf = ml_dtypes.bfloat16
    g = {k: np.asarray(v) for k, v in inputs.items()}
    xp, xs = g["x_prompt"], g["x_sample"]
    B = xp.shape[0]
    assert B * NCPB == 8 and xs.shape[0] == 16

    def chunked(v, n):
        return np.ascontiguousarray(v.reshape(n, 128).T).astype(f32)

    def bc(v):
        return np.ascontiguousarray(np.broadcast_to(v[None, :], (128, v.shape[0]))).astype(f32)
    common = {
        "w_in": g["w_in"][0], "w_uq": g["w_uq"][0], "w_ukv": g["w_ukv"][0], "w_out": g["w_out"][0],
        "w_up": g["w_up"][0], "w_down": g["w_down"][0],
        "g_attn": chunked(g["attn_norm"][0], 8), "g_q": chunked(g["q_norm"][0], 3),
        "g_ffn": chunked(g["ffn_norm"][0], 8), "g_kv": bc(g["kv_norm"][0]),
        "g_hq": bc(g["qk_norm_q"][0]), "g_hk": bc(g["qk_norm_k"][0]),
        "cw": np.ascontiguousarray(g["conv_w"][0].T.reshape(4, 128, CK).transpose(1, 0, 2)),
        "cb": chunked(g["conv_b"][0], 4), "cg": chunked(g["conv_norm"][0], 4),
        "fw": np.ascontiguousarray(g["ffn_conv_w"][0].T.reshape(NFC, 128, 3).transpose(1, 0, 2)),
        "fb": chunked(g["ffn_conv_b"][0], NFC),
        "identb": np.eye(128, dtype=f32).astype(bf), "identf": np.eye(128, dtype=f32),
        "onesb": np.ones((128, 128), f32).astype(bf),
    }
    nch = QG * 2
    kh = np.zeros((32, QG * 128), f32)
    qm = np.zeros((32, QG * 128), f32)
    for c in range(nch):
        kh[c, c * 64:(c + 1) * 64] = 1.0
        qm[c, :c * 64] = NEG
    common["khot"] = kh.astype(bf)
    common["qmask"] = qm.astype(bf)
    common["rope_sp"] = np.ascontiguousarray(
        _rope_tab(np.arange(max(PT, 1) * 128)).reshape(max(PT, 1), 128, 32).transpose(1, 0, 2))
    common["rope_sn"] = _rope_tab(PAST + (np.arange(128) % 64))

    in_maps, metas = [], []
    for core in range(8):
        b, j = divmod(core, NCPB)
        others = [r for r in range(NCPB) if r != j]
        order = [j] + others
        gt = np.array([t * G + order[r] * RT + i for t in range(NSLOT) for r in range(NCPB) for i in range(RT)])
        tok = (gt[:, None] * 128 + np.arange(128)[None, :]).reshape(-1)
        m = dict(common)
        m["x_all"] = np.ascontiguousarray(xp[b][tok])
        xh = np.zeros((NSLOT, 128, D), f32)
        hf = np.zeros((128, NSLOT), f32)
        rh = np.zeros((NSLOT, 128, 32), f32)
        for t in range(NSLOT):
            ht = t * G + j * RT - 1
            if ht >= 0:
                xh[t] = xp[b, ht * 128:(ht + 1) * 128]
                hf[:, t] = 1.0
                rh[t] = _rope_tab(ht * 128 + np.arange(128))
        m["x_halo"] = xh.reshape(NSLOT * 128, D)
        m["hflag"] = hf
        m["rope_h"] = np.ascontiguousarray(rh.transpose(1, 0, 2))
        m["rope_k"] = np.ascontiguousarray(_rope_tab(tok).reshape(NT, 128, 32).transpose(1, 0, 2))
        bt = np.zeros((128, NCPB), f32)
        for r in range(1, NCPB):
            bt[:, r] = 0.0 if others[r - 1] < j else NEG
        m["biast"] = bt
        e0 = 2 * core
        m["x_s"] = np.ascontiguousarray(xs[e0:e0 + 2].reshape(128, D))
        m["cckv"] = np.ascontiguousarray(g["cache_ckv"][0, e0:e0 + 2].reshape(2 * PAST, KVL))
        m["ckpe"] = np.ascontiguousarray(g["cache_kpe"][0, e0:e0 + 2].reshape(2 * PAST, RD))
        sc = g["state_conv"][0, e0:e0 + 2]
        m["sconvT"] = np.ascontiguousarray(sc.transpose(2, 0, 1).reshape(4, 128, 2, CK - 1).transpose(1, 2, 0, 3))
        sf = g["state_ffn_conv"][0, e0:e0 + 2]
        m["sffnT"] = np.ascontiguousarray(sf.transpose(2, 0, 1).reshape(NFC, 128, 2, 2).transpose(1, 2, 0, 3))
        in_maps.append(m)
        own_tok = (np.array([t * G + j * RT + i for t in range(NSLOT) for i in range(RT)])[:, None] * 128
                   + np.arange(128)[None, :]).reshape(-1)
        metas.append((b, j, own_tok))

    nc = build(cfg)
    res = run_bass_kernel_spmd(nc, in_maps, core_ids=list(range(8)))
    R = res.results

    y_p = np.zeros((B, SEQ, D), f32)
    ckv_p = np.zeros((1, B, SEQ, KVL), f32)
    kpe_p = np.zeros((1, B, SEQ, RD), f32)
    conv_p = np.zeros((1, B, CK - 1, CC), f32)
    ffn_p = np.zeros((1, B, 2, DFF), f32)
    y_s = np.zeros((16, 64, D), f32)
    ckv_s = np.zeros((1, 16, 64, KVL), f32)
    kpe_s = np.zeros((1, 16, 64, RD), f32)
    conv_s = np.zeros((1, 16, CK - 1, CC), f32)
    ffn_s = np.zeros((1, 16, 2, DFF), f32)
    for core in range(8):
        b, j, own_tok = metas[core]
        r = R[core]
        y_p[b, own_tok] = r["o_y"]
        ckv_p[0, b, own_tok] = r["o_ckv"]
        kpe_p[0, b, own_tok] = r["o_kpe"]
        if j == NCPB - 1:
            conv_p[0, b] = r["o_conv"][2:32]
            ffn_p[0, b] = r["o_ffn"][30:32]
        e0 = 2 * core
        y_s[e0:e0 + 2] = r["o_ys"].reshape(2, 64, D)
        ckv_s[0, e0:e0 + 2] = r["o_ckvs"].reshape(2, 64, KVL)
        kpe_s[0, e0:e0 + 2] = r["o_kpes"].reshape(2, 64, RD)
        conv_s[0, e0:e0 + 2] = r["o_convs"][:, 2:32]
        ffn_s[0, e0:e0 + 2] = r["o_ffns"][:, 30:32]
    return (y_p, y_s, ckv_p, kpe_p, conv_p, ffn_p, ckv_s, kpe_s, conv_s, ffn_s)


def kernel(**inputs):
    return _run(inputs, CFG_FULL)
```

```python
import math
from contextlib import ExitStack

import numpy as np
import ml_dtypes

import concourse.bass as bass
import concourse.mybir as mybir
from concourse.bass_utils import run_bass_kernel_spmd

F32 = mybir.dt.float32
BF16 = mybir.dt.bfloat16
ALU = mybir.AluOpType
AF = mybir.ActivationFunctionType
AX = mybir.AxisListType

D = 1024
QL, KVL, RD, CC = 384, 256, 32, 512
H, HD, NOPE, VD = 8, 96, 64, 64
INW = QL + KVL + RD + 2 * CC
DFF = 2816
NFC = DFF // 128
CK = 31
EPS = 1e-6
SCALE = HD ** -0.5
NEG = -30000.0
NCPB = 4
KC = 16

CFG_FULL = dict(SEQ=16384, PAST=2048, RT=8)


class T:
    __slots__ = ("name", "w", "r", "dsem", "dcnt", "excl")

    def __init__(self, name, excl=False):
        self.name = name
        self.excl = excl
        self.w = None
        self.r = {}
        self.dsem = None
        self.dcnt = 0


class Eng:
    ROT = 30000

    def __init__(self, trk, eng, name):
        self.trk, self.eng, self.name = trk, eng, name
        self.sem = trk.new_sem(name)
        self.cnt = 0
        self.seen = {}

    def _wait(self, sem, val):
        if self.seen.get(sem, 0) >= val:
            return
        self.eng.wait_ge(sem, val)
        self.seen[sem] = val
        self.trk.nwaits += 1

    def _deps(self, reads, writes):
        need = {}

        def add(p, same_ok):
            if p is None:
                return
            sem, val = p
            if sem is self.sem and same_ok and self.name == "pe":
                return
            if need.get(sem, 0) < val:
                need[sem] = val
        for t in reads:
            add(t.w, False)
        for t in writes:
            add(t.w, True)
            for sem, val in t.r.items():
                add((sem, val), True)
        for sem, val in need.items():
            self._wait(sem, val)

    def op(self, fn, reads=(), writes=()):
        ex = [t for t in reads if t.excl and t not in writes]
        if ex:
            reads = [t for t in reads if not t.excl or t in writes]
            writes = list(writes) + ex
        self._deps(reads, writes)
        if self.cnt >= self.ROT:
            self.sem = self.trk.new_sem(self.name)
            self.cnt = 0
        inst = fn(self.eng)
        self.cnt += 1
        inst.then_inc(self.sem, 1)
        self.trk.ninst += 1
        for t in reads:
            if t.r.get(self.sem, 0) < self.cnt:
                t.r[self.sem] = self.cnt
        for t in writes:
            t.w = (self.sem, self.cnt)
            t.r = {}
        return inst

    def dma(self, out, in_, reads=(), writes=()):
        self._deps(reads, writes)
        tw = writes[0]
        if tw.dsem is None:
            tw.dsem = self.trk.new_sem("d_" + tw.name)
            self.trk.dts.append(tw)
        inst = self.eng.dma_start(out=out, in_=in_)
        inst.then_inc(tw.dsem, 16)
        tw.dcnt += 16
        self.trk.ninst += 1
        for t in reads:
            if t.r.get(tw.dsem, 0) < tw.dcnt:
                t.r[tw.dsem] = tw.dcnt
        tw.w = (tw.dsem, tw.dcnt)
        tw.r = {}
        return inst

    def wait_for(self, t):
        if t.w is not None:
            self._wait(*t.w)


class Tracker:
    def __init__(self, nc, stack):
        self.nc, self.stack = nc, stack
        self.nsem = 0
        self.nwaits = 0
        self.ninst = 0
        self.dts = []
        self.pe = Eng(self, nc.tensor, "pe")
        self.act = Eng(self, nc.scalar, "act")
        self.dve = Eng(self, nc.vector, "dve")
        self.pool = Eng(self, nc.gpsimd, "pool")
        self.sp = Eng(self, nc.sync, "sp")
        self.engs = [self.pe, self.act, self.dve, self.pool, self.sp]

    def new_sem(self, name):
        self.nsem += 1
        return self.stack.enter_context(self.nc.semaphore(f"s{self.nsem}_{name}"))

    def barrier(self):
        pts = [(e.sem, e.cnt) for e in self.engs if e.cnt > 0]
        pts += [(t.dsem, t.dcnt) for t in self.dts if t.dcnt > 0]
        for e in self.engs:
            for sem, val in pts:
                e._wait(sem, val)


def build(cfg):
    SEQ, PAST, RT = cfg["SEQ"], cfg["PAST"], cfg["RT"]
    NT = SEQ // 128
    G = NCPB * RT
    NSLOT = NT // G
    assert NSLOT * G == NT
    QG = min(4, RT)
    assert RT % QG == 0
    NOWN = NSLOT * RT
    PT = PAST // 128
    NX1 = NSLOT * (RT + 1) + 1

    nc = bass.Bass("TRN2", target_bir_lowering=False)

    def din(name, shape, dt=F32):
        return nc.dram_tensor(name, list(shape), dt, kind="ExternalInput").ap()

    def dout(name, shape, dt=F32):
        return nc.dram_tensor(name, list(shape), dt, kind="ExternalOutput").ap()

    def dscr(name, shape, dt):
        return nc.dram_tensor(name, list(shape), dt, kind="Internal").ap()

    x_all = din("x_all", [NT * 128, D])
    x_halo = din("x_halo", [NSLOT * 128, D])
    x_s = din("x_s", [128, D])
    cckv = din("cckv", [2 * PAST, KVL])
    ckpe = din("ckpe", [2 * PAST, RD])
    sconvT = din("sconvT", [128, 2, 4, CK - 1])
    sffnT = din("sffnT", [128, 2, NFC, 2])
    rope_k = din("rope_k", [128, NT, 32])
    rope_h = din("rope_h", [128, NSLOT, 32])
    rope_sp = din("rope_sp", [128, max(PT, 1), 32])
    rope_sn = din("rope_sn", [128, 32])
    hflag = din("hflag", [128, NSLOT])
    biast = din("biast", [128, NCPB])
    w_in = din("w_in", [D, INW])
    w_uq = din("w_uq", [QL, H * HD])
    w_ukv = din("w_ukv", [KVL, H * 128])
    w_out = din("w_out", [D, D])
    w_up = din("w_up", [D, 2 * DFF])
    w_down = din("w_down", [DFF, D])
    g_attn = din("g_attn", [128, 8])
    g_q = din("g_q", [128, 3])
    g_ffn = din("g_ffn", [128, 8])
    g_kv = din("g_kv", [128, KVL])
    g_hq = din("g_hq", [128, HD])
    g_hk = din("g_hk", [128, HD])
    cw = din("cw", [128, 4, CK])
    cb = din("cb", [128, 4])
    cg = din("cg", [128, 4])
    fw = din("fw", [128, NFC, 3])
    fb = din("fb", [128, NFC])
    identb = din("identb", [128, 128], BF16)
    identf = din("identf", [128, 128])
    onesb = din("onesb", [128, 128], BF16)
    khot = din("khot", [32, QG * 128], BF16)
    qmask = din("qmask", [32, QG * 128], BF16)

    o_y = dout("o_y", [NOWN * 128, D])
    o_ckv = dout("o_ckv", [NOWN * 128, KVL])
    o_kpe = dout("o_kpe", [NOWN * 128, RD])
    o_conv = dout("o_conv", [32, CC])
    o_ffn = dout("o_ffn", [32, DFF])
    o_ys = dout("o_ys", [128, D])
    o_ckvs = dout("o_ckvs", [128, KVL])
    o_kpes = dout("o_kpes", [128, RD])
    o_convs = dout("o_convs", [2, 32, CC])
    o_ffns = dout("o_ffns", [2, 32, DFF])

    KT = dscr("KT", [H, HD, NT * 128], BF16)
    VV = dscr("VV", [H, 128, NT, 128], BF16)
    KTs = dscr("KTs", [2, H, HD, (PT + 1) * 128], BF16)
    VVs = dscr("VVs", [2, H, 128, PT + 1, 128], BF16)
    X1 = dscr("X1", [NX1 * 128, D], F32)
    NCI = NSLOT * (RT + 1)
    QT = dscr("QT", [HD, H, NCI * 128], BF16)
    UT = dscr("UT", [128, 4, 32 + NCI * 128], F32)

    class _Stop(Exception):
        pass

    def ckpt(name):
        if cfg.get("STOP") == name:
            tk.barrier()
            raise _Stop()
    try:
      with ExitStack() as top:
          tk = Tracker(nc, top)
          pe, act, dve, pool, sp = tk.pe, tk.act, tk.dve, tk.pool, tk.sp

          def sb(st, name, shape, dt):
              return st.enter_context(nc.sbuf_tensor(name, list(shape), dt))

          def ps(st, name, shape, dt):
              return st.enter_context(nc.psum_tensor(name, list(shape), dt))

          pSS = ps(top, "pSS", [128, 2048], F32)
          pS = [pSS[:, i * 512:(i + 1) * 512] for i in range(4)]
          TpS = [T(f"pS{i}", excl=True) for i in range(4)]
          TpSS = [T(f"pSS{i}", excl=True) for i in range(2)]
          pOO = ps(top, "pOO", [128, 1024], F32)
          pO = [pOO[:, i * 512:(i + 1) * 512] for i in range(2)]
          TpO = [T(f"pO{i}", excl=True) for i in range(2)]
          pM0 = ps(top, "pM0", [128, 512], F32)
          pM = [pM0[:, :], pO[0], pO[1]]
          TpM = [T("pM0", excl=True), TpO[0], TpO[1]]
          pT = ps(top, "pT", [128, 1024], BF16)
          TpT = T("pT", excl=True)

          cst = {}
          Tc = T("consts")

          def cload(name, src, shape, dt=F32):
              t = sb(top, "c_" + name, shape, dt)
              sp.dma(t[:], src, writes=[Tc])
              cst[name] = t
              return t
          c_idb = cload("idb", identb[:, :], [128, 128], BF16)
          c_idf = cload("idf", identf[:, :], [128, 128])
          c_ones = cload("ones", onesb[:, :], [128, 128], BF16)
          c_gkv = cload("gkv", g_kv[:, :], [128, KVL])
          c_ghq = cload("ghq", g_hq[:, :], [128, HD])
          c_ghk = cload("ghk", g_hk[:, :], [128, HD])
          c_cw = cload("cw", cw[:, :, :], [128, 4, CK])
          c_cb = cload("cb", cb[:, :], [128, 4])
          c_cg = cload("cg", cg[:, :], [128, 4])
          c_fw = cload("fw", fw[:, :, :], [128, NFC, 3])
          c_fb = cload("fb", fb[:, :], [128, NFC])
          c_hflag = cload("hflag", hflag[:, :], [128, NSLOT])
          c_bias = cload("bias", biast[:, :], [128, NCPB])
          c_gattn = cload("gattn", g_attn[:, :], [128, 8])
          c_gq = cload("gq", g_q[:, :], [128, 3])
          c_gffn = cload("gffn", g_ffn[:, :], [128, 8])
          c_ropeh = cload("ropeh", rope_h[:, :, :], [128, NSLOT, 32])
          c_ropesn = cload("ropesn", rope_sn[:, :], [128, 32])
          c_zero = sb(top, "c_zero", [128, 1], F32)
          dve.op(lambda e: e.memset(c_zero[:], 0.0), writes=[Tc])
          c_eps = sb(top, "c_eps", [128, 1], F32)
          dve.op(lambda e: e.memset(c_eps[:], EPS), writes=[Tc])

          def load_weight(st, dst, src2d, nk, ncols, gain, kp=128, name="w"):
              CH = 2048
              stg = [sb(st, f"stg_{name}{i}", [128, CH], F32) for i in range(2)]
              Tst = [T(f"stg_{name}{i}") for i in range(2)]
              n = 0
              for k in range(nk):
                  for c0 in range(0, ncols, CH):
                      cwid = min(CH, ncols - c0)
                      b = n % 2
                      n += 1
                      sp.dma(stg[b][0:kp, 0:cwid], src2d[k * kp:(k + 1) * kp, c0:c0 + cwid], writes=[Tst[b]])
                      if n % 2:
                          if gain is not None:
                              dve.op(lambda e, b=b, k=k, c0=c0, cwid=cwid: e.tensor_scalar(
                                  out=dst[0:kp, k, c0:c0 + cwid], in0=stg[b][0:kp, 0:cwid],
                                  scalar1=gain[0:kp, k:k + 1], scalar2=None, op0=ALU.mult),
                                  reads=[Tst[b], Tc], writes=[Tw])
                          else:
                              dve.op(lambda e, b=b, k=k, c0=c0, cwid=cwid: e.tensor_copy(
                                  out=dst[0:kp, k, c0:c0 + cwid], in_=stg[b][0:kp, 0:cwid]),
                                  reads=[Tst[b]], writes=[Tw])
                      else:
                          if gain is not None:
                              act.op(lambda e, b=b, k=k, c0=c0, cwid=cwid: e.activation(
                                  out=dst[0:kp, k, c0:c0 + cwid], in_=stg[b][0:kp, 0:cwid], func=AF.Copy,
                                  scale=gain[0:kp, k:k + 1]), reads=[Tst[b], Tc], writes=[Tw])
                          else:
                              act.op(lambda e, b=b, k=k, c0=c0, cwid=cwid: e.activation(
                                  out=dst[0:kp, k, c0:c0 + cwid], in_=stg[b][0:kp, 0:cwid], func=AF.Copy),
                                  reads=[Tst[b]], writes=[Tw])

          Tw = T("weights")

          def rstd_from_msq(st_bufs, msq, n):
              ap, Tm = msq
              act.op(lambda e: e.activation(out=ap, in_=ap, func=AF.Sqrt, bias=c_eps[:, 0:1]), reads=[Tm, Tc], writes=[Tm])
              dve.op(lambda e: e.reciprocal(out=ap, in_=ap), reads=[Tm], writes=[Tm])

          class TileBufs:
              def __init__(self, st, tag, nx=2):
                  self.xt = [sb(st, f"xt{tag}{i}", [128, D], F32) for i in range(nx)]
                  self.Txt = [T(f"xt{tag}{i}") for i in range(nx)]
                  self.junk = sb(st, f"junk{tag}", [128, D], BF16)
                  self.Tjunk = T("junk" + tag)
                  self.st = sb(st, f"stat{tag}", [128, 8], F32)
                  self.Tst = [T(f"stat{tag}{i}") for i in range(8)]
                  self.xn = sb(st, f"xn{tag}", [128, D], BF16)
                  self.Txn = T("xn" + tag)
                  self.xnT = sb(st, f"xnT{tag}", [128, 8, 128], BF16)
                  self.TxnT = T("xnT" + tag)
                  self.n = 0

          def front_end(tb, src_rows, w_reads=()):
              b = tb.n % len(tb.xt)
              tb.n += 1
              xt, Txt = tb.xt[b], tb.Txt[b]
              sp.dma(xt[:], src_rows, reads=list(w_reads), writes=[Txt])
              ms, Tms = tb.st[:, 0:1], tb.Tst[0]
              act.op(lambda e: e.activation(out=tb.junk[:], in_=xt[:], func=AF.Square, scale=1.0 / math.sqrt(D),
                                            accum_out=ms), reads=[Txt], writes=[tb.Tjunk, Tms])
              rstd_from_msq(None, (ms, Tms), 1)
              dve.op(lambda e: e.tensor_scalar(out=tb.xn[:], in0=xt[:], scalar1=ms, scalar2=None, op0=ALU.mult),
                     reads=[Txt, Tms], writes=[tb.Txn])
              for k in range(8):
                  pe.op(lambda e, k=k: e.transpose(out=pT[:, k * 128:(k + 1) * 128], in_=tb.xn[:, k * 128:(k + 1) * 128],
                                                   identity=c_idb[:]), reads=[tb.Txn, Tc], writes=[TpT])
              act.op(lambda e: e.activation(out=tb.xnT[:].rearrange("p k t -> p (k t)"), in_=pT[:], func=AF.Copy),
                     reads=[TpT], writes=[tb.TxnT])
              return b

          class HeadBufs:
              def __init__(self, st, tag):
                  self.raw = sb(st, f"hraw{tag}", [128, H, HD], F32)
                  self.Traw = T("hraw" + tag)
                  self.sq = sb(st, f"hsq{tag}", [128, H, HD], F32)
                  self.Tsq = T("hsq" + tag)
                  self.rs = sb(st, f"hrs{tag}", [128, H], F32)
                  self.Trs = T("hrs" + tag)
                  self.t1, self.Tt1 = self.sq, self.Tsq
                  self.ra = sb(st, f"hra{tag}", [128, H, 16], F32)
                  self.rb = sb(st, f"hrb{tag}", [128, H, 16], F32)
                  self.Tra, self.Trb = T("hra" + tag), T("hrb" + tag)
                  self.fin = sb(st, f"hfin{tag}", [128, H, HD], BF16)
                  self.Tfin = T("hfin" + tag)

          def head_norm_rope(hb, gain, cs, Tcs):
              raw, sq, rs, t1, fin = hb.raw, hb.sq, hb.rs, hb.t1, hb.fin
              act.op(lambda e: e.activation(out=sq[:], in_=raw[:], func=AF.Square, scale=1.0 / math.sqrt(HD)),
                     reads=[hb.Traw], writes=[hb.Tsq])
              dve.op(lambda e: e.tensor_reduce(out=rs[:], in_=sq[:], axis=AX.X, op=ALU.add),
                     reads=[hb.Tsq], writes=[hb.Trs])
              rstd_from_msq(None, (rs[:], hb.Trs), H)
              dve.op(lambda e: e.tensor_tensor(out=t1[:], in0=raw[:], in1=rs[:].unsqueeze(2).to_broadcast([128, H, HD]),
                                               op=ALU.mult), reads=[hb.Traw, hb.Trs], writes=[hb.Tt1])
              pool.op(lambda e: e.tensor_tensor(out=t1[:], in0=t1[:], in1=gain[:].unsqueeze(1).to_broadcast([128, H, HD]),
                                                op=ALU.mult), reads=[hb.Tt1, Tc], writes=[hb.Tt1])
              cosb = cs[:, 0:16].unsqueeze(1).to_broadcast([128, H, 16])
              sinb = cs[:, 16:32].unsqueeze(1).to_broadcast([128, H, 16])
              p1, p2 = t1[:, :, 64:80], t1[:, :, 80:96]
              act.op(lambda e: e.activation(out=fin[:, :, 0:64], in_=t1[:, :, 0:64], func=AF.Copy),
                     reads=[hb.Tt1], writes=[hb.Tfin])
              dve.op(lambda e: e.tensor_tensor(out=hb.ra[:], in0=p1, in1=cosb, op=ALU.mult),
                     reads=[hb.Tt1, Tcs], writes=[hb.Tra])
              dve.op(lambda e: e.tensor_tensor(out=hb.rb[:], in0=p2, in1=sinb, op=ALU.mult),
                     reads=[hb.Tt1, Tcs], writes=[hb.Trb])
              dve.op(lambda e: e.tensor_tensor(out=fin[:, :, 64:80], in0=hb.ra[:], in1=hb.rb[:], op=ALU.subtract),
                     reads=[hb.Tra, hb.Trb], writes=[hb.Tfin])
              dve.op(lambda e: e.tensor_tensor(out=hb.ra[:], in0=p2, in1=cosb, op=ALU.mult),
                     reads=[hb.Tt1, Tcs], writes=[hb.Tra])
              dve.op(lambda e: e.tensor_tensor(out=hb.rb[:], in0=p1, in1=sinb, op=ALU.mult),
                     reads=[hb.Tt1, Tcs], writes=[hb.Trb])
              dve.op(lambda e: e.tensor_tensor(out=fin[:, :, 80:96], in0=hb.ra[:], in1=hb.rb[:], op=ALU.add),
                     reads=[hb.Tra, hb.Trb], writes=[hb.Tfin])

          NWAYS = 1
          NWAYS_P1 = 4

          def run_ways(tasks, make_gen, nways=None):
              nways = len(ways) if nways is None else nways
              it = iter(tasks)
              free = list(range(nways))
              active = []
              more = True
              while True:
                  while free and more:
                      try:
                          tsk = next(it)
                      except StopIteration:
                          more = False
                          break
                      w = free.pop(0)
                      active.append((make_gen(tsk, ways[w]), w))
                  if not active:
                      break
                  for gw in list(active):
                      try:
                          next(gw[0])
                      except StopIteration:
                          active.remove(gw)
                          free.append(gw[1])

          def rstd_g(ap, Tm):
              act.op(lambda e: e.activation(out=ap, in_=ap, func=AF.Sqrt, bias=c_eps[:, 0:1]), reads=[Tm, Tc], writes=[Tm])
              yield
              dve.op(lambda e: e.reciprocal(out=ap, in_=ap), reads=[Tm], writes=[Tm])
              yield

          pS2b = pS[2].bitcast(BF16)
          Tbanks = [(pT[:, :], TpT), (pS2b, TpS[2])]
          Pbanks = [(pO[1], TpO[1]), (pO[0], TpO[0])]
          Kbanks = [((pM[0], pM[1]), (TpM[0], TpM[1])), ((pS[0], pS[1]), (TpS[0], TpS[1]))]

          class Way:
              def __init__(self, st, w):
                  tag = f"W{w}"
                  self.w = w
                  self.xt = sb(st, "xt" + tag, [128, D], F32)
                  self.Txt = T("xt" + tag)
                  self.st = sb(st, "stat" + tag, [128, 8], F32)
                  self.Tst = [T(f"stat{tag}{i}") for i in range(8)]
                  self.xn = sb(st, "xn" + tag, [128, D], BF16)
                  self.Txn = T("xn" + tag)
                  self.junk, self.Tjunk = self.xn, self.Txn
                  self.xnT = sb(st, "xnT" + tag, [128, 8, 128], BF16)
                  self.TxnT = T("xnT" + tag)
                  self.hb = HeadBufs(st, tag)
                  self.rk = sb(st, "rk" + tag, [128, 32], F32)
                  self.Trk = T("rk" + tag)
                  self.cqb = sb(st, "cqb" + tag, [128, QL], BF16)
                  self.Tcqb = T("cqb" + tag)
                  self.cqf = self.hb.sq[:].rearrange("p h d -> p (h d)")[:, 0:QL]
                  self.Tcqf = self.hb.Tsq
                  self.cqT = sb(st, "cqT" + tag, [128, 3, 128], BF16)
                  self.TcqT = T("cqT" + tag)
                  self.sig = sb(st, "sig" + tag, [128, 4, 128], F32)
                  self.Tsig = T("sig" + tag)
                  self.pT, self.TpT = Tbanks[w % 2]
                  self.pP, self.TpP = Pbanks[w % 2]
                  self.pK, self.TpK = Kbanks[w % 2]

              def alloc_p1(self, st):
                  tag = f"W{self.w}"
                  self.ckv = sb(st, "ckv" + tag, [128, KVL], F32)
                  self.Tckv = T("ckv" + tag)
                  self.kpe = sb(st, "kpe" + tag, [128, RD], F32)
                  self.Tkpe = T("kpe" + tag)
                  self.ckvb = sb(st, "ckvb" + tag, [128, KVL], BF16)
                  self.Tckvb = T("ckvb" + tag)
                  self.ckvT = sb(st, "ckvT" + tag, [128, 2, 128], BF16)
                  self.TckvT = T("ckvT" + tag)
                  self.qst = sb(st, "qst" + tag, [HD, H, 128], BF16)
                  self.Tqst = T("qst" + tag)
                  self.ust, self.Tust = self.sig, self.Tsig
                  self.prj = sb(st, "prj" + tag, [128, KVL + RD], F32)
                  self.Tprj = T("prj" + tag)
                  self.cin = sb(st, "cin" + tag, [128, KVL], F32)
                  self.Tcin = T("cin" + tag)
                  self.kin = sb(st, "kin" + tag, [128, RD], F32)
                  self.Tkin = T("kin" + tag)

          def front_end_g(W, src_rows):
              sp.dma(W.xt[:], src_rows, writes=[W.Txt])
              ms, Tms = W.st[:, 0:1], W.Tst[0]
              act.op(lambda e: e.activation(out=W.junk[:], in_=W.xt[:], func=AF.Square, scale=1.0 / math.sqrt(D),
                                            accum_out=ms), reads=[W.Txt], writes=[W.Tjunk, Tms])
              yield
              yield from rstd_g(ms, Tms)
              dve.op(lambda e: e.tensor_scalar(out=W.xn[:], in0=W.xt[:], scalar1=ms, scalar2=None, op0=ALU.mult),
                     reads=[W.Txt, Tms], writes=[W.Txn])
              yield
              for k in range(8):
                  pe.op(lambda e, k=k: e.transpose(out=W.pT[:, k * 128:(k + 1) * 128], in_=W.xn[:, k * 128:(k + 1) * 128],
                                                   identity=c_idb[:]), reads=[W.Txn, Tc], writes=[W.TpT])
              act.op(lambda e: e.activation(out=W.xnT[:].rearrange("p k t -> p (k t)"), in_=W.pT, func=AF.Copy),
                     reads=[W.TpT], writes=[W.TxnT])
              yield

          def head_norm_rope_g(hb, gain, cs, Tcs):
              raw, sq, rs, t1, fin = hb.raw, hb.sq, hb.rs, hb.t1, hb.fin
              act.op(lambda e: e.activation(out=sq[:], in_=raw[:], func=AF.Square, scale=1.0 / math.sqrt(HD)),
                     reads=[hb.Traw], writes=[hb.Tsq])
              yield
              dve.op(lambda e: e.tensor_reduce(out=rs[:], in_=sq[:], axis=AX.X, op=ALU.add),
                     reads=[hb.Tsq], writes=[hb.Trs])
              yield
              yield from rstd_g(rs[:], hb.Trs)
              dve.op(lambda e: e.tensor_tensor(out=t1[:], in0=raw[:], in1=rs[:].unsqueeze(2).to_broadcast([128, H, HD]),
                                               op=ALU.mult), reads=[hb.Traw, hb.Trs], writes=[hb.Tt1])
              yield
              pool.op(lambda e: e.tensor_tensor(out=t1[:], in0=t1[:], in1=gain[:].unsqueeze(1).to_broadcast([128, H, HD]),
                                                op=ALU.mult), reads=[hb.Tt1, Tc], writes=[hb.Tt1])
              yield
              cosb = cs[:, 0:16].unsqueeze(1).to_broadcast([128, H, 16])
              sinb = cs[:, 16:32].unsqueeze(1).to_broadcast([128, H, 16])
              p1, p2 = t1[:, :, 64:80], t1[:, :, 80:96]
              act.op(lambda e: e.activation(out=fin[:, :, 0:64], in_=t1[:, :, 0:64], func=AF.Copy),
                     reads=[hb.Tt1], writes=[hb.Tfin])
              dve.op(lambda e: e.tensor_tensor(out=hb.ra[:], in0=p1, in1=cosb, op=ALU.mult),
                     reads=[hb.Tt1, Tcs], writes=[hb.Tra])
              pool.op(lambda e: e.tensor_tensor(out=hb.rb[:], in0=p2, in1=sinb, op=ALU.mult),
                      reads=[hb.Tt1, Tcs], writes=[hb.Trb])
              yield
              dve.op(lambda e: e.tensor_tensor(out=fin[:, :, 64:80], in0=hb.ra[:], in1=hb.rb[:], op=ALU.subtract),
                     reads=[hb.Tra, hb.Trb], writes=[hb.Tfin])
              yield
              dve.op(lambda e: e.tensor_tensor(out=hb.ra[:], in0=p2, in1=cosb, op=ALU.mult),
                     reads=[hb.Tt1, Tcs], writes=[hb.Tra])
              pool.op(lambda e: e.tensor_tensor(out=hb.rb[:], in0=p1, in1=sinb, op=ALU.mult),
                      reads=[hb.Tt1, Tcs], writes=[hb.Trb])
              yield
              dve.op(lambda e: e.tensor_tensor(out=fin[:, :, 80:96], in0=hb.ra[:], in1=hb.rb[:], op=ALU.add),
                     reads=[hb.Tra, hb.Trb], writes=[hb.Tfin])
              yield

          with ExitStack() as stA:
              wA_in = sb(stA, "wA_in", [128, 8, INW], BF16)
              wA_uq = sb(stA, "wA_uq", [128, 3, H * HD], BF16)
              wA_ukv = sb(stA, "wA_ukv", [128, 2, H * 128], BF16)
              wA_oa = sb(stA, "wA_oa", [64, 8, D], BF16)
              wA_oc = sb(stA, "wA_oc", [128, 4, D], BF16)
              with ExitStack() as stW:
                  load_weight(stW, wA_in, w_in, 8, INW, c_gattn, name="in")
                  load_weight(stW, wA_uq, w_uq, 3, H * HD, c_gq, name="uq")
                  load_weight(stW, wA_ukv, w_ukv, 2, H * 128, None, name="ukv")
                  load_weight(stW, wA_oa, w_out, 8, D, None, kp=64, name="oa")
                  load_weight(stW, wA_oc, w_out[512:1024, :], 4, D, None, name="oc")
                  tk.barrier()
              ckpt("w")

              def tile_front_A_g(W, src_rows, cs, Tcs, qcol, ntok_groups):
                  yield from front_end_g(W, src_rows)
                  yield from qglu_g(W, cs, Tcs, qTa[0:HD, :, qcol:qcol + 128], TqTa, ntok_groups)

              def qglu_g(W, cs, Tcs, qdst, Tqdst, ntok_groups):
                  hb = W.hb
                  for k in range(8):
                      pe.op(lambda e, k=k: e.matmul(W.pP[:, 0:QL], lhsT=W.xnT[:, k, :], rhs=wA_in[:, k, 0:QL],
                                                    start=(k == 0), stop=(k == 7)), reads=[W.TxnT, Tw], writes=[W.TpP])
                  dve.op(lambda e: e.tensor_copy(out=W.cqf, in_=W.pP[:, 0:QL]), reads=[W.TpP], writes=[W.Tcqf])
                  yield
                  ms, Tms = W.st[:, 2:3], W.Tst[2]
                  act.op(lambda e: e.activation(out=W.junk[:, 0:QL], in_=W.cqf, func=AF.Square,
                                                scale=1.0 / math.sqrt(QL), accum_out=ms),
                         reads=[W.Tcqf], writes=[W.Tjunk, Tms])
                  yield
                  yield from rstd_g(ms, Tms)
                  dve.op(lambda e: e.tensor_scalar(out=W.cqb[:], in0=W.cqf, scalar1=ms, scalar2=None, op0=ALU.mult),
                         reads=[W.Tcqf, Tms], writes=[W.Tcqb])
                  yield
                  for k in range(3):
                      pe.op(lambda e, k=k: e.transpose(out=W.pT[:, k * 128:(k + 1) * 128], in_=W.cqb[:, k * 128:(k + 1) * 128],
                                                       identity=c_idb[:]), reads=[W.Tcqb, Tc], writes=[W.TpT])
                  dve.op(lambda e: e.tensor_copy(out=W.cqT[:].rearrange("p k t -> p (k t)"), in_=W.pT[:, 0:384]),
                         reads=[W.TpT], writes=[W.TcqT])
                  yield
                  for nb, (c0, cw_) in enumerate(((0, 512), (512, 256))):
                      for k in range(3):
                          pe.op(lambda e, nb=nb, k=k, c0=c0, cw_=cw_: e.matmul(
                              W.pK[nb][:, 0:cw_], lhsT=W.cqT[:, k, :], rhs=wA_uq[:, k, c0:c0 + cw_],
                              start=(k == 0), stop=(k == 2)), reads=[W.TcqT, Tw], writes=[W.TpK[nb]])
                  rawf = hb.raw[:].rearrange("p h d -> p (h d)")
                  act.op(lambda e: e.activation(out=rawf[:, 0:512], in_=W.pK[0][:, 0:512], func=AF.Copy),
                         reads=[W.TpK[0]], writes=[hb.Traw])
                  dve.op(lambda e: e.tensor_copy(out=rawf[:, 512:768], in_=W.pK[1][:, 0:256]),
                         reads=[W.TpK[1]], writes=[hb.Traw])
                  yield
                  yield from head_norm_rope_g(hb, c_ghq, cs, Tcs)
                  for h in range(H):
                      pe.op(lambda e, h=h: e.transpose(out=W.pT[0:HD, h * 128:(h + 1) * 128], in_=hb.fin[:, h, :],
                                                       identity=c_idb[:]), reads=[hb.Tfin, Tc], writes=[W.TpT])
                  act.op(lambda e: e.activation(out=qdst,
                                                in_=W.pT[0:HD, :].rearrange("p (h t) -> p h t", h=H), func=AF.Copy),
                         reads=[W.TpT], writes=[Tqdst])
                  yield
                  for half in (1, 0):
                      for c in range(4):
                          col = QL + KVL + RD + half * CC + c * 128
                          for k in range(8):
                              pe.op(lambda e, half=half, c=c, k=k, col=col: e.matmul(
                                  W.pK[half][:, c * 128:(c + 1) * 128], lhsT=wA_in[:, k, col:col + 128], rhs=W.xnT[:, k, :],
                                  start=(k == 0), stop=(k == 7)), reads=[W.TxnT, Tw], writes=[W.TpK[half]])
                      if half == 1:
                          act.op(lambda e: e.activation(out=W.sig[:].rearrange("p c t -> p (c t)"), in_=W.pK[1][:, :],
                                                        func=AF.Sigmoid), reads=[W.TpK[1]], writes=[W.Tsig])
                  for (tok0, ntok, ucol_) in ntok_groups:
                      tgt = ucol_ if isinstance(ucol_, tuple) else (uT, TuT, ucol_)
                      ub, Tub, uc = tgt
                      dve.op(lambda e, tok0=tok0, ntok=ntok, ub=ub, uc=uc: e.tensor_tensor(
                          out=ub[:, :, uc:uc + ntok],
                          in0=W.pK[0][:, :].rearrange("p (c t) -> p c t", c=4)[:, :, tok0:tok0 + ntok],
                          in1=W.sig[:, :, tok0:tok0 + ntok], op=ALU.mult), reads=[W.TpK[0], W.Tsig], writes=[Tub])
                  yield

              ways = [Way(stA, w) for w in range(NWAYS)]
              TKT, TVV = T("KT"), T("VV")
              TKTs, TVVs = T("KTs"), T("VVs")
              TX1 = T("X1")
              Tout = T("outs")
              with ExitStack() as stP1:
                  for w in range(NWAYS, NWAYS_P1):
                      ways.append(Way(stP1, w))
                  for W_ in ways:
                      W_.alloc_p1(stP1)
                  KS = 4
                  kst = [sb(stP1, f"kst{i}", [HD, H, KS * 128], BF16) for i in range(2)]
                  Tkst = [T(f"kst{i}") for i in range(2)]
                  vst = [sb(stP1, f"vst{i}", [128, H, KS, 128], BF16) for i in range(2)]
                  Tvst = [T(f"vst{i}") for i in range(2)]
                  for i in range(2):
                      pool.op(lambda e, i=i: e.memset(vst[i][:], 1.0), writes=[Tvst[i]])

                  def kv_from_ckv_g(W, ckv_ap, Tck, kpe_ap, Tkp, cs, Tcs, stage_slot):
                      sbuf, slot = stage_slot
                      hb = W.hb
                      act.op(lambda e: e.activation(out=W.ckvb[:], in_=ckv_ap, func=AF.Copy), reads=[Tck], writes=[W.Tckvb])
                      yield
                      for k in range(2):
                          pe.op(lambda e, k=k: e.transpose(out=W.pT[:, k * 128:(k + 1) * 128],
                                                           in_=W.ckvb[:, k * 128:(k + 1) * 128], identity=c_idb[:]),
                                reads=[W.Tckvb, Tc], writes=[W.TpT])
                      dve.op(lambda e: e.tensor_copy(out=W.ckvT[:].rearrange("p k t -> p (k t)"), in_=W.pT[:, 0:256]),
                             reads=[W.TpT], writes=[W.TckvT])
                      yield
                      for nb in range(2):
                          for k in range(2):
                              pe.op(lambda e, nb=nb, k=k: e.matmul(W.pK[nb][:, :], lhsT=W.ckvT[:, k, :],
                                                                   rhs=wA_ukv[:, k, nb * 512:(nb + 1) * 512],
                                                                   start=(k == 0), stop=(k == 1)),
                                    reads=[W.TckvT, Tw], writes=[W.TpK[nb]])
                      for nb in range(2):
                          src = W.pK[nb][:, :].rearrange("p (h c) -> p h c", h=4)
                          act.op(lambda e, nb=nb, src=src: e.activation(out=hb.raw[:, nb * 4:(nb + 1) * 4, 0:64],
                                                                        in_=src[:, :, 0:64], func=AF.Copy),
                                 reads=[W.TpK[nb]], writes=[hb.Traw])
                          dve.op(lambda e, nb=nb, src=src: e.tensor_copy(out=vst[sbuf][:, nb * 4:(nb + 1) * 4, slot, 0:64],
                                                                         in_=src[:, :, 64:128]),
                                 reads=[W.TpK[nb]], writes=[Tvst[sbuf]])
                      yield
                      pool.op(lambda e: e.tensor_copy(out=hb.raw[:, :, 64:96],
                                                      in_=kpe_ap.unsqueeze(1).to_broadcast([128, H, RD])),
                              reads=[Tkp], writes=[hb.Traw])
                      yield
                      yield from head_norm_rope_g(hb, c_ghk, cs, Tcs)
                      for h in range(H):
                          pe.op(lambda e, h=h: e.transpose(out=W.pT[0:HD, h * 128:(h + 1) * 128], in_=hb.fin[:, h, :],
                                                           identity=c_idb[:]), reads=[hb.Tfin, Tc], writes=[W.TpT])
                      act.op(lambda e: e.activation(out=kst[sbuf][:, :, slot * 128:(slot + 1) * 128],
                                                    in_=W.pT[0:HD, :].rearrange("p (h t) -> p h t", h=H), func=AF.Copy),
                             reads=[W.TpT], writes=[Tkst[sbuf]])
                      yield

                  def ckv_from_x_g(W, own_row=None, o_ck=None, o_kp=None):
                      for k in range(8):
                          pe.op(lambda e, k=k: e.matmul(W.pP[:, 0:KVL + RD], lhsT=W.xnT[:, k, :],
                                                        rhs=wA_in[:, k, QL:QL + KVL + RD], start=(k == 0), stop=(k == 7)),
                                reads=[W.TxnT, Tw], writes=[W.TpP])
                      dve.op(lambda e: e.tensor_copy(out=W.prj[:], in_=W.pP[:, 0:KVL + RD]), reads=[W.TpP], writes=[W.Tprj])
                      yield
                      ms, Tms = W.st[:, 1:2], W.Tst[1]
                      act.op(lambda e: e.activation(out=W.junk[:, 0:KVL], in_=W.prj[:, 0:KVL], func=AF.Square,
                                                    scale=1.0 / math.sqrt(KVL), accum_out=ms),
                             reads=[W.Tprj], writes=[W.Tjunk, Tms])
                      pool.op(lambda e: e.tensor_copy(out=W.kpe[:], in_=W.prj[:, KVL:KVL + RD]),
                              reads=[W.Tprj], writes=[W.Tkpe])
                      yield
                      yield from rstd_g(ms, Tms)
                      dve.op(lambda e: e.scalar_tensor_tensor(out=W.ckv[:], in0=W.prj[:, 0:KVL], scalar=ms, in1=c_gkv[:],
                                                              op0=ALU.mult, op1=ALU.mult),
                             reads=[W.Tprj, Tms, Tc], writes=[W.Tckv])
                      yield
                      if own_row is not None:
                          sp.dma(o_ck[own_row:own_row + 128, :], W.ckv[:], reads=[W.Tckv], writes=[Tout])
                          sp.dma(o_kp[own_row:own_row + 128, :], W.kpe[:], reads=[W.Tkpe], writes=[Tout])

                  gdone = {}

                  TQT, TUT = T("QT"), T("UT")
                  zpad = sb(stP1, "zpad", [128, 4, 32], F32)
                  Tzpad = T("zpad")
                  dve.op(lambda e: e.memset(zpad[:], 0.0), writes=[Tzpad])
                  sp.dma(UT[:, :, 0:32], zpad[:], reads=[Tzpad], writes=[TUT])

                  def qu_to_scratch_g(W, cs, Tcs, ci):
                      yield from qglu_g(W, cs, Tcs, W.qst[:], W.Tqst, [(0, 128, (W.ust, W.Tust, 0))])
                      sp.dma(QT[:, :, ci * 128:(ci + 1) * 128], W.qst[:], reads=[W.Tqst], writes=[TQT])
                      sp.dma(UT[:, :, 32 + ci * 128:32 + (ci + 1) * 128], W.ust[:], reads=[W.Tust], writes=[TUT])

                  def p1_tile_g(lt, W):
                      if isinstance(lt, tuple):
                          t = lt[1]
                          yield from front_end_g(W, x_halo[t * 128:(t + 1) * 128, :])
                          yield from qu_to_scratch_g(W, c_ropeh[:, t, :], Tc, t * (RT + 1))
                          return
                      gi = lt // KS
                      sbuf, slot = gi % 2, lt % KS
                      sp.dma(W.rk[:], rope_k[:, lt, :], writes=[W.Trk])
                      yield from front_end_g(W, x_all[lt * 128:(lt + 1) * 128, :])
                      t, rem = divmod(lt, G)
                      own = rem < RT
                      yield from ckv_from_x_g(W, own_row=(t * RT + rem) * 128 if own else None, o_ck=o_ckv, o_kp=o_kpe)
                      yield from kv_from_ckv_g(W, W.ckv[:], W.Tckv, W.kpe[:], W.Tkpe, W.rk, W.Trk, (sbuf, slot))
                      if own:
                          yield from qu_to_scratch_g(W, W.rk, W.Trk, t * (RT + 1) + 1 + rem)
                      gdone[gi] = gdone.get(gi, 0) + 1
                      if gdone[gi] == KS:
                          lt0 = gi * KS
                          sp.dma(KT[:, :, lt0 * 128:(lt0 + KS) * 128].rearrange("h d t -> d h t"), kst[sbuf][:],
                                 reads=[Tkst[sbuf]], writes=[TKT])
                          sp.dma(VV[:, :, lt0:lt0 + KS, :].rearrange("h p s c -> p h s c"), vst[sbuf][:],
                                 reads=[Tvst[sbuf]], writes=[TVV])
                  run_ways(list(range(NT)) + [("h", t) for t in range(NSLOT)], p1_tile_g)

                  ckpt("p1")
                  NG0 = NT // KS
                  PG = (PT + KS - 1) // KS
                  sdone = {}

                  def p1s_tile_g(ep, W):
                      e_, p = ep
                      gi = NG0 + e_ * PG + p // KS
                      sbuf, slot = gi % 2, p % KS
                      r0 = e_ * PAST + p * 128
                      sp.dma(W.cin[:], cckv[r0:r0 + 128, :], writes=[W.Tcin])
                      sp.dma(W.kin[:], ckpe[r0:r0 + 128, :], writes=[W.Tkin])
                      sp.dma(W.rk[:], rope_sp[:, p, :], writes=[W.Trk])
                      yield from kv_from_ckv_g(W, W.cin[:], W.Tcin, W.kin[:], W.Tkin, W.rk, W.Trk, (sbuf, slot))
                      sdone[gi] = sdone.get(gi, 0) + 1
                      p0 = (p // KS) * KS
                      ns = min(KS, PT - p0)
                      if sdone[gi] == ns:
                          sp.dma(KTs[e_, :, :, p0 * 128:(p0 + ns) * 128].rearrange("h d t -> d h t"),
                                 kst[sbuf][:, :, 0:ns * 128], reads=[Tkst[sbuf]], writes=[TKTs])
                          sp.dma(VVs[e_, :, :, p0:p0 + ns, :].rearrange("h p s c -> p h s c"),
                                 vst[sbuf][:, :, 0:ns, :], reads=[Tvst[sbuf]], writes=[TVVs])
                  run_ways([(e_, p) for e_ in range(2) for p in range(PT)], p1s_tile_g)
                  sbuf = (NG0 + 2 * PG) % 2

                  def p1n_g(_, W):
                      yield from front_end_g(W, x_s[:, :])
                      yield from ckv_from_x_g(W, own_row=0, o_ck=o_ckvs, o_kp=o_kpes)
                      yield from kv_from_ckv_g(W, W.ckv[:], W.Tckv, W.kpe[:], W.Tkpe, c_ropesn, Tc, (sbuf, 0))
                  run_ways([0], p1n_g)
                  for e_ in range(2):
                      sp.dma(KTs[e_, :, :, PT * 128:PT * 128 + 64].rearrange("h d t -> d h t"),
                             kst[sbuf][:, :, e_ * 64:(e_ + 1) * 64], reads=[Tkst[sbuf]], writes=[TKTs])
                      sp.dma(VVs[e_, :, 0:64, PT:PT + 1, :].rearrange("h p s c -> p h s c"),
                             vst[sbuf][e_ * 64:(e_ + 1) * 64, :, 0:1, :], reads=[Tvst[sbuf]], writes=[TVVs])

                  tk.barrier()
                  del ways[NWAYS:]
              ckpt("p1s")
              qTas = [sb(stA, f"qTa{i}", [128, H, QG * 128], BF16) for i in range(2)]
              TqTas = [T(f"qTa{i}") for i in range(2)]
              for i in range(2):
                  for h in range(H):
                      sp.dma(qTas[i][96:128, h, :], qmask[:, :], writes=[TqTas[i]])
              qTa, TqTa = qTas[0], TqTas[0]
              cTgs = [sb(stA, f"cTg{i}", [128, 4, QG * 128], BF16) for i in range(2)]
              TcTgs = [T(f"cTg{i}") for i in range(2)]
              cTg, TcTg = cTgs[0], TcTgs[0]
              attTs = [sb(stA, f"attT{i}", [64, H, QG * 128], BF16) for i in range(2)]
              TattTs = [T(f"attT{i}") for i in range(2)]
              attT, TattT = attTs[0], TattTs[0]
              for i in range(2):
                  pool.op(lambda e, i=i: e.memset(attTs[i][:], 0.0), writes=[TattTs[i]])
              UW = CK - 1 + QG * 128
              uTs = [sb(stA, f"uT{i}", [128, 4, UW], F32) for i in range(2)]
              TuTs = [T(f"uT{i}") for i in range(2)]
              uT, TuT = uTs[0], TuTs[0]
              acc = sb(stA, "acc", [128, 4, QG * 128], F32)
              Tacc = T("acc")
              sqc = sb(stA, "sqc", [128, 4, QG * 128], BF16)
              Tsqc = T("sqc")
              rsc = sb(stA, "rsc", [128, QG * 128], F32)
              Trsc = T("rsc")
              cpre, Tcpre = acc, Tacc
              NKB = 3
              kb = [sb(stA, f"kb{i}", [HD, KC * 128], BF16) for i in range(NKB)]
              Tkb = [T(f"kb{i}") for i in range(NKB)]
              vb = [sb(stA, f"vb{i}", [128, KC, 128], BF16) for i in range(NKB)]
              Tvb = [T(f"vb{i}") for i in range(NKB)]
              kd = [sb(stA, f"kd{i}", [128, QG * 128], BF16) for i in range(2)]
              Tkd = [T(f"kd{i}") for i in range(2)]
              for i in range(2):
                  sp.dma(kd[i][96:128, :], khot[:, :], writes=[Tkd[i]])
              vd = [sb(stA, f"vd{i}", [128, QG, 128], BF16) for i in range(2)]
              Tvd = [T(f"vd{i}") for i in range(2)]
              pb = [sb(stA, f"pb{i}", [128, 1024], BF16) for i in range(2)]
              Tpb = [T(f"pb{i}") for i in range(2)]
              rcp = sb(stA, "rcp", [64, 512], F32)
              Trcp = T("rcp")
              rcs = sb(stA, "rcs", [128, 512], F32)
              Trcs = T("rcs")
              xr = [sb(stA, f"xr{i}", [128, D], F32) for i in range(2)]
              Txr = [T(f"xr{i}") for i in range(2)]
              x1o, Tx1o = xr, Txr
              cvo = sb(stA, "cvo", [32, CC], F32)
              Tcvo = T("cvo")
              cnt = dict(kb=0, pb=0, x1=0, po=0, kd=0)

              def conv_module_g(uT_, TuT_, cT_, TcT_, ucol, ncol, ccol, bank=0):
                  PB_, TPB_ = pM[bank], TpM[bank]
                  for c in range(4):
                      dve.op(lambda e, c=c: e.tensor_scalar(out=acc[:, c, 0:ncol], in0=uT_[:, c, ucol - 30:ucol - 30 + ncol],
                                                            scalar1=c_cw[:, c, 0:1], scalar2=c_cb[:, c:c + 1],
                                                            op0=ALU.mult, op1=ALU.add),
                             reads=[TuT_, Tc], writes=[Tacc])
                      yield
                      for k in range(1, CK):
                          dve.op(lambda e, c=c, k=k: e.scalar_tensor_tensor(
                              out=acc[:, c, 0:ncol], in0=uT_[:, c, ucol - 30 + k:ucol - 30 + k + ncol],
                              scalar=c_cw[:, c, k:k + 1], in1=acc[:, c, 0:ncol], op0=ALU.mult, op1=ALU.add),
                              reads=[TuT_, Tc, Tacc], writes=[Tacc])
                          yield
                  act.op(lambda e: e.activation(out=sqc[:, :, 0:ncol], in_=acc[:, :, 0:ncol], func=AF.Square,
                                                scale=1.0 / math.sqrt(CC)), reads=[Tacc], writes=[Tsqc])
                  yield
                  for c in range(4):
                      pe.op(lambda e, c=c: e.matmul(PB_[:, 0:ncol], lhsT=c_ones[:], rhs=sqc[:, c, 0:ncol],
                                                    start=(c == 0), stop=(c == 3)), reads=[Tsqc, Tc], writes=[TPB_])
                  dve.op(lambda e: e.tensor_scalar(out=rsc[:, 0:ncol], in0=PB_[:, 0:ncol], scalar1=EPS, scalar2=None,
                                                   op0=ALU.add), reads=[TPB_], writes=[Trsc])
                  yield
                  act.op(lambda e: e.activation(out=rsc[:, 0:ncol], in_=rsc[:, 0:ncol], func=AF.Sqrt),
                         reads=[Trsc], writes=[Trsc])
                  yield
                  dve.op(lambda e: e.reciprocal(out=rsc[:, 0:ncol], in_=rsc[:, 0:ncol]), reads=[Trsc], writes=[Trsc])
                  yield
                  for c in range(4):
                      dve.op(lambda e, c=c: e.scalar_tensor_tensor(out=cpre[:, c, 0:ncol], in0=acc[:, c, 0:ncol],
                                                                   scalar=c_cg[:, c:c + 1], in1=rsc[:, 0:ncol],
                                                                   op0=ALU.mult, op1=ALU.mult),
                             reads=[Tacc, Trsc, Tc], writes=[Tcpre])
                      yield
                  act.op(lambda e: e.activation(out=cT_[:, :, ccol:ccol + ncol], in_=cpre[:, :, 0:ncol], func=AF.Silu),
                         reads=[Tcpre], writes=[TcT_])
                  yield

              def conv_module(ucol, ncol, ccol):
                  for _ in conv_module_g(uT, TuT, cTg, TcTg, ucol, ncol, ccol, bank=2):
                      pass

              def attention(ncols, segs, kt_src, vv_src, Tk, Tv, qc0=0, ksz_last=128, side=None):
                  items = []
                  for h in range(H):
                      po_i = cnt["po"] % 2
                      cnt["po"] += 1
                      PO, TPO = pO[po_i], TpO[po_i]
                      first = True
                      nseg = len(segs)
                      for si, (kind, t0, ntl, bcol, ksz) in enumerate(segs):
                          last_seg = si == nseg - 1
                          if kind == "d":
                              di = cnt["kd"] % 2
                              cnt["kd"] += 1
                              KD, TKD, VD, TVD = kd[di], Tkd[di], vd[di], Tvd[di]
                              loads = [(KD[0:HD, 0:ntl * 128], kt_src(h, t0, ntl), Tk, TKD),
                                       (VD[:, 0:ntl, :], vv_src(h, t0, ntl), Tv, TVD)]
                              chunks = [(t0, ntl, KD, TKD, VD, TVD, 128, loads)]
                          else:
                              chunks = []
                              for c0 in range(t0, t0 + ntl, KC):
                                  cn = min(KC, t0 + ntl - c0)
                                  bi = cnt["kb"] % NKB
                                  cnt["kb"] += 1
                                  KB, TKB, VB, TVB = kb[bi], Tkb[bi], vb[bi], Tvb[bi]
                                  lastc = (c0 + cn == t0 + ntl)
                                  kz_l = ksz if lastc else 128
                                  loads = [(KB[0:HD, 0:(cn - 1) * 128 + kz_l], kt_src(h, c0, cn, kz_l), Tk, TKB)]
                                  if kz_l == 128:
                                      loads.append((VB[:, 0:cn, :], vv_src(h, c0, cn), Tv, TVB))
                                  else:
                                      if cn > 1:
                                          loads.append((VB[:, 0:cn - 1, :], vv_src(h, c0, cn - 1), Tv, TVB))
                                      loads.append((VB[0:kz_l, cn - 1:cn, :], vv_src(h, c0 + cn - 1, 1, kz_l), Tv, TVB))
                                  chunks.append((c0, cn, KB, TKB, VB, TVB, HD, loads))
                          for ci_, (c0, cn, KB, TKB, VB, TVB, KR, loads) in enumerate(chunks):
                              for j in range(cn):
                                  lastt = (c0 + j == t0 + ntl - 1)
                                  kz = ksz if lastt else 128
                                  cs0 = j * 128 if kind == "d" else 0
                                  items.append(dict(h=h, PO=PO, TPO=TPO, KB=KB, TKB=TKB, VB=VB, TVB=TVB, KR=KR, j=j, kz=kz,
                                                    cs0=cs0, bcol=bcol, first=first, last=(last_seg and lastt),
                                                    loads=(loads if j == 0 else None), key=(h, si, ci_, kind)))
                                  first = False
                  units = []
                  i = 0
                  while i < len(items):
                      a = items[i]
                      if (i + 1 < len(items) and a["key"][3] == "n" and items[i + 1]["key"] == a["key"]
                              and a["kz"] == 128 and items[i + 1]["kz"] == 128):
                          units.append([a, items[i + 1]])
                          i += 2
                      else:
                          units.append([a])
                          i += 1

                  def emit_qk(u):
                      pi = cnt["pb"] % 2
                      cnt["pb"] += 1
                      PSp = pSS[:, pi * 1024:(pi + 1) * 1024].rearrange("p (i c) -> p i c", i=2)
                      PBp = pb[pi][:, :].rearrange("p (i c) -> p i c", i=2)
                      for i, it in enumerate(u):
                          if it["loads"]:
                              for (dst, src, Tsrc, Tdst) in it["loads"]:
                                  sp.dma(dst, src, reads=[Tsrc], writes=[Tdst])
                          it["PS"], it["PB"], it["TPS"], it["TPB"] = PSp, PBp, TpSS[pi], Tpb[pi]
                          pe.op(lambda e, it=it, i=i: e.matmul(
                              PSp[0:it["kz"], i, it["cs0"]:ncols],
                              lhsT=it["KB"][0:it["KR"], it["j"] * 128:it["j"] * 128 + it["kz"]],
                              rhs=qTa[0:it["KR"], it["h"], qc0 + it["cs0"]:qc0 + ncols], start=True, stop=True),
                              reads=[it["TKB"], TqTa], writes=[TpSS[pi]])

                  def emit_rest(u):
                      a = u[0]
                      kz, cs0, n = a["kz"], a["cs0"], len(u)
                      PSp, PBp, TPS, TPB = a["PS"], a["PB"], a["TPS"], a["TPB"]
                      bias_ap = c_zero[0:kz, 0:1] if a["bcol"] is None else c_bias[0:kz, a["bcol"]:a["bcol"] + 1]
                      act.op(lambda e: e.activation(out=PBp[0:kz, 0:n, cs0:ncols], in_=PSp[0:kz, 0:n, cs0:ncols],
                                                    func=AF.Exp, bias=bias_ap, scale=SCALE),
                             reads=[TPS, Tc], writes=[TPB])
                      for i, it in enumerate(u):
                          pe.op(lambda e, it=it, i=i: e.matmul(it["PO"][:, cs0:ncols], lhsT=it["VB"][0:kz, it["j"], :],
                                                               rhs=PBp[0:kz, i, cs0:ncols], start=it["first"], stop=it["last"]),
                                reads=[it["TVB"], TPB], writes=[it["TPO"]])
                          if it["last"]:
                              PO, TPO, h = it["PO"], it["TPO"], it["h"]
                              dve.op(lambda e, PO=PO: e.tensor_scalar(out=rcs[64:128, 0:ncols], in0=PO[64:128, 0:ncols],
                                                                      scalar1=1e-30, scalar2=None, op0=ALU.add),
                                     reads=[TPO], writes=[Trcs])
                              dve.op(lambda e: e.reciprocal(out=rcs[64:128, 0:ncols], in_=rcs[64:128, 0:ncols]),
                                     reads=[Trcs], writes=[Trcs])
                              dve.op(lambda e: e.tensor_copy(out=rcp[0:64, 0:ncols], in_=rcs[64:128, 0:ncols]),
                                     reads=[Trcs], writes=[Trcp])
                              dve.op(lambda e, PO=PO, h=h: e.tensor_tensor(out=attT[:, h, qc0:qc0 + ncols], in0=PO[0:64, 0:ncols],
                                                                           in1=rcp[0:64, 0:ncols], op=ALU.mult),
                                     reads=[TPO, Trcp], writes=[TattT])
                  nu = len(units)
                  if nu:
                      emit_qk(units[0])
                  for i in range(nu):
                      if i + 1 < nu:
                          emit_qk(units[i + 1])
                      emit_rest(units[i])
                      if side is not None:
                          next(side, None)
                  if side is not None:
                      for _ in side:
                          pass

              def attention_halo(segs, side=None):
                  tiles = []
                  for (kind, t0, ntl, bcol, ksz) in segs:
                      for c0 in range(t0, t0 + ntl, 2):
                          cn = min(2, t0 + ntl - c0)
                          for j in range(cn):
                              tiles.append((c0, cn, j, bcol))
                  nt_ = len(tiles)
                  st_ = {}

                  def emit_qk(n):
                      c0, cn, j, bcol = tiles[n]
                      if j == 0:
                          bi = cnt["kb"] % NKB
                          cnt["kb"] += 1
                          KB3 = kb[bi][0:HD, 0:H * 256].rearrange("p (h t) -> p h t", h=H)
                          VB4 = vb[bi][:, 0:H * 2, :].rearrange("p (h s) c -> p h s c", h=H)
                          sp.dma(KB3[:, :, 0:cn * 128], KT[:, :, c0 * 128:(c0 + cn) * 128].rearrange("h d t -> d h t"),
                                 reads=[TKT], writes=[Tkb[bi]])
                          sp.dma(VB4[:, :, 0:cn, :], VV[:, :, c0:c0 + cn, :].rearrange("h p s c -> p h s c"),
                                 reads=[TVV], writes=[Tvb[bi]])
                          st_["cur"] = (KB3, VB4, Tkb[bi], Tvb[bi])
                      KB3, VB4, TKB, TVB = st_["cur"]
                      pi = cnt["pb"] % 2
                      cnt["pb"] += 1
                      PSp = pSS[:, pi * 1024:(pi + 1) * 1024]
                      for h in range(H):
                          pe.op(lambda e, h=h: e.matmul(PSp[:, h * 64:(h + 1) * 64], lhsT=KB3[:, h, j * 128:(j + 1) * 128],
                                                        rhs=qTa[0:HD, h, 64:128], start=True, stop=True,
                                                        skip_group_check=True),
                                reads=[TKB, TqTa], writes=[TpSS[pi]])
                      st_[n] = (PSp, pb[pi], TpSS[pi], Tpb[pi], VB4, TVB, j, bcol)

                  def emit_rest(n):
                      PSp, PB, TPS, TPB, VB4, TVB, j, bcol = st_.pop(n)
                      bias_ap = c_zero[:, 0:1] if bcol is None else c_bias[:, bcol:bcol + 1]
                      act.op(lambda e: e.activation(out=PB[:, 0:512], in_=PSp[:, 0:512], func=AF.Exp, bias=bias_ap, scale=SCALE),
                             reads=[TPS, Tc], writes=[TPB])
                      for h in range(H):
                          pe.op(lambda e, h=h: e.matmul(pO[0][:, h * 64:(h + 1) * 64], lhsT=VB4[:, h, j, :],
                                                        rhs=PB[:, h * 64:(h + 1) * 64],
                                                        start=(n == 0 and h == 0), stop=(n == nt_ - 1),
                                                        skip_group_check=True),
                                reads=[TVB, TPB], writes=[TpO[0]])
                  if nt_:
                      emit_qk(0)
                  for n in range(nt_):
                      if n + 1 < nt_:
                          emit_qk(n + 1)
                      emit_rest(n)
                      if side is not None:
                          next(side, None)
                          next(side, None)
                  if side is not None:
                      for _ in side:
                          pass
                  dve.op(lambda e: e.tensor_scalar(out=rcs[64:128, 0:512], in0=pO[0][64:128, :], scalar1=1e-30,
                                                   scalar2=None, op0=ALU.add), reads=[TpO[0]], writes=[Trcs])
                  dve.op(lambda e: e.reciprocal(out=rcs[64:128, 0:512], in_=rcs[64:128, 0:512]), reads=[Trcs], writes=[Trcs])
                  dve.op(lambda e: e.tensor_copy(out=rcp[0:64, 0:512], in_=rcs[64:128, 0:512]), reads=[Trcs], writes=[Trcp])
                  dve.op(lambda e: e.tensor_tensor(
                      out=attT[:, :, 64:128], in0=pO[0][0:64, :].rearrange("p (h t) -> p h t", h=H),
                      in1=rcp[0:64, 0:512].rearrange("p (h t) -> p h t", h=H), op=ALU.mult),
                      reads=[TpO[0], Trcp], writes=[TattT])

              def out_proj_g(aT_, TaT_, cT_, TcT_, src_rows, col, x1_row, bank=0):
                  bi = cnt["x1"] % 2
                  cnt["x1"] += 1
                  PB_, TPB_ = pM[bank], TpM[bank]
                  sp.dma(xr[bi][:], src_rows, writes=[Txr[bi]])
                  for nb in range(2):
                      for h in range(H):
                          pe.op(lambda e, nb=nb, h=h: e.matmul(PB_[:, :], lhsT=aT_[:, h, col:col + 128],
                                                               rhs=wA_oa[:, h, nb * 512:(nb + 1) * 512],
                                                               start=(h == 0), stop=False),
                                reads=[TaT_, Tw], writes=[TPB_])
                      for c in range(4):
                          pe.op(lambda e, nb=nb, c=c: e.matmul(PB_[:, :], lhsT=cT_[:, c, col:col + 128],
                                                               rhs=wA_oc[:, c, nb * 512:(nb + 1) * 512],
                                                               start=False, stop=(c == 3)),
                                reads=[TcT_, Tw], writes=[TPB_])
                      dve.op(lambda e, nb=nb: e.tensor_tensor(out=x1o[bi][:, nb * 512:(nb + 1) * 512], in0=PB_[:, :],
                                                              in1=xr[bi][:, nb * 512:(nb + 1) * 512], op=ALU.add),
                             reads=[TPB_, Txr[bi]], writes=[Tx1o[bi]])
                      yield
                  sp.dma(X1[x1_row * 128:(x1_row + 1) * 128, :], x1o[bi][:], reads=[Tx1o[bi]], writes=[TX1])
                  yield

              def out_proj(src_rows, col, x1_row):
                  for _ in out_proj_g(attT, TattT, cTg, TcTg, src_rows, col, x1_row, bank=0):
                      pass

              def emit_conv_state(ub, Tub, col0, dst):
                  for c in range(4):
                      pe.op(lambda e, c=c: e.transpose(out=pM[2][0:32, c * 128:(c + 1) * 128], in_=ub[:, c, col0:col0 + 32],
                                                       identity=c_idf[:]), reads=[Tub, Tc], writes=[TpM[2]])
                  dve.op(lambda e: e.tensor_copy(out=cvo[:], in_=pM[2][0:32, :]), reads=[TpM[2]], writes=[Tcvo])
                  sp.dma(dst, cvo[:], reads=[Tcvo], writes=[Tout])

              def kt_p(h, t0, n, ksz=128):
                  return KT[h, :, t0 * 128:(t0 + n - 1) * 128 + ksz]

              def vv_p(h, t0, n, ksz=128):
                  return VV[h, 0:ksz, t0:t0 + n, :]


              groups = []
              for t in range(NSLOT):
                  groups.append((t, None))
                  for g in range(RT // QG):
                      groups.append((t, g))

              def load_group(n):
                  t, g = groups[n]
                  b = n % 2
                  ci0 = t * (RT + 1) + (0 if g is None else 1 + g * QG)
                  ntl = 1 if g is None else QG
                  sp.dma(qTas[b][0:HD, :, 0:ntl * 128], QT[:, :, ci0 * 128:(ci0 + ntl) * 128], reads=[TQT], writes=[TqTas[b]])
                  sp.dma(uTs[b][:, :, 0:CK - 1 + ntl * 128],
                         UT[:, :, 32 + ci0 * 128 - (CK - 1):32 + (ci0 + ntl) * 128], reads=[TUT], writes=[TuTs[b]])
              def conv_of(n):
                  ncol_ = 128 if groups[n][1] is None else QG * 128
                  return conv_module_g(uTs[n % 2], TuTs[n % 2], cTgs[n % 2], TcTgs[n % 2], CK - 1, ncol_, 0, bank=0)

              def outproj_of(n):
                  t, g = groups[n]
                  b = n % 2
                  if g is None:
                      yield from out_proj_g(attTs[b], TattTs[b], cTgs[b], TcTgs[b], x_halo[t * 128:(t + 1) * 128, :], 0,
                                            t * (RT + 1), bank=0)
                  else:
                      for i in range(QG):
                          lt = t * G + g * QG + i
                          yield from out_proj_g(attTs[b], TattTs[b], cTgs[b], TcTgs[b], x_all[lt * 128:(lt + 1) * 128, :],
                                                i * 128, t * (RT + 1) + 1 + g * QG + i, bank=0)

              def chain(*gens):
                  for g_ in gens:
                      if g_ is not None:
                          yield from g_
              load_group(0)
              for _ in conv_of(0):
                  pass
              for n, (t, g) in enumerate(groups):
                  base = t * G
                  qTa, TqTa, uT, TuT = qTas[n % 2], TqTas[n % 2], uTs[n % 2], TuTs[n % 2]
                  attT, TattT = attTs[n % 2], TattTs[n % 2]
                  nxt = None
                  if n + 1 < len(groups):
                      load_group(n + 1)
                      nxt = conv_of(n + 1)
                  side = chain(outproj_of(n - 1) if n > 0 else None, nxt)
                  if g is None:
                      segs = []
                      if t > 0:
                          segs.append(("n", 0, base, None, 128))
                      for r in range(1, NCPB):
                          segs.append(("n", base + r * RT, RT, r, 128))
                      attention_halo(segs, side=side)
                  else:
                      i0 = g * QG
                      segs = []
                      if base + i0 > 0:
                          segs.append(("n", 0, base + i0, None, 128))
                      segs.append(("d", base + i0, QG, None, 128))
                      for r in range(1, NCPB):
                          segs.append(("n", base + r * RT, RT, r, 128))
                      attention(QG * 128, segs, kt_p, vv_p, TKT, TVV, side=side)
                      if t == NSLOT - 1 and g == RT // QG - 1:
                          emit_conv_state(uT, TuT, UW - 32, o_conv[:, :])
              for _ in outproj_of(len(groups) - 1):
                  pass
              attT, TattT = attTs[0], TattTs[0]
              qTa, TqTa, uT, TuT = qTas[0], TqTas[0], uTs[0], TuTs[0]
              cTg, TcTg = cTgs[0], TcTgs[0]

              ckpt("p2")
              uS = [sb(stA, f"uS{i}", [128, 4, CK - 1 + 64], F32) for i in range(2)]
              TuS = [T(f"uS{i}") for i in range(2)]
              for e_ in range(2):
                  sp.dma(uS[e_][:, :, 0:CK - 1], sconvT[:, e_, :, :], writes=[TuS[e_]])
              run_ways([0], lambda _, W: tile_front_A_g(W, x_s[:, :], c_ropesn, Tc, 0,
                                                        [(0, 64, (uS[0], TuS[0], CK - 1)), (64, 64, (uS[1], TuS[1], CK - 1))]))
              for e_ in range(2):
                  dve.op(lambda e, e_=e_: e.tensor_copy(out=uT[:, :, 0:CK - 1 + 64], in_=uS[e_][:, :, :]),
                         reads=[TuS[e_]], writes=[TuT])
                  conv_module(CK - 1, 64, e_ * 64)
                  emit_conv_state(uS[e_], TuS[e_], CK - 1 + 64 - 32, o_convs[e_, :, :])

                  def kt_s(h, t0, n, ksz=128, e_=e_):
                      return KTs[e_, h, :, t0 * 128:(t0 + n - 1) * 128 + ksz]

                  def vv_s(h, t0, n, ksz=128, e_=e_):
                      return VVs[e_, h, 0:ksz, t0:t0 + n, :]
                  attention(64, [("n", 0, PT + 1, None, 64)], kt_s, vv_s, TKTs, TVVs, qc0=e_ * 64)
              out_proj(x_s[:, :], 0, NX1 - 1)
              tk.barrier()
              ckpt("p2s")

          with ExitStack() as stB:
              wB_up = sb(stB, "wB_up", [128, 8, 2 * DFF], BF16)
              wB_dn = sb(stB, "wB_dn", [128, NFC, D], BF16)
              with ExitStack() as stW:
                  load_weight(stW, wB_up, w_up, 8, 2 * DFF, c_gffn, name="up")
                  load_weight(stW, wB_dn, w_down, NFC, D, None, name="dn")
                  tk.barrier()
              QB = QG
              NB_ = QB * 128
              xtB = sb(stB, "xtB", [128, QB, D], F32)
              TxtB = [T(f"xtB{j}") for j in range(QB)]
              msB = sb(stB, "msB", [128, QB], F32)
              TmsB = T("msB")
              xnB = sb(stB, "xnB", [128, D], BF16)
              TxnB = T("xnB")
              xnTB = sb(stB, "xnTB", [128, 8, NB_], BF16)
              TxnTB = T("xnTB")
              hT = sb(stB, "hT", [128, NFC, NB_], BF16)
              ThT = T("hT")
              AW = NB_ + 8
              aTc = [sb(stB, f"aTc{i}", [128, AW], F32) for i in range(2)]
              TaTc = [T(f"aTc{i}") for i in range(2)]
              accB = [sb(stB, f"accB{i}", [128, AW], F32) for i in range(2)]
              TaccB = [T(f"accB{i}") for i in range(2)]
              yo = [sb(stB, f"yo{i}", [128, D], F32) for i in range(2)]
              Tyo = [T(f"yo{i}") for i in range(2)]
              arow = [sb(stB, f"arow{i}", [128, 512], F32) for i in range(2)]
              Tarow = [T(f"arow{i}") for i in range(2)]
              hist = sb(stB, "hist", [128, NFC, 2], F32)
              Thist = T("hist")
              hnew = sb(stB, "hnew", [128, NFC, 2], F32)
              Thnew = T("hnew")
              Toutb = T("outsB")
              Abank = [(pM[0], TpM[0]), (pS[3], TpS[3])]
              Gbank = [(pS[0], TpS[0]), (pS[1], TpS[1])]
              Dbank = [(pO[0], TpO[0]), (pO[1], TpO[1])]
              cb_ = dict(ab=0, gb=0, db=0, yo=0, ar=0)

              def ffn_batch(x1_rows, subs, outs, a_rows_dst=None):
                  nt = len(x1_rows)
                  N = nt * 128
                  halo = outs is None
                  for j, r in enumerate(x1_rows):
                      sp.dma(xtB[:, j, :], X1[r * 128:(r + 1) * 128, :], writes=[TxtB[j]])
                      act.op(lambda e, j=j: e.activation(out=xnB[:], in_=xtB[:, j, :], func=AF.Square, scale=1.0 / math.sqrt(D),
                                                         accum_out=msB[:, j:j + 1]), reads=[TxtB[j]], writes=[TxnB, TmsB])
                  rstd_from_msq(None, (msB[:, 0:nt], TmsB), nt)
                  for j in range(nt):
                      dve.op(lambda e, j=j: e.tensor_scalar(out=xnB[:], in0=xtB[:, j, :], scalar1=msB[:, j:j + 1], scalar2=None,
                                                            op0=ALU.mult), reads=[TxtB[j], TmsB], writes=[TxnB])
                      for k in range(8):
                          pe.op(lambda e, k=k: e.transpose(out=pT[:, k * 128:(k + 1) * 128], in_=xnB[:, k * 128:(k + 1) * 128],
                                                           identity=c_idb[:]), reads=[TxnB, Tc], writes=[TpT])
                      act.op(lambda e, j=j: e.activation(out=xnTB[:, :, j * 128:(j + 1) * 128],
                                                         in_=pT[:, :].rearrange("p (k t) -> p k t", k=8), func=AF.Copy),
                             reads=[TpT], writes=[TxnTB])
                  W_ = N + 2 * len(subs)
                  state = {}

                  def stage1(c):
                      ai = cb_["ab"] % 2
                      cb_["ab"] += 1
                      PA, TPA = Abank[ai]
                      AT, TAT, AC, TAC = aTc[ai], TaTc[ai], accB[ai], TaccB[ai]
                      for k in range(8):
                          pe.op(lambda e, k=k: e.matmul(PA[:, 0:N], lhsT=wB_up[:, k, c * 128:(c + 1) * 128], rhs=xnTB[:, k, 0:N],
                                                        start=(k == 0), stop=(k == 7)), reads=[TxnTB, Tw], writes=[TPA])
                      if not halo:
                          gi = cb_["gb"] % 2
                          cb_["gb"] += 1
                          PG, TPG = Gbank[gi]
                          col = DFF + c * 128
                          for k in range(8):
                              pe.op(lambda e, k=k: e.matmul(PG[:, 0:N], lhsT=wB_up[:, k, col:col + 128], rhs=xnTB[:, k, 0:N],
                                                            start=(k == 0), stop=(k == 7)), reads=[TxnTB, Tw], writes=[TPG])
                      else:
                          PG = TPG = None
                      for i, (tok0, ntok, hap, Th) in enumerate(subs):
                          w0 = tok0 + 2 * i
                          act.op(lambda e, w0=w0, tok0=tok0, ntok=ntok: e.activation(
                              out=AT[:, w0 + 2:w0 + 2 + ntok], in_=PA[:, tok0:tok0 + ntok], func=AF.Copy),
                              reads=[TPA], writes=[TAT])
                          if not halo:
                              act.op(lambda e, w0=w0, tok0=tok0, ntok=ntok: e.activation(
                                  out=AC[:, w0:w0 + ntok], in_=PA[:, tok0:tok0 + ntok], func=AF.Identity,
                                  scale=c_fw[:, c, 2:3], bias=c_fb[:, c:c + 1]), reads=[TPA, Tc], writes=[TAC])
                              dve.op(lambda e, w0=w0, hap=hap: e.tensor_copy(out=AT[:, w0:w0 + 2], in_=hap[:, c, :]),
                                     reads=[Th], writes=[TAT])
                          if i == len(subs) - 1:
                              dve.op(lambda e, w0=w0, ntok=ntok: e.tensor_copy(out=hnew[:, c, :], in_=AT[:, w0 + ntok:w0 + ntok + 2]),
                                     reads=[TAT], writes=[Thnew])
                      state[c] = (AT, TAT, AC, TAC, PG, TPG)

                  def stage2(c):
                      AT, TAT, AC, TAC, PG, TPG = state.pop(c)
                      si = cb_["ab"] % 2
                      SL, TSL = AC, TAC
                      dve.op(lambda e: e.scalar_tensor_tensor(out=AC[:, 0:W_ - 2], in0=AT[:, 0:W_ - 2], scalar=c_fw[:, c, 0:1],
                                                              in1=AC[:, 0:W_ - 2], op0=ALU.mult, op1=ALU.add),
                             reads=[TAT, TAC, Tc], writes=[TAC])
                      dve.op(lambda e: e.scalar_tensor_tensor(out=AC[:, 0:W_ - 2], in0=AT[:, 1:W_ - 1], scalar=c_fw[:, c, 1:2],
                                                              in1=AC[:, 0:W_ - 2], op0=ALU.mult, op1=ALU.add),
                             reads=[TAT, TAC, Tc], writes=[TAC])
                      act.op(lambda e: e.activation(out=SL[:, 0:W_ - 2], in_=AC[:, 0:W_ - 2], func=AF.Silu),
                             reads=[TAC], writes=[TSL])
                      for i, (tok0, ntok, hap, Th) in enumerate(subs):
                          w0 = tok0 + 2 * i
                          dve.op(lambda e, w0=w0, tok0=tok0, ntok=ntok: e.tensor_tensor(
                              out=hT[:, c, tok0:tok0 + ntok], in0=PG[:, tok0:tok0 + ntok], in1=SL[:, w0:w0 + ntok], op=ALU.mult),
                              reads=[TPG, TSL], writes=[ThT])

                  if len(subs) > 1:
                      for i in range(2):
                          dve.op(lambda e, i=i: e.memset(accB[i][:], 0.0), writes=[TaccB[i]])
                  for c in range(NFC + 1):
                      if c < NFC:
                          stage1(c)
                      if c >= 1 and not halo:
                          stage2(c - 1)
                  dve.op(lambda e: e.tensor_copy(out=hist[:], in_=hnew[:]), reads=[Thnew], writes=[Thist])
                  if halo:
                      return
                  if a_rows_dst is not None:
                      jl = nt - 1
                      for n0 in range(0, DFF, 512):
                          nw = min(512, DFF - n0)
                          PA, TPA = pS[2], TpS[2]
                          ri = cb_["ar"] % 2
                          cb_["ar"] += 1
                          for k in range(8):
                              pe.op(lambda e, k=k, n0=n0, nw=nw: e.matmul(
                                  PA[:, 0:nw], lhsT=xnTB[:, k, jl * 128:(jl + 1) * 128], rhs=wB_up[:, k, n0:n0 + nw],
                                  start=(k == 0), stop=(k == 7)), reads=[TxnTB, Tw], writes=[TPA])
                          dve.op(lambda e, nw=nw, ri=ri: e.tensor_copy(out=arow[ri][:, 0:nw], in_=PA[:, 0:nw]),
                                 reads=[TPA], writes=[Tarow[ri]])
                          for (r0, dst) in a_rows_dst:
                              sp.dma(dst[:, n0:n0 + nw], arow[ri][r0:r0 + 32, 0:nw], reads=[Tarow[ri]], writes=[Toutb])
                  for j in range(nt):
                      yi = cb_["yo"] % 2
                      cb_["yo"] += 1
                      for nb in range(2):
                          PD, TPD = Dbank[cb_["db"] % 2]
                          cb_["db"] += 1
                          for c in range(NFC):
                              pe.op(lambda e, nb=nb, c=c, j=j, PD=PD: e.matmul(
                                  PD[:, :], lhsT=hT[:, c, j * 128:(j + 1) * 128], rhs=wB_dn[:, c, nb * 512:(nb + 1) * 512],
                                  start=(c == 0), stop=(c == NFC - 1)), reads=[ThT, Tw], writes=[TPD])
                          dve.op(lambda e, nb=nb, j=j, PD=PD, yi=yi: e.tensor_tensor(
                              out=yo[yi][:, nb * 512:(nb + 1) * 512], in0=PD[:, :], in1=xtB[:, j, nb * 512:(nb + 1) * 512],
                              op=ALU.add), reads=[TPD, TxtB[j]], writes=[Tyo[yi]])
                      sp.dma(outs[j], yo[yi][:], reads=[Tyo[yi]], writes=[Toutb])

              for t in range(NSLOT):
                  r0 = t * (RT + 1)
                  ffn_batch([r0], [(0, 128, None, None)], None)
                  dve.op(lambda e, t=t: e.tensor_scalar(out=hist[:], in0=hist[:], scalar1=c_hflag[:, t:t + 1],
                                                        scalar2=None, op0=ALU.mult), reads=[Thist, Tc], writes=[Thist])
                  for i0 in range(0, RT, QB):
                      last = (t == NSLOT - 1 and i0 + QB == RT)
                      ffn_batch([r0 + 1 + i0 + i for i in range(QB)], [(0, NB_, hist, Thist)],
                                [o_y[(t * RT + i0 + i) * 128:(t * RT + i0 + i + 1) * 128, :] for i in range(QB)],
                                a_rows_dst=[(96, o_ffn)] if last else None)
              hs = [sb(stB, f"hs{i}", [128, NFC, 2], F32) for i in range(2)]
              Ths = [T(f"hs{i}") for i in range(2)]
              for e_ in range(2):
                  sp.dma(hs[e_][:], sffnT[:, e_, :, :], writes=[Ths[e_]])
              ffn_batch([NX1 - 1], [(0, 64, hs[0], Ths[0]), (64, 64, hs[1], Ths[1])], [o_ys[:, :]],
                        a_rows_dst=[(32, o_ffns[0]), (96, o_ffns[1])])
              tk.barrier()
          print(f"[build] sems={tk.nsem} waits={tk.nwaits} insts={tk.ninst}", flush=True)
    except _Stop:
        pass
    return nc


def _rope_tab(pos):
    inv = (1.0 / (10000.0 ** (np.arange(0, RD, 2, dtype=np.float32) / np.float32(RD)))).astype(np.float32)
    ang = pos.astype(np.float32)[:, None] * inv[None, :]
    return np.concatenate([np.cos(ang.astype(np.float64)), np.sin(ang.astype(np.float64))], axis=1).astype(np.float32)


def _run(inputs, cfg):
    SEQ, PAST, RT = cfg["SEQ"], cfg["PAST"], cfg["RT"]
    NT = SEQ // 128
    G = NCPB * RT
    NSLOT = NT // G
    QG = min(4, RT)
    PT = PAST // 128
    f32 = np.float32
    bf = ml_dtypes.bfloat16
    g = {k: np.asarray(v) for k, v in inputs.items()}
    xp, xs = g["x_prompt"], g["x_sample"]
    B = xp.shape[0]
    assert B * NCPB == 8 and xs.shape[0] == 16

    def chunked(v, n):
        return np.ascontiguousarray(v.reshape(n, 128).T).astype(f32)

    def bc(v):
        return np.ascontiguousarray(np.broadcast_to(v[None, :], (128, v.shape[0]))).astype(f32)
    common = {
        "w_in": g["w_in"][0], "w_uq": g["w_uq"][0], "w_ukv": g["w_ukv"][0], "w_out": g["w_out"][0],
        "w_up": g["w_up"][0], "w_down": g["w_down"][0],
        "g_attn": chunked(g["attn_norm"][0], 8), "g_q": chunked(g["q_norm"][0], 3),
        "g_ffn": chunked(g["ffn_norm"][0], 8), "g_kv": bc(g["kv_norm"][0]),
        "g_hq": bc(g["qk_norm_q"][0]), "g_hk": bc(g["qk_norm_k"][0]),
        "cw": np.ascontiguousarray(g["conv_w"][0].T.reshape(4, 128, CK).transpose(1, 0, 2)),
        "cb": chunked(g["conv_b"][0], 4), "cg": chunked(g["conv_norm"][0], 4),
        "fw": np.ascontiguousarray(g["ffn_conv_w"][0].T.reshape(NFC, 128, 3).transpose(1, 0, 2)),
        "fb": chunked(g["ffn_conv_b"][0], NFC),
        "identb": np.eye(128, dtype=f32).astype(bf), "identf": np.eye(128, dtype=f32),
        "onesb": np.ones((128, 128), f32).astype(bf),
    }
    nch = QG * 2
    kh = np.zeros((32, QG * 128), f32)
    qm = np.zeros((32, QG * 128), f32)
    for c in range(nch):
        kh[c, c * 64:(c + 1) * 64] = 1.0
        qm[c, :c * 64] = NEG
    common["khot"] = kh.astype(bf)
    common["qmask"] = qm.astype(bf)
    common["rope_sp"] = np.ascontiguousarray(
        _rope_tab(np.arange(max(PT, 1) * 128)).reshape(max(PT, 1), 128, 32).transpose(1, 0, 2))
    common["rope_sn"] = _rope_tab(PAST + (np.arange(128) % 64))

    in_maps, metas = [], []
    for core in range(8):
        b, j = divmod(core, NCPB)
        others = [r for r in range(NCPB) if r != j]
        order = [j] + others
        gt = np.array([t * G + order[r] * RT + i for t in range(NSLOT) for r in range(NCPB) for i in range(RT)])
        tok = (gt[:, None] * 128 + np.arange(128)[None, :]).reshape(-1)
        m = dict(common)
        m["x_all"] = np.ascontiguousarray(xp[b][tok])
        xh = np.zeros((NSLOT, 128, D), f32)
        hf = np.zeros((128, NSLOT), f32)
        rh = np.zeros((NSLOT, 128, 32), f32)
        for t in range(NSLOT):
            ht = t * G + j * RT - 1
            if ht >= 0:
                xh[t] = xp[b, ht * 128:(ht + 1) * 128]
                hf[:, t] = 1.0
                rh[t] = _rope_tab(ht * 128 + np.arange(128))
        m["x_halo"] = xh.reshape(NSLOT * 128, D)
        m["hflag"] = hf
        m["rope_h"] = np.ascontiguousarray(rh.transpose(1, 0, 2))
        m["rope_k"] = np.ascontiguousarray(_rope_tab(tok).reshape(NT, 128, 32).transpose(1, 0, 2))
        bt = np.zeros((128, NCPB), f32)
        for r in range(1, NCPB):
            bt[:, r] = 0.0 if others[r - 1] < j else NEG
        m["biast"] = bt
        e0 = 2 * core
        m["x_s"] = np.ascontiguousarray(xs[e0:e0 + 2].reshape(128, D))
        m["cckv"] = np.ascontiguousarray(g["cache_ckv"][0, e0:e0 + 2].reshape(2 * PAST, KVL))
        m["ckpe"] = np.ascontiguousarray(g["cache_kpe"][0, e0:e0 + 2].reshape(2 * PAST, RD))
        sc = g["state_conv"][0, e0:e0 + 2]
        m["sconvT"] = np.ascontiguousarray(sc.transpose(2, 0, 1).reshape(4, 128, 2, CK - 1).transpose(1, 2, 0, 3))
        sf = g["state_ffn_conv"][0, e0:e0 + 2]
        m["sffnT"] = np.ascontiguousarray(sf.transpose(2, 0, 1).reshape(NFC, 128, 2, 2).transpose(1, 2, 0, 3))
        in_maps.append(m)
        own_tok = (np.array([t * G + j * RT + i for t in range(NSLOT) for i in range(RT)])[:, None] * 128
                   + np.arange(128)[None, :]).reshape(-1)
        metas.append((b, j, own_tok))

    nc = build(cfg)
    res = run_bass_kernel_spmd(nc, in_maps, core_ids=list(range(8)))
    R = res.results

    y_p = np.zeros((B, SEQ, D), f32)
    ckv_p = np.zeros((1, B, SEQ, KVL), f32)
    kpe_p = np.zeros((1, B, SEQ, RD), f32)
    conv_p = np.zeros((1, B, CK - 1, CC), f32)
    ffn_p = np.zeros((1, B, 2, DFF), f32)
    y_s = np.zeros((16, 64, D), f32)
    ckv_s = np.zeros((1, 16, 64, KVL), f32)
    kpe_s = np.zeros((1, 16, 64, RD), f32)
    conv_s = np.zeros((1, 16, CK - 1, CC), f32)
    ffn_s = np.zeros((1, 16, 2, DFF), f32)
    for core in range(8):
        b, j, own_tok = metas[core]
        r = R[core]
        y_p[b, own_tok] = r["o_y"]
        ckv_p[0, b, own_tok] = r["o_ckv"]
        kpe_p[0, b, own_tok] = r["o_kpe"]
        if j == NCPB - 1:
            conv_p[0, b] = r["o_conv"][2:32]
            ffn_p[0, b] = r["o_ffn"][30:32]
        e0 = 2 * core
        y_s[e0:e0 + 2] = r["o_ys"].reshape(2, 64, D)
        ckv_s[0, e0:e0 + 2] = r["o_ckvs"].reshape(2, 64, KVL)
        kpe_s[0, e0:e0 + 2] = r["o_kpes"].reshape(2, 64, RD)
        conv_s[0, e0:e0 + 2] = r["o_convs"][:, 2:32]
        ffn_s[0, e0:e0 + 2] = r["o_ffns"][:, 30:32]
    return (y_p, y_s, ckv_p, kpe_p, conv_p, ffn_p, ckv_s, kpe_s, conv_s, ffn_s)


def kernel(**inputs):
    return _run(inputs, CFG_FULL)
```

```python
import math
from contextlib import ExitStack

import numpy as np
import ml_dtypes

import concourse.bass as bass
import concourse.mybir as mybir
from concourse.bass_utils import run_bass_kernel_spmd

F32 = mybir.dt.float32
BF16 = mybir.dt.bfloat16
ALU = mybir.AluOpType
AF = mybir.ActivationFunctionType
AX = mybir.AxisListType

D = 1024
QL, KVL, RD, CC = 384, 256, 32, 512
H, HD, NOPE, VD = 8, 96, 64, 64
INW = QL + KVL + RD + 2 * CC
DFF = 2816
NFC = DFF // 128
CK = 31
EPS = 1e-6
SCALE = HD ** -0.5
NEG = -30000.0
NCPB = 4
KC = 16

CFG_FULL = dict(SEQ=16384, PAST=2048, RT=8)


class T:
    __slots__ = ("name", "w", "r", "dsem", "dcnt", "excl")

    def __init__(self, name, excl=False):
        self.name = name
        self.excl = excl
        self.w = None
        self.r = {}
        self.dsem = None
        self.dcnt = 0


class Eng:
    ROT = 30000

    def __init__(self, trk, eng, name):
        self.trk, self.eng, self.name = trk, eng, name
        self.sem = trk.new_sem(name)
        self.cnt = 0
        self.seen = {}

    def _wait(self, sem, val):
        if self.seen.get(sem, 0) >= val:
            return
        self.eng.wait_ge(sem, val)
        self.seen[sem] = val
        self.trk.nwaits += 1

    def _deps(self, reads, writes):
        need = {}

        def add(p, same_ok):
            if p is None:
                return
            sem, val = p
            if sem is self.sem and same_ok and self.name == "pe":
                return
            if need.get(sem, 0) < val:
                need[sem] = val
        for t in reads:
            add(t.w, False)
        for t in writes:
            add(t.w, True)
            for sem, val in t.r.items():
                add((sem, val), True)
        for sem, val in need.items():
            self._wait(sem, val)

    def op(self, fn, reads=(), writes=()):
        ex = [t for t in reads if t.excl and t not in writes]
        if ex:
            reads = [t for t in reads if not t.excl or t in writes]
            writes = list(writes) + ex
        self._deps(reads, writes)
        if self.cnt >= self.ROT:
            self.sem = self.trk.new_sem(self.name)
            self.cnt = 0
        inst = fn(self.eng)
        self.cnt += 1
        inst.then_inc(self.sem, 1)
        self.trk.ninst += 1
        for t in reads:
            if t.r.get(self.sem, 0) < self.cnt:
                t.r[self.sem] = self.cnt
        for t in writes:
            t.w = (self.sem, self.cnt)
            t.r = {}
        return inst

    def dma(self, out, in_, reads=(), writes=()):
        self._deps(reads, writes)
        tw = writes[0]
        if tw.dsem is None:
            tw.dsem = self.trk.new_sem("d_" + tw.name)
            self.trk.dts.append(tw)
        inst = self.eng.dma_start(out=out, in_=in_)
        inst.then_inc(tw.dsem, 16)
        tw.dcnt += 16
        self.trk.ninst += 1
        for t in reads:
            if t.r.get(tw.dsem, 0) < tw.dcnt:
                t.r[tw.dsem] = tw.dcnt
        tw.w = (tw.dsem, tw.dcnt)
        tw.r = {}
        return inst

    def wait_for(self, t):
        if t.w is not None:
            self._wait(*t.w)


class Tracker:
    def __init__(self, nc, stack):
        self.nc, self.stack = nc, stack
        self.nsem = 0
        self.nwaits = 0
        self.ninst = 0
        self.dts = []
        self.pe = Eng(self, nc.tensor, "pe")
        self.act = Eng(self, nc.scalar, "act")
        self.dve = Eng(self, nc.vector, "dve")
        self.pool = Eng(self, nc.gpsimd, "pool")
        self.sp = Eng(self, nc.sync, "sp")
        self.engs = [self.pe, self.act, self.dve, self.pool, self.sp]

    def new_sem(self, name):
        self.nsem += 1
        return self.stack.enter_context(self.nc.semaphore(f"s{self.nsem}_{name}"))

    def barrier(self):
        pts = [(e.sem, e.cnt) for e in self.engs if e.cnt > 0]
        pts += [(t.dsem, t.dcnt) for t in self.dts if t.dcnt > 0]
        for e in self.engs:
            for sem, val in pts:
                e._wait(sem, val)


def build(cfg):
    SEQ, PAST, RT = cfg["SEQ"], cfg["PAST"], cfg["RT"]
    NT = SEQ // 128
    G = NCPB * RT
    NSLOT = NT // G
    assert NSLOT * G == NT
    QG = min(4, RT)
    assert RT % QG == 0
    NOWN = NSLOT * RT
    PT = PAST // 128
    NX1 = NSLOT * (RT + 1) + 1

    nc = bass.Bass("TRN2", target_bir_lowering=False)

    def din(name, shape, dt=F32):
        return nc.dram_tensor(name, list(shape), dt, kind="ExternalInput").ap()

    def dout(name, shape, dt=F32):
        return nc.dram_tensor(name, list(shape), dt, kind="ExternalOutput").ap()

    def dscr(name, shape, dt):
        return nc.dram_tensor(name, list(shape), dt, kind="Internal").ap()

    x_all = din("x_all", [NT * 128, D])
    x_halo = din("x_halo", [NSLOT * 128, D])
    x_s = din("x_s", [128, D])
    cckv = din("cckv", [2 * PAST, KVL])
    ckpe = din("ckpe", [2 * PAST, RD])
    sconvT = din("sconvT", [128, 2, 4, CK - 1])
    sffnT = din("sffnT", [128, 2, NFC, 2])
    rope_k = din("rope_k", [128, NT, 32])
    rope_h = din("rope_h", [128, NSLOT, 32])
    rope_sp = din("rope_sp", [128, max(PT, 1), 32])
    rope_sn = din("rope_sn", [128, 32])
    hflag = din("hflag", [128, NSLOT])
    biast = din("biast", [128, NCPB])
    w_in = din("w_in", [D, INW])
    w_uq = din("w_uq", [QL, H * HD])
    w_ukv = din("w_ukv", [KVL, H * 128])
    w_out = din("w_out", [D, D])
    w_up = din("w_up", [D, 2 * DFF])
    w_down = din("w_down", [DFF, D])
    g_attn = din("g_attn", [128, 8])
    g_q = din("g_q", [128, 3])
    g_ffn = din("g_ffn", [128, 8])
    g_kv = din("g_kv", [128, KVL])
    g_hq = din("g_hq", [128, HD])
    g_hk = din("g_hk", [128, HD])
    cw = din("cw", [128, 4, CK])
    cb = din("cb", [128, 4])
    cg = din("cg", [128, 4])
    fw = din("fw", [128, NFC, 3])
    fb = din("fb", [128, NFC])
    identb = din("identb", [128, 128], BF16)
    identf = din("identf", [128, 128])
    onesb = din("onesb", [128, 128], BF16)
    khot = din("khot", [32, QG * 128], BF16)
    qmask = din("qmask", [32, QG * 128], BF16)

    o_y = dout("o_y", [NOWN * 128, D])
    o_ckv = dout("o_ckv", [NOWN * 128, KVL])
    o_kpe = dout("o_kpe", [NOWN * 128, RD])
    o_conv = dout("o_conv", [32, CC])
    o_ffn = dout("o_ffn", [32, DFF])
    o_ys = dout("o_ys", [128, D])
    o_ckvs = dout("o_ckvs", [128, KVL])
    o_kpes = dout("o_kpes", [128, RD])
    o_convs = dout("o_convs", [2, 32, CC])
    o_ffns = dout("o_ffns", [2, 32, DFF])

    KT = dscr("KT", [H, HD, NT * 128], BF16)
    VV = dscr("VV", [H, 128, NT, 128], BF16)
    KTs = dscr("KTs", [2, H, HD, (PT + 1) * 128], BF16)
    VVs = dscr("VVs", [2, H, 128, PT + 1, 128], BF16)
    X1 = dscr("X1", [NX1 * 128, D], F32)
    NCI = NSLOT * (RT + 1)
    QT = dscr("QT", [HD, H, NCI * 128], BF16)
    UT = dscr("UT", [128, 4, 32 + NCI * 128], F32)

    class _Stop(Exception):
        pass

    def ckpt(name):
        if cfg.get("STOP") == name:
            tk.barrier()
            raise _Stop()
    try:
      with ExitStack() as top:
          tk = Tracker(nc, top)
          pe, act, dve, pool, sp = tk.pe, tk.act, tk.dve, tk.pool, tk.sp

          def sb(st, name, shape, dt):
              return st.enter_context(nc.sbuf_tensor(name, list(shape), dt))

          def ps(st, name, shape, dt):
              return st.enter_context(nc.psum_tensor(name, list(shape), dt))

          pSS = ps(top, "pSS", [128, 2048], F32)
          pS = [pSS[:, i * 512:(i + 1) * 512] for i in range(4)]
          TpS = [T(f"pS{i}", excl=True) for i in range(4)]
          TpSS = [T(f"pSS{i}", excl=True) for i in range(2)]
          pOO = ps(top, "pOO", [128, 1024], F32)
          pO = [pOO[:, i * 512:(i + 1) * 512] for i in range(2)]
          TpO = [T(f"pO{i}", excl=True) for i in range(2)]
          pM0 = ps(top, "pM0", [128, 512], F32)
          pM = [pM0[:, :], pO[0], pO[1]]
          TpM = [T("pM0", excl=True), TpO[0], TpO[1]]
          pT = ps(top, "pT", [128, 1024], BF16)
          TpT = T("pT", excl=True)

          cst = {}
          Tc = T("consts")

          def cload(name, src, shape, dt=F32):
              t = sb(top, "c_" + name, shape, dt)
              sp.dma(t[:], src, writes=[Tc])
              cst[name] = t
              return t
          c_idb = cload("idb", identb[:, :], [128, 128], BF16)
          c_idf = cload("idf", identf[:, :], [128, 128])
          c_ones = cload("ones", onesb[:, :], [128, 128], BF16)
          c_gkv = cload("gkv", g_kv[:, :], [128, KVL])
          c_ghq = cload("ghq", g_hq[:, :], [128, HD])
          c_ghk = cload("ghk", g_hk[:, :], [128, HD])
          c_cw = cload("cw", cw[:, :, :], [128, 4, CK])
          c_cb = cload("cb", cb[:, :], [128, 4])
          c_cg = cload("cg", cg[:, :], [128, 4])
          c_fw = cload("fw", fw[:, :, :], [128, NFC, 3])
          c_fb = cload("fb", fb[:, :], [128, NFC])
          c_hflag = cload("hflag", hflag[:, :], [128, NSLOT])
          c_bias = cload("bias", biast[:, :], [128, NCPB])
          c_gattn = cload("gattn", g_attn[:, :], [128, 8])
          c_gq = cload("gq", g_q[:, :], [128, 3])
          c_gffn = cload("gffn", g_ffn[:, :], [128, 8])
          c_ropeh = cload("ropeh", rope_h[:, :, :], [128, NSLOT, 32])
          c_ropesn = cload("ropesn", rope_sn[:, :], [128, 32])
          c_zero = sb(top, "c_zero", [128, 1], F32)
          dve.op(lambda e: e.memset(c_zero[:], 0.0), writes=[Tc])
          c_eps = sb(top, "c_eps", [128, 1], F32)
          dve.op(lambda e: e.memset(c_eps[:], EPS), writes=[Tc])

          def load_weight(st, dst, src2d, nk, ncols, gain, kp=128, name="w"):
              CH = 2048
              stg = [sb(st, f"stg_{name}{i}", [128, CH], F32) for i in range(2)]
              Tst = [T(f"stg_{name}{i}") for i in range(2)]
              n = 0
              for k in range(nk):
                  for c0 in range(0, ncols, CH):
                      cwid = min(CH, ncols - c0)
                      b = n % 2
                      n += 1
                      sp.dma(stg[b][0:kp, 0:cwid], src2d[k * kp:(k + 1) * kp, c0:c0 + cwid], writes=[Tst[b]])
                      if n % 2:
                          if gain is not None:
                              dve.op(lambda e, b=b, k=k, c0=c0, cwid=cwid: e.tensor_scalar(
                                  out=dst[0:kp, k, c0:c0 + cwid], in0=stg[b][0:kp, 0:cwid],
                                  scalar1=gain[0:kp, k:k + 1], scalar2=None, op0=ALU.mult),
                                  reads=[Tst[b], Tc], writes=[Tw])
                          else:
                              dve.op(lambda e, b=b, k=k, c0=c0, cwid=cwid: e.tensor_copy(
                                  out=dst[0:kp, k, c0:c0 + cwid], in_=stg[b][0:kp, 0:cwid]),
                                  reads=[Tst[b]], writes=[Tw])
                      else:
                          if gain is not None:
                              act.op(lambda e, b=b, k=k, c0=c0, cwid=cwid: e.activation(
                                  out=dst[0:kp, k, c0:c0 + cwid], in_=stg[b][0:kp, 0:cwid], func=AF.Copy,
                                  scale=gain[0:kp, k:k + 1]), reads=[Tst[b], Tc], writes=[Tw])
                          else:
                              act.op(lambda e, b=b, k=k, c0=c0, cwid=cwid: e.activation(
                                  out=dst[0:kp, k, c0:c0 + cwid], in_=stg[b][0:kp, 0:cwid], func=AF.Copy),
                                  reads=[Tst[b]], writes=[Tw])

          Tw = T("weights")

          def rstd_from_msq(st_bufs, msq, n):
              ap, Tm = msq
              act.op(lambda e: e.activation(out=ap, in_=ap, func=AF.Sqrt, bias=c_eps[:, 0:1]), reads=[Tm, Tc], writes=[Tm])
              dve.op(lambda e: e.reciprocal(out=ap, in_=ap), reads=[Tm], writes=[Tm])

          class TileBufs:
              def __init__(self, st, tag, nx=2):
                  self.xt = [sb(st, f"xt{tag}{i}", [128, D], F32) for i in range(nx)]
                  self.Txt = [T(f"xt{tag}{i}") for i in range(nx)]
                  self.junk = sb(st, f"junk{tag}", [128, D], BF16)
                  self.Tjunk = T("junk" + tag)
                  self.st = sb(st, f"stat{tag}", [128, 8], F32)
                  self.Tst = [T(f"stat{tag}{i}") for i in range(8)]
                  self.xn = sb(st, f"xn{tag}", [128, D], BF16)
                  self.Txn = T("xn" + tag)
                  self.xnT = sb(st, f"xnT{tag}", [128, 8, 128], BF16)
                  self.TxnT = T("xnT" + tag)
                  self.n = 0

          def front_end(tb, src_rows, w_reads=()):
              b = tb.n % len(tb.xt)
              tb.n += 1
              xt, Txt = tb.xt[b], tb.Txt[b]
              sp.dma(xt[:], src_rows, reads=list(w_reads), writes=[Txt])
              ms, Tms = tb.st[:, 0:1], tb.Tst[0]
              act.op(lambda e: e.activation(out=tb.junk[:], in_=xt[:], func=AF.Square, scale=1.0 / math.sqrt(D),
                                            accum_out=ms), reads=[Txt], writes=[tb.Tjunk, Tms])
              rstd_from_msq(None, (ms, Tms), 1)
              dve.op(lambda e: e.tensor_scalar(out=tb.xn[:], in0=xt[:], scalar1=ms, scalar2=None, op0=ALU.mult),
                     reads=[Txt, Tms], writes=[tb.Txn])
              for k in range(8):
                  pe.op(lambda e, k=k: e.transpose(out=pT[:, k * 128:(k + 1) * 128], in_=tb.xn[:, k * 128:(k + 1) * 128],
                                                   identity=c_idb[:]), reads=[tb.Txn, Tc], writes=[TpT])
              act.op(lambda e: e.activation(out=tb.xnT[:].rearrange("p k t -> p (k t)"), in_=pT[:], func=AF.Copy),
                     reads=[TpT], writes=[tb.TxnT])
              return b

          class HeadBufs:
              def __init__(self, st, tag):
                  self.raw = sb(st, f"hraw{tag}", [128, H, HD], F32)
                  self.Traw = T("hraw" + tag)
                  self.sq = sb(st, f"hsq{tag}", [128, H, HD], F32)
                  self.Tsq = T("hsq" + tag)
                  self.rs = sb(st, f"hrs{tag}", [128, H], F32)
                  self.Trs = T("hrs" + tag)
                  self.t1, self.Tt1 = self.sq, self.Tsq
                  self.ra = sb(st, f"hra{tag}", [128, H, 16], F32)
                  self.rb = sb(st, f"hrb{tag}", [128, H, 16], F32)
                  self.Tra, self.Trb = T("hra" + tag), T("hrb" + tag)
                  self.fin = sb(st, f"hfin{tag}", [128, H, HD], BF16)
                  self.Tfin = T("hfin" + tag)

          def head_norm_rope(hb, gain, cs, Tcs):
              raw, sq, rs, t1, fin = hb.raw, hb.sq, hb.rs, hb.t1, hb.fin
              act.op(lambda e: e.activation(out=sq[:], in_=raw[:], func=AF.Square, scale=1.0 / math.sqrt(HD)),
                     reads=[hb.Traw], writes=[hb.Tsq])
              dve.op(lambda e: e.tensor_reduce(out=rs[:], in_=sq[:], axis=AX.X, op=ALU.add),
                     reads=[hb.Tsq], writes=[hb.Trs])
              rstd_from_msq(None, (rs[:], hb.Trs), H)
              dve.op(lambda e: e.tensor_tensor(out=t1[:], in0=raw[:], in1=rs[:].unsqueeze(2).to_broadcast([128, H, HD]),
                                               op=ALU.mult), reads=[hb.Traw, hb.Trs], writes=[hb.Tt1])
              pool.op(lambda e: e.tensor_tensor(out=t1[:], in0=t1[:], in1=gain[:].unsqueeze(1).to_broadcast([128, H, HD]),
                                                op=ALU.mult), reads=[hb.Tt1, Tc], writes=[hb.Tt1])
              cosb = cs[:, 0:16].unsqueeze(1).to_broadcast([128, H, 16])
              sinb = cs[:, 16:32].unsqueeze(1).to_broadcast([128, H, 16])
              p1, p2 = t1[:, :, 64:80], t1[:, :, 80:96]
              act.op(lambda e: e.activation(out=fin[:, :, 0:64], in_=t1[:, :, 0:64], func=AF.Copy),
                     reads=[hb.Tt1], writes=[hb.Tfin])
              dve.op(lambda e: e.tensor_tensor(out=hb.ra[:], in0=p1, in1=cosb, op=ALU.mult),
                     reads=[hb.Tt1, Tcs], writes=[hb.Tra])
              dve.op(lambda e: e.tensor_tensor(out=hb.rb[:], in0=p2, in1=sinb, op=ALU.mult),
                     reads=[hb.Tt1, Tcs], writes=[hb.Trb])
              dve.op(lambda e: e.tensor_tensor(out=fin[:, :, 64:80], in0=hb.ra[:], in1=hb.rb[:], op=ALU.subtract),
                     reads=[hb.Tra, hb.Trb], writes=[hb.Tfin])
              dve.op(lambda e: e.tensor_tensor(out=hb.ra[:], in0=p2, in1=cosb, op=ALU.mult),
                     reads=[hb.Tt1, Tcs], writes=[hb.Tra])
              dve.op(lambda e: e.tensor_tensor(out=hb.rb[:], in0=p1, in1=sinb, op=ALU.mult),
                     reads=[hb.Tt1, Tcs], writes=[hb.Trb])
              dve.op(lambda e: e.tensor_tensor(out=fin[:, :, 80:96], in0=hb.ra[:], in1=hb.rb[:], op=ALU.add),
                     reads=[hb.Tra, hb.Trb], writes=[hb.Tfin])

          NWAYS = 1
          NWAYS_P1 = 4

          def run_ways(tasks, make_gen, nways=None):
              nways = len(ways) if nways is None else nways
              it = iter(tasks)
              free = list(range(nways))
              active = []
              more = True
              while True:
                  while free and more:
                      try:
                          tsk = next(it)
                      except StopIteration:
                          more = False
                          break
                      w = free.pop(0)
                      active.append((make_gen(tsk, ways[w]), w))
                  if not active:
                      break
                  for gw in list(active):
                      try:
                          next(gw[0])
                      except StopIteration:
                          active.remove(gw)
                          free.append(gw[1])

          def rstd_g(ap, Tm):
              act.op(lambda e: e.activation(out=ap, in_=ap, func=AF.Sqrt, bias=c_eps[:, 0:1]), reads=[Tm, Tc], writes=[Tm])
              yield
              dve.op(lambda e: e.reciprocal(out=ap, in_=ap), reads=[Tm], writes=[Tm])
              yield

          pS2b = pS[2].bitcast(BF16)
          Tbanks = [(pT[:, :], TpT), (pS2b, TpS[2])]
          Pbanks = [(pO[1], TpO[1]), (pO[0], TpO[0])]
          Kbanks = [((pM[0], pM[1]), (TpM[0], TpM[1])), ((pS[0], pS[1]), (TpS[0], TpS[1]))]

          class Way:
              def __init__(self, st, w):
                  tag = f"W{w}"
                  self.w = w
                  self.xt = sb(st, "xt" + tag, [128, D], F32)
                  self.Txt = T("xt" + tag)
                  self.st = sb(st, "stat" + tag, [128, 8], F32)
                  self.Tst = [T(f"stat{tag}{i}") for i in range(8)]
                  self.xn = sb(st, "xn" + tag, [128, D], BF16)
                  self.Txn = T("xn" + tag)
                  self.junk, self.Tjunk = self.xn, self.Txn
                  self.xnT = sb(st, "xnT" + tag, [128, 8, 128], BF16)
                  self.TxnT = T("xnT" + tag)
                  self.hb = HeadBufs(st, tag)
                  self.rk = sb(st, "rk" + tag, [128, 32], F32)
                  self.Trk = T("rk" + tag)
                  self.cqb = sb(st, "cqb" + tag, [128, QL], BF16)
                  self.Tcqb = T("cqb" + tag)
                  self.cqf = self.hb.sq[:].rearrange("p h d -> p (h d)")[:, 0:QL]
                  self.Tcqf = self.hb.Tsq
                  self.cqT = sb(st, "cqT" + tag, [128, 3, 128], BF16)
                  self.TcqT = T("cqT" + tag)
                  self.sig = sb(st, "sig" + tag, [128, 4, 128], F32)
                  self.Tsig = T("sig" + tag)
                  self.pT, self.TpT = Tbanks[w % 2]
                  self.pP, self.TpP = Pbanks[w % 2]
                  self.pK, self.TpK = Kbanks[w % 2]

              def alloc_p1(self, st):
                  tag = f"W{self.w}"
                  self.ckv = sb(st, "ckv" + tag, [128, KVL], F32)
                  self.Tckv = T("ckv" + tag)
                  self.kpe = sb(st, "kpe" + tag, [128, RD], F32)
                  self.Tkpe = T("kpe" + tag)
                  self.ckvb = sb(st, "ckvb" + tag, [128, KVL], BF16)
                  self.Tckvb = T("ckvb" + tag)
                  self.ckvT = sb(st, "ckvT" + tag, [128, 2, 128], BF16)
                  self.TckvT = T("ckvT" + tag)
                  self.qst = sb(st, "qst" + tag, [HD, H, 128], BF16)
                  self.Tqst = T("qst" + tag)
                  self.ust, self.Tust = self.sig, self.Tsig
                  self.prj = sb(st, "prj" + tag, [128, KVL + RD], F32)
                  self.Tprj = T("prj" + tag)
                  self.cin = sb(st, "cin" + tag, [128, KVL], F32)
                  self.Tcin = T("cin" + tag)
                  self.kin = sb(st, "kin" + tag, [128, RD], F32)
                  self.Tkin = T("kin" + tag)

          def front_end_g(W, src_rows):
              sp.dma(W.xt[:], src_rows, writes=[W.Txt])
              ms, Tms = W.st[:, 0:1], W.Tst[0]
              act.op(lambda e: e.activation(out=W.junk[:], in_=W.xt[:], func=AF.Square, scale=1.0 / math.sqrt(D),
                                            accum_out=ms), reads=[W.Txt], writes=[W.Tjunk, Tms])
              yield
              yield from rstd_g(ms, Tms)
              dve.op(lambda e: e.tensor_scalar(out=W.xn[:], in0=W.xt[:], scalar1=ms, scalar2=None, op0=ALU.mult),
                     reads=[W.Txt, Tms], writes=[W.Txn])
              yield
              for k in range(8):
                  pe.op(lambda e, k=k: e.transpose(out=W.pT[:, k * 128:(k + 1) * 128], in_=W.xn[:, k * 128:(k + 1) * 128],
                                                   identity=c_idb[:]), reads=[W.Txn, Tc], writes=[W.TpT])
              act.op(lambda e: e.activation(out=W.xnT[:].rearrange("p k t -> p (k t)"), in_=W.pT, func=AF.Copy),
                     reads=[W.TpT], writes=[W.TxnT])
              yield

          def head_norm_rope_g(hb, gain, cs, Tcs):
              raw, sq, rs, t1, fin = hb.raw, hb.sq, hb.rs, hb.t1, hb.fin
              act.op(lambda e: e.activation(out=sq[:], in_=raw[:], func=AF.Square, scale=1.0 / math.sqrt(HD)),
                     reads=[hb.Traw], writes=[hb.Tsq])
              yield
              dve.op(lambda e: e.tensor_reduce(out=rs[:], in_=sq[:], axis=AX.X, op=ALU.add),
                     reads=[hb.Tsq], writes=[hb.Trs])
              yield
              yield from rstd_g(rs[:], hb.Trs)
              dve.op(lambda e: e.tensor_tensor(out=t1[:], in0=raw[:], in1=rs[:].unsqueeze(2).to_broadcast([128, H, HD]),
                                               op=ALU.mult), reads=[hb.Traw, hb.Trs], writes=[hb.Tt1])
              yield
              pool.op(lambda e: e.tensor_tensor(out=t1[:], in0=t1[:], in1=gain[:].unsqueeze(1).to_broadcast([128, H, HD]),
                                                op=ALU.mult), reads=[hb.Tt1, Tc], writes=[hb.Tt1])
              yield
              cosb = cs[:, 0:16].unsqueeze(1).to_broadcast([128, H, 16])
              sinb = cs[:, 16:32].unsqueeze(1).to_broadcast([128, H, 16])
              p1, p2 = t1[:, :, 64:80], t1[:, :, 80:96]
              act.op(lambda e: e.activation(out=fin[:, :, 0:64], in_=t1[:, :, 0:64], func=AF.Copy),
                     reads=[hb.Tt1], writes=[hb.Tfin])
              dve.op(lambda e: e.tensor_tensor(out=hb.ra[:], in0=p1, in1=cosb, op=ALU.mult),
                     reads=[hb.Tt1, Tcs], writes=[hb.Tra])
              pool.op(lambda e: e.tensor_tensor(out=hb.rb[:], in0=p2, in1=sinb, op=ALU.mult),
                      reads=[hb.Tt1, Tcs], writes=[hb.Trb])
              yield
              dve.op(lambda e: e.tensor_tensor(out=fin[:, :, 64:80], in0=hb.ra[:], in1=hb.rb[:], op=ALU.subtract),
                     reads=[hb.Tra, hb.Trb], writes=[hb.Tfin])
              yield
              dve.op(lambda e: e.tensor_tensor(out=hb.ra[:], in0=p2, in1=cosb, op=ALU.mult),
                     reads=[hb.Tt1, Tcs], writes=[hb.Tra])
              pool.op(lambda e: e.tensor_tensor(out=hb.rb[:], in0=p1, in1=sinb, op=ALU.mult),
                      reads=[hb.Tt1, Tcs], writes=[hb.Trb])
              yield
              dve.op(lambda e: e.tensor_tensor(out=fin[:, :, 80:96], in0=hb.ra[:], in1=hb.rb[:], op=ALU.add),
                     reads=[hb.Tra, hb.Trb], writes=[hb.Tfin])
              yield

          with ExitStack() as stA:
              wA_in = sb(stA, "wA_in", [128, 8, INW], BF16)
              wA_uq = sb(stA, "wA_uq", [128, 3, H * HD], BF16)
              wA_ukv = sb(stA, "wA_ukv", [128, 2, H * 128], BF16)
              wA_oa = sb(stA, "wA_oa", [64, 8, D], BF16)
              wA_oc = sb(stA, "wA_oc", [128, 4, D], BF16)
              with ExitStack() as stW:
                  load_weight(stW, wA_in, w_in, 8, INW, c_gattn, name="in")
                  load_weight(stW, wA_uq, w_uq, 3, H * HD, c_gq, name="uq")
                  load_weight(stW, wA_ukv, w_ukv, 2, H * 128, None, name="ukv")
                  load_weight(stW, wA_oa, w_out, 8, D, None, kp=64, name="oa")
                  load_weight(stW, wA_oc, w_out[512:1024, :], 4, D, None, name="oc")
                  tk.barrier()
              ckpt("w")

              def tile_front_A_g(W, src_rows, cs, Tcs, qcol, ntok_groups):
                  yield from front_end_g(W, src_rows)
                  yield from qglu_g(W, cs, Tcs, qTa[0:HD, :, qcol:qcol + 128], TqTa, ntok_groups)

              def qglu_g(W, cs, Tcs, qdst, Tqdst, ntok_groups):
                  hb = W.hb
                  for k in range(8):
                      pe.op(lambda e, k=k: e.matmul(W.pP[:, 0:QL], lhsT=W.xnT[:, k, :], rhs=wA_in[:, k, 0:QL],
                                                    start=(k == 0), stop=(k == 7)), reads=[W.TxnT, Tw], writes=[W.TpP])
                  dve.op(lambda e: e.tensor_copy(out=W.cqf, in_=W.pP[:, 0:QL]), reads=[W.TpP], writes=[W.Tcqf])
                  yield
                  ms, Tms = W.st[:, 2:3], W.Tst[2]
                  act.op(lambda e: e.activation(out=W.junk[:, 0:QL], in_=W.cqf, func=AF.Square,
                                                scale=1.0 / math.sqrt(QL), accum_out=ms),
                         reads=[W.Tcqf], writes=[W.Tjunk, Tms])
                  yield
                  yield from rstd_g(ms, Tms)
                  dve.op(lambda e: e.tensor_scalar(out=W.cqb[:], in0=W.cqf, scalar1=ms, scalar2=None, op0=ALU.mult),
                         reads=[W.Tcqf, Tms], writes=[W.Tcqb])
                  yield
                  for k in range(3):
                      pe.op(lambda e, k=k: e.transpose(out=W.pT[:, k * 128:(k + 1) * 128], in_=W.cqb[:, k * 128:(k + 1) * 128],
                                                       identity=c_idb[:]), reads=[W.Tcqb, Tc], writes=[W.TpT])
                  dve.op(lambda e: e.tensor_copy(out=W.cqT[:].rearrange("p k t -> p (k t)"), in_=W.pT[:, 0:384]),
                         reads=[W.TpT], writes=[W.TcqT])
                  yield
                  for nb, (c0, cw_) in enumerate(((0, 512), (512, 256))):
                      for k in range(3):
                          pe.op(lambda e, nb=nb, k=k, c0=c0, cw_=cw_: e.matmul(
                              W.pK[nb][:, 0:cw_], lhsT=W.cqT[:, k, :], rhs=wA_uq[:, k, c0:c0 + cw_],
                              start=(k == 0), stop=(k == 2)), reads=[W.TcqT, Tw], writes=[W.TpK[nb]])
                  rawf = hb.raw[:].rearrange("p h d -> p (h d)")
                  act.op(lambda e: e.activation(out=rawf[:, 0:512], in_=W.pK[0][:, 0:512], func=AF.Copy),
                         reads=[W.TpK[0]], writes=[hb.Traw])
                  dve.op(lambda e: e.tensor_copy(out=rawf[:, 512:768], in_=W.pK[1][:, 0:256]),
                         reads=[W.TpK[1]], writes=[hb.Traw])
                  yield
                  yield from head_norm_rope_g(hb, c_ghq, cs, Tcs)
                  for h in range(H):
                      pe.op(lambda e, h=h: e.transpose(out=W.pT[0:HD, h * 128:(h + 1) * 128], in_=hb.fin[:, h, :],
                                                       identity=c_idb[:]), reads=[hb.Tfin, Tc], writes=[W.TpT])
                  act.op(lambda e: e.activation(out=qdst,
                                                in_=W.pT[0:HD, :].rearrange("p (h t) -> p h t", h=H), func=AF.Copy),
                         reads=[W.TpT], writes=[Tqdst])
                  yield
                  for half in (1, 0):
                      for c in range(4):
                          col = QL + KVL + RD + half * CC + c * 128
                          for k in range(8):
                              pe.op(lambda e, half=half, c=c, k=k, col=col: e.matmul(
                                  W.pK[half][:, c * 128:(c + 1) * 128], lhsT=wA_in[:, k, col:col + 128], rhs=W.xnT[:, k, :],
                                  start=(k == 0), stop=(k == 7)), reads=[W.TxnT, Tw], writes=[W.TpK[half]])
                      if half == 1:
                          act.op(lambda e: e.activation(out=W.sig[:].rearrange("p c t -> p (c t)"), in_=W.pK[1][:, :],
                                                        func=AF.Sigmoid), reads=[W.TpK[1]], writes=[W.Tsig])
                  for (tok0, ntok, ucol_) in ntok_groups:
                      tgt = ucol_ if isinstance(ucol_, tuple) else (uT, TuT, ucol_)
                      ub, Tub, uc = tgt
                      dve.op(lambda e, tok0=tok0, ntok=ntok, ub=ub, uc=uc: e.tensor_tensor(
                          out=ub[:, :, uc:uc + ntok],
                          in0=W.pK[0][:, :].rearrange("p (c t) -> p c t", c=4)[:, :, tok0:tok0 + ntok],
                          in1=W.sig[:, :, tok0:tok0 + ntok], op=ALU.mult), reads=[W.TpK[0], W.Tsig], writes=[Tub])
                  yield

              ways = [Way(stA, w) for w in range(NWAYS)]
              TKT, TVV = T("KT"), T("VV")
              TKTs, TVVs = T("KTs"), T("VVs")
              TX1 = T("X1")
              Tout = T("outs")
              with ExitStack() as stP1:
                  for w in range(NWAYS, NWAYS_P1):
                      ways.append(Way(stP1, w))
                  for W_ in ways:
                      W_.alloc_p1(stP1)
                  KS = 4
                  kst = [sb(stP1, f"kst{i}", [HD, H, KS * 128], BF16) for i in range(2)]
                  Tkst = [T(f"kst{i}") for i in range(2)]
                  vst = [sb(stP1, f"vst{i}", [128, H, KS, 128], BF16) for i in range(2)]
                  Tvst = [T(f"vst{i}") for i in range(2)]
                  for i in range(2):
                      pool.op(lambda e, i=i: e.memset(vst[i][:], 1.0), writes=[Tvst[i]])

                  def kv_from_ckv_g(W, ckv_ap, Tck, kpe_ap, Tkp, cs, Tcs, stage_slot, have_bf16=False):
                      sbuf, slot = stage_slot
                      hb = W.hb
                      if not have_bf16:
                          act.op(lambda e: e.activation(out=W.ckvb[:], in_=ckv_ap, func=AF.Copy), reads=[Tck], writes=[W.Tckvb])
                          yield
                      for k in range(2):
                          pe.op(lambda e, k=k: e.transpose(out=W.pT[:, k * 128:(k + 1) * 128],
                                                           in_=W.ckvb[:, k * 128:(k + 1) * 128], identity=c_idb[:]),
                                reads=[W.Tckvb, Tc], writes=[W.TpT])
                      dve.op(lambda e: e.tensor_copy(out=W.ckvT[:].rearrange("p k t -> p (k t)"), in_=W.pT[:, 0:256]),
                             reads=[W.TpT], writes=[W.TckvT])
                      yield
                      for nb in range(2):
                          for k in range(2):
                              pe.op(lambda e, nb=nb, k=k: e.matmul(W.pK[nb][:, :], lhsT=W.ckvT[:, k, :],
                                                                   rhs=wA_ukv[:, k, nb * 512:(nb + 1) * 512],
                                                                   start=(k == 0), stop=(k == 1)),
                                    reads=[W.TckvT, Tw], writes=[W.TpK[nb]])
                      for nb in range(2):
                          src = W.pK[nb][:, :].rearrange("p (h c) -> p h c", h=4)
                          act.op(lambda e, nb=nb, src=src: e.activation(out=hb.raw[:, nb * 4:(nb + 1) * 4, 0:64],
                                                                        in_=src[:, :, 0:64], func=AF.Copy),
                                 reads=[W.TpK[nb]], writes=[hb.Traw])
                          dve.op(lambda e, nb=nb, src=src: e.tensor_copy(out=vst[sbuf][:, nb * 4:(nb + 1) * 4, slot, 0:64],
                                                                         in_=src[:, :, 64:128]),
                                 reads=[W.TpK[nb]], writes=[Tvst[sbuf]])
                      yield
                      pool.op(lambda e: e.tensor_copy(out=hb.raw[:, :, 64:96],
                                                      in_=kpe_ap.unsqueeze(1).to_broadcast([128, H, RD])),
                              reads=[Tkp], writes=[hb.Traw])
                      yield
                      yield from head_norm_rope_g(hb, c_ghk, cs, Tcs)
                      for h in range(H):
                          pe.op(lambda e, h=h: e.transpose(out=W.pT[0:HD, h * 128:(h + 1) * 128], in_=hb.fin[:, h, :],
                                                           identity=c_idb[:]), reads=[hb.Tfin, Tc], writes=[W.TpT])
                      act.op(lambda e: e.activation(out=kst[sbuf][:, :, slot * 128:(slot + 1) * 128],
                                                    in_=W.pT[0:HD, :].rearrange("p (h t) -> p h t", h=H), func=AF.Copy),
                             reads=[W.TpT], writes=[Tkst[sbuf]])
                      yield

                  def ckv_from_x_g(W, own_row=None, o_ck=None, o_kp=None):
                      for k in range(8):
                          pe.op(lambda e, k=k: e.matmul(W.pP[:, 0:KVL + RD], lhsT=W.xnT[:, k, :],
                                                        rhs=wA_in[:, k, QL:QL + KVL + RD], start=(k == 0), stop=(k == 7)),
                                reads=[W.TxnT, Tw], writes=[W.TpP])
                      dve.op(lambda e: e.tensor_copy(out=W.prj[:], in_=W.pP[:, 0:KVL + RD]), reads=[W.TpP], writes=[W.Tprj])
                      yield
                      ms, Tms = W.st[:, 1:2], W.Tst[1]
                      act.op(lambda e: e.activation(out=W.junk[:, 0:KVL], in_=W.prj[:, 0:KVL], func=AF.Square,
                                                    scale=1.0 / math.sqrt(KVL), accum_out=ms),
                             reads=[W.Tprj], writes=[W.Tjunk, Tms])
                      pool.op(lambda e: e.tensor_copy(out=W.kpe[:], in_=W.prj[:, KVL:KVL + RD]),
                              reads=[W.Tprj], writes=[W.Tkpe])
                      yield
                      yield from rstd_g(ms, Tms)
                      if own_row is None:
                          dve.op(lambda e: e.scalar_tensor_tensor(out=W.ckvb[:], in0=W.prj[:, 0:KVL], scalar=ms, in1=c_gkv[:],
                                                                  op0=ALU.mult, op1=ALU.mult),
                                 reads=[W.Tprj, Tms, Tc], writes=[W.Tckvb])
                          yield
                          return
                      dve.op(lambda e: e.scalar_tensor_tensor(out=W.ckv[:], in0=W.prj[:, 0:KVL], scalar=ms, in1=c_gkv[:],
                                                              op0=ALU.mult, op1=ALU.mult),
                             reads=[W.Tprj, Tms, Tc], writes=[W.Tckv])
                      yield
                      if own_row is not None:
                          sp.dma(o_ck[own_row:own_row + 128, :], W.ckv[:], reads=[W.Tckv], writes=[Tout])
                          sp.dma(o_kp[own_row:own_row + 128, :], W.kpe[:], reads=[W.Tkpe], writes=[Tout])

                  gdone = {}

                  TQT, TUT = T("QT"), T("UT")
                  zpad = sb(stP1, "zpad", [128, 4, 32], F32)
                  Tzpad = T("zpad")
                  dve.op(lambda e: e.memset(zpad[:], 0.0), writes=[Tzpad])
                  sp.dma(UT[:, :, 0:32], zpad[:], reads=[Tzpad], writes=[TUT])

                  def qu_to_scratch_g(W, cs, Tcs, ci):
                      yield from qglu_g(W, cs, Tcs, W.qst[:], W.Tqst, [(0, 128, (W.ust, W.Tust, 0))])
                      sp.dma(QT[:, :, ci * 128:(ci + 1) * 128], W.qst[:], reads=[W.Tqst], writes=[TQT])
                      sp.dma(UT[:, :, 32 + ci * 128:32 + (ci + 1) * 128], W.ust[:], reads=[W.Tust], writes=[TUT])

                  def p1_tile_g(lt, W):
                      if isinstance(lt, tuple):
                          t = lt[1]
                          yield from front_end_g(W, x_halo[t * 128:(t + 1) * 128, :])
                          yield from qu_to_scratch_g(W, c_ropeh[:, t, :], Tc, t * (RT + 1))
                          return
                      gi = lt // KS
                      sbuf, slot = gi % 2, lt % KS
                      sp.dma(W.rk[:], rope_k[:, lt, :], writes=[W.Trk])
                      yield from front_end_g(W, x_all[lt * 128:(lt + 1) * 128, :])
                      t, rem = divmod(lt, G)
                      own = rem < RT
                      yield from ckv_from_x_g(W, own_row=(t * RT + rem) * 128 if own else None, o_ck=o_ckv, o_kp=o_kpe)
                      yield from kv_from_ckv_g(W, W.ckv[:], W.Tckv, W.kpe[:], W.Tkpe, W.rk, W.Trk, (sbuf, slot),
                                               have_bf16=not own)
                      if own:
                          yield from qu_to_scratch_g(W, W.rk, W.Trk, t * (RT + 1) + 1 + rem)
                      gdone[gi] = gdone.get(gi, 0) + 1
                      if gdone[gi] == KS:
                          lt0 = gi * KS
                          sp.dma(KT[:, :, lt0 * 128:(lt0 + KS) * 128].rearrange("h d t -> d h t"), kst[sbuf][:],
                                 reads=[Tkst[sbuf]], writes=[TKT])
                          sp.dma(VV[:, :, lt0:lt0 + KS, :].rearrange("h p s c -> p h s c"), vst[sbuf][:],
                                 reads=[Tvst[sbuf]], writes=[TVV])
                  run_ways(list(range(NT)) + [("h", t) for t in range(NSLOT)], p1_tile_g)

                  ckpt("p1")
                  NG0 = NT // KS
                  PG = (PT + KS - 1) // KS
                  sdone = {}

                  def p1s_tile_g(ep, W):
                      e_, p = ep
                      gi = NG0 + e_ * PG + p // KS
                      sbuf, slot = gi % 2, p % KS
                      r0 = e_ * PAST + p * 128
                      sp.dma(W.cin[:], cckv[r0:r0 + 128, :], writes=[W.Tcin])
                      sp.dma(W.kin[:], ckpe[r0:r0 + 128, :], writes=[W.Tkin])
                      sp.dma(W.rk[:], rope_sp[:, p, :], writes=[W.Trk])
                      yield from kv_from_ckv_g(W, W.cin[:], W.Tcin, W.kin[:], W.Tkin, W.rk, W.Trk, (sbuf, slot))
                      sdone[gi] = sdone.get(gi, 0) + 1
                      p0 = (p // KS) * KS
                      ns = min(KS, PT - p0)
                      if sdone[gi] == ns:
                          sp.dma(KTs[e_, :, :, p0 * 128:(p0 + ns) * 128].rearrange("h d t -> d h t"),
                                 kst[sbuf][:, :, 0:ns * 128], reads=[Tkst[sbuf]], writes=[TKTs])
                          sp.dma(VVs[e_, :, :, p0:p0 + ns, :].rearrange("h p s c -> p h s c"),
                                 vst[sbuf][:, :, 0:ns, :], reads=[Tvst[sbuf]], writes=[TVVs])
                  run_ways([(e_, p) for e_ in range(2) for p in range(PT)], p1s_tile_g)
                  sbuf = (NG0 + 2 * PG) % 2

                  def p1n_g(_, W):
                      yield from front_end_g(W, x_s[:, :])
                      yield from ckv_from_x_g(W, own_row=0, o_ck=o_ckvs, o_kp=o_kpes)
                      yield from kv_from_ckv_g(W, W.ckv[:], W.Tckv, W.kpe[:], W.Tkpe, c_ropesn, Tc, (sbuf, 0))
                  run_ways([0], p1n_g)
                  for e_ in range(2):
                      sp.dma(KTs[e_, :, :, PT * 128:PT * 128 + 64].rearrange("h d t -> d h t"),
                             kst[sbuf][:, :, e_ * 64:(e_ + 1) * 64], reads=[Tkst[sbuf]], writes=[TKTs])
                      sp.dma(VVs[e_, :, 0:64, PT:PT + 1, :].rearrange("h p s c -> p h s c"),
                             vst[sbuf][e_ * 64:(e_ + 1) * 64, :, 0:1, :], reads=[Tvst[sbuf]], writes=[TVVs])

                  tk.barrier()
                  del ways[NWAYS:]
              ckpt("p1s")
              qTas = [sb(stA, f"qTa{i}", [128, H, QG * 128], BF16) for i in range(2)]
              TqTas = [T(f"qTa{i}") for i in range(2)]
              for i in range(2):
                  for h in range(H):
                      sp.dma(qTas[i][96:128, h, :], qmask[:, :], writes=[TqTas[i]])
              qTa, TqTa = qTas[0], TqTas[0]
              cTgs = [sb(stA, f"cTg{i}", [128, 4, QG * 128], BF16) for i in range(2)]
              TcTgs = [T(f"cTg{i}") for i in range(2)]
              cTg, TcTg = cTgs[0], TcTgs[0]
              attTs = [sb(stA, f"attT{i}", [64, H, QG * 128], BF16) for i in range(2)]
              TattTs = [T(f"attT{i}") for i in range(2)]
              attT, TattT = attTs[0], TattTs[0]
              for i in range(2):
                  pool.op(lambda e, i=i: e.memset(attTs[i][:], 0.0), writes=[TattTs[i]])
              UW = CK - 1 + QG * 128
              uTs = [sb(stA, f"uT{i}", [128, 4, UW], F32) for i in range(2)]
              TuTs = [T(f"uT{i}") for i in range(2)]
              uT, TuT = uTs[0], TuTs[0]
              acc = sb(stA, "acc", [128, 4, QG * 128], F32)
              Tacc = T("acc")
              sqc = sb(stA, "sqc", [128, 4, QG * 128], BF16)
              Tsqc = T("sqc")
              rsc = sb(stA, "rsc", [128, QG * 128], F32)
              Trsc = T("rsc")
              cpre, Tcpre = acc, Tacc
              NKB = 3
              kb = [sb(stA, f"kb{i}", [HD, KC * 128], BF16) for i in range(NKB)]
              Tkb = [T(f"kb{i}") for i in range(NKB)]
              vb = [sb(stA, f"vb{i}", [128, KC, 128], BF16) for i in range(NKB)]
              Tvb = [T(f"vb{i}") for i in range(NKB)]
              kd = [sb(stA, f"kd{i}", [128, QG * 128], BF16) for i in range(2)]
              Tkd = [T(f"kd{i}") for i in range(2)]
              for i in range(2):
                  sp.dma(kd[i][96:128, :], khot[:, :], writes=[Tkd[i]])
              vd = [sb(stA, f"vd{i}", [128, QG, 128], BF16) for i in range(2)]
              Tvd = [T(f"vd{i}") for i in range(2)]
              pb = [sb(stA, f"pb{i}", [128, 1024], BF16) for i in range(2)]
              Tpb = [T(f"pb{i}") for i in range(2)]
              rcp = sb(stA, "rcp", [64, 512], F32)
              Trcp = T("rcp")
              rcs = sb(stA, "rcs", [128, 512], F32)
              Trcs = T("rcs")
              xr = [sb(stA, f"xr{i}", [128, D], F32) for i in range(2)]
              Txr = [T(f"xr{i}") for i in range(2)]
              x1o, Tx1o = xr, Txr
              cvo = sb(stA, "cvo", [32, CC], F32)
              Tcvo = T("cvo")
              cnt = dict(kb=0, pb=0, x1=0, po=0, kd=0)

              def conv_module_g(uT_, TuT_, cT_, TcT_, ucol, ncol, ccol, bank=0):
                  PB_, TPB_ = pM[bank], TpM[bank]
                  for c in range(4):
                      dve.op(lambda e, c=c: e.tensor_scalar(out=acc[:, c, 0:ncol], in0=uT_[:, c, ucol - 30:ucol - 30 + ncol],
                                                            scalar1=c_cw[:, c, 0:1], scalar2=c_cb[:, c:c + 1],
                                                            op0=ALU.mult, op1=ALU.add),
                             reads=[TuT_, Tc], writes=[Tacc])
                      yield
                      for k in range(1, CK):
                          dve.op(lambda e, c=c, k=k: e.scalar_tensor_tensor(
                              out=acc[:, c, 0:ncol], in0=uT_[:, c, ucol - 30 + k:ucol - 30 + k + ncol],
                              scalar=c_cw[:, c, k:k + 1], in1=acc[:, c, 0:ncol], op0=ALU.mult, op1=ALU.add),
                              reads=[TuT_, Tc, Tacc], writes=[Tacc])
                          yield
                  act.op(lambda e: e.activation(out=sqc[:, :, 0:ncol], in_=acc[:, :, 0:ncol], func=AF.Square,
                                                scale=1.0 / math.sqrt(CC)), reads=[Tacc], writes=[Tsqc])
                  yield
                  for c in range(4):
                      pe.op(lambda e, c=c: e.matmul(PB_[:, 0:ncol], lhsT=c_ones[:], rhs=sqc[:, c, 0:ncol],
                                                    start=(c == 0), stop=(c == 3)), reads=[Tsqc, Tc], writes=[TPB_])
                  dve.op(lambda e: e.tensor_scalar(out=rsc[:, 0:ncol], in0=PB_[:, 0:ncol], scalar1=EPS, scalar2=None,
                                                   op0=ALU.add), reads=[TPB_], writes=[Trsc])
                  yield
                  act.op(lambda e: e.activation(out=rsc[:, 0:ncol], in_=rsc[:, 0:ncol], func=AF.Sqrt),
                         reads=[Trsc], writes=[Trsc])
                  yield
                  dve.op(lambda e: e.reciprocal(out=rsc[:, 0:ncol], in_=rsc[:, 0:ncol]), reads=[Trsc], writes=[Trsc])
                  yield
                  for c in range(4):
                      dve.op(lambda e, c=c: e.scalar_tensor_tensor(out=cpre[:, c, 0:ncol], in0=acc[:, c, 0:ncol],
                                                                   scalar=c_cg[:, c:c + 1], in1=rsc[:, 0:ncol],
                                                                   op0=ALU.mult, op1=ALU.mult),
                             reads=[Tacc, Trsc, Tc], writes=[Tcpre])
                      yield
                  act.op(lambda e: e.activation(out=cT_[:, :, ccol:ccol + ncol], in_=cpre[:, :, 0:ncol], func=AF.Silu),
                         reads=[Tcpre], writes=[TcT_])
                  yield

              def conv_module(ucol, ncol, ccol):
                  for _ in conv_module_g(uT, TuT, cTg, TcTg, ucol, ncol, ccol, bank=2):
                      pass

              def attention(ncols, segs, kt_src, vv_src, Tk, Tv, qc0=0, ksz_last=128, side=None):
                  items = []
                  for h in range(H):
                      po_i = cnt["po"] % 2
                      cnt["po"] += 1
                      PO, TPO = pO[po_i], TpO[po_i]
                      first = True
                      nseg = len(segs)
                      for si, (kind, t0, ntl, bcol, ksz) in enumerate(segs):
                          last_seg = si == nseg - 1
                          if kind == "d":
                              di = cnt["kd"] % 2
                              cnt["kd"] += 1
                              KD, TKD, VD, TVD = kd[di], Tkd[di], vd[di], Tvd[di]
                              loads = [(KD[0:HD, 0:ntl * 128], kt_src(h, t0, ntl), Tk, TKD),
                                       (VD[:, 0:ntl, :], vv_src(h, t0, ntl), Tv, TVD)]
                              chunks = [(t0, ntl, KD, TKD, VD, TVD, 128, loads)]
                          else:
                              chunks = []
                              for c0 in range(t0, t0 + ntl, KC):
                                  cn = min(KC, t0 + ntl - c0)
                                  bi = cnt["kb"] % NKB
                                  cnt["kb"] += 1
                                  KB, TKB, VB, TVB = kb[bi], Tkb[bi], vb[bi], Tvb[bi]
                                  lastc = (c0 + cn == t0 + ntl)
                                  kz_l = ksz if lastc else 128
                                  loads = [(KB[0:HD, 0:(cn - 1) * 128 + kz_l], kt_src(h, c0, cn, kz_l), Tk, TKB)]
                                  if kz_l == 128:
                                      loads.append((VB[:, 0:cn, :], vv_src(h, c0, cn), Tv, TVB))
                                  else:
                                      if cn > 1:
                                          loads.append((VB[:, 0:cn - 1, :], vv_src(h, c0, cn - 1), Tv, TVB))
                                      loads.append((VB[0:kz_l, cn - 1:cn, :], vv_src(h, c0 + cn - 1, 1, kz_l), Tv, TVB))
                                  chunks.append((c0, cn, KB, TKB, VB, TVB, HD, loads))
                          for ci_, (c0, cn, KB, TKB, VB, TVB, KR, loads) in enumerate(chunks):
                              for j in range(cn):
                                  lastt = (c0 + j == t0 + ntl - 1)
                                  kz = ksz if lastt else 128
                                  cs0 = j * 128 if kind == "d" else 0
                                  items.append(dict(h=h, PO=PO, TPO=TPO, KB=KB, TKB=TKB, VB=VB, TVB=TVB, KR=KR, j=j, kz=kz,
                                                    cs0=cs0, bcol=bcol, first=first, last=(last_seg and lastt),
                                                    loads=(loads if j == 0 else None), key=(h, si, ci_, kind)))
                                  first = False
                  units = []
                  i = 0
                  while i < len(items):
                      a = items[i]
                      if (i + 1 < len(items) and a["key"][3] == "n" and items[i + 1]["key"] == a["key"]
                              and a["kz"] == 128 and items[i + 1]["kz"] == 128):
                          units.append([a, items[i + 1]])
                          i += 2
                      else:
                          units.append([a])
                          i += 1

                  def emit_qk(u):
                      pi = cnt["pb"] % 2
                      cnt["pb"] += 1
                      PSp = pSS[:, pi * 1024:(pi + 1) * 1024].rearrange("p (i c) -> p i c", i=2)
                      PBp = pb[pi][:, :].rearrange("p (i c) -> p i c", i=2)
                      for i, it in enumerate(u):
                          if it["loads"]:
                              for (dst, src, Tsrc, Tdst) in it["loads"]:
                                  sp.dma(dst, src, reads=[Tsrc], writes=[Tdst])
                          it["PS"], it["PB"], it["TPS"], it["TPB"] = PSp, PBp, TpSS[pi], Tpb[pi]
                          pe.op(lambda e, it=it, i=i: e.matmul(
                              PSp[0:it["kz"], i, it["cs0"]:ncols],
                              lhsT=it["KB"][0:it["KR"], it["j"] * 128:it["j"] * 128 + it["kz"]],
                              rhs=qTa[0:it["KR"], it["h"], qc0 + it["cs0"]:qc0 + ncols], start=True, stop=True),
                              reads=[it["TKB"], TqTa], writes=[TpSS[pi]])

                  def emit_rest(u):
                      a = u[0]
                      kz, cs0, n = a["kz"], a["cs0"], len(u)
                      PSp, PBp, TPS, TPB = a["PS"], a["PB"], a["TPS"], a["TPB"]
                      bias_ap = c_zero[0:kz, 0:1] if a["bcol"] is None else c_bias[0:kz, a["bcol"]:a["bcol"] + 1]
                      act.op(lambda e: e.activation(out=PBp[0:kz, 0:n, cs0:ncols], in_=PSp[0:kz, 0:n, cs0:ncols],
                                                    func=AF.Exp, bias=bias_ap, scale=SCALE),
                             reads=[TPS, Tc], writes=[TPB])
                      for i, it in enumerate(u):
                          pe.op(lambda e, it=it, i=i: e.matmul(it["PO"][:, cs0:ncols], lhsT=it["VB"][0:kz, it["j"], :],
                                                               rhs=PBp[0:kz, i, cs0:ncols], start=it["first"], stop=it["last"]),
                                reads=[it["TVB"], TPB], writes=[it["TPO"]])
                          if it["last"]:
                              PO, TPO, h = it["PO"], it["TPO"], it["h"]
                              dve.op(lambda e, PO=PO: e.tensor_scalar(out=rcs[64:128, 0:ncols], in0=PO[64:128, 0:ncols],
                                                                      scalar1=1e-30, scalar2=None, op0=ALU.add),
                                     reads=[TPO], writes=[Trcs])
                              dve.op(lambda e: e.reciprocal(out=rcs[64:128, 0:ncols], in_=rcs[64:128, 0:ncols]),
                                     reads=[Trcs], writes=[Trcs])
                              dve.op(lambda e: e.tensor_copy(out=rcp[0:64, 0:ncols], in_=rcs[64:128, 0:ncols]),
                                     reads=[Trcs], writes=[Trcp])
                              dve.op(lambda e, PO=PO, h=h: e.tensor_tensor(out=attT[:, h, qc0:qc0 + ncols], in0=PO[0:64, 0:ncols],
                                                                           in1=rcp[0:64, 0:ncols], op=ALU.mult),
                                     reads=[TPO, Trcp], writes=[TattT])
                  nu = len(units)
                  if nu:
                      emit_qk(units[0])
                  for i in range(nu):
                      if i + 1 < nu:
                          emit_qk(units[i + 1])
                      emit_rest(units[i])
                      if side is not None:
                          next(side, None)
                  if side is not None:
                      for _ in side:
                          pass

              def attention_halo(segs, side=None):
                  tiles = []
                  for (kind, t0, ntl, bcol, ksz) in segs:
                      for c0 in range(t0, t0 + ntl, 2):
                          cn = min(2, t0 + ntl - c0)
                          for j in range(cn):
                              tiles.append((c0, cn, j, bcol))
                  nt_ = len(tiles)
                  st_ = {}

                  def emit_qk(n):
                      c0, cn, j, bcol = tiles[n]
                      if j == 0:
                          bi = cnt["kb"] % NKB
                          cnt["kb"] += 1
                          KB3 = kb[bi][0:HD, 0:H * 256].rearrange("p (h t) -> p h t", h=H)
                          VB4 = vb[bi][:, 0:H * 2, :].rearrange("p (h s) c -> p h s c", h=H)
                          sp.dma(KB3[:, :, 0:cn * 128], KT[:, :, c0 * 128:(c0 + cn) * 128].rearrange("h d t -> d h t"),
                                 reads=[TKT], writes=[Tkb[bi]])
                          sp.dma(VB4[:, :, 0:cn, :], VV[:, :, c0:c0 + cn, :].rearrange("h p s c -> p h s c"),
                                 reads=[TVV], writes=[Tvb[bi]])
                          st_["cur"] = (KB3, VB4, Tkb[bi], Tvb[bi])
                      KB3, VB4, TKB, TVB = st_["cur"]
                      pi = cnt["pb"] % 2
                      cnt["pb"] += 1
                      PSp = pSS[:, pi * 1024:(pi + 1) * 1024]
                      for h in range(H):
                          pe.op(lambda e, h=h: e.matmul(PSp[:, h * 64:(h + 1) * 64], lhsT=KB3[:, h, j * 128:(j + 1) * 128],
                                                        rhs=qTa[0:HD, h, 64:128], start=True, stop=True,
                                                        skip_group_check=True),
                                reads=[TKB, TqTa], writes=[TpSS[pi]])
                      st_[n] = (PSp, pb[pi], TpSS[pi], Tpb[pi], VB4, TVB, j, bcol)

                  def emit_rest(n):
                      PSp, PB, TPS, TPB, VB4, TVB, j, bcol = st_.pop(n)
                      bias_ap = c_zero[:, 0:1] if bcol is None else c_bias[:, bcol:bcol + 1]
                      act.op(lambda e: e.activation(out=PB[:, 0:512], in_=PSp[:, 0:512], func=AF.Exp, bias=bias_ap, scale=SCALE),
                             reads=[TPS, Tc], writes=[TPB])
                      for h in range(H):
                          pe.op(lambda e, h=h: e.matmul(pO[0][:, h * 64:(h + 1) * 64], lhsT=VB4[:, h, j, :],
                                                        rhs=PB[:, h * 64:(h + 1) * 64],
                                                        start=(n == 0 and h == 0), stop=(n == nt_ - 1),
                                                        skip_group_check=True),
                                reads=[TVB, TPB], writes=[TpO[0]])
                  if nt_:
                      emit_qk(0)
                  for n in range(nt_):
                      if n + 1 < nt_:
                          emit_qk(n + 1)
                      emit_rest(n)
                      if side is not None:
                          next(side, None)
                          next(side, None)
                  if side is not None:
                      for _ in side:
                          pass
                  dve.op(lambda e: e.tensor_scalar(out=rcs[64:128, 0:512], in0=pO[0][64:128, :], scalar1=1e-30,
                                                   scalar2=None, op0=ALU.add), reads=[TpO[0]], writes=[Trcs])
                  dve.op(lambda e: e.reciprocal(out=rcs[64:128, 0:512], in_=rcs[64:128, 0:512]), reads=[Trcs], writes=[Trcs])
                  dve.op(lambda e: e.tensor_copy(out=rcp[0:64, 0:512], in_=rcs[64:128, 0:512]), reads=[Trcs], writes=[Trcp])
                  dve.op(lambda e: e.tensor_tensor(
                      out=attT[:, :, 64:128], in0=pO[0][0:64, :].rearrange("p (h t) -> p h t", h=H),
                      in1=rcp[0:64, 0:512].rearrange("p (h t) -> p h t", h=H), op=ALU.mult),
                      reads=[TpO[0], Trcp], writes=[TattT])

              def out_proj_g(aT_, TaT_, cT_, TcT_, src_rows, col, x1_row, bank=0):
                  bi = cnt["x1"] % 2
                  cnt["x1"] += 1
                  PB_, TPB_ = pM[bank], TpM[bank]
                  sp.dma(xr[bi][:], src_rows, writes=[Txr[bi]])
                  for nb in range(2):
                      for h in range(H):
                          pe.op(lambda e, nb=nb, h=h: e.matmul(PB_[:, :], lhsT=aT_[:, h, col:col + 128],
                                                               rhs=wA_oa[:, h, nb * 512:(nb + 1) * 512],
                                                               start=(h == 0), stop=False),
                                reads=[TaT_, Tw], writes=[TPB_])
                      for c in range(4):
                          pe.op(lambda e, nb=nb, c=c: e.matmul(PB_[:, :], lhsT=cT_[:, c, col:col + 128],
                                                               rhs=wA_oc[:, c, nb * 512:(nb + 1) * 512],
                                                               start=False, stop=(c == 3)),
                                reads=[TcT_, Tw], writes=[TPB_])
                      dve.op(lambda e, nb=nb: e.tensor_tensor(out=x1o[bi][:, nb * 512:(nb + 1) * 512], in0=PB_[:, :],
                                                              in1=xr[bi][:, nb * 512:(nb + 1) * 512], op=ALU.add),
                             reads=[TPB_, Txr[bi]], writes=[Tx1o[bi]])
                      yield
                  sp.dma(X1[x1_row * 128:(x1_row + 1) * 128, :], x1o[bi][:], reads=[Tx1o[bi]], writes=[TX1])
                  yield

              def out_proj(src_rows, col, x1_row):
                  for _ in out_proj_g(attT, TattT, cTg, TcTg, src_rows, col, x1_row, bank=0):
                      pass

              def emit_conv_state(ub, Tub, col0, dst):
                  for c in range(4):
                      pe.op(lambda e, c=c: e.transpose(out=pM[2][0:32, c * 128:(c + 1) * 128], in_=ub[:, c, col0:col0 + 32],
                                                       identity=c_idf[:]), reads=[Tub, Tc], writes=[TpM[2]])
                  dve.op(lambda e: e.tensor_copy(out=cvo[:], in_=pM[2][0:32, :]), reads=[TpM[2]], writes=[Tcvo])
                  sp.dma(dst, cvo[:], reads=[Tcvo], writes=[Tout])

              def kt_p(h, t0, n, ksz=128):
                  return KT[h, :, t0 * 128:(t0 + n - 1) * 128 + ksz]

              def vv_p(h, t0, n, ksz=128):
                  return VV[h, 0:ksz, t0:t0 + n, :]


              groups = []
              for t in range(NSLOT):
                  groups.append((t, None))
                  for g in range(RT // QG):
                      groups.append((t, g))

              def load_group(n):
                  t, g = groups[n]
                  b = n % 2
                  ci0 = t * (RT + 1) + (0 if g is None else 1 + g * QG)
                  ntl = 1 if g is None else QG
                  sp.dma(qTas[b][0:HD, :, 0:ntl * 128], QT[:, :, ci0 * 128:(ci0 + ntl) * 128], reads=[TQT], writes=[TqTas[b]])
                  sp.dma(uTs[b][:, :, 0:CK - 1 + ntl * 128],
                         UT[:, :, 32 + ci0 * 128 - (CK - 1):32 + (ci0 + ntl) * 128], reads=[TUT], writes=[TuTs[b]])
              def conv_of(n):
                  ncol_ = 128 if groups[n][1] is None else QG * 128
                  return conv_module_g(uTs[n % 2], TuTs[n % 2], cTgs[n % 2], TcTgs[n % 2], CK - 1, ncol_, 0, bank=0)

              def outproj_of(n):
                  t, g = groups[n]
                  b = n % 2
                  if g is None:
                      yield from out_proj_g(attTs[b], TattTs[b], cTgs[b], TcTgs[b], x_halo[t * 128:(t + 1) * 128, :], 0,
                                            t * (RT + 1), bank=0)
                  else:
                      for i in range(QG):
                          lt = t * G + g * QG + i
                          yield from out_proj_g(attTs[b], TattTs[b], cTgs[b], TcTgs[b], x_all[lt * 128:(lt + 1) * 128, :],
                                                i * 128, t * (RT + 1) + 1 + g * QG + i, bank=0)

              def chain(*gens):
                  for g_ in gens:
                      if g_ is not None:
                          yield from g_
              load_group(0)
              for _ in conv_of(0):
                  pass
              for n, (t, g) in enumerate(groups):
                  base = t * G
                  qTa, TqTa, uT, TuT = qTas[n % 2], TqTas[n % 2], uTs[n % 2], TuTs[n % 2]
                  attT, TattT = attTs[n % 2], TattTs[n % 2]
                  nxt = None
                  if n + 1 < len(groups):
                      load_group(n + 1)
                      nxt = conv_of(n + 1)
                  side = chain(outproj_of(n - 1) if n > 0 else None, nxt)
                  if g is None:
                      segs = []
                      if t > 0:
                          segs.append(("n", 0, base, None, 128))
                      for r in range(1, NCPB):
                          segs.append(("n", base + r * RT, RT, r, 128))
                      attention_halo(segs, side=side)
                  else:
                      i0 = g * QG
                      segs = []
                      if base + i0 > 0:
                          segs.append(("n", 0, base + i0, None, 128))
                      segs.append(("d", base + i0, QG, None, 128))
                      for r in range(1, NCPB):
                          segs.append(("n", base + r * RT, RT, r, 128))
                      attention(QG * 128, segs, kt_p, vv_p, TKT, TVV, side=side)
                      if t == NSLOT - 1 and g == RT // QG - 1:
                          emit_conv_state(uT, TuT, UW - 32, o_conv[:, :])
              for _ in outproj_of(len(groups) - 1):
                  pass
              attT, TattT = attTs[0], TattTs[0]
              qTa, TqTa, uT, TuT = qTas[0], TqTas[0], uTs[0], TuTs[0]
              cTg, TcTg = cTgs[0], TcTgs[0]

              ckpt("p2")
              uS = [sb(stA, f"uS{i}", [128, 4, CK - 1 + 64], F32) for i in range(2)]
              TuS = [T(f"uS{i}") for i in range(2)]
              for e_ in range(2):
                  sp.dma(uS[e_][:, :, 0:CK - 1], sconvT[:, e_, :, :], writes=[TuS[e_]])
              run_ways([0], lambda _, W: tile_front_A_g(W, x_s[:, :], c_ropesn, Tc, 0,
                                                        [(0, 64, (uS[0], TuS[0], CK - 1)), (64, 64, (uS[1], TuS[1], CK - 1))]))
              for e_ in range(2):
                  dve.op(lambda e, e_=e_: e.tensor_copy(out=uT[:, :, 0:CK - 1 + 64], in_=uS[e_][:, :, :]),
                         reads=[TuS[e_]], writes=[TuT])
                  conv_module(CK - 1, 64, e_ * 64)
                  emit_conv_state(uS[e_], TuS[e_], CK - 1 + 64 - 32, o_convs[e_, :, :])

                  def kt_s(h, t0, n, ksz=128, e_=e_):
                      return KTs[e_, h, :, t0 * 128:(t0 + n - 1) * 128 + ksz]

                  def vv_s(h, t0, n, ksz=128, e_=e_):
                      return VVs[e_, h, 0:ksz, t0:t0 + n, :]
                  attention(64, [("n", 0, PT + 1, None, 64)], kt_s, vv_s, TKTs, TVVs, qc0=e_ * 64)
              out_proj(x_s[:, :], 0, NX1 - 1)
              tk.barrier()
              ckpt("p2s")

          with ExitStack() as stB:
              wB_up = sb(stB, "wB_up", [128, 8, 2 * DFF], BF16)
              wB_dn = sb(stB, "wB_dn", [128, NFC, D], BF16)
              with ExitStack() as stW:
                  load_weight(stW, wB_up, w_up, 8, 2 * DFF, c_gffn, name="up")
                  load_weight(stW, wB_dn, w_down, NFC, D, None, name="dn")
                  tk.barrier()
              QB = QG
              NB_ = QB * 128
              xtB = sb(stB, "xtB", [128, QB, D], F32)
              TxtB = [T(f"xtB{j}") for j in range(QB)]
              msB = sb(stB, "msB", [128, QB], F32)
              TmsB = T("msB")
              xnB = sb(stB, "xnB", [128, D], BF16)
              TxnB = T("xnB")
              xnTB = sb(stB, "xnTB", [128, 8, NB_], BF16)
              TxnTB = T("xnTB")
              hT = sb(stB, "hT", [128, NFC, NB_], BF16)
              ThT = T("hT")
              AW = NB_ + 8
              aTc = [sb(stB, f"aTc{i}", [128, AW], F32) for i in range(2)]
              TaTc = [T(f"aTc{i}") for i in range(2)]
              accB = [sb(stB, f"accB{i}", [128, AW], F32) for i in range(2)]
              TaccB = [T(f"accB{i}") for i in range(2)]
              yo = [sb(stB, f"yo{i}", [128, D], F32) for i in range(2)]
              Tyo = [T(f"yo{i}") for i in range(2)]
              arow = [sb(stB, f"arow{i}", [128, 512], F32) for i in range(2)]
              Tarow = [T(f"arow{i}") for i in range(2)]
              hist = sb(stB, "hist", [128, NFC, 2], F32)
              Thist = T("hist")
              hnew = sb(stB, "hnew", [128, NFC, 2], F32)
              Thnew = T("hnew")
              Toutb = T("outsB")
              Abank = [(pM[0], TpM[0]), (pS[3], TpS[3])]
              Gbank = [(pS[0], TpS[0]), (pS[1], TpS[1])]
              Dbank = [(pO[0], TpO[0]), (pO[1], TpO[1])]
              cb_ = dict(ab=0, gb=0, db=0, yo=0, ar=0)

              def ffn_batch(x1_rows, subs, outs, a_rows_dst=None):
                  nt = len(x1_rows)
                  N = nt * 128
                  halo = outs is None
                  for j, r in enumerate(x1_rows):
                      sp.dma(xtB[:, j, :], X1[r * 128:(r + 1) * 128, :], writes=[TxtB[j]])
                      act.op(lambda e, j=j: e.activation(out=xnB[:], in_=xtB[:, j, :], func=AF.Square, scale=1.0 / math.sqrt(D),
                                                         accum_out=msB[:, j:j + 1]), reads=[TxtB[j]], writes=[TxnB, TmsB])
                  rstd_from_msq(None, (msB[:, 0:nt], TmsB), nt)
                  for j in range(nt):
                      dve.op(lambda e, j=j: e.tensor_scalar(out=xnB[:], in0=xtB[:, j, :], scalar1=msB[:, j:j + 1], scalar2=None,
                                                            op0=ALU.mult), reads=[TxtB[j], TmsB], writes=[TxnB])
                      for k in range(8):
                          pe.op(lambda e, k=k: e.transpose(out=pT[:, k * 128:(k + 1) * 128], in_=xnB[:, k * 128:(k + 1) * 128],
                                                           identity=c_idb[:]), reads=[TxnB, Tc], writes=[TpT])
                      act.op(lambda e, j=j: e.activation(out=xnTB[:, :, j * 128:(j + 1) * 128],
                                                         in_=pT[:, :].rearrange("p (k t) -> p k t", k=8), func=AF.Copy),
                             reads=[TpT], writes=[TxnTB])
                  W_ = N + 2 * len(subs)
                  state = {}

                  def stage1(c):
                      ai = cb_["ab"] % 2
                      cb_["ab"] += 1
                      PA, TPA = Abank[ai]
                      AT, TAT, AC, TAC = aTc[ai], TaTc[ai], accB[ai], TaccB[ai]
                      for k in range(8):
                          pe.op(lambda e, k=k: e.matmul(PA[:, 0:N], lhsT=wB_up[:, k, c * 128:(c + 1) * 128], rhs=xnTB[:, k, 0:N],
                                                        start=(k == 0), stop=(k == 7)), reads=[TxnTB, Tw], writes=[TPA])
                      if not halo:
                          gi = cb_["gb"] % 2
                          cb_["gb"] += 1
                          PG, TPG = Gbank[gi]
                          col = DFF + c * 128
                          for k in range(8):
                              pe.op(lambda e, k=k: e.matmul(PG[:, 0:N], lhsT=wB_up[:, k, col:col + 128], rhs=xnTB[:, k, 0:N],
                                                            start=(k == 0), stop=(k == 7)), reads=[TxnTB, Tw], writes=[TPG])
                      else:
                          PG = TPG = None
                      for i, (tok0, ntok, hap, Th) in enumerate(subs):
                          w0 = tok0 + 2 * i
                          act.op(lambda e, w0=w0, tok0=tok0, ntok=ntok: e.activation(
                              out=AT[:, w0 + 2:w0 + 2 + ntok], in_=PA[:, tok0:tok0 + ntok], func=AF.Copy),
                              reads=[TPA], writes=[TAT])
                          if not halo:
                              act.op(lambda e, w0=w0, tok0=tok0, ntok=ntok: e.activation(
                                  out=AC[:, w0:w0 + ntok], in_=PA[:, tok0:tok0 + ntok], func=AF.Identity,
                                  scale=c_fw[:, c, 2:3], bias=c_fb[:, c:c + 1]), reads=[TPA, Tc], writes=[TAC])
                              dve.op(lambda e, w0=w0, hap=hap: e.tensor_copy(out=AT[:, w0:w0 + 2], in_=hap[:, c, :]),
                                     reads=[Th], writes=[TAT])
                          if i == len(subs) - 1:
                              dve.op(lambda e, w0=w0, ntok=ntok: e.tensor_copy(out=hnew[:, c, :], in_=AT[:, w0 + ntok:w0 + ntok + 2]),
                                     reads=[TAT], writes=[Thnew])
                      state[c] = (AT, TAT, AC, TAC, PG, TPG)

                  def stage2(c):
                      AT, TAT, AC, TAC, PG, TPG = state.pop(c)
                      si = cb_["ab"] % 2
                      SL, TSL = AC, TAC
                      dve.op(lambda e: e.scalar_tensor_tensor(out=AC[:, 0:W_ - 2], in0=AT[:, 0:W_ - 2], scalar=c_fw[:, c, 0:1],
                                                              in1=AC[:, 0:W_ - 2], op0=ALU.mult, op1=ALU.add),
                             reads=[TAT, TAC, Tc], writes=[TAC])
                      dve.op(lambda e: e.scalar_tensor_tensor(out=AC[:, 0:W_ - 2], in0=AT[:, 1:W_ - 1], scalar=c_fw[:, c, 1:2],
                                                              in1=AC[:, 0:W_ - 2], op0=ALU.mult, op1=ALU.add),
                             reads=[TAT, TAC, Tc], writes=[TAC])
                      act.op(lambda e: e.activation(out=SL[:, 0:W_ - 2], in_=AC[:, 0:W_ - 2], func=AF.Silu),
                             reads=[TAC], writes=[TSL])
                      for i, (tok0, ntok, hap, Th) in enumerate(subs):
                          w0 = tok0 + 2 * i
                          dve.op(lambda e, w0=w0, tok0=tok0, ntok=ntok: e.tensor_tensor(
                              out=hT[:, c, tok0:tok0 + ntok], in0=PG[:, tok0:tok0 + ntok], in1=SL[:, w0:w0 + ntok], op=ALU.mult),
                              reads=[TPG, TSL], writes=[ThT])

                  if len(subs) > 1:
                      for i in range(2):
                          dve.op(lambda e, i=i: e.memset(accB[i][:], 0.0), writes=[TaccB[i]])
                  for c in range(NFC + 1):
                      if c < NFC:
                          stage1(c)
                      if c >= 1 and not halo:
                          stage2(c - 1)
                  dve.op(lambda e: e.tensor_copy(out=hist[:], in_=hnew[:]), reads=[Thnew], writes=[Thist])
                  if halo:
                      return
                  if a_rows_dst is not None:
                      jl = nt - 1
                      for n0 in range(0, DFF, 512):
                          nw = min(512, DFF - n0)
                          PA, TPA = pS[2], TpS[2]
                          ri = cb_["ar"] % 2
                          cb_["ar"] += 1
                          for k in range(8):
                              pe.op(lambda e, k=k, n0=n0, nw=nw: e.matmul(
                                  PA[:, 0:nw], lhsT=xnTB[:, k, jl * 128:(jl + 1) * 128], rhs=wB_up[:, k, n0:n0 + nw],
                                  start=(k == 0), stop=(k == 7)), reads=[TxnTB, Tw], writes=[TPA])
                          dve.op(lambda e, nw=nw, ri=ri: e.tensor_copy(out=arow[ri][:, 0:nw], in_=PA[:, 0:nw]),
                                 reads=[TPA], writes=[Tarow[ri]])
                          for (r0, dst) in a_rows_dst:
                              sp.dma(dst[:, n0:n0 + nw], arow[ri][r0:r0 + 32, 0:nw], reads=[Tarow[ri]], writes=[Toutb])
                  for j in range(nt):
                      yi = cb_["yo"] % 2
                      cb_["yo"] += 1
                      for nb in range(2):
                          PD, TPD = Dbank[cb_["db"] % 2]
                          cb_["db"] += 1
                          for c in range(NFC):
                              pe.op(lambda e, nb=nb, c=c, j=j, PD=PD: e.matmul(
                                  PD[:, :], lhsT=hT[:, c, j * 128:(j + 1) * 128], rhs=wB_dn[:, c, nb * 512:(nb + 1) * 512],
                                  start=(c == 0), stop=(c == NFC - 1)), reads=[ThT, Tw], writes=[TPD])
                          dve.op(lambda e, nb=nb, j=j, PD=PD, yi=yi: e.tensor_tensor(
                              out=yo[yi][:, nb * 512:(nb + 1) * 512], in0=PD[:, :], in1=xtB[:, j, nb * 512:(nb + 1) * 512],
                              op=ALU.add), reads=[TPD, TxtB[j]], writes=[Tyo[yi]])
                      sp.dma(outs[j], yo[yi][:], reads=[Tyo[yi]], writes=[Toutb])

              for t in range(NSLOT):
                  r0 = t * (RT + 1)
                  ffn_batch([r0], [(0, 128, None, None)], None)
                  dve.op(lambda e, t=t: e.tensor_scalar(out=hist[:], in0=hist[:], scalar1=c_hflag[:, t:t + 1],
                                                        scalar2=None, op0=ALU.mult), reads=[Thist, Tc], writes=[Thist])
                  for i0 in range(0, RT, QB):
                      last = (t == NSLOT - 1 and i0 + QB == RT)
                      ffn_batch([r0 + 1 + i0 + i for i in range(QB)], [(0, NB_, hist, Thist)],
                                [o_y[(t * RT + i0 + i) * 128:(t * RT + i0 + i + 1) * 128, :] for i in range(QB)],
                                a_rows_dst=[(96, o_ffn)] if last else None)
              hs = [sb(stB, f"hs{i}", [128, NFC, 2], F32) for i in range(2)]
              Ths = [T(f"hs{i}") for i in range(2)]
              for e_ in range(2):
                  sp.dma(hs[e_][:], sffnT[:, e_, :, :], writes=[Ths[e_]])
              ffn_batch([NX1 - 1], [(0, 64, hs[0], Ths[0]), (64, 64, hs[1], Ths[1])], [o_ys[:, :]],
                        a_rows_dst=[(32, o_ffns[0]), (96, o_ffns[1])])
              tk.barrier()
          print(f"[build] sems={tk.nsem} waits={tk.nwaits} insts={tk.ninst}", flush=True)
    except _Stop:
        pass
    return nc


def _rope_tab(pos):
    inv = (1.0 / (10000.0 ** (np.arange(0, RD, 2, dtype=np.float32) / np.float32(RD)))).astype(np.float32)
    ang = pos.astype(np.float32)[:, None] * inv[None, :]
    return np.concatenate([np.cos(ang.astype(np.float64)), np.sin(ang.astype(np.float64))], axis=1).astype(np.float32)


def _run(inputs, cfg):
    SEQ, PAST, RT = cfg["SEQ"], cfg["PAST"], cfg["RT"]
    NT = SEQ // 128
    G = NCPB * RT
    NSLOT = NT // G
    QG = min(4, RT)
    PT = PAST // 128
    f32 = np.float32
    bf = ml_dtypes.bfloat16
    g = {k: np.asarray(v) for k, v in inputs.items()}
    xp, xs = g["x_prompt"], g["x_sample"]
    B = xp.shape[0]
    assert B * NCPB == 8 and xs.shape[0] == 16

    def chunked(v, n):
        return np.ascontiguousarray(v.reshape(n, 128).T).astype(f32)

    def bc(v):
        return np.ascontiguousarray(np.broadcast_to(v[None, :], (128, v.shape[0]))).astype(f32)
    common = {
        "w_in": g["w_in"][0], "w_uq": g["w_uq"][0], "w_ukv": g["w_ukv"][0], "w_out": g["w_out"][0],
        "w_up": g["w_up"][0], "w_down": g["w_down"][0],
        "g_attn": chunked(g["attn_norm"][0], 8), "g_q": chunked(g["q_norm"][0], 3),
        "g_ffn": chunked(g["ffn_norm"][0], 8), "g_kv": bc(g["kv_norm"][0]),
        "g_hq": bc(g["qk_norm_q"][0]), "g_hk": bc(g["qk_norm_k"][0]),
        "cw": np.ascontiguousarray(g["conv_w"][0].T.reshape(4, 128, CK).transpose(1, 0, 2)),
        "cb": chunked(g["conv_b"][0], 4), "cg": chunked(g["conv_norm"][0], 4),
        "fw": np.ascontiguousarray(g["ffn_conv_w"][0].T.reshape(NFC, 128, 3).transpose(1, 0, 2)),
        "fb": chunked(g["ffn_conv_b"][0], NFC),
        "identb": np.eye(128, dtype=f32).astype(bf), "identf": np.eye(128, dtype=f32),
        "onesb": np.ones((128, 128), f32).astype(bf),
    }
    nch = QG * 2
    kh = np.zeros((32, QG * 128), f32)
    qm = np.zeros((32, QG * 128), f32)
    for c in range(nch):
        kh[c, c * 64:(c + 1) * 64] = 1.0
        qm[c, :c * 64] = NEG
    common["khot"] = kh.astype(bf)
    common["qmask"] = qm.astype(bf)
    common["rope_sp"] = np.ascontiguousarray(
        _rope_tab(np.arange(max(PT, 1) * 128)).reshape(max(PT, 1), 128, 32).transpose(1, 0, 2))
    common["rope_sn"] = _rope_tab(PAST + (np.arange(128) % 64))

    in_maps, metas = [], []
    for core in range(8):
        b, j = divmod(core, NCPB)
        others = [r for r in range(NCPB) if r != j]
        order = [j] + others
        gt = np.array([t * G + order[r] * RT + i for t in range(NSLOT) for r in range(NCPB) for i in range(RT)])
        tok = (gt[:, None] * 128 + np.arange(128)[None, :]).reshape(-1)
        m = dict(common)
        m["x_all"] = np.ascontiguousarray(xp[b][tok])
        xh = np.zeros((NSLOT, 128, D), f32)
        hf = np.zeros((128, NSLOT), f32)
        rh = np.zeros((NSLOT, 128, 32), f32)
        for t in range(NSLOT):
            ht = t * G + j * RT - 1
            if ht >= 0:
                xh[t] = xp[b, ht * 128:(ht + 1) * 128]
                hf[:, t] = 1.0
                rh[t] = _rope_tab(ht * 128 + np.arange(128))
        m["x_halo"] = xh.reshape(NSLOT * 128, D)
        m["hflag"] = hf
        m["rope_h"] = np.ascontiguousarray(rh.transpose(1, 0, 2))
        m["rope_k"] = np.ascontiguousarray(_rope_tab(tok).reshape(NT, 128, 32).transpose(1, 0, 2))
        bt = np.zeros((128, NCPB), f32)
        for r in range(1, NCPB):
            bt[:, r] = 0.0 if others[r - 1] < j else NEG
        m["biast"] = bt
        e0 = 2 * core
        m["x_s"] = np.ascontiguousarray(xs[e0:e0 + 2].reshape(128, D))
        m["cckv"] = np.ascontiguousarray(g["cache_ckv"][0, e0:e0 + 2].reshape(2 * PAST, KVL))
        m["ckpe"] = np.ascontiguousarray(g["cache_kpe"][0, e0:e0 + 2].reshape(2 * PAST, RD))
        sc = g["state_conv"][0, e0:e0 + 2]
        m["sconvT"] = np.ascontiguousarray(sc.transpose(2, 0, 1).reshape(4, 128, 2, CK - 1).transpose(1, 2, 0, 3))
        sf = g["state_ffn_conv"][0, e0:e0 + 2]
        m["sffnT"] = np.ascontiguousarray(sf.transpose(2, 0, 1).reshape(NFC, 128, 2, 2).transpose(1, 2, 0, 3))
        in_maps.append(m)
        own_tok = (np.array([t * G + j * RT + i for t in range(NSLOT) for i in range(RT)])[:, None] * 128
                   + np.arange(128)[None, :]).reshape(-1)
        metas.append((b, j, own_tok))

    nc = build(cfg)
    res = run_bass_kernel_spmd(nc, in_maps, core_ids=list(range(8)))
    R = res.results

    y_p = np.zeros((B, SEQ, D), f32)
    ckv_p = np.zeros((1, B, SEQ, KVL), f32)
    kpe_p = np.zeros((1, B, SEQ, RD), f32)
    conv_p = np.zeros((1, B, CK - 1, CC), f32)
    ffn_p = np.zeros((1, B, 2, DFF), f32)
    y_s = np.zeros((16, 64, D), f32)
    ckv_s = np.zeros((1, 16, 64, KVL), f32)
    kpe_s = np.zeros((1, 16, 64, RD), f32)
    conv_s = np.zeros((1, 16, CK - 1, CC), f32)
    ffn_s = np.zeros((1, 16, 2, DFF), f32)
    for core in range(8):
        b, j, own_tok = metas[core]
        r = R[core]
        y_p[b, own_tok] = r["o_y"]
        ckv_p[0, b, own_tok] = r["o_ckv"]
        kpe_p[0, b, own_tok] = r["o_kpe"]
        if j == NCPB - 1:
            conv_p[0, b] = r["o_conv"][2:32]
            ffn_p[0, b] = r["o_ffn"][30:32]
        e0 = 2 * core
        y_s[e0:e0 + 2] = r["o_ys"].reshape(2, 64, D)
        ckv_s[0, e0:e0 + 2] = r["o_ckvs"].reshape(2, 64, KVL)
        kpe_s[0, e0:e0 + 2] = r["o_kpes"].reshape(2, 64, RD)
        conv_s[0, e0:e0 + 2] = r["o_convs"][:, 2:32]
        ffn_s[0, e0:e0 + 2] = r["o_ffns"][:, 30:32]
    return (y_p, y_s, ckv_p, kpe_p, conv_p, ffn_p, ckv_s, kpe_s, conv_s, ffn_s)


def kernel(**inputs):
    return _run(inputs, CFG_FULL)
```

```python
import math
from contextlib import ExitStack

import numpy as np
import ml_dtypes

import concourse.bass as bass
import concourse.mybir as mybir
from concourse.bass_utils import run_bass_kernel_spmd

F32 = mybir.dt.float32
BF16 = mybir.dt.bfloat16
ALU = mybir.AluOpType
AF = mybir.ActivationFunctionType
AX = mybir.AxisListType

D = 1024
QL, KVL, RD, CC = 384, 256, 32, 512
H, HD, NOPE, VD = 8, 96, 64, 64
INW = QL + KVL + RD + 2 * CC
DFF = 2816
NFC = DFF // 128
CK = 31
EPS = 1e-6
SCALE = HD ** -0.5
NEG = -30000.0
NCPB = 4
KC = 16

CFG_FULL = dict(SEQ=16384, PAST=2048, RT=8)


class T:
    __slots__ = ("name", "w", "r", "dsem", "dcnt", "excl")

    def __init__(self, name, excl=False):
        self.name = name
        self.excl = excl
        self.w = None
        self.r = {}
        self.dsem = None
        self.dcnt = 0


class Eng:
    ROT = 30000

    def __init__(self, trk, eng, name):
        self.trk, self.eng, self.name = trk, eng, name
        self.sem = trk.new_sem(name)
        self.cnt = 0
        self.seen = {}

    def _wait(self, sem, val):
        if self.seen.get(sem, 0) >= val:
            return
        self.eng.wait_ge(sem, val)
        self.seen[sem] = val
        self.trk.nwaits += 1

    def _deps(self, reads, writes):
        need = {}

        def add(p, same_ok):
            if p is None:
                return
            sem, val = p
            if sem is self.sem and same_ok and self.name == "pe":
                return
            if need.get(sem, 0) < val:
                need[sem] = val
        for t in reads:
            add(t.w, False)
        for t in writes:
            add(t.w, True)
            for sem, val in t.r.items():
                add((sem, val), True)
        for sem, val in need.items():
            self._wait(sem, val)

    def op(self, fn, reads=(), writes=()):
        ex = [t for t in reads if t.excl and t not in writes]
        if ex:
            reads = [t for t in reads if not t.excl or t in writes]
            writes = list(writes) + ex
        self._deps(reads, writes)
        if self.cnt >= self.ROT:
            self.sem = self.trk.new_sem(self.name)
            self.cnt = 0
        inst = fn(self.eng)
        self.cnt += 1
        inst.then_inc(self.sem, 1)
        self.trk.ninst += 1
        for t in reads:
            if t.r.get(self.sem, 0) < self.cnt:
                t.r[self.sem] = self.cnt
        for t in writes:
            t.w = (self.sem, self.cnt)
            t.r = {}
        return inst

    def dma(self, out, in_, reads=(), writes=()):
        self._deps(reads, writes)
        tw = writes[0]
        if tw.dsem is None:
            tw.dsem = self.trk.new_sem("d_" + tw.name)
            self.trk.dts.append(tw)
        inst = self.eng.dma_start(out=out, in_=in_)
        inst.then_inc(tw.dsem, 16)
        tw.dcnt += 16
        self.trk.ninst += 1
        for t in reads:
            if t.r.get(tw.dsem, 0) < tw.dcnt:
                t.r[tw.dsem] = tw.dcnt
        tw.w = (tw.dsem, tw.dcnt)
        tw.r = {}
        return inst

    def wait_for(self, t):
        if t.w is not None:
            self._wait(*t.w)


class Tracker:
    def __init__(self, nc, stack):
        self.nc, self.stack = nc, stack
        self.nsem = 0
        self.nwaits = 0
        self.ninst = 0
        self.dts = []
        self.pe = Eng(self, nc.tensor, "pe")
        self.act = Eng(self, nc.scalar, "act")
        self.dve = Eng(self, nc.vector, "dve")
        self.pool = Eng(self, nc.gpsimd, "pool")
        self.sp = Eng(self, nc.sync, "sp")
        self.engs = [self.pe, self.act, self.dve, self.pool, self.sp]

    def new_sem(self, name):
        self.nsem += 1
        return self.stack.enter_context(self.nc.semaphore(f"s{self.nsem}_{name}"))

    def barrier(self):
        pts = [(e.sem, e.cnt) for e in self.engs if e.cnt > 0]
        pts += [(t.dsem, t.dcnt) for t in self.dts if t.dcnt > 0]
        for e in self.engs:
            for sem, val in pts:
                e._wait(sem, val)


def build(cfg):
    SEQ, PAST, RT = cfg["SEQ"], cfg["PAST"], cfg["RT"]
    NT = SEQ // 128
    G = NCPB * RT
    NSLOT = NT // G
    assert NSLOT * G == NT
    QG = min(4, RT)
    assert RT % QG == 0
    NOWN = NSLOT * RT
    PT = PAST // 128
    NX1 = NSLOT * (RT + 1) + 1

    nc = bass.Bass("TRN2", target_bir_lowering=False)

    def din(name, shape, dt=F32):
        return nc.dram_tensor(name, list(shape), dt, kind="ExternalInput").ap()

    def dout(name, shape, dt=F32):
        return nc.dram_tensor(name, list(shape), dt, kind="ExternalOutput").ap()

    def dscr(name, shape, dt):
        return nc.dram_tensor(name, list(shape), dt, kind="Internal").ap()

    x_all = din("x_all", [NT * 128, D])
    x_halo = din("x_halo", [NSLOT * 128, D])
    x_s = din("x_s", [128, D])
    cckv = din("cckv", [2 * PAST, KVL])
    ckpe = din("ckpe", [2 * PAST, RD])
    sconvT = din("sconvT", [128, 2, 4, CK - 1])
    sffnT = din("sffnT", [128, 2, NFC, 2])
    rope_k = din("rope_k", [128, NT, 32])
    rope_h = din("rope_h", [128, NSLOT, 32])
    rope_sp = din("rope_sp", [128, max(PT, 1), 32])
    rope_sn = din("rope_sn", [128, 32])
    hflag = din("hflag", [128, NSLOT])
    biast = din("biast", [128, NCPB])
    w_in = din("w_in", [D, INW])
    w_uq = din("w_uq", [QL, H * HD])
    w_ukv = din("w_ukv", [KVL, H * 128])
    w_out = din("w_out", [D, D])
    w_up = din("w_up", [D, 2 * DFF])
    w_down = din("w_down", [DFF, D])
    g_attn = din("g_attn", [128, 8])
    g_q = din("g_q", [128, 3])
    g_ffn = din("g_ffn", [128, 8])
    g_kv = din("g_kv", [128, KVL])
    g_hq = din("g_hq", [128, HD])
    g_hk = din("g_hk", [128, HD])
    cw = din("cw", [128, 4, CK])
    cb = din("cb", [128, 4])
    cg = din("cg", [128, 4])
    fw = din("fw", [128, NFC, 3])
    fb = din("fb", [128, NFC])
    identb = din("identb", [128, 128], BF16)
    identf = din("identf", [128, 128])
    onesb = din("onesb", [128, 128], BF16)
    khot = din("khot", [32, QG * 128], BF16)
    qmask = din("qmask", [32, QG * 128], BF16)

    o_y = dout("o_y", [NOWN * 128, D])
    o_ckv = dout("o_ckv", [NOWN * 128, KVL])
    o_kpe = dout("o_kpe", [NOWN * 128, RD])
    o_conv = dout("o_conv", [32, CC])
    o_ffn = dout("o_ffn", [32, DFF])
    o_ys = dout("o_ys", [128, D])
    o_ckvs = dout("o_ckvs", [128, KVL])
    o_kpes = dout("o_kpes", [128, RD])
    o_convs = dout("o_convs", [2, 32, CC])
    o_ffns = dout("o_ffns", [2, 32, DFF])

    KT = dscr("KT", [H, HD, NT * 128], BF16)
    VV = dscr("VV", [H, 128, NT, 128], BF16)
    KTs = dscr("KTs", [2, H, HD, (PT + 1) * 128], BF16)
    VVs = dscr("VVs", [2, H, 128, PT + 1, 128], BF16)
    X1 = dscr("X1", [NX1 * 128, D], F32)
    NCI = NSLOT * (RT + 1)
    QT = dscr("QT", [HD, H, NCI * 128], BF16)
    UT = dscr("UT", [128, 4, 32 + NCI * 128], F32)

    class _Stop(Exception):
        pass

    def ckpt(name):
        if cfg.get("STOP") == name:
            tk.barrier()
            raise _Stop()
    try:
      with ExitStack() as top:
          tk = Tracker(nc, top)
          pe, act, dve, pool, sp = tk.pe, tk.act, tk.dve, tk.pool, tk.sp

          def sb(st, name, shape, dt):
              return st.enter_context(nc.sbuf_tensor(name, list(shape), dt))

          def ps(st, name, shape, dt):
              return st.enter_context(nc.psum_tensor(name, list(shape), dt))

          pSS = ps(top, "pSS", [128, 2048], F32)
          pS = [pSS[:, i * 512:(i + 1) * 512] for i in range(4)]
          TpS = [T(f"pS{i}", excl=True) for i in range(4)]
          TpSS = [T(f"pSS{i}", excl=True) for i in range(2)]
          pOO = ps(top, "pOO", [128, 1024], F32)
          pO = [pOO[:, i * 512:(i + 1) * 512] for i in range(2)]
          TpO = [T(f"pO{i}", excl=True) for i in range(2)]
          pM0 = ps(top, "pM0", [128, 512], F32)
          pM = [pM0[:, :], pO[0], pO[1]]
          TpM = [T("pM0", excl=True), TpO[0], TpO[1]]
          pT = ps(top, "pT", [128, 1024], BF16)
          TpT = T("pT", excl=True)

          cst = {}
          Tc = T("consts")

          def cload(name, src, shape, dt=F32):
              t = sb(top, "c_" + name, shape, dt)
              sp.dma(t[:], src, writes=[Tc])
              cst[name] = t
              return t
          c_idb = cload("idb", identb[:, :], [128, 128], BF16)
          c_idf = cload("idf", identf[:, :], [128, 128])
          c_ones = cload("ones", onesb[:, :], [128, 128], BF16)
          c_gkv = cload("gkv", g_kv[:, :], [128, KVL])
          c_ghq = cload("ghq", g_hq[:, :], [128, HD])
          c_ghk = cload("ghk", g_hk[:, :], [128, HD])
          c_cw = cload("cw", cw[:, :, :], [128, 4, CK])
          c_cb = cload("cb", cb[:, :], [128, 4])
          c_cg = cload("cg", cg[:, :], [128, 4])
          c_fw = cload("fw", fw[:, :, :], [128, NFC, 3])
          c_fb = cload("fb", fb[:, :], [128, NFC])
          c_hflag = cload("hflag", hflag[:, :], [128, NSLOT])
          c_bias = cload("bias", biast[:, :], [128, NCPB])
          c_gattn = cload("gattn", g_attn[:, :], [128, 8])
          c_gq = cload("gq", g_q[:, :], [128, 3])
          c_gffn = cload("gffn", g_ffn[:, :], [128, 8])
          c_ropeh = cload("ropeh", rope_h[:, :, :], [128, NSLOT, 32])
          c_ropesn = cload("ropesn", rope_sn[:, :], [128, 32])
          c_zero = sb(top, "c_zero", [128, 1], F32)
          dve.op(lambda e: e.memset(c_zero[:], 0.0), writes=[Tc])
          c_eps = sb(top, "c_eps", [128, 1], F32)
          dve.op(lambda e: e.memset(c_eps[:], EPS), writes=[Tc])

          WCH = 2048
          NSTG = 4

          def make_stg(st, tag):
              return dict(stg=[sb(st, f"stg_{tag}{i}", [128, WCH], F32) for i in range(NSTG)],
                          T=[T(f"stg_{tag}{i}") for i in range(NSTG)], n=0)

          def load_weight(ss, dst, src2d, nk, ncols, gain, kp=128, name="w"):
              CH = WCH
              stg, Tst = ss["stg"], ss["T"]
              for k in range(nk):
                  for c0 in range(0, ncols, CH):
                      cwid = min(CH, ncols - c0)
                      b = ss["n"] % NSTG
                      ss["n"] += 1
                      n = ss["n"]
                      Tw = T("wchunk")
                      sp.dma(stg[b][0:kp, 0:cwid], src2d[k * kp:(k + 1) * kp, c0:c0 + cwid], writes=[Tst[b]])
                      if n % 2:
                          if gain is not None:
                              dve.op(lambda e, b=b, k=k, c0=c0, cwid=cwid: e.tensor_scalar(
                                  out=dst[0:kp, k, c0:c0 + cwid], in0=stg[b][0:kp, 0:cwid],
                                  scalar1=gain[0:kp, k:k + 1], scalar2=None, op0=ALU.mult),
                                  reads=[Tst[b], Tc], writes=[Tw])
                          else:
                              dve.op(lambda e, b=b, k=k, c0=c0, cwid=cwid: e.tensor_copy(
                                  out=dst[0:kp, k, c0:c0 + cwid], in_=stg[b][0:kp, 0:cwid]),
                                  reads=[Tst[b]], writes=[Tw])
                      else:
                          if gain is not None:
                              act.op(lambda e, b=b, k=k, c0=c0, cwid=cwid: e.activation(
                                  out=dst[0:kp, k, c0:c0 + cwid], in_=stg[b][0:kp, 0:cwid], func=AF.Copy,
                                  scale=gain[0:kp, k:k + 1]), reads=[Tst[b], Tc], writes=[Tw])
                          else:
                              act.op(lambda e, b=b, k=k, c0=c0, cwid=cwid: e.activation(
                                  out=dst[0:kp, k, c0:c0 + cwid], in_=stg[b][0:kp, 0:cwid], func=AF.Copy),
                                  reads=[Tst[b]], writes=[Tw])

          Tw = T("weights")

          def rstd_from_msq(st_bufs, msq, n):
              ap, Tm = msq
              act.op(lambda e: e.activation(out=ap, in_=ap, func=AF.Sqrt, bias=c_eps[:, 0:1]), reads=[Tm, Tc], writes=[Tm])
              dve.op(lambda e: e.reciprocal(out=ap, in_=ap), reads=[Tm], writes=[Tm])

          class TileBufs:
              def __init__(self, st, tag, nx=2):
                  self.xt = [sb(st, f"xt{tag}{i}", [128, D], F32) for i in range(nx)]
                  self.Txt = [T(f"xt{tag}{i}") for i in range(nx)]
                  self.junk = sb(st, f"junk{tag}", [128, D], BF16)
                  self.Tjunk = T("junk" + tag)
                  self.st = sb(st, f"stat{tag}", [128, 8], F32)
                  self.Tst = [T(f"stat{tag}{i}") for i in range(8)]
                  self.xn = sb(st, f"xn{tag}", [128, D], BF16)
                  self.Txn = T("xn" + tag)
                  self.xnT = sb(st, f"xnT{tag}", [128, 8, 128], BF16)
                  self.TxnT = T("xnT" + tag)
                  self.n = 0

          def front_end(tb, src_rows, w_reads=()):
              b = tb.n % len(tb.xt)
              tb.n += 1
              xt, Txt = tb.xt[b], tb.Txt[b]
              sp.dma(xt[:], src_rows, reads=list(w_reads), writes=[Txt])
              ms, Tms = tb.st[:, 0:1], tb.Tst[0]
              act.op(lambda e: e.activation(out=tb.junk[:], in_=xt[:], func=AF.Square, scale=1.0 / math.sqrt(D),
                                            accum_out=ms), reads=[Txt], writes=[tb.Tjunk, Tms])
              rstd_from_msq(None, (ms, Tms), 1)
              dve.op(lambda e: e.tensor_scalar(out=tb.xn[:], in0=xt[:], scalar1=ms, scalar2=None, op0=ALU.mult),
                     reads=[Txt, Tms], writes=[tb.Txn])
              for k in range(8):
                  pe.op(lambda e, k=k: e.transpose(out=pT[:, k * 128:(k + 1) * 128], in_=tb.xn[:, k * 128:(k + 1) * 128],
                                                   identity=c_idb[:]), reads=[tb.Txn, Tc], writes=[TpT])
              act.op(lambda e: e.activation(out=tb.xnT[:].rearrange("p k t -> p (k t)"), in_=pT[:], func=AF.Copy),
                     reads=[TpT], writes=[tb.TxnT])
              return b

          class HeadBufs:
              def __init__(self, st, tag):
                  self.raw = sb(st, f"hraw{tag}", [128, H, HD], F32)
                  self.Traw = T("hraw" + tag)
                  self.sq = sb(st, f"hsq{tag}", [128, H, HD], F32)
                  self.Tsq = T("hsq" + tag)
                  self.rs = sb(st, f"hrs{tag}", [128, H], F32)
                  self.Trs = T("hrs" + tag)
                  self.t1, self.Tt1 = self.sq, self.Tsq
                  self.ra = sb(st, f"hra{tag}", [128, H, 16], F32)
                  self.rb = sb(st, f"hrb{tag}", [128, H, 16], F32)
                  self.Tra, self.Trb = T("hra" + tag), T("hrb" + tag)
                  self.fin = sb(st, f"hfin{tag}", [128, H, HD], BF16)
                  self.Tfin = T("hfin" + tag)

          def head_norm_rope(hb, gain, cs, Tcs):
              raw, sq, rs, t1, fin = hb.raw, hb.sq, hb.rs, hb.t1, hb.fin
              act.op(lambda e: e.activation(out=sq[:], in_=raw[:], func=AF.Square, scale=1.0 / math.sqrt(HD)),
                     reads=[hb.Traw], writes=[hb.Tsq])
              dve.op(lambda e: e.tensor_reduce(out=rs[:], in_=sq[:], axis=AX.X, op=ALU.add),
                     reads=[hb.Tsq], writes=[hb.Trs])
              rstd_from_msq(None, (rs[:], hb.Trs), H)
              dve.op(lambda e: e.tensor_tensor(out=t1[:], in0=raw[:], in1=rs[:].unsqueeze(2).to_broadcast([128, H, HD]),
                                               op=ALU.mult), reads=[hb.Traw, hb.Trs], writes=[hb.Tt1])
              pool.op(lambda e: e.tensor_tensor(out=t1[:], in0=t1[:], in1=gain[:].unsqueeze(1).to_broadcast([128, H, HD]),
                                                op=ALU.mult), reads=[hb.Tt1, Tc], writes=[hb.Tt1])
              cosb = cs[:, 0:16].unsqueeze(1).to_broadcast([128, H, 16])
              sinb = cs[:, 16:32].unsqueeze(1).to_broadcast([128, H, 16])
              p1, p2 = t1[:, :, 64:80], t1[:, :, 80:96]
              act.op(lambda e: e.activation(out=fin[:, :, 0:64], in_=t1[:, :, 0:64], func=AF.Copy),
                     reads=[hb.Tt1], writes=[hb.Tfin])
              dve.op(lambda e: e.tensor_tensor(out=hb.ra[:], in0=p1, in1=cosb, op=ALU.mult),
                     reads=[hb.Tt1, Tcs], writes=[hb.Tra])
              dve.op(lambda e: e.tensor_tensor(out=hb.rb[:], in0=p2, in1=sinb, op=ALU.mult),
                     reads=[hb.Tt1, Tcs], writes=[hb.Trb])
              dve.op(lambda e: e.tensor_tensor(out=fin[:, :, 64:80], in0=hb.ra[:], in1=hb.rb[:], op=ALU.subtract),
                     reads=[hb.Tra, hb.Trb], writes=[hb.Tfin])
              dve.op(lambda e: e.tensor_tensor(out=hb.ra[:], in0=p2, in1=cosb, op=ALU.mult),
                     reads=[hb.Tt1, Tcs], writes=[hb.Tra])
              dve.op(lambda e: e.tensor_tensor(out=hb.rb[:], in0=p1, in1=sinb, op=ALU.mult),
                     reads=[hb.Tt1, Tcs], writes=[hb.Trb])
              dve.op(lambda e: e.tensor_tensor(out=fin[:, :, 80:96], in0=hb.ra[:], in1=hb.rb[:], op=ALU.add),
                     reads=[hb.Tra, hb.Trb], writes=[hb.Tfin])

          NWAYS = 1
          NWAYS_P1 = 4

          def run_ways(tasks, make_gen, nways=None):
              nways = len(ways) if nways is None else nways
              it = iter(tasks)
              free = list(range(nways))
              active = []
              more = True
              while True:
                  while free and more:
                      try:
                          tsk = next(it)
                      except StopIteration:
                          more = False
                          break
                      w = free.pop(0)
                      active.append((make_gen(tsk, ways[w]), w))
                  if not active:
                      break
                  for gw in list(active):
                      try:
                          next(gw[0])
                      except StopIteration:
                          active.remove(gw)
                          free.append(gw[1])

          def rstd_g(ap, Tm):
              act.op(lambda e: e.activation(out=ap, in_=ap, func=AF.Sqrt, bias=c_eps[:, 0:1]), reads=[Tm, Tc], writes=[Tm])
              yield
              dve.op(lambda e: e.reciprocal(out=ap, in_=ap), reads=[Tm], writes=[Tm])
              yield

          pS2b = pS[2].bitcast(BF16)
          Tbanks = [(pT[:, :], TpT), (pS2b, TpS[2])]
          Pbanks = [(pO[1], TpO[1]), (pO[0], TpO[0])]
          Kbanks = [((pM[0], pM[1]), (TpM[0], TpM[1])), ((pS[0], pS[1]), (TpS[0], TpS[1]))]

          class Way:
              def __init__(self, st, w):
                  tag = f"W{w}"
                  self.w = w
                  self.xt = sb(st, "xt" + tag, [128, D], F32)
                  self.Txt = T("xt" + tag)
                  self.st = sb(st, "stat" + tag, [128, 8], F32)
                  self.Tst = [T(f"stat{tag}{i}") for i in range(8)]
                  self.xn = sb(st, "xn" + tag, [128, D], BF16)
                  self.Txn = T("xn" + tag)
                  self.junk, self.Tjunk = self.xn, self.Txn
                  self.xnT = sb(st, "xnT" + tag, [128, 8, 128], BF16)
                  self.TxnT = T("xnT" + tag)
                  self.hb = HeadBufs(st, tag)
                  self.rk = sb(st, "rk" + tag, [128, 32], F32)
                  self.Trk = T("rk" + tag)
                  self.cqb = sb(st, "cqb" + tag, [128, QL], BF16)
                  self.Tcqb = T("cqb" + tag)
                  self.cqf = self.hb.sq[:].rearrange("p h d -> p (h d)")[:, 0:QL]
                  self.Tcqf = self.hb.Tsq
                  self.cqT = sb(st, "cqT" + tag, [128, 3, 128], BF16)
                  self.TcqT = T("cqT" + tag)
                  self.sig = sb(st, "sig" + tag, [128, 4, 128], F32)
                  self.Tsig = T("sig" + tag)
                  self.pT, self.TpT = Tbanks[w % 2]
                  self.pP, self.TpP = Pbanks[w % 2]
                  self.pK, self.TpK = Kbanks[w % 2]

              def alloc_p1(self, st):
                  tag = f"W{self.w}"
                  self.ckv = sb(st, "ckv" + tag, [128, KVL], F32)
                  self.Tckv = T("ckv" + tag)
                  self.kpe = sb(st, "kpe" + tag, [128, RD], F32)
                  self.Tkpe = T("kpe" + tag)
                  self.ckvb = sb(st, "ckvb" + tag, [128, KVL], BF16)
                  self.Tckvb = T("ckvb" + tag)
                  self.ckvT = sb(st, "ckvT" + tag, [128, 2, 128], BF16)
                  self.TckvT = T("ckvT" + tag)
                  self.qst = sb(st, "qst" + tag, [HD, H, 128], BF16)
                  self.Tqst = T("qst" + tag)
                  self.ust, self.Tust = self.sig, self.Tsig
                  self.prj = sb(st, "prj" + tag, [128, KVL + RD], F32)
                  self.Tprj = T("prj" + tag)
                  self.cin = sb(st, "cin" + tag, [128, KVL], F32)
                  self.Tcin = T("cin" + tag)
                  self.kin = sb(st, "kin" + tag, [128, RD], F32)
                  self.Tkin = T("kin" + tag)

          def front_end_g(W, src_rows):
              sp.dma(W.xt[:], src_rows, writes=[W.Txt])
              ms, Tms = W.st[:, 0:1], W.Tst[0]
              act.op(lambda e: e.activation(out=W.junk[:], in_=W.xt[:], func=AF.Square, scale=1.0 / math.sqrt(D),
                                            accum_out=ms), reads=[W.Txt], writes=[W.Tjunk, Tms])
              yield
              yield from rstd_g(ms, Tms)
              dve.op(lambda e: e.tensor_scalar(out=W.xn[:], in0=W.xt[:], scalar1=ms, scalar2=None, op0=ALU.mult),
                     reads=[W.Txt, Tms], writes=[W.Txn])
              yield
              for k in range(8):
                  pe.op(lambda e, k=k: e.transpose(out=W.pT[:, k * 128:(k + 1) * 128], in_=W.xn[:, k * 128:(k + 1) * 128],
                                                   identity=c_idb[:]), reads=[W.Txn, Tc], writes=[W.TpT])
              act.op(lambda e: e.activation(out=W.xnT[:].rearrange("p k t -> p (k t)"), in_=W.pT, func=AF.Copy),
                     reads=[W.TpT], writes=[W.TxnT])
              yield

          def head_norm_rope_g(hb, gain, cs, Tcs):
              raw, sq, rs, t1, fin = hb.raw, hb.sq, hb.rs, hb.t1, hb.fin
              act.op(lambda e: e.activation(out=sq[:], in_=raw[:], func=AF.Square, scale=1.0 / math.sqrt(HD)),
                     reads=[hb.Traw], writes=[hb.Tsq])
              yield
              dve.op(lambda e: e.tensor_reduce(out=rs[:], in_=sq[:], axis=AX.X, op=ALU.add),
                     reads=[hb.Tsq], writes=[hb.Trs])
              yield
              yield from rstd_g(rs[:], hb.Trs)
              dve.op(lambda e: e.tensor_tensor(out=t1[:], in0=raw[:], in1=rs[:].unsqueeze(2).to_broadcast([128, H, HD]),
                                               op=ALU.mult), reads=[hb.Traw, hb.Trs], writes=[hb.Tt1])
              yield
              pool.op(lambda e: e.tensor_tensor(out=t1[:], in0=t1[:], in1=gain[:].unsqueeze(1).to_broadcast([128, H, HD]),
                                                op=ALU.mult), reads=[hb.Tt1, Tc], writes=[hb.Tt1])
              yield
              cosb = cs[:, 0:16].unsqueeze(1).to_broadcast([128, H, 16])
              sinb = cs[:, 16:32].unsqueeze(1).to_broadcast([128, H, 16])
              p1, p2 = t1[:, :, 64:80], t1[:, :, 80:96]
              act.op(lambda e: e.activation(out=fin[:, :, 0:64], in_=t1[:, :, 0:64], func=AF.Copy),
                     reads=[hb.Tt1], writes=[hb.Tfin])
              dve.op(lambda e: e.tensor_tensor(out=hb.ra[:], in0=p1, in1=cosb, op=ALU.mult),
                     reads=[hb.Tt1, Tcs], writes=[hb.Tra])
              pool.op(lambda e: e.tensor_tensor(out=hb.rb[:], in0=p2, in1=sinb, op=ALU.mult),
                      reads=[hb.Tt1, Tcs], writes=[hb.Trb])
              yield
              dve.op(lambda e: e.tensor_tensor(out=fin[:, :, 64:80], in0=hb.ra[:], in1=hb.rb[:], op=ALU.subtract),
                     reads=[hb.Tra, hb.Trb], writes=[hb.Tfin])
              yield
              dve.op(lambda e: e.tensor_tensor(out=hb.ra[:], in0=p2, in1=cosb, op=ALU.mult),
                     reads=[hb.Tt1, Tcs], writes=[hb.Tra])
              pool.op(lambda e: e.tensor_tensor(out=hb.rb[:], in0=p1, in1=sinb, op=ALU.mult),
                      reads=[hb.Tt1, Tcs], writes=[hb.Trb])
              yield
              dve.op(lambda e: e.tensor_tensor(out=fin[:, :, 80:96], in0=hb.ra[:], in1=hb.rb[:], op=ALU.add),
                     reads=[hb.Tra, hb.Trb], writes=[hb.Tfin])
              yield

          with ExitStack() as stA:
              wA_in = sb(stA, "wA_in", [128, 8, INW], BF16)
              wA_uq = sb(stA, "wA_uq", [128, 3, H * HD], BF16)
              wA_ukv = sb(stA, "wA_ukv", [128, 2, H * 128], BF16)
              wA_oa = sb(stA, "wA_oa", [64, 8, D], BF16)
              wA_oc = sb(stA, "wA_oc", [128, 4, D], BF16)
              with ExitStack() as stW:
                  ssA = make_stg(stW, "A")
                  load_weight(ssA, wA_in, w_in, 8, INW, c_gattn, name="in")
                  load_weight(ssA, wA_uq, w_uq, 3, H * HD, c_gq, name="uq")
                  load_weight(ssA, wA_ukv, w_ukv, 2, H * 128, None, name="ukv")
                  load_weight(ssA, wA_oa, w_out, 8, D, None, kp=64, name="oa")
                  load_weight(ssA, wA_oc, w_out[512:1024, :], 4, D, None, name="oc")
                  tk.barrier()
              ckpt("w")

              def tile_front_A_g(W, src_rows, cs, Tcs, qcol, ntok_groups):
                  yield from front_end_g(W, src_rows)
                  yield from qglu_g(W, cs, Tcs, qTa[0:HD, :, qcol:qcol + 128], TqTa, ntok_groups)

              def qglu_g(W, cs, Tcs, qdst, Tqdst, ntok_groups):
                  hb = W.hb
                  for k in range(8):
                      pe.op(lambda e, k=k: e.matmul(W.pP[:, 0:QL], lhsT=W.xnT[:, k, :], rhs=wA_in[:, k, 0:QL],
                                                    start=(k == 0), stop=(k == 7)), reads=[W.TxnT, Tw], writes=[W.TpP])
                  dve.op(lambda e: e.tensor_copy(out=W.cqf, in_=W.pP[:, 0:QL]), reads=[W.TpP], writes=[W.Tcqf])
                  yield
                  ms, Tms = W.st[:, 2:3], W.Tst[2]
                  act.op(lambda e: e.activation(out=W.junk[:, 0:QL], in_=W.cqf, func=AF.Square,
                                                scale=1.0 / math.sqrt(QL), accum_out=ms),
                         reads=[W.Tcqf], writes=[W.Tjunk, Tms])
                  yield
                  yield from rstd_g(ms, Tms)
                  dve.op(lambda e: e.tensor_scalar(out=W.cqb[:], in0=W.cqf, scalar1=ms, scalar2=None, op0=ALU.mult),
                         reads=[W.Tcqf, Tms], writes=[W.Tcqb])
                  yield
                  for k in range(3):
                      pe.op(lambda e, k=k: e.transpose(out=W.pT[:, k * 128:(k + 1) * 128], in_=W.cqb[:, k * 128:(k + 1) * 128],
                                                       identity=c_idb[:]), reads=[W.Tcqb, Tc], writes=[W.TpT])
                  dve.op(lambda e: e.tensor_copy(out=W.cqT[:].rearrange("p k t -> p (k t)"), in_=W.pT[:, 0:384]),
                         reads=[W.TpT], writes=[W.TcqT])
                  yield
                  for nb, (c0, cw_) in enumerate(((0, 512), (512, 256))):
                      for k in range(3):
                          pe.op(lambda e, nb=nb, k=k, c0=c0, cw_=cw_: e.matmul(
                              W.pK[nb][:, 0:cw_], lhsT=W.cqT[:, k, :], rhs=wA_uq[:, k, c0:c0 + cw_],
                              start=(k == 0), stop=(k == 2)), reads=[W.TcqT, Tw], writes=[W.TpK[nb]])
                  rawf = hb.raw[:].rearrange("p h d -> p (h d)")
                  act.op(lambda e: e.activation(out=rawf[:, 0:512], in_=W.pK[0][:, 0:512], func=AF.Copy),
                         reads=[W.TpK[0]], writes=[hb.Traw])
                  dve.op(lambda e: e.tensor_copy(out=rawf[:, 512:768], in_=W.pK[1][:, 0:256]),
                         reads=[W.TpK[1]], writes=[hb.Traw])
                  yield
                  yield from head_norm_rope_g(hb, c_ghq, cs, Tcs)
                  for h in range(H):
                      pe.op(lambda e, h=h: e.transpose(out=W.pT[0:HD, h * 128:(h + 1) * 128], in_=hb.fin[:, h, :],
                                                       identity=c_idb[:]), reads=[hb.Tfin, Tc], writes=[W.TpT])
                  act.op(lambda e: e.activation(out=qdst,
                                                in_=W.pT[0:HD, :].rearrange("p (h t) -> p h t", h=H), func=AF.Copy),
                         reads=[W.TpT], writes=[Tqdst])
                  yield
                  for half in (1, 0):
                      for c in range(4):
                          col = QL + KVL + RD + half * CC + c * 128
                          for k in range(8):
                              pe.op(lambda e, half=half, c=c, k=k, col=col: e.matmul(
                                  W.pK[half][:, c * 128:(c + 1) * 128], lhsT=wA_in[:, k, col:col + 128], rhs=W.xnT[:, k, :],
                                  start=(k == 0), stop=(k == 7)), reads=[W.TxnT, Tw], writes=[W.TpK[half]])
                      if half == 1:
                          act.op(lambda e: e.activation(out=W.sig[:].rearrange("p c t -> p (c t)"), in_=W.pK[1][:, :],
                                                        func=AF.Sigmoid), reads=[W.TpK[1]], writes=[W.Tsig])
                  for (tok0, ntok, ucol_) in ntok_groups:
                      tgt = ucol_ if isinstance(ucol_, tuple) else (uT, TuT, ucol_)
                      ub, Tub, uc = tgt
                      dve.op(lambda e, tok0=tok0, ntok=ntok, ub=ub, uc=uc: e.tensor_tensor(
                          out=ub[:, :, uc:uc + ntok],
                          in0=W.pK[0][:, :].rearrange("p (c t) -> p c t", c=4)[:, :, tok0:tok0 + ntok],
                          in1=W.sig[:, :, tok0:tok0 + ntok], op=ALU.mult), reads=[W.TpK[0], W.Tsig], writes=[Tub])
                  yield

              ways = [Way(stA, w) for w in range(NWAYS)]
              TKT, TVV = T("KT"), T("VV")
              TKTs, TVVs = T("KTs"), T("VVs")
              TX1 = T("X1")
              Tout = T("outs")
              with ExitStack() as stP1:
                  for w in range(NWAYS, NWAYS_P1):
                      ways.append(Way(stP1, w))
                  for W_ in ways:
                      W_.alloc_p1(stP1)
                  KS = 4
                  kst = [sb(stP1, f"kst{i}", [HD, H, KS * 128], BF16) for i in range(2)]
                  Tkst = [T(f"kst{i}") for i in range(2)]
                  vst = [sb(stP1, f"vst{i}", [128, H, KS, 128], BF16) for i in range(2)]
                  Tvst = [T(f"vst{i}") for i in range(2)]
                  for i in range(2):
                      pool.op(lambda e, i=i: e.memset(vst[i][:], 1.0), writes=[Tvst[i]])

                  def kv_from_ckv_g(W, ckv_ap, Tck, kpe_ap, Tkp, cs, Tcs, stage_slot, have_bf16=False):
                      sbuf, slot = stage_slot
                      hb = W.hb
                      if not have_bf16:
                          act.op(lambda e: e.activation(out=W.ckvb[:], in_=ckv_ap, func=AF.Copy), reads=[Tck], writes=[W.Tckvb])
                          yield
                      for k in range(2):
                          pe.op(lambda e, k=k: e.transpose(out=W.pT[:, k * 128:(k + 1) * 128],
                                                           in_=W.ckvb[:, k * 128:(k + 1) * 128], identity=c_idb[:]),
                                reads=[W.Tckvb, Tc], writes=[W.TpT])
                      dve.op(lambda e: e.tensor_copy(out=W.ckvT[:].rearrange("p k t -> p (k t)"), in_=W.pT[:, 0:256]),
                             reads=[W.TpT], writes=[W.TckvT])
                      yield
                      for nb in range(2):
                          for k in range(2):
                              pe.op(lambda e, nb=nb, k=k: e.matmul(W.pK[nb][:, :], lhsT=W.ckvT[:, k, :],
                                                                   rhs=wA_ukv[:, k, nb * 512:(nb + 1) * 512],
                                                                   start=(k == 0), stop=(k == 1)),
                                    reads=[W.TckvT, Tw], writes=[W.TpK[nb]])
                      for nb in range(2):
                          src = W.pK[nb][:, :].rearrange("p (h c) -> p h c", h=4)
                          act.op(lambda e, nb=nb, src=src: e.activation(out=hb.raw[:, nb * 4:(nb + 1) * 4, 0:64],
                                                                        in_=src[:, :, 0:64], func=AF.Copy),
                                 reads=[W.TpK[nb]], writes=[hb.Traw])
                          dve.op(lambda e, nb=nb, src=src: e.tensor_copy(out=vst[sbuf][:, nb * 4:(nb + 1) * 4, slot, 0:64],
                                                                         in_=src[:, :, 64:128]),
                                 reads=[W.TpK[nb]], writes=[Tvst[sbuf]])
                      yield
                      pool.op(lambda e: e.tensor_copy(out=hb.raw[:, :, 64:96],
                                                      in_=kpe_ap.unsqueeze(1).to_broadcast([128, H, RD])),
                              reads=[Tkp], writes=[hb.Traw])
                      yield
                      yield from head_norm_rope_g(hb, c_ghk, cs, Tcs)
                      for h in range(H):
                          pe.op(lambda e, h=h: e.transpose(out=W.pT[0:HD, h * 128:(h + 1) * 128], in_=hb.fin[:, h, :],
                                                           identity=c_idb[:]), reads=[hb.Tfin, Tc], writes=[W.TpT])
                      act.op(lambda e: e.activation(out=kst[sbuf][:, :, slot * 128:(slot + 1) * 128],
                                                    in_=W.pT[0:HD, :].rearrange("p (h t) -> p h t", h=H), func=AF.Copy),
                             reads=[W.TpT], writes=[Tkst[sbuf]])
                      yield

                  def ckv_from_x_g(W, own_row=None, o_ck=None, o_kp=None):
                      for k in range(8):
                          pe.op(lambda e, k=k: e.matmul(W.pP[:, 0:KVL + RD], lhsT=W.xnT[:, k, :],
                                                        rhs=wA_in[:, k, QL:QL + KVL + RD], start=(k == 0), stop=(k == 7)),
                                reads=[W.TxnT, Tw], writes=[W.TpP])
                      dve.op(lambda e: e.tensor_copy(out=W.prj[:], in_=W.pP[:, 0:KVL + RD]), reads=[W.TpP], writes=[W.Tprj])
                      yield
                      ms, Tms = W.st[:, 1:2], W.Tst[1]
                      act.op(lambda e: e.activation(out=W.junk[:, 0:KVL], in_=W.prj[:, 0:KVL], func=AF.Square,
                                                    scale=1.0 / math.sqrt(KVL), accum_out=ms),
                             reads=[W.Tprj], writes=[W.Tjunk, Tms])
                      pool.op(lambda e: e.tensor_copy(out=W.kpe[:], in_=W.prj[:, KVL:KVL + RD]),
                              reads=[W.Tprj], writes=[W.Tkpe])
                      yield
                      yield from rstd_g(ms, Tms)
                      if own_row is None:
                          dve.op(lambda e: e.scalar_tensor_tensor(out=W.ckvb[:], in0=W.prj[:, 0:KVL], scalar=ms, in1=c_gkv[:],
                                                                  op0=ALU.mult, op1=ALU.mult),
                                 reads=[W.Tprj, Tms, Tc], writes=[W.Tckvb])
                          yield
                          return
                      dve.op(lambda e: e.scalar_tensor_tensor(out=W.ckv[:], in0=W.prj[:, 0:KVL], scalar=ms, in1=c_gkv[:],
                                                              op0=ALU.mult, op1=ALU.mult),
                             reads=[W.Tprj, Tms, Tc], writes=[W.Tckv])
                      yield
                      if own_row is not None:
                          sp.dma(o_ck[own_row:own_row + 128, :], W.ckv[:], reads=[W.Tckv], writes=[Tout])
                          sp.dma(o_kp[own_row:own_row + 128, :], W.kpe[:], reads=[W.Tkpe], writes=[Tout])

                  gdone = {}

                  TQT, TUT = T("QT"), T("UT")
                  zpad = sb(stP1, "zpad", [128, 4, 32], F32)
                  Tzpad = T("zpad")
                  dve.op(lambda e: e.memset(zpad[:], 0.0), writes=[Tzpad])
                  sp.dma(UT[:, :, 0:32], zpad[:], reads=[Tzpad], writes=[TUT])

                  def qu_to_scratch_g(W, cs, Tcs, ci):
                      yield from qglu_g(W, cs, Tcs, W.qst[:], W.Tqst, [(0, 128, (W.ust, W.Tust, 0))])
                      sp.dma(QT[:, :, ci * 128:(ci + 1) * 128], W.qst[:], reads=[W.Tqst], writes=[TQT])
                      sp.dma(UT[:, :, 32 + ci * 128:32 + (ci + 1) * 128], W.ust[:], reads=[W.Tust], writes=[TUT])

                  def p1_tile_g(lt, W):
                      if isinstance(lt, tuple):
                          t = lt[1]
                          yield from front_end_g(W, x_halo[t * 128:(t + 1) * 128, :])
                          yield from qu_to_scratch_g(W, c_ropeh[:, t, :], Tc, t * (RT + 1))
                          return
                      gi = lt // KS
                      sbuf, slot = gi % 2, lt % KS
                      sp.dma(W.rk[:], rope_k[:, lt, :], writes=[W.Trk])
                      yield from front_end_g(W, x_all[lt * 128:(lt + 1) * 128, :])
                      t, rem = divmod(lt, G)
                      own = rem < RT
                      yield from ckv_from_x_g(W, own_row=(t * RT + rem) * 128 if own else None, o_ck=o_ckv, o_kp=o_kpe)
                      yield from kv_from_ckv_g(W, W.ckv[:], W.Tckv, W.kpe[:], W.Tkpe, W.rk, W.Trk, (sbuf, slot),
                                               have_bf16=not own)
                      if own:
                          yield from qu_to_scratch_g(W, W.rk, W.Trk, t * (RT + 1) + 1 + rem)
                      gdone[gi] = gdone.get(gi, 0) + 1
                      if gdone[gi] == KS:
                          lt0 = gi * KS
                          sp.dma(KT[:, :, lt0 * 128:(lt0 + KS) * 128].rearrange("h d t -> d h t"), kst[sbuf][:],
                                 reads=[Tkst[sbuf]], writes=[TKT])
                          sp.dma(VV[:, :, lt0:lt0 + KS, :].rearrange("h p s c -> p h s c"), vst[sbuf][:],
                                 reads=[Tvst[sbuf]], writes=[TVV])
                  run_ways(list(range(NT)) + [("h", t) for t in range(NSLOT)], p1_tile_g)

                  ckpt("p1")
                  NG0 = NT // KS
                  PG = (PT + KS - 1) // KS
                  sdone = {}

                  def p1s_tile_g(ep, W):
                      e_, p = ep
                      gi = NG0 + e_ * PG + p // KS
                      sbuf, slot = gi % 2, p % KS
                      r0 = e_ * PAST + p * 128
                      sp.dma(W.cin[:], cckv[r0:r0 + 128, :], writes=[W.Tcin])
                      sp.dma(W.kin[:], ckpe[r0:r0 + 128, :], writes=[W.Tkin])
                      sp.dma(W.rk[:], rope_sp[:, p, :], writes=[W.Trk])
                      yield from kv_from_ckv_g(W, W.cin[:], W.Tcin, W.kin[:], W.Tkin, W.rk, W.Trk, (sbuf, slot))
                      sdone[gi] = sdone.get(gi, 0) + 1
                      p0 = (p // KS) * KS
                      ns = min(KS, PT - p0)
                      if sdone[gi] == ns:
                          sp.dma(KTs[e_, :, :, p0 * 128:(p0 + ns) * 128].rearrange("h d t -> d h t"),
                                 kst[sbuf][:, :, 0:ns * 128], reads=[Tkst[sbuf]], writes=[TKTs])
                          sp.dma(VVs[e_, :, :, p0:p0 + ns, :].rearrange("h p s c -> p h s c"),
                                 vst[sbuf][:, :, 0:ns, :], reads=[Tvst[sbuf]], writes=[TVVs])
                  run_ways([(e_, p) for e_ in range(2) for p in range(PT)], p1s_tile_g)
                  sbuf = (NG0 + 2 * PG) % 2

                  def p1n_g(_, W):
                      yield from front_end_g(W, x_s[:, :])
                      yield from ckv_from_x_g(W, own_row=0, o_ck=o_ckvs, o_kp=o_kpes)
                      yield from kv_from_ckv_g(W, W.ckv[:], W.Tckv, W.kpe[:], W.Tkpe, c_ropesn, Tc, (sbuf, 0))
                  run_ways([0], p1n_g)
                  for e_ in range(2):
                      sp.dma(KTs[e_, :, :, PT * 128:PT * 128 + 64].rearrange("h d t -> d h t"),
                             kst[sbuf][:, :, e_ * 64:(e_ + 1) * 64], reads=[Tkst[sbuf]], writes=[TKTs])
                      sp.dma(VVs[e_, :, 0:64, PT:PT + 1, :].rearrange("h p s c -> p h s c"),
                             vst[sbuf][e_ * 64:(e_ + 1) * 64, :, 0:1, :], reads=[Tvst[sbuf]], writes=[TVVs])

                  tk.barrier()
                  del ways[NWAYS:]
              ckpt("p1s")
              qTas = [sb(stA, f"qTa{i}", [128, H, QG * 128], BF16) for i in range(2)]
              TqTas = [T(f"qTa{i}") for i in range(2)]
              for i in range(2):
                  for h in range(H):
                      sp.dma(qTas[i][96:128, h, :], qmask[:, :], writes=[TqTas[i]])
              qTa, TqTa = qTas[0], TqTas[0]
              cTgs = [sb(stA, f"cTg{i}", [128, 4, QG * 128], BF16) for i in range(2)]
              TcTgs = [T(f"cTg{i}") for i in range(2)]
              cTg, TcTg = cTgs[0], TcTgs[0]
              attTs = [sb(stA, f"attT{i}", [64, H, QG * 128], BF16) for i in range(2)]
              TattTs = [T(f"attT{i}") for i in range(2)]
              attT, TattT = attTs[0], TattTs[0]
              for i in range(2):
                  pool.op(lambda e, i=i: e.memset(attTs[i][:], 0.0), writes=[TattTs[i]])
              UW = CK - 1 + QG * 128
              uTs = [sb(stA, f"uT{i}", [128, 4, UW], F32) for i in range(2)]
              TuTs = [T(f"uT{i}") for i in range(2)]
              uT, TuT = uTs[0], TuTs[0]
              acc = sb(stA, "acc", [128, 4, QG * 128], F32)
              Tacc = T("acc")
              sqc = sb(stA, "sqc", [128, 4, QG * 128], BF16)
              Tsqc = T("sqc")
              rsc = sb(stA, "rsc", [128, QG * 128], F32)
              Trsc = T("rsc")
              cpre, Tcpre = acc, Tacc
              NKB = 3
              kb = [sb(stA, f"kb{i}", [HD, KC * 128], BF16) for i in range(NKB)]
              Tkb = [T(f"kb{i}") for i in range(NKB)]
              vb = [sb(stA, f"vb{i}", [128, KC, 128], BF16) for i in range(NKB)]
              Tvb = [T(f"vb{i}") for i in range(NKB)]
              kd = [sb(stA, f"kd{i}", [128, QG * 128], BF16) for i in range(2)]
              Tkd = [T(f"kd{i}") for i in range(2)]
              for i in range(2):
                  sp.dma(kd[i][96:128, :], khot[:, :], writes=[Tkd[i]])
              vd = [sb(stA, f"vd{i}", [128, QG, 128], BF16) for i in range(2)]
              Tvd = [T(f"vd{i}") for i in range(2)]
              pb = [sb(stA, f"pb{i}", [128, 1024], BF16) for i in range(2)]
              Tpb = [T(f"pb{i}") for i in range(2)]
              rcp = sb(stA, "rcp", [64, 512], F32)
              Trcp = T("rcp")
              rcs = sb(stA, "rcs", [128, 512], F32)
              Trcs = T("rcs")
              xr = [sb(stA, f"xr{i}", [128, D], F32) for i in range(2)]
              Txr = [T(f"xr{i}") for i in range(2)]
              x1o, Tx1o = xr, Txr
              cvo = sb(stA, "cvo", [32, CC], F32)
              Tcvo = T("cvo")
              cnt = dict(kb=0, pb=0, x1=0, po=0, kd=0)

              def conv_module_g(uT_, TuT_, cT_, TcT_, ucol, ncol, ccol, bank=0):
                  PB_, TPB_ = pM[bank], TpM[bank]
                  for c in range(4):
                      dve.op(lambda e, c=c: e.tensor_scalar(out=acc[:, c, 0:ncol], in0=uT_[:, c, ucol - 30:ucol - 30 + ncol],
                                                            scalar1=c_cw[:, c, 0:1], scalar2=c_cb[:, c:c + 1],
                                                            op0=ALU.mult, op1=ALU.add),
                             reads=[TuT_, Tc], writes=[Tacc])
                      yield
                      for k in range(1, CK):
                          dve.op(lambda e, c=c, k=k: e.scalar_tensor_tensor(
                              out=acc[:, c, 0:ncol], in0=uT_[:, c, ucol - 30 + k:ucol - 30 + k + ncol],
                              scalar=c_cw[:, c, k:k + 1], in1=acc[:, c, 0:ncol], op0=ALU.mult, op1=ALU.add),
                              reads=[TuT_, Tc, Tacc], writes=[Tacc])
                          yield
                  act.op(lambda e: e.activation(out=sqc[:, :, 0:ncol], in_=acc[:, :, 0:ncol], func=AF.Square,
                                                scale=1.0 / math.sqrt(CC)), reads=[Tacc], writes=[Tsqc])
                  yield
                  for c in range(4):
                      pe.op(lambda e, c=c: e.matmul(PB_[:, 0:ncol], lhsT=c_ones[:], rhs=sqc[:, c, 0:ncol],
                                                    start=(c == 0), stop=(c == 3)), reads=[Tsqc, Tc], writes=[TPB_])
                  dve.op(lambda e: e.tensor_scalar(out=rsc[:, 0:ncol], in0=PB_[:, 0:ncol], scalar1=EPS, scalar2=None,
                                                   op0=ALU.add), reads=[TPB_], writes=[Trsc])
                  yield
                  act.op(lambda e: e.activation(out=rsc[:, 0:ncol], in_=rsc[:, 0:ncol], func=AF.Sqrt),
                         reads=[Trsc], writes=[Trsc])
                  yield
                  dve.op(lambda e: e.reciprocal(out=rsc[:, 0:ncol], in_=rsc[:, 0:ncol]), reads=[Trsc], writes=[Trsc])
                  yield
                  for c in range(4):
                      dve.op(lambda e, c=c: e.scalar_tensor_tensor(out=cpre[:, c, 0:ncol], in0=acc[:, c, 0:ncol],
                                                                   scalar=c_cg[:, c:c + 1], in1=rsc[:, 0:ncol],
                                                                   op0=ALU.mult, op1=ALU.mult),
                             reads=[Tacc, Trsc, Tc], writes=[Tcpre])
                      yield
                  act.op(lambda e: e.activation(out=cT_[:, :, ccol:ccol + ncol], in_=cpre[:, :, 0:ncol], func=AF.Silu),
                         reads=[Tcpre], writes=[TcT_])
                  yield

              def conv_module(ucol, ncol, ccol):
                  for _ in conv_module_g(uT, TuT, cTg, TcTg, ucol, ncol, ccol, bank=2):
                      pass

              def attention(ncols, segs, kt_src, vv_src, Tk, Tv, qc0=0, ksz_last=128, side=None):
                  items = []
                  for h in range(H):
                      po_i = cnt["po"] % 2
                      cnt["po"] += 1
                      PO, TPO = pO[po_i], TpO[po_i]
                      first = True
                      nseg = len(segs)
                      for si, (kind, t0, ntl, bcol, ksz) in enumerate(segs):
                          last_seg = si == nseg - 1
                          if kind == "d":
                              di = cnt["kd"] % 2
                              cnt["kd"] += 1
                              KD, TKD, VD, TVD = kd[di], Tkd[di], vd[di], Tvd[di]
                              loads = [(KD[0:HD, 0:ntl * 128], kt_src(h, t0, ntl), Tk, TKD),
                                       (VD[:, 0:ntl, :], vv_src(h, t0, ntl), Tv, TVD)]
                              chunks = [(t0, ntl, KD, TKD, VD, TVD, 128, loads)]
                          else:
                              chunks = []
                              for c0 in range(t0, t0 + ntl, KC):
                                  cn = min(KC, t0 + ntl - c0)
                                  bi = cnt["kb"] % NKB
                                  cnt["kb"] += 1
                                  KB, TKB, VB, TVB = kb[bi], Tkb[bi], vb[bi], Tvb[bi]
                                  lastc = (c0 + cn == t0 + ntl)
                                  kz_l = ksz if lastc else 128
                                  loads = [(KB[0:HD, 0:(cn - 1) * 128 + kz_l], kt_src(h, c0, cn, kz_l), Tk, TKB)]
                                  if kz_l == 128:
                                      loads.append((VB[:, 0:cn, :], vv_src(h, c0, cn), Tv, TVB))
                                  else:
                                      if cn > 1:
                                          loads.append((VB[:, 0:cn - 1, :], vv_src(h, c0, cn - 1), Tv, TVB))
                                      loads.append((VB[0:kz_l, cn - 1:cn, :], vv_src(h, c0 + cn - 1, 1, kz_l), Tv, TVB))
                                  chunks.append((c0, cn, KB, TKB, VB, TVB, HD, loads))
                          for ci_, (c0, cn, KB, TKB, VB, TVB, KR, loads) in enumerate(chunks):
                              for j in range(cn):
                                  lastt = (c0 + j == t0 + ntl - 1)
                                  kz = ksz if lastt else 128
                                  cs0 = j * 128 if kind == "d" else 0
                                  items.append(dict(h=h, PO=PO, TPO=TPO, KB=KB, TKB=TKB, VB=VB, TVB=TVB, KR=KR, j=j, kz=kz,
                                                    cs0=cs0, bcol=bcol, first=first, last=(last_seg and lastt),
                                                    loads=(loads if j == 0 else None), key=(h, si, ci_, kind)))
                                  first = False
                  units = []
                  i = 0
                  while i < len(items):
                      a = items[i]
                      if (i + 1 < len(items) and a["key"][3] == "n" and items[i + 1]["key"] == a["key"]
                              and a["kz"] == 128 and items[i + 1]["kz"] == 128):
                          units.append([a, items[i + 1]])
                          i += 2
                      else:
                          units.append([a])
                          i += 1

                  def emit_qk(u):
                      pi = cnt["pb"] % 2
                      cnt["pb"] += 1
                      PSp = pSS[:, pi * 1024:(pi + 1) * 1024].rearrange("p (i c) -> p i c", i=2)
                      PBp = pb[pi][:, :].rearrange("p (i c) -> p i c", i=2)
                      for i, it in enumerate(u):
                          if it["loads"]:
                              for (dst, src, Tsrc, Tdst) in it["loads"]:
                                  sp.dma(dst, src, reads=[Tsrc], writes=[Tdst])
                          it["PS"], it["PB"], it["TPS"], it["TPB"] = PSp, PBp, TpSS[pi], Tpb[pi]
                          pe.op(lambda e, it=it, i=i: e.matmul(
                              PSp[0:it["kz"], i, it["cs0"]:ncols],
                              lhsT=it["KB"][0:it["KR"], it["j"] * 128:it["j"] * 128 + it["kz"]],
                              rhs=qTa[0:it["KR"], it["h"], qc0 + it["cs0"]:qc0 + ncols], start=True, stop=True),
                              reads=[it["TKB"], TqTa], writes=[TpSS[pi]])

                  def emit_rest(u):
                      a = u[0]
                      kz, cs0, n = a["kz"], a["cs0"], len(u)
                      PSp, PBp, TPS, TPB = a["PS"], a["PB"], a["TPS"], a["TPB"]
                      bias_ap = c_zero[0:kz, 0:1] if a["bcol"] is None else c_bias[0:kz, a["bcol"]:a["bcol"] + 1]
                      act.op(lambda e: e.activation(out=PBp[0:kz, 0:n, cs0:ncols], in_=PSp[0:kz, 0:n, cs0:ncols],
                                                    func=AF.Exp, bias=bias_ap, scale=SCALE),
                             reads=[TPS, Tc], writes=[TPB])
                      for i, it in enumerate(u):
                          pe.op(lambda e, it=it, i=i: e.matmul(it["PO"][:, cs0:ncols], lhsT=it["VB"][0:kz, it["j"], :],
                                                               rhs=PBp[0:kz, i, cs0:ncols], start=it["first"], stop=it["last"]),
                                reads=[it["TVB"], TPB], writes=[it["TPO"]])
                          if it["last"]:
                              PO, TPO, h = it["PO"], it["TPO"], it["h"]
                              dve.op(lambda e, PO=PO: e.tensor_scalar(out=rcs[64:128, 0:ncols], in0=PO[64:128, 0:ncols],
                                                                      scalar1=1e-30, scalar2=None, op0=ALU.add),
                                     reads=[TPO], writes=[Trcs])
                              dve.op(lambda e: e.reciprocal(out=rcs[64:128, 0:ncols], in_=rcs[64:128, 0:ncols]),
                                     reads=[Trcs], writes=[Trcs])
                              dve.op(lambda e: e.tensor_copy(out=rcp[0:64, 0:ncols], in_=rcs[64:128, 0:ncols]),
                                     reads=[Trcs], writes=[Trcp])
                              dve.op(lambda e, PO=PO, h=h: e.tensor_tensor(out=attT[:, h, qc0:qc0 + ncols], in0=PO[0:64, 0:ncols],
                                                                           in1=rcp[0:64, 0:ncols], op=ALU.mult),
                                     reads=[TPO, Trcp], writes=[TattT])
                  nu = len(units)
                  if nu:
                      emit_qk(units[0])
                  for i in range(nu):
                      if i + 1 < nu:
                          emit_qk(units[i + 1])
                      emit_rest(units[i])
                      if side is not None:
                          next(side, None)
                  if side is not None:
                      for _ in side:
                          pass

              def attention_halo(segs, side=None):
                  tiles = []
                  for (kind, t0, ntl, bcol, ksz) in segs:
                      for c0 in range(t0, t0 + ntl, 2):
                          cn = min(2, t0 + ntl - c0)
                          for j in range(cn):
                              tiles.append((c0, cn, j, bcol))
                  nt_ = len(tiles)
                  st_ = {}

                  def emit_qk(n):
                      c0, cn, j, bcol = tiles[n]
                      if j == 0:
                          bi = cnt["kb"] % NKB
                          cnt["kb"] += 1
                          KB3 = kb[bi][0:HD, 0:H * 256].rearrange("p (h t) -> p h t", h=H)
                          VB4 = vb[bi][:, 0:H * 2, :].rearrange("p (h s) c -> p h s c", h=H)
                          sp.dma(KB3[:, :, 0:cn * 128], KT[:, :, c0 * 128:(c0 + cn) * 128].rearrange("h d t -> d h t"),
                                 reads=[TKT], writes=[Tkb[bi]])
                          sp.dma(VB4[:, :, 0:cn, :], VV[:, :, c0:c0 + cn, :].rearrange("h p s c -> p h s c"),
                                 reads=[TVV], writes=[Tvb[bi]])
                          st_["cur"] = (KB3, VB4, Tkb[bi], Tvb[bi])
                      KB3, VB4, TKB, TVB = st_["cur"]
                      pi = cnt["pb"] % 2
                      cnt["pb"] += 1
                      PSp = pSS[:, pi * 1024:(pi + 1) * 1024]
                      for h in range(H):
                          pe.op(lambda e, h=h: e.matmul(PSp[:, h * 64:(h + 1) * 64], lhsT=KB3[:, h, j * 128:(j + 1) * 128],
                                                        rhs=qTa[0:HD, h, 64:128], start=True, stop=True,
                                                        skip_group_check=True),
                                reads=[TKB, TqTa], writes=[TpSS[pi]])
                      st_[n] = (PSp, pb[pi], TpSS[pi], Tpb[pi], VB4, TVB, j, bcol)

                  def emit_rest(n):
                      PSp, PB, TPS, TPB, VB4, TVB, j, bcol = st_.pop(n)
                      bias_ap = c_zero[:, 0:1] if bcol is None else c_bias[:, bcol:bcol + 1]
                      act.op(lambda e: e.activation(out=PB[:, 0:512], in_=PSp[:, 0:512], func=AF.Exp, bias=bias_ap, scale=SCALE),
                             reads=[TPS, Tc], writes=[TPB])
                      for h in range(H):
                          pe.op(lambda e, h=h: e.matmul(pO[0][:, h * 64:(h + 1) * 64], lhsT=VB4[:, h, j, :],
                                                        rhs=PB[:, h * 64:(h + 1) * 64],
                                                        start=(n == 0 and h == 0), stop=(n == nt_ - 1),
                                                        skip_group_check=True),
                                reads=[TVB, TPB], writes=[TpO[0]])
                  if nt_:
                      emit_qk(0)
                  for n in range(nt_):
                      if n + 1 < nt_:
                          emit_qk(n + 1)
                      emit_rest(n)
                      if side is not None:
                          next(side, None)
                          next(side, None)
                  if side is not None:
                      for _ in side:
                          pass
                  dve.op(lambda e: e.tensor_scalar(out=rcs[64:128, 0:512], in0=pO[0][64:128, :], scalar1=1e-30,
                                                   scalar2=None, op0=ALU.add), reads=[TpO[0]], writes=[Trcs])
                  dve.op(lambda e: e.reciprocal(out=rcs[64:128, 0:512], in_=rcs[64:128, 0:512]), reads=[Trcs], writes=[Trcs])
                  dve.op(lambda e: e.tensor_copy(out=rcp[0:64, 0:512], in_=rcs[64:128, 0:512]), reads=[Trcs], writes=[Trcp])
                  dve.op(lambda e: e.tensor_tensor(
                      out=attT[:, :, 64:128], in0=pO[0][0:64, :].rearrange("p (h t) -> p h t", h=H),
                      in1=rcp[0:64, 0:512].rearrange("p (h t) -> p h t", h=H), op=ALU.mult),
                      reads=[TpO[0], Trcp], writes=[TattT])

              def out_proj_g(aT_, TaT_, cT_, TcT_, src_rows, col, x1_row, bank=0):
                  bi = cnt["x1"] % 2
                  cnt["x1"] += 1
                  PB_, TPB_ = pM[bank], TpM[bank]
                  sp.dma(xr[bi][:], src_rows, writes=[Txr[bi]])
                  for nb in range(2):
                      for h in range(H):
                          pe.op(lambda e, nb=nb, h=h: e.matmul(PB_[:, :], lhsT=aT_[:, h, col:col + 128],
                                                               rhs=wA_oa[:, h, nb * 512:(nb + 1) * 512],
                                                               start=(h == 0), stop=False),
                                reads=[TaT_, Tw], writes=[TPB_])
                      for c in range(4):
                          pe.op(lambda e, nb=nb, c=c: e.matmul(PB_[:, :], lhsT=cT_[:, c, col:col + 128],
                                                               rhs=wA_oc[:, c, nb * 512:(nb + 1) * 512],
                                                               start=False, stop=(c == 3)),
                                reads=[TcT_, Tw], writes=[TPB_])
                      dve.op(lambda e, nb=nb: e.tensor_tensor(out=x1o[bi][:, nb * 512:(nb + 1) * 512], in0=PB_[:, :],
                                                              in1=xr[bi][:, nb * 512:(nb + 1) * 512], op=ALU.add),
                             reads=[TPB_, Txr[bi]], writes=[Tx1o[bi]])
                      yield
                  sp.dma(X1[x1_row * 128:(x1_row + 1) * 128, :], x1o[bi][:], reads=[Tx1o[bi]], writes=[TX1])
                  yield

              def out_proj(src_rows, col, x1_row):
                  for _ in out_proj_g(attT, TattT, cTg, TcTg, src_rows, col, x1_row, bank=0):
                      pass

              def emit_conv_state(ub, Tub, col0, dst):
                  for c in range(4):
                      pe.op(lambda e, c=c: e.transpose(out=pM[2][0:32, c * 128:(c + 1) * 128], in_=ub[:, c, col0:col0 + 32],
                                                       identity=c_idf[:]), reads=[Tub, Tc], writes=[TpM[2]])
                  dve.op(lambda e: e.tensor_copy(out=cvo[:], in_=pM[2][0:32, :]), reads=[TpM[2]], writes=[Tcvo])
                  sp.dma(dst, cvo[:], reads=[Tcvo], writes=[Tout])

              def kt_p(h, t0, n, ksz=128):
                  return KT[h, :, t0 * 128:(t0 + n - 1) * 128 + ksz]

              def vv_p(h, t0, n, ksz=128):
                  return VV[h, 0:ksz, t0:t0 + n, :]


              groups = []
              for t in range(NSLOT):
                  groups.append((t, None))
                  for g in range(RT // QG):
                      groups.append((t, g))

              def load_group(n):
                  t, g = groups[n]
                  b = n % 2
                  ci0 = t * (RT + 1) + (0 if g is None else 1 + g * QG)
                  ntl = 1 if g is None else QG
                  sp.dma(qTas[b][0:HD, :, 0:ntl * 128], QT[:, :, ci0 * 128:(ci0 + ntl) * 128], reads=[TQT], writes=[TqTas[b]])
                  sp.dma(uTs[b][:, :, 0:CK - 1 + ntl * 128],
                         UT[:, :, 32 + ci0 * 128 - (CK - 1):32 + (ci0 + ntl) * 128], reads=[TUT], writes=[TuTs[b]])
              def conv_of(n):
                  ncol_ = 128 if groups[n][1] is None else QG * 128
                  return conv_module_g(uTs[n % 2], TuTs[n % 2], cTgs[n % 2], TcTgs[n % 2], CK - 1, ncol_, 0, bank=0)

              def outproj_of(n):
                  t, g = groups[n]
                  b = n % 2
                  if g is None:
                      yield from out_proj_g(attTs[b], TattTs[b], cTgs[b], TcTgs[b], x_halo[t * 128:(t + 1) * 128, :], 0,
                                            t * (RT + 1), bank=0)
                  else:
                      for i in range(QG):
                          lt = t * G + g * QG + i
                          yield from out_proj_g(attTs[b], TattTs[b], cTgs[b], TcTgs[b], x_all[lt * 128:(lt + 1) * 128, :],
                                                i * 128, t * (RT + 1) + 1 + g * QG + i, bank=0)

              def chain(*gens):
                  for g_ in gens:
                      if g_ is not None:
                          yield from g_
              load_group(0)
              for _ in conv_of(0):
                  pass
              for n, (t, g) in enumerate(groups):
                  base = t * G
                  qTa, TqTa, uT, TuT = qTas[n % 2], TqTas[n % 2], uTs[n % 2], TuTs[n % 2]
                  attT, TattT = attTs[n % 2], TattTs[n % 2]
                  nxt = None
                  if n + 1 < len(groups):
                      load_group(n + 1)
                      nxt = conv_of(n + 1)
                  side = chain(outproj_of(n - 1) if n > 0 else None, nxt)
                  if g is None:
                      segs = []
                      if t > 0:
                          segs.append(("n", 0, base, None, 128))
                      for r in range(1, NCPB):
                          segs.append(("n", base + r * RT, RT, r, 128))
                      attention_halo(segs, side=side)
                  else:
                      i0 = g * QG
                      segs = []
                      if base + i0 > 0:
                          segs.append(("n", 0, base + i0, None, 128))
                      segs.append(("d", base + i0, QG, None, 128))
                      for r in range(1, NCPB):
                          segs.append(("n", base + r * RT, RT, r, 128))
                      attention(QG * 128, segs, kt_p, vv_p, TKT, TVV, side=side)
                      if t == NSLOT - 1 and g == RT // QG - 1:
                          emit_conv_state(uT, TuT, UW - 32, o_conv[:, :])
              for _ in outproj_of(len(groups) - 1):
                  pass
              attT, TattT = attTs[0], TattTs[0]
              qTa, TqTa, uT, TuT = qTas[0], TqTas[0], uTs[0], TuTs[0]
              cTg, TcTg = cTgs[0], TcTgs[0]

              ckpt("p2")
              uS = [sb(stA, f"uS{i}", [128, 4, CK - 1 + 64], F32) for i in range(2)]
              TuS = [T(f"uS{i}") for i in range(2)]
              for e_ in range(2):
                  sp.dma(uS[e_][:, :, 0:CK - 1], sconvT[:, e_, :, :], writes=[TuS[e_]])
              run_ways([0], lambda _, W: tile_front_A_g(W, x_s[:, :], c_ropesn, Tc, 0,
                                                        [(0, 64, (uS[0], TuS[0], CK - 1)), (64, 64, (uS[1], TuS[1], CK - 1))]))
              for e_ in range(2):
                  dve.op(lambda e, e_=e_: e.tensor_copy(out=uT[:, :, 0:CK - 1 + 64], in_=uS[e_][:, :, :]),
                         reads=[TuS[e_]], writes=[TuT])
                  conv_module(CK - 1, 64, e_ * 64)
                  emit_conv_state(uS[e_], TuS[e_], CK - 1 + 64 - 32, o_convs[e_, :, :])

                  def kt_s(h, t0, n, ksz=128, e_=e_):
                      return KTs[e_, h, :, t0 * 128:(t0 + n - 1) * 128 + ksz]

                  def vv_s(h, t0, n, ksz=128, e_=e_):
                      return VVs[e_, h, 0:ksz, t0:t0 + n, :]
                  attention(64, [("n", 0, PT + 1, None, 64)], kt_s, vv_s, TKTs, TVVs, qc0=e_ * 64)
              out_proj(x_s[:, :], 0, NX1 - 1)
              tk.barrier()
              ckpt("p2s")

          with ExitStack() as stB:
              wB_up = sb(stB, "wB_up", [128, 8, 2 * DFF], BF16)
              wB_dn = sb(stB, "wB_dn", [128, NFC, D], BF16)
              with ExitStack() as stW:
                  ssB = make_stg(stW, "B")
                  load_weight(ssB, wB_up, w_up, 8, 2 * DFF, c_gffn, name="up")
                  load_weight(ssB, wB_dn, w_down, NFC, D, None, name="dn")
                  tk.barrier()
              QB = QG
              NB_ = QB * 128
              xtB = sb(stB, "xtB", [128, QB, D], F32)
              TxtB = [T(f"xtB{j}") for j in range(QB)]
              msB = sb(stB, "msB", [128, QB], F32)
              TmsB = T("msB")
              xnB = sb(stB, "xnB", [128, D], BF16)
              TxnB = T("xnB")
              xnTB = sb(stB, "xnTB", [128, 8, NB_], BF16)
              TxnTB = T("xnTB")
              hT = sb(stB, "hT", [128, NFC, NB_], BF16)
              ThT = T("hT")
              AW = NB_ + 8
              aTc = [sb(stB, f"aTc{i}", [128, AW], F32) for i in range(2)]
              TaTc = [T(f"aTc{i}") for i in range(2)]
              accB = [sb(stB, f"accB{i}", [128, AW], F32) for i in range(2)]
              TaccB = [T(f"accB{i}") for i in range(2)]
              yo = [sb(stB, f"yo{i}", [128, D], F32) for i in range(2)]
              Tyo = [T(f"yo{i}") for i in range(2)]
              arow = [sb(stB, f"arow{i}", [128, 512], F32) for i in range(2)]
              Tarow = [T(f"arow{i}") for i in range(2)]
              hist = sb(stB, "hist", [128, NFC, 2], F32)
              Thist = T("hist")
              hnew = sb(stB, "hnew", [128, NFC, 2], F32)
              Thnew = T("hnew")
              Toutb = T("outsB")
              Abank = [(pM[0], TpM[0]), (pS[3], TpS[3])]
              Gbank = [(pS[0], TpS[0]), (pS[1], TpS[1])]
              Dbank = [(pO[0], TpO[0]), (pO[1], TpO[1])]
              cb_ = dict(ab=0, gb=0, db=0, yo=0, ar=0)

              def ffn_batch(x1_rows, subs, outs, a_rows_dst=None):
                  nt = len(x1_rows)
                  N = nt * 128
                  halo = outs is None
                  for j, r in enumerate(x1_rows):
                      sp.dma(xtB[:, j, :], X1[r * 128:(r + 1) * 128, :], writes=[TxtB[j]])
                      act.op(lambda e, j=j: e.activation(out=xnB[:], in_=xtB[:, j, :], func=AF.Square, scale=1.0 / math.sqrt(D),
                                                         accum_out=msB[:, j:j + 1]), reads=[TxtB[j]], writes=[TxnB, TmsB])
                  rstd_from_msq(None, (msB[:, 0:nt], TmsB), nt)
                  for j in range(nt):
                      dve.op(lambda e, j=j: e.tensor_scalar(out=xnB[:], in0=xtB[:, j, :], scalar1=msB[:, j:j + 1], scalar2=None,
                                                            op0=ALU.mult), reads=[TxtB[j], TmsB], writes=[TxnB])
                      for k in range(8):
                          pe.op(lambda e, k=k: e.transpose(out=pT[:, k * 128:(k + 1) * 128], in_=xnB[:, k * 128:(k + 1) * 128],
                                                           identity=c_idb[:]), reads=[TxnB, Tc], writes=[TpT])
                      act.op(lambda e, j=j: e.activation(out=xnTB[:, :, j * 128:(j + 1) * 128],
                                                         in_=pT[:, :].rearrange("p (k t) -> p k t", k=8), func=AF.Copy),
                             reads=[TpT], writes=[TxnTB])
                  W_ = N + 2 * len(subs)
                  state = {}

                  def stage1(c):
                      ai = cb_["ab"] % 2
                      cb_["ab"] += 1
                      PA, TPA = Abank[ai]
                      AT, TAT, AC, TAC = aTc[ai], TaTc[ai], accB[ai], TaccB[ai]
                      for k in range(8):
                          pe.op(lambda e, k=k: e.matmul(PA[:, 0:N], lhsT=wB_up[:, k, c * 128:(c + 1) * 128], rhs=xnTB[:, k, 0:N],
                                                        start=(k == 0), stop=(k == 7)), reads=[TxnTB, Tw], writes=[TPA])
                      if not halo:
                          gi = cb_["gb"] % 2
                          cb_["gb"] += 1
                          PG, TPG = Gbank[gi]
                          col = DFF + c * 128
                          for k in range(8):
                              pe.op(lambda e, k=k: e.matmul(PG[:, 0:N], lhsT=wB_up[:, k, col:col + 128], rhs=xnTB[:, k, 0:N],
                                                            start=(k == 0), stop=(k == 7)), reads=[TxnTB, Tw], writes=[TPG])
                      else:
                          PG = TPG = None
                      for i, (tok0, ntok, hap, Th) in enumerate(subs):
                          w0 = tok0 + 2 * i
                          act.op(lambda e, w0=w0, tok0=tok0, ntok=ntok: e.activation(
                              out=AT[:, w0 + 2:w0 + 2 + ntok], in_=PA[:, tok0:tok0 + ntok], func=AF.Copy),
                              reads=[TPA], writes=[TAT])
                          if not halo:
                              act.op(lambda e, w0=w0, tok0=tok0, ntok=ntok: e.activation(
                                  out=AC[:, w0:w0 + ntok], in_=PA[:, tok0:tok0 + ntok], func=AF.Identity,
                                  scale=c_fw[:, c, 2:3], bias=c_fb[:, c:c + 1]), reads=[TPA, Tc], writes=[TAC])
                              dve.op(lambda e, w0=w0, hap=hap: e.tensor_copy(out=AT[:, w0:w0 + 2], in_=hap[:, c, :]),
                                     reads=[Th], writes=[TAT])
                          if i == len(subs) - 1:
                              dve.op(lambda e, w0=w0, ntok=ntok: e.tensor_copy(out=hnew[:, c, :], in_=AT[:, w0 + ntok:w0 + ntok + 2]),
                                     reads=[TAT], writes=[Thnew])
                      state[c] = (AT, TAT, AC, TAC, PG, TPG)

                  def stage2(c):
                      AT, TAT, AC, TAC, PG, TPG = state.pop(c)
                      si = cb_["ab"] % 2
                      SL, TSL = AC, TAC
                      dve.op(lambda e: e.scalar_tensor_tensor(out=AC[:, 0:W_ - 2], in0=AT[:, 0:W_ - 2], scalar=c_fw[:, c, 0:1],
                                                              in1=AC[:, 0:W_ - 2], op0=ALU.mult, op1=ALU.add),
                             reads=[TAT, TAC, Tc], writes=[TAC])
                      dve.op(lambda e: e.scalar_tensor_tensor(out=AC[:, 0:W_ - 2], in0=AT[:, 1:W_ - 1], scalar=c_fw[:, c, 1:2],
                                                              in1=AC[:, 0:W_ - 2], op0=ALU.mult, op1=ALU.add),
                             reads=[TAT, TAC, Tc], writes=[TAC])
                      act.op(lambda e: e.activation(out=SL[:, 0:W_ - 2], in_=AC[:, 0:W_ - 2], func=AF.Silu),
                             reads=[TAC], writes=[TSL])
                      for i, (tok0, ntok, hap, Th) in enumerate(subs):
                          w0 = tok0 + 2 * i
                          dve.op(lambda e, w0=w0, tok0=tok0, ntok=ntok: e.tensor_tensor(
                              out=hT[:, c, tok0:tok0 + ntok], in0=PG[:, tok0:tok0 + ntok], in1=SL[:, w0:w0 + ntok], op=ALU.mult),
                              reads=[TPG, TSL], writes=[ThT])

                  if len(subs) > 1:
                      for i in range(2):
                          dve.op(lambda e, i=i: e.memset(accB[i][:], 0.0), writes=[TaccB[i]])
                  for c in range(NFC + 1):
                      if c < NFC:
                          stage1(c)
                      if c >= 1 and not halo:
                          stage2(c - 1)
                  dve.op(lambda e: e.tensor_copy(out=hist[:], in_=hnew[:]), reads=[Thnew], writes=[Thist])
                  if halo:
                      return
                  if a_rows_dst is not None:
                      jl = nt - 1
                      for n0 in range(0, DFF, 512):
                          nw = min(512, DFF - n0)
                          PA, TPA = pS[2], TpS[2]
                          ri = cb_["ar"] % 2
                          cb_["ar"] += 1
                          for k in range(8):
                              pe.op(lambda e, k=k, n0=n0, nw=nw: e.matmul(
                                  PA[:, 0:nw], lhsT=xnTB[:, k, jl * 128:(jl + 1) * 128], rhs=wB_up[:, k, n0:n0 + nw],
                                  start=(k == 0), stop=(k == 7)), reads=[TxnTB, Tw], writes=[TPA])
                          dve.op(lambda e, nw=nw, ri=ri: e.tensor_copy(out=arow[ri][:, 0:nw], in_=PA[:, 0:nw]),
                                 reads=[TPA], writes=[Tarow[ri]])
                          for (r0, dst) in a_rows_dst:
                              sp.dma(dst[:, n0:n0 + nw], arow[ri][r0:r0 + 32, 0:nw], reads=[Tarow[ri]], writes=[Toutb])
                  for j in range(nt):
                      yi = cb_["yo"] % 2
                      cb_["yo"] += 1
                      for nb in range(2):
                          PD, TPD = Dbank[cb_["db"] % 2]
                          cb_["db"] += 1
                          for c in range(NFC):
                              pe.op(lambda e, nb=nb, c=c, j=j, PD=PD: e.matmul(
                                  PD[:, :], lhsT=hT[:, c, j * 128:(j + 1) * 128], rhs=wB_dn[:, c, nb * 512:(nb + 1) * 512],
                                  start=(c == 0), stop=(c == NFC - 1)), reads=[ThT, Tw], writes=[TPD])
                          dve.op(lambda e, nb=nb, j=j, PD=PD, yi=yi: e.tensor_tensor(
                              out=yo[yi][:, nb * 512:(nb + 1) * 512], in0=PD[:, :], in1=xtB[:, j, nb * 512:(nb + 1) * 512],
                              op=ALU.add), reads=[TPD, TxtB[j]], writes=[Tyo[yi]])
                      sp.dma(outs[j], yo[yi][:], reads=[Tyo[yi]], writes=[Toutb])

              for t in range(NSLOT):
                  r0 = t * (RT + 1)
                  ffn_batch([r0], [(0, 128, None, None)], None)
                  dve.op(lambda e, t=t: e.tensor_scalar(out=hist[:], in0=hist[:], scalar1=c_hflag[:, t:t + 1],
                                                        scalar2=None, op0=ALU.mult), reads=[Thist, Tc], writes=[Thist])
                  for i0 in range(0, RT, QB):
                      last = (t == NSLOT - 1 and i0 + QB == RT)
                      ffn_batch([r0 + 1 + i0 + i for i in range(QB)], [(0, NB_, hist, Thist)],
                                [o_y[(t * RT + i0 + i) * 128:(t * RT + i0 + i + 1) * 128, :] for i in range(QB)],
                                a_rows_dst=[(96, o_ffn)] if last else None)
              hs = [sb(stB, f"hs{i}", [128, NFC, 2], F32) for i in range(2)]
              Ths = [T(f"hs{i}") for i in range(2)]
              for e_ in range(2):
                  sp.dma(hs[e_][:], sffnT[:, e_, :, :], writes=[Ths[e_]])
              ffn_batch([NX1 - 1], [(0, 64, hs[0], Ths[0]), (64, 64, hs[1], Ths[1])], [o_ys[:, :]],
                        a_rows_dst=[(32, o_ffns[0]), (96, o_ffns[1])])
              tk.barrier()
          print(f"[build] sems={tk.nsem} waits={tk.nwaits} insts={tk.ninst}", flush=True)
    except _Stop:
        pass
    return nc


def _rope_tab(pos):
    inv = (1.0 / (10000.0 ** (np.arange(0, RD, 2, dtype=np.float32) / np.float32(RD)))).astype(np.float32)
    ang = pos.astype(np.float32)[:, None] * inv[None, :]
    return np.concatenate([np.cos(ang.astype(np.float64)), np.sin(ang.astype(np.float64))], axis=1).astype(np.float32)


def _run(inputs, cfg):
    SEQ, PAST, RT = cfg["SEQ"], cfg["PAST"], cfg["RT"]
    NT = SEQ // 128
    G = NCPB * RT
    NSLOT = NT // G
    QG = min(4, RT)
    PT = PAST // 128
    f32 = np.float32
    bf = ml_dtypes.bfloat16
    g = {k: np.asarray(v) for k, v in inputs.items()}
    xp, xs = g["x_prompt"], g["x_sample"]
    B = xp.shape[0]
    assert B * NCPB == 8 and xs.shape[0] == 16

    def chunked(v, n):
        return np.ascontiguousarray(v.reshape(n, 128).T).astype(f32)

    def bc(v):
        return np.ascontiguousarray(np.broadcast_to(v[None, :], (128, v.shape[0]))).astype(f32)
    common = {
        "w_in": g["w_in"][0], "w_uq": g["w_uq"][0], "w_ukv": g["w_ukv"][0], "w_out": g["w_out"][0],
        "w_up": g["w_up"][0], "w_down": g["w_down"][0],
        "g_attn": chunked(g["attn_norm"][0], 8), "g_q": chunked(g["q_norm"][0], 3),
        "g_ffn": chunked(g["ffn_norm"][0], 8), "g_kv": bc(g["kv_norm"][0]),
        "g_hq": bc(g["qk_norm_q"][0]), "g_hk": bc(g["qk_norm_k"][0]),
        "cw": np.ascontiguousarray(g["conv_w"][0].T.reshape(4, 128, CK).transpose(1, 0, 2)),
        "cb": chunked(g["conv_b"][0], 4), "cg": chunked(g["conv_norm"][0], 4),
        "fw": np.ascontiguousarray(g["ffn_conv_w"][0].T.reshape(NFC, 128, 3).transpose(1, 0, 2)),
        "fb": chunked(g["ffn_conv_b"][0], NFC),
        "identb": np.eye(128, dtype=f32).astype(bf), "identf": np.eye(128, dtype=f32),
        "onesb": np.ones((128, 128), f32).astype(bf),
    }
    nch = QG * 2
    kh = np.zeros((32, QG * 128), f32)
    qm = np.zeros((32, QG * 128), f32)
    for c in range(nch):
        kh[c, c * 64:(c + 1) * 64] = 1.0
        qm[c, :c * 64] = NEG
    common["khot"] = kh.astype(bf)
    common["qmask"] = qm.astype(bf)
    common["rope_sp"] = np.ascontiguousarray(
        _rope_tab(np.arange(max(PT, 1) * 128)).reshape(max(PT, 1), 128, 32).transpose(1, 0, 2))
    common["rope_sn"] = _rope_tab(PAST + (np.arange(128) % 64))

    in_maps, metas = [], []
    for core in range(8):
        b, j = divmod(core, NCPB)
        others = [r for r in range(NCPB) if r != j]
        order = [j] + others
        gt = np.array([t * G + order[r] * RT + i for t in range(NSLOT) for r in range(NCPB) for i in range(RT)])
        tok = (gt[:, None] * 128 + np.arange(128)[None, :]).reshape(-1)
        m = dict(common)
        m["x_all"] = np.ascontiguousarray(xp[b][tok])
        xh = np.zeros((NSLOT, 128, D), f32)
        hf = np.zeros((128, NSLOT), f32)
        rh = np.zeros((NSLOT, 128, 32), f32)
        for t in range(NSLOT):
            ht = t * G + j * RT - 1
            if ht >= 0:
                xh[t] = xp[b, ht * 128:(ht + 1) * 128]
                hf[:, t] = 1.0
                rh[t] = _rope_tab(ht * 128 + np.arange(128))
        m["x_halo"] = xh.reshape(NSLOT * 128, D)
        m["hflag"] = hf
        m["rope_h"] = np.ascontiguousarray(rh.transpose(1, 0, 2))
        m["rope_k"] = np.ascontiguousarray(_rope_tab(tok).reshape(NT, 128, 32).transpose(1, 0, 2))
        bt = np.zeros((128, NCPB), f32)
        for r in range(1, NCPB):
            bt[:, r] = 0.0 if others[r - 1] < j else NEG
        m["biast"] = bt
        e0 = 2 * core
        m["x_s"] = np.ascontiguousarray(xs[e0:e0 + 2].reshape(128, D))
        m["cckv"] = np.ascontiguousarray(g["cache_ckv"][0, e0:e0 + 2].reshape(2 * PAST, KVL))
        m["ckpe"] = np.ascontiguousarray(g["cache_kpe"][0, e0:e0 + 2].reshape(2 * PAST, RD))
        sc = g["state_conv"][0, e0:e0 + 2]
        m["sconvT"] = np.ascontiguousarray(sc.transpose(2, 0, 1).reshape(4, 128, 2, CK - 1).transpose(1, 2, 0, 3))
        sf = g["state_ffn_conv"][0, e0:e0 + 2]
        m["sffnT"] = np.ascontiguousarray(sf.transpose(2, 0, 1).reshape(NFC, 128, 2, 2).transpose(1, 2, 0, 3))
        in_maps.append(m)
        own_tok = (np.array([t * G + j * RT + i for t in range(NSLOT) for i in range(RT)])[:, None] * 128
                   + np.arange(128)[None, :]).reshape(-1)
        metas.append((b, j, own_tok))

    nc = build(cfg)
    res = run_bass_kernel_spmd(nc, in_maps, core_ids=list(range(8)))
    R = res.results

    y_p = np.zeros((B, SEQ, D), f32)
    ckv_p = np.zeros((1, B, SEQ, KVL), f32)
    kpe_p = np.zeros((1, B, SEQ, RD), f32)
    conv_p = np.zeros((1, B, CK - 1, CC), f32)
    ffn_p = np.zeros((1, B, 2, DFF), f32)
    y_s = np.zeros((16, 64, D), f32)
    ckv_s = np.zeros((1, 16, 64, KVL), f32)
    kpe_s = np.zeros((1, 16, 64, RD), f32)
    conv_s = np.zeros((1, 16, CK - 1, CC), f32)
    ffn_s = np.zeros((1, 16, 2, DFF), f32)
    for core in range(8):
        b, j, own_tok = metas[core]
        r = R[core]
        y_p[b, own_tok] = r["o_y"]
        ckv_p[0, b, own_tok] = r["o_ckv"]
        kpe_p[0, b, own_tok] = r["o_kpe"]
        if j == NCPB - 1:
            conv_p[0, b] = r["o_conv"][2:32]
            ffn_p[0, b] = r["o_ffn"][30:32]
        e0 = 2 * core
        y_s[e0:e0 + 2] = r["o_ys"].reshape(2, 64, D)
        ckv_s[0, e0:e0 + 2] = r["o_ckvs"].reshape(2, 64, KVL)
        kpe_s[0, e0:e0 + 2] = r["o_kpes"].reshape(2, 64, RD)
        conv_s[0, e0:e0 + 2] = r["o_convs"][:, 2:32]
        ffn_s[0, e0:e0 + 2] = r["o_ffns"][:, 30:32]
    return (y_p, y_s, ckv_p, kpe_p, conv_p, ffn_p, ckv_s, kpe_s, conv_s, ffn_s)


def kernel(**inputs):
    return _run(inputs, CFG_FULL)
```

```python
import math
from contextlib import ExitStack

import numpy as np
import ml_dtypes

import concourse.bass as bass
import concourse.mybir as mybir
from concourse.bass_utils import run_bass_kernel_spmd

F32 = mybir.dt.float32
BF16 = mybir.dt.bfloat16
ALU = mybir.AluOpType
AF = mybir.ActivationFunctionType
AX = mybir.AxisListType

D = 1024
QL, KVL, RD, CC = 384, 256, 32, 512
H, HD, NOPE, VD = 8, 96, 64, 64
INW = QL + KVL + RD + 2 * CC
DFF = 2816
NFC = DFF // 128
CK = 31
EPS = 1e-6
SCALE = HD ** -0.5
NEG = -30000.0
NCPB = 4
KC = 16

CFG_FULL = dict(SEQ=16384, PAST=2048, RT=8)


class T:
    __slots__ = ("name", "w", "r", "dsem", "dcnt", "excl", "dram", "ssem", "scnt")

    def __init__(self, name, excl=False, dram=False):
        self.name = name
        self.excl = excl
        self.dram = dram
        self.w = None
        self.r = {}
        self.dsem = None
        self.dcnt = 0
        self.ssem = None
        self.scnt = 0


class Eng:
    ROT = 30000

    def __init__(self, trk, eng, name):
        self.trk, self.eng, self.name = trk, eng, name
        self.sem = trk.new_sem(name)
        self.cnt = 0
        self.seen = {}

    def _wait(self, sem, val):
        if self.seen.get(sem, 0) >= val:
            return
        self.eng.wait_ge(sem, val)
        self.seen[sem] = val
        self.trk.nwaits += 1

    def _deps(self, reads, writes):
        need = {}

        def add(p, same_ok):
            if p is None:
                return
            sem, val = p
            if sem is self.sem and same_ok and self.name == "pe":
                return
            if need.get(sem, 0) < val:
                need[sem] = val
        for t in reads:
            add(t.w, False)
        for t in writes:
            add(t.w, True)
            for sem, val in t.r.items():
                add((sem, val), True)
        for sem, val in need.items():
            self._wait(sem, val)

    def op(self, fn, reads=(), writes=()):
        ex = [t for t in reads if t.excl and t not in writes]
        if ex:
            reads = [t for t in reads if not t.excl or t in writes]
            writes = list(writes) + ex
        self._deps(reads, writes)
        if self.cnt >= self.ROT:
            self.sem = self.trk.new_sem(self.name)
            self.cnt = 0
        inst = fn(self.eng)
        self.cnt += 1
        inst.then_inc(self.sem, 1)
        self.trk.ninst += 1
        for t in reads:
            if t.r.get(self.sem, 0) < self.cnt:
                t.r[self.sem] = self.cnt
        for t in writes:
            t.w = (self.sem, self.cnt)
            t.r = {}
        return inst

    def dma(self, out, in_, reads=(), writes=()):
        tw = writes[0]
        if tw.dram and reads:
            self._deps(reads, [])
            src = reads[0]
            if src.ssem is None:
                src.ssem = self.trk.new_sem("st_" + src.name)
                self.trk.sts.append(src)
            inst = self.eng.dma_start(out=out, in_=in_)
            inst.then_inc(src.ssem, 16)
            src.scnt += 16
            self.trk.ninst += 1
            for t in reads:
                if t.r.get(src.ssem, 0) < src.scnt:
                    t.r[src.ssem] = src.scnt
            return inst
        self._deps(reads, writes)
        if tw.dsem is None:
            tw.dsem = self.trk.new_sem("d_" + tw.name)
            self.trk.dts.append(tw)
        inst = self.eng.dma_start(out=out, in_=in_)
        inst.then_inc(tw.dsem, 16)
        tw.dcnt += 16
        self.trk.ninst += 1
        for t in reads:
            if t.r.get(tw.dsem, 0) < tw.dcnt:
                t.r[tw.dsem] = tw.dcnt
        tw.w = (tw.dsem, tw.dcnt)
        tw.r = {}
        return inst

    def wait_for(self, t):
        if t.w is not None:
            self._wait(*t.w)


class Tracker:
    def __init__(self, nc, stack):
        self.nc, self.stack = nc, stack
        self.nsem = 0
        self.nwaits = 0
        self.ninst = 0
        self.dts = []
        self.sts = []
        self.pe = Eng(self, nc.tensor, "pe")
        self.act = Eng(self, nc.scalar, "act")
        self.dve = Eng(self, nc.vector, "dve")
        self.pool = Eng(self, nc.gpsimd, "pool")
        self.sp = Eng(self, nc.sync, "sp")
        self.engs = [self.pe, self.act, self.dve, self.pool, self.sp]

    def new_sem(self, name):
        self.nsem += 1
        return self.stack.enter_context(self.nc.semaphore(f"s{self.nsem}_{name}"))

    def barrier(self):
        pts = [(e.sem, e.cnt) for e in self.engs if e.cnt > 0]
        pts += [(t.dsem, t.dcnt) for t in self.dts if t.dcnt > 0]
        pts += [(t.ssem, t.scnt) for t in self.sts if t.scnt > 0]
        for e in self.engs:
            for sem, val in pts:
                e._wait(sem, val)


def build(cfg):
    SEQ, PAST, RT = cfg["SEQ"], cfg["PAST"], cfg["RT"]
    NT = SEQ // 128
    G = NCPB * RT
    NSLOT = NT // G
    assert NSLOT * G == NT
    QG = min(4, RT)
    assert RT % QG == 0
    NOWN = NSLOT * RT
    PT = PAST // 128
    NX1 = NSLOT * (RT + 1) + 1

    nc = bass.Bass("TRN2", target_bir_lowering=False)

    def din(name, shape, dt=F32):
        return nc.dram_tensor(name, list(shape), dt, kind="ExternalInput").ap()

    def dout(name, shape, dt=F32):
        return nc.dram_tensor(name, list(shape), dt, kind="ExternalOutput").ap()

    def dscr(name, shape, dt):
        return nc.dram_tensor(name, list(shape), dt, kind="Internal").ap()

    x_all = din("x_all", [NT * 128, D])
    x_halo = din("x_halo", [NSLOT * 128, D])
    x_s = din("x_s", [128, D])
    cckv = din("cckv", [2 * PAST, KVL])
    ckpe = din("ckpe", [2 * PAST, RD])
    sconvT = din("sconvT", [128, 2, 4, CK - 1])
    sffnT = din("sffnT", [128, 2, NFC, 2])
    rope_k = din("rope_k", [128, NT, 32])
    rope_h = din("rope_h", [128, NSLOT, 32])
    rope_sp = din("rope_sp", [128, max(PT, 1), 32])
    rope_sn = din("rope_sn", [128, 32])
    hflag = din("hflag", [128, NSLOT])
    biast = din("biast", [128, NCPB])
    w_in = din("w_in", [D, INW])
    w_uq = din("w_uq", [QL, H * HD])
    w_ukv = din("w_ukv", [KVL, H * 128])
    w_out = din("w_out", [D, D])
    w_up = din("w_up", [D, 2 * DFF])
    w_down = din("w_down", [DFF, D])
    g_attn = din("g_attn", [128, 8])
    g_q = din("g_q", [128, 3])
    g_ffn = din("g_ffn", [128, 8])
    g_kv = din("g_kv", [128, KVL])
    g_hq = din("g_hq", [128, HD])
    g_hk = din("g_hk", [128, HD])
    cw = din("cw", [128, 4, CK])
    cb = din("cb", [128, 4])
    cg = din("cg", [128, 4])
    fw = din("fw", [128, NFC, 3])
    fb = din("fb", [128, NFC])
    identb = din("identb", [128, 128], BF16)
    identf = din("identf", [128, 128])
    onesb = din("onesb", [128, 128], BF16)
    khot = din("khot", [32, QG * 128], BF16)
    qmask = din("qmask", [32, QG * 128], BF16)

    o_y = dout("o_y", [NOWN * 128, D])
    o_ckv = dout("o_ckv", [NOWN * 128, KVL])
    o_kpe = dout("o_kpe", [NOWN * 128, RD])
    o_conv = dout("o_conv", [32, CC])
    o_ffn = dout("o_ffn", [32, DFF])
    o_ys = dout("o_ys", [128, D])
    o_ckvs = dout("o_ckvs", [128, KVL])
    o_kpes = dout("o_kpes", [128, RD])
    o_convs = dout("o_convs", [2, 32, CC])
    o_ffns = dout("o_ffns", [2, 32, DFF])

    KT = dscr("KT", [H, HD, NT * 128], BF16)
    VV = dscr("VV", [H, 128, NT, 128], BF16)
    KTs = dscr("KTs", [2, H, HD, (PT + 1) * 128], BF16)
    VVs = dscr("VVs", [2, H, 128, PT + 1, 128], BF16)
    X1 = dscr("X1", [NX1 * 128, D], F32)
    NCI = NSLOT * (RT + 1)
    QT = dscr("QT", [HD, H, NCI * 128], BF16)
    UT = dscr("UT", [128, 4, 32 + NCI * 128], F32)

    class _Stop(Exception):
        pass

    def ckpt(name):
        if cfg.get("STOP") == name:
            tk.barrier()
            raise _Stop()
    try:
      with ExitStack() as top:
          tk = Tracker(nc, top)
          pe, act, dve, pool, sp = tk.pe, tk.act, tk.dve, tk.pool, tk.sp

          def sb(st, name, shape, dt):
              return st.enter_context(nc.sbuf_tensor(name, list(shape), dt))

          def ps(st, name, shape, dt):
              return st.enter_context(nc.psum_tensor(name, list(shape), dt))

          pSS = ps(top, "pSS", [128, 2048], F32)
          pS = [pSS[:, i * 512:(i + 1) * 512] for i in range(4)]
          TpS = [T(f"pS{i}", excl=True) for i in range(4)]
          TpSS = [T(f"pSS{i}", excl=True) for i in range(2)]
          pOO = ps(top, "pOO", [128, 1024], F32)
          pO = [pOO[:, i * 512:(i + 1) * 512] for i in range(2)]
          TpO = [T(f"pO{i}", excl=True) for i in range(2)]
          pM0 = ps(top, "pM0", [128, 512], F32)
          pM = [pM0[:, :], pO[0], pO[1]]
          TpM = [T("pM0", excl=True), TpO[0], TpO[1]]
          pT = ps(top, "pT", [128, 1024], BF16)
          TpT = T("pT", excl=True)

          cst = {}
          Tc = T("consts")

          def cload(name, src, shape, dt=F32):
              t = sb(top, "c_" + name, shape, dt)
              sp.dma(t[:], src, writes=[Tc])
              cst[name] = t
              return t
          c_idb = cload("idb", identb[:, :], [128, 128], BF16)
          c_idf = cload("idf", identf[:, :], [128, 128])
          c_ones = cload("ones", onesb[:, :], [128, 128], BF16)
          c_gkv = cload("gkv", g_kv[:, :], [128, KVL])
          c_ghq = cload("ghq", g_hq[:, :], [128, HD])
          c_ghk = cload("ghk", g_hk[:, :], [128, HD])
          c_cw = cload("cw", cw[:, :, :], [128, 4, CK])
          c_cb = cload("cb", cb[:, :], [128, 4])
          c_cg = cload("cg", cg[:, :], [128, 4])
          c_fw = cload("fw", fw[:, :, :], [128, NFC, 3])
          c_fb = cload("fb", fb[:, :], [128, NFC])
          c_hflag = cload("hflag", hflag[:, :], [128, NSLOT])
          c_bias = cload("bias", biast[:, :], [128, NCPB])
          c_gattn = cload("gattn", g_attn[:, :], [128, 8])
          c_gq = cload("gq", g_q[:, :], [128, 3])
          c_gffn = cload("gffn", g_ffn[:, :], [128, 8])
          c_ropeh = cload("ropeh", rope_h[:, :, :], [128, NSLOT, 32])
          c_ropesn = cload("ropesn", rope_sn[:, :], [128, 32])
          c_zero = sb(top, "c_zero", [128, 1], F32)
          dve.op(lambda e: e.memset(c_zero[:], 0.0), writes=[Tc])
          c_eps = sb(top, "c_eps", [128, 1], F32)
          dve.op(lambda e: e.memset(c_eps[:], EPS), writes=[Tc])

          WCH = 2048
          NSTG = 4

          def make_stg(st, tag):
              return dict(stg=[sb(st, f"stg_{tag}{i}", [128, WCH], F32) for i in range(NSTG)],
                          T=[T(f"stg_{tag}{i}") for i in range(NSTG)], n=0)

          def load_weight(ss, dst, src2d, nk, ncols, gain, kp=128, name="w"):
              CH = WCH
              stg, Tst = ss["stg"], ss["T"]
              for k in range(nk):
                  for c0 in range(0, ncols, CH):
                      cwid = min(CH, ncols - c0)
                      b = ss["n"] % NSTG
                      ss["n"] += 1
                      n = ss["n"]
                      Tw = T("wchunk")
                      sp.dma(stg[b][0:kp, 0:cwid], src2d[k * kp:(k + 1) * kp, c0:c0 + cwid], writes=[Tst[b]])
                      if n % 2:
                          if gain is not None:
                              dve.op(lambda e, b=b, k=k, c0=c0, cwid=cwid: e.tensor_scalar(
                                  out=dst[0:kp, k, c0:c0 + cwid], in0=stg[b][0:kp, 0:cwid],
                                  scalar1=gain[0:kp, k:k + 1], scalar2=None, op0=ALU.mult),
                                  reads=[Tst[b], Tc], writes=[Tw])
                          else:
                              dve.op(lambda e, b=b, k=k, c0=c0, cwid=cwid: e.tensor_copy(
                                  out=dst[0:kp, k, c0:c0 + cwid], in_=stg[b][0:kp, 0:cwid]),
                                  reads=[Tst[b]], writes=[Tw])
                      else:
                          if gain is not None:
                              act.op(lambda e, b=b, k=k, c0=c0, cwid=cwid: e.activation(
                                  out=dst[0:kp, k, c0:c0 + cwid], in_=stg[b][0:kp, 0:cwid], func=AF.Copy,
                                  scale=gain[0:kp, k:k + 1]), reads=[Tst[b], Tc], writes=[Tw])
                          else:
                              act.op(lambda e, b=b, k=k, c0=c0, cwid=cwid: e.activation(
                                  out=dst[0:kp, k, c0:c0 + cwid], in_=stg[b][0:kp, 0:cwid], func=AF.Copy),
                                  reads=[Tst[b]], writes=[Tw])

          Tw = T("weights")

          def rstd_from_msq(st_bufs, msq, n):
              ap, Tm = msq
              act.op(lambda e: e.activation(out=ap, in_=ap, func=AF.Sqrt, bias=c_eps[:, 0:1]), reads=[Tm, Tc], writes=[Tm])
              dve.op(lambda e: e.reciprocal(out=ap, in_=ap), reads=[Tm], writes=[Tm])

          class TileBufs:
              def __init__(self, st, tag, nx=2):
                  self.xt = [sb(st, f"xt{tag}{i}", [128, D], F32) for i in range(nx)]
                  self.Txt = [T(f"xt{tag}{i}") for i in range(nx)]
                  self.junk = sb(st, f"junk{tag}", [128, D], BF16)
                  self.Tjunk = T("junk" + tag)
                  self.st = sb(st, f"stat{tag}", [128, 8], F32)
                  self.Tst = [T(f"stat{tag}{i}") for i in range(8)]
                  self.xn = sb(st, f"xn{tag}", [128, D], BF16)
                  self.Txn = T("xn" + tag)
                  self.xnT = sb(st, f"xnT{tag}", [128, 8, 128], BF16)
                  self.TxnT = T("xnT" + tag)
                  self.n = 0

          def front_end(tb, src_rows, w_reads=()):
              b = tb.n % len(tb.xt)
              tb.n += 1
              xt, Txt = tb.xt[b], tb.Txt[b]
              sp.dma(xt[:], src_rows, reads=list(w_reads), writes=[Txt])
              ms, Tms = tb.st[:, 0:1], tb.Tst[0]
              act.op(lambda e: e.activation(out=tb.junk[:], in_=xt[:], func=AF.Square, scale=1.0 / math.sqrt(D),
                                            accum_out=ms), reads=[Txt], writes=[tb.Tjunk, Tms])
              rstd_from_msq(None, (ms, Tms), 1)
              dve.op(lambda e: e.tensor_scalar(out=tb.xn[:], in0=xt[:], scalar1=ms, scalar2=None, op0=ALU.mult),
                     reads=[Txt, Tms], writes=[tb.Txn])
              for k in range(8):
                  pe.op(lambda e, k=k: e.transpose(out=pT[:, k * 128:(k + 1) * 128], in_=tb.xn[:, k * 128:(k + 1) * 128],
                                                   identity=c_idb[:]), reads=[tb.Txn, Tc], writes=[TpT])
              act.op(lambda e: e.activation(out=tb.xnT[:].rearrange("p k t -> p (k t)"), in_=pT[:], func=AF.Copy),
                     reads=[TpT], writes=[tb.TxnT])
              return b

          class HeadBufs:
              def __init__(self, st, tag):
                  self.raw = sb(st, f"hraw{tag}", [128, H, HD], F32)
                  self.Traw = T("hraw" + tag)
                  self.sq = sb(st, f"hsq{tag}", [128, H, HD], F32)
                  self.Tsq = T("hsq" + tag)
                  self.rs = sb(st, f"hrs{tag}", [128, H], F32)
                  self.Trs = T("hrs" + tag)
                  self.t1, self.Tt1 = self.sq, self.Tsq
                  self.ra = sb(st, f"hra{tag}", [128, H, 16], F32)
                  self.rb = sb(st, f"hrb{tag}", [128, H, 16], F32)
                  self.Tra, self.Trb = T("hra" + tag), T("hrb" + tag)
                  self.fin = sb(st, f"hfin{tag}", [128, H, HD], BF16)
                  self.Tfin = T("hfin" + tag)

          def head_norm_rope(hb, gain, cs, Tcs):
              raw, sq, rs, t1, fin = hb.raw, hb.sq, hb.rs, hb.t1, hb.fin
              act.op(lambda e: e.activation(out=sq[:], in_=raw[:], func=AF.Square, scale=1.0 / math.sqrt(HD)),
                     reads=[hb.Traw], writes=[hb.Tsq])
              dve.op(lambda e: e.tensor_reduce(out=rs[:], in_=sq[:], axis=AX.X, op=ALU.add),
                     reads=[hb.Tsq], writes=[hb.Trs])
              rstd_from_msq(None, (rs[:], hb.Trs), H)
              dve.op(lambda e: e.tensor_tensor(out=t1[:], in0=raw[:], in1=rs[:].unsqueeze(2).to_broadcast([128, H, HD]),
                                               op=ALU.mult), reads=[hb.Traw, hb.Trs], writes=[hb.Tt1])
              pool.op(lambda e: e.tensor_tensor(out=t1[:], in0=t1[:], in1=gain[:].unsqueeze(1).to_broadcast([128, H, HD]),
                                                op=ALU.mult), reads=[hb.Tt1, Tc], writes=[hb.Tt1])
              cosb = cs[:, 0:16].unsqueeze(1).to_broadcast([128, H, 16])
              sinb = cs[:, 16:32].unsqueeze(1).to_broadcast([128, H, 16])
              p1, p2 = t1[:, :, 64:80], t1[:, :, 80:96]
              act.op(lambda e: e.activation(out=fin[:, :, 0:64], in_=t1[:, :, 0:64], func=AF.Copy),
                     reads=[hb.Tt1], writes=[hb.Tfin])
              dve.op(lambda e: e.tensor_tensor(out=hb.ra[:], in0=p1, in1=cosb, op=ALU.mult),
                     reads=[hb.Tt1, Tcs], writes=[hb.Tra])
              dve.op(lambda e: e.tensor_tensor(out=hb.rb[:], in0=p2, in1=sinb, op=ALU.mult),
                     reads=[hb.Tt1, Tcs], writes=[hb.Trb])
              dve.op(lambda e: e.tensor_tensor(out=fin[:, :, 64:80], in0=hb.ra[:], in1=hb.rb[:], op=ALU.subtract),
                     reads=[hb.Tra, hb.Trb], writes=[hb.Tfin])
              dve.op(lambda e: e.tensor_tensor(out=hb.ra[:], in0=p2, in1=cosb, op=ALU.mult),
                     reads=[hb.Tt1, Tcs], writes=[hb.Tra])
              dve.op(lambda e: e.tensor_tensor(out=hb.rb[:], in0=p1, in1=sinb, op=ALU.mult),
                     reads=[hb.Tt1, Tcs], writes=[hb.Trb])
              dve.op(lambda e: e.tensor_tensor(out=fin[:, :, 80:96], in0=hb.ra[:], in1=hb.rb[:], op=ALU.add),
                     reads=[hb.Tra, hb.Trb], writes=[hb.Tfin])

          NWAYS = 1
          NWAYS_P1 = 4

          def run_ways(tasks, make_gen, nways=None):
              nways = len(ways) if nways is None else nways
              it = iter(tasks)
              free = list(range(nways))
              active = []
              more = True
              while True:
                  while free and more:
                      try:
                          tsk = next(it)
                      except StopIteration:
                          more = False
                          break
                      w = free.pop(0)
                      active.append((make_gen(tsk, ways[w]), w))
                  if not active:
                      break
                  for gw in list(active):
                      try:
                          next(gw[0])
                      except StopIteration:
                          active.remove(gw)
                          free.append(gw[1])

          def rstd_g(ap, Tm):
              act.op(lambda e: e.activation(out=ap, in_=ap, func=AF.Sqrt, bias=c_eps[:, 0:1]), reads=[Tm, Tc], writes=[Tm])
              yield
              dve.op(lambda e: e.reciprocal(out=ap, in_=ap), reads=[Tm], writes=[Tm])
              yield

          pS2b = pS[2].bitcast(BF16)
          Tbanks = [(pT[:, :], TpT), (pS2b, TpS[2])]
          Pbanks = [(pO[1], TpO[1]), (pO[0], TpO[0])]
          Kbanks = [((pM[0], pM[1]), (TpM[0], TpM[1])), ((pS[0], pS[1]), (TpS[0], TpS[1]))]

          class Way:
              def __init__(self, st, w):
                  tag = f"W{w}"
                  self.w = w
                  self.xt = sb(st, "xt" + tag, [128, D], F32)
                  self.Txt = T("xt" + tag)
                  self.st = sb(st, "stat" + tag, [128, 8], F32)
                  self.Tst = [T(f"stat{tag}{i}") for i in range(8)]
                  self.xn = sb(st, "xn" + tag, [128, D], BF16)
                  self.Txn = T("xn" + tag)
                  self.junk, self.Tjunk = self.xn, self.Txn
                  self.xnT = sb(st, "xnT" + tag, [128, 8, 128], BF16)
                  self.TxnT = T("xnT" + tag)
                  self.hb = HeadBufs(st, tag)
                  self.rk = sb(st, "rk" + tag, [128, 32], F32)
                  self.Trk = T("rk" + tag)
                  self.cqb = sb(st, "cqb" + tag, [128, QL], BF16)
                  self.Tcqb = T("cqb" + tag)
                  self.cqf = self.hb.sq[:].rearrange("p h d -> p (h d)")[:, 0:QL]
                  self.Tcqf = self.hb.Tsq
                  self.cqT = sb(st, "cqT" + tag, [128, 3, 128], BF16)
                  self.TcqT = T("cqT" + tag)
                  self.sig = sb(st, "sig" + tag, [128, 4, 128], F32)
                  self.Tsig = T("sig" + tag)
                  self.pT, self.TpT = Tbanks[w % 2]
                  self.pP, self.TpP = Pbanks[w % 2]
                  self.pK, self.TpK = Kbanks[w % 2]

              def alloc_p1(self, st):
                  tag = f"W{self.w}"
                  self.ckv = sb(st, "ckv" + tag, [128, KVL], F32)
                  self.Tckv = T("ckv" + tag)
                  self.kpe = sb(st, "kpe" + tag, [128, RD], F32)
                  self.Tkpe = T("kpe" + tag)
                  self.ckvb = sb(st, "ckvb" + tag, [128, KVL], BF16)
                  self.Tckvb = T("ckvb" + tag)
                  self.ckvT = sb(st, "ckvT" + tag, [128, 2, 128], BF16)
                  self.TckvT = T("ckvT" + tag)
                  self.qst = sb(st, "qst" + tag, [HD, H, 128], BF16)
                  self.Tqst = T("qst" + tag)
                  self.ust, self.Tust = self.sig, self.Tsig
                  self.prj = sb(st, "prj" + tag, [128, KVL + RD], F32)
                  self.Tprj = T("prj" + tag)
                  self.cin = sb(st, "cin" + tag, [128, KVL], F32)
                  self.Tcin = T("cin" + tag)
                  self.kin = sb(st, "kin" + tag, [128, RD], F32)
                  self.Tkin = T("kin" + tag)

          def front_end_g(W, src_rows):
              sp.dma(W.xt[:], src_rows, writes=[W.Txt])
              ms, Tms = W.st[:, 0:1], W.Tst[0]
              act.op(lambda e: e.activation(out=W.junk[:], in_=W.xt[:], func=AF.Square, scale=1.0 / math.sqrt(D),
                                            accum_out=ms), reads=[W.Txt], writes=[W.Tjunk, Tms])
              yield
              yield from rstd_g(ms, Tms)
              dve.op(lambda e: e.tensor_scalar(out=W.xn[:], in0=W.xt[:], scalar1=ms, scalar2=None, op0=ALU.mult),
                     reads=[W.Txt, Tms], writes=[W.Txn])
              yield
              for k in range(8):
                  pe.op(lambda e, k=k: e.transpose(out=W.pT[:, k * 128:(k + 1) * 128], in_=W.xn[:, k * 128:(k + 1) * 128],
                                                   identity=c_idb[:]), reads=[W.Txn, Tc], writes=[W.TpT])
              act.op(lambda e: e.activation(out=W.xnT[:].rearrange("p k t -> p (k t)"), in_=W.pT, func=AF.Copy),
                     reads=[W.TpT], writes=[W.TxnT])
              yield

          def head_norm_rope_g(hb, gain, cs, Tcs):
              raw, sq, rs, t1, fin = hb.raw, hb.sq, hb.rs, hb.t1, hb.fin
              act.op(lambda e: e.activation(out=sq[:], in_=raw[:], func=AF.Square, scale=1.0 / math.sqrt(HD)),
                     reads=[hb.Traw], writes=[hb.Tsq])
              yield
              dve.op(lambda e: e.tensor_reduce(out=rs[:], in_=sq[:], axis=AX.X, op=ALU.add),
                     reads=[hb.Tsq], writes=[hb.Trs])
              yield
              yield from rstd_g(rs[:], hb.Trs)
              dve.op(lambda e: e.tensor_tensor(out=t1[:], in0=raw[:], in1=rs[:].unsqueeze(2).to_broadcast([128, H, HD]),
                                               op=ALU.mult), reads=[hb.Traw, hb.Trs], writes=[hb.Tt1])
              yield
              pool.op(lambda e: e.tensor_tensor(out=t1[:], in0=t1[:], in1=gain[:].unsqueeze(1).to_broadcast([128, H, HD]),
                                                op=ALU.mult), reads=[hb.Tt1, Tc], writes=[hb.Tt1])
              yield
              cosb = cs[:, 0:16].unsqueeze(1).to_broadcast([128, H, 16])
              sinb = cs[:, 16:32].unsqueeze(1).to_broadcast([128, H, 16])
              p1, p2 = t1[:, :, 64:80], t1[:, :, 80:96]
              act.op(lambda e: e.activation(out=fin[:, :, 0:64], in_=t1[:, :, 0:64], func=AF.Copy),
                     reads=[hb.Tt1], writes=[hb.Tfin])
              dve.op(lambda e: e.tensor_tensor(out=hb.ra[:], in0=p1, in1=cosb, op=ALU.mult),
                     reads=[hb.Tt1, Tcs], writes=[hb.Tra])
              pool.op(lambda e: e.tensor_tensor(out=hb.rb[:], in0=p2, in1=sinb, op=ALU.mult),
                      reads=[hb.Tt1, Tcs], writes=[hb.Trb])
              yield
              dve.op(lambda e: e.tensor_tensor(out=fin[:, :, 64:80], in0=hb.ra[:], in1=hb.rb[:], op=ALU.subtract),
                     reads=[hb.Tra, hb.Trb], writes=[hb.Tfin])
              yield
              dve.op(lambda e: e.tensor_tensor(out=hb.ra[:], in0=p2, in1=cosb, op=ALU.mult),
                     reads=[hb.Tt1, Tcs], writes=[hb.Tra])
              pool.op(lambda e: e.tensor_tensor(out=hb.rb[:], in0=p1, in1=sinb, op=ALU.mult),
                      reads=[hb.Tt1, Tcs], writes=[hb.Trb])
              yield
              dve.op(lambda e: e.tensor_tensor(out=fin[:, :, 80:96], in0=hb.ra[:], in1=hb.rb[:], op=ALU.add),
                     reads=[hb.Tra, hb.Trb], writes=[hb.Tfin])
              yield

          with ExitStack() as stA:
              wA_in = sb(stA, "wA_in", [128, 8, INW], BF16)
              wA_uq = sb(stA, "wA_uq", [128, 3, H * HD], BF16)
              wA_ukv = sb(stA, "wA_ukv", [128, 2, H * 128], BF16)
              wA_oa = sb(stA, "wA_oa", [64, 8, D], BF16)
              wA_oc = sb(stA, "wA_oc", [128, 4, D], BF16)
              with ExitStack() as stW:
                  ssA = make_stg(stW, "A")
                  load_weight(ssA, wA_in, w_in, 8, INW, c_gattn, name="in")
                  load_weight(ssA, wA_uq, w_uq, 3, H * HD, c_gq, name="uq")
                  load_weight(ssA, wA_ukv, w_ukv, 2, H * 128, None, name="ukv")
                  load_weight(ssA, wA_oa, w_out, 8, D, None, kp=64, name="oa")
                  load_weight(ssA, wA_oc, w_out[512:1024, :], 4, D, None, name="oc")
                  tk.barrier()
              ckpt("w")

              def tile_front_A_g(W, src_rows, cs, Tcs, qcol, ntok_groups):
                  yield from front_end_g(W, src_rows)
                  yield from qglu_g(W, cs, Tcs, qTa[0:HD, :, qcol:qcol + 128], TqTa, ntok_groups)

              def qglu_g(W, cs, Tcs, qdst, Tqdst, ntok_groups):
                  hb = W.hb
                  for k in range(8):
                      pe.op(lambda e, k=k: e.matmul(W.pP[:, 0:QL], lhsT=W.xnT[:, k, :], rhs=wA_in[:, k, 0:QL],
                                                    start=(k == 0), stop=(k == 7)), reads=[W.TxnT, Tw], writes=[W.TpP])
                  dve.op(lambda e: e.tensor_copy(out=W.cqf, in_=W.pP[:, 0:QL]), reads=[W.TpP], writes=[W.Tcqf])
                  yield
                  ms, Tms = W.st[:, 2:3], W.Tst[2]
                  act.op(lambda e: e.activation(out=W.junk[:, 0:QL], in_=W.cqf, func=AF.Square,
                                                scale=1.0 / math.sqrt(QL), accum_out=ms),
                         reads=[W.Tcqf], writes=[W.Tjunk, Tms])
                  yield
                  yield from rstd_g(ms, Tms)
                  dve.op(lambda e: e.tensor_scalar(out=W.cqb[:], in0=W.cqf, scalar1=ms, scalar2=None, op0=ALU.mult),
                         reads=[W.Tcqf, Tms], writes=[W.Tcqb])
                  yield
                  for k in range(3):
                      pe.op(lambda e, k=k: e.transpose(out=W.pT[:, k * 128:(k + 1) * 128], in_=W.cqb[:, k * 128:(k + 1) * 128],
                                                       identity=c_idb[:]), reads=[W.Tcqb, Tc], writes=[W.TpT])
                  dve.op(lambda e: e.tensor_copy(out=W.cqT[:].rearrange("p k t -> p (k t)"), in_=W.pT[:, 0:384]),
                         reads=[W.TpT], writes=[W.TcqT])
                  yield
                  for nb, (c0, cw_) in enumerate(((0, 512), (512, 256))):
                      for k in range(3):
                          pe.op(lambda e, nb=nb, k=k, c0=c0, cw_=cw_: e.matmul(
                              W.pK[nb][:, 0:cw_], lhsT=W.cqT[:, k, :], rhs=wA_uq[:, k, c0:c0 + cw_],
                              start=(k == 0), stop=(k == 2)), reads=[W.TcqT, Tw], writes=[W.TpK[nb]])
                  rawf = hb.raw[:].rearrange("p h d -> p (h d)")
                  act.op(lambda e: e.activation(out=rawf[:, 0:512], in_=W.pK[0][:, 0:512], func=AF.Copy),
                         reads=[W.TpK[0]], writes=[hb.Traw])
                  dve.op(lambda e: e.tensor_copy(out=rawf[:, 512:768], in_=W.pK[1][:, 0:256]),
                         reads=[W.TpK[1]], writes=[hb.Traw])
                  yield
                  yield from head_norm_rope_g(hb, c_ghq, cs, Tcs)
                  for h in range(H):
                      pe.op(lambda e, h=h: e.transpose(out=W.pT[0:HD, h * 128:(h + 1) * 128], in_=hb.fin[:, h, :],
                                                       identity=c_idb[:]), reads=[hb.Tfin, Tc], writes=[W.TpT])
                  act.op(lambda e: e.activation(out=qdst,
                                                in_=W.pT[0:HD, :].rearrange("p (h t) -> p h t", h=H), func=AF.Copy),
                         reads=[W.TpT], writes=[Tqdst])
                  yield
                  for half in (1, 0):
                      for c in range(4):
                          col = QL + KVL + RD + half * CC + c * 128
                          for k in range(8):
                              pe.op(lambda e, half=half, c=c, k=k, col=col: e.matmul(
                                  W.pK[half][:, c * 128:(c + 1) * 128], lhsT=wA_in[:, k, col:col + 128], rhs=W.xnT[:, k, :],
                                  start=(k == 0), stop=(k == 7)), reads=[W.TxnT, Tw], writes=[W.TpK[half]])
                      if half == 1:
                          act.op(lambda e: e.activation(out=W.sig[:].rearrange("p c t -> p (c t)"), in_=W.pK[1][:, :],
                                                        func=AF.Sigmoid), reads=[W.TpK[1]], writes=[W.Tsig])
                  for (tok0, ntok, ucol_) in ntok_groups:
                      tgt = ucol_ if isinstance(ucol_, tuple) else (uT, TuT, ucol_)
                      ub, Tub, uc = tgt
                      dve.op(lambda e, tok0=tok0, ntok=ntok, ub=ub, uc=uc: e.tensor_tensor(
                          out=ub[:, :, uc:uc + ntok],
                          in0=W.pK[0][:, :].rearrange("p (c t) -> p c t", c=4)[:, :, tok0:tok0 + ntok],
                          in1=W.sig[:, :, tok0:tok0 + ntok], op=ALU.mult), reads=[W.TpK[0], W.Tsig], writes=[Tub])
                  yield

              ways = [Way(stA, w) for w in range(NWAYS)]
              TKT, TVV = T("KT", dram=True), T("VV", dram=True)
              TKTs, TVVs = T("KTs", dram=True), T("VVs", dram=True)
              TX1 = T("X1", dram=True)
              Tout = T("outs", dram=True)
              with ExitStack() as stP1:
                  for w in range(NWAYS, NWAYS_P1):
                      ways.append(Way(stP1, w))
                  for W_ in ways:
                      W_.alloc_p1(stP1)
                  KS = 4
                  kst = [sb(stP1, f"kst{i}", [HD, H, KS * 128], BF16) for i in range(2)]
                  Tkst = [T(f"kst{i}") for i in range(2)]
                  vst = [sb(stP1, f"vst{i}", [128, H, KS, 128], BF16) for i in range(2)]
                  Tvst = [T(f"vst{i}") for i in range(2)]
                  for i in range(2):
                      pool.op(lambda e, i=i: e.memset(vst[i][:], 1.0), writes=[Tvst[i]])

                  def kv_from_ckv_g(W, ckv_ap, Tck, kpe_ap, Tkp, cs, Tcs, stage_slot, have_bf16=False):
                      sbuf, slot = stage_slot
                      hb = W.hb
                      if not have_bf16:
                          act.op(lambda e: e.activation(out=W.ckvb[:], in_=ckv_ap, func=AF.Copy), reads=[Tck], writes=[W.Tckvb])
                          yield
                      for k in range(2):
                          pe.op(lambda e, k=k: e.transpose(out=W.pT[:, k * 128:(k + 1) * 128],
                                                           in_=W.ckvb[:, k * 128:(k + 1) * 128], identity=c_idb[:]),
                                reads=[W.Tckvb, Tc], writes=[W.TpT])
                      dve.op(lambda e: e.tensor_copy(out=W.ckvT[:].rearrange("p k t -> p (k t)"), in_=W.pT[:, 0:256]),
                             reads=[W.TpT], writes=[W.TckvT])
                      yield
                      for nb in range(2):
                          for k in range(2):
                              pe.op(lambda e, nb=nb, k=k: e.matmul(W.pK[nb][:, :], lhsT=W.ckvT[:, k, :],
                                                                   rhs=wA_ukv[:, k, nb * 512:(nb + 1) * 512],
                                                                   start=(k == 0), stop=(k == 1)),
                                    reads=[W.TckvT, Tw], writes=[W.TpK[nb]])
                      for nb in range(2):
                          src = W.pK[nb][:, :].rearrange("p (h c) -> p h c", h=4)
                          act.op(lambda e, nb=nb, src=src: e.activation(out=hb.raw[:, nb * 4:(nb + 1) * 4, 0:64],
                                                                        in_=src[:, :, 0:64], func=AF.Copy),
                                 reads=[W.TpK[nb]], writes=[hb.Traw])
                          dve.op(lambda e, nb=nb, src=src: e.tensor_copy(out=vst[sbuf][:, nb * 4:(nb + 1) * 4, slot, 0:64],
                                                                         in_=src[:, :, 64:128]),
                                 reads=[W.TpK[nb]], writes=[Tvst[sbuf]])
                      yield
                      pool.op(lambda e: e.tensor_copy(out=hb.raw[:, :, 64:96],
                                                      in_=kpe_ap.unsqueeze(1).to_broadcast([128, H, RD])),
                              reads=[Tkp], writes=[hb.Traw])
                      yield
                      yield from head_norm_rope_g(hb, c_ghk, cs, Tcs)
                      for h in range(H):
                          pe.op(lambda e, h=h: e.transpose(out=W.pT[0:HD, h * 128:(h + 1) * 128], in_=hb.fin[:, h, :],
                                                           identity=c_idb[:]), reads=[hb.Tfin, Tc], writes=[W.TpT])
                      act.op(lambda e: e.activation(out=kst[sbuf][:, :, slot * 128:(slot + 1) * 128],
                                                    in_=W.pT[0:HD, :].rearrange("p (h t) -> p h t", h=H), func=AF.Copy),
                             reads=[W.TpT], writes=[Tkst[sbuf]])
                      yield

                  def ckv_from_x_g(W, own_row=None, o_ck=None, o_kp=None):
                      for k in range(8):
                          pe.op(lambda e, k=k: e.matmul(W.pP[:, 0:KVL + RD], lhsT=W.xnT[:, k, :],
                                                        rhs=wA_in[:, k, QL:QL + KVL + RD], start=(k == 0), stop=(k == 7)),
                                reads=[W.TxnT, Tw], writes=[W.TpP])
                      dve.op(lambda e: e.tensor_copy(out=W.prj[:], in_=W.pP[:, 0:KVL + RD]), reads=[W.TpP], writes=[W.Tprj])
                      yield
                      ms, Tms = W.st[:, 1:2], W.Tst[1]
                      act.op(lambda e: e.activation(out=W.junk[:, 0:KVL], in_=W.prj[:, 0:KVL], func=AF.Square,
                                                    scale=1.0 / math.sqrt(KVL), accum_out=ms),
                             reads=[W.Tprj], writes=[W.Tjunk, Tms])
                      pool.op(lambda e: e.tensor_copy(out=W.kpe[:], in_=W.prj[:, KVL:KVL + RD]),
                              reads=[W.Tprj], writes=[W.Tkpe])
                      yield
                      yield from rstd_g(ms, Tms)
                      if own_row is None:
                          dve.op(lambda e: e.scalar_tensor_tensor(out=W.ckvb[:], in0=W.prj[:, 0:KVL], scalar=ms, in1=c_gkv[:],
                                                                  op0=ALU.mult, op1=ALU.mult),
                                 reads=[W.Tprj, Tms, Tc], writes=[W.Tckvb])
                          yield
                          return
                      dve.op(lambda e: e.scalar_tensor_tensor(out=W.ckv[:], in0=W.prj[:, 0:KVL], scalar=ms, in1=c_gkv[:],
                                                              op0=ALU.mult, op1=ALU.mult),
                             reads=[W.Tprj, Tms, Tc], writes=[W.Tckv])
                      yield
                      if own_row is not None:
                          sp.dma(o_ck[own_row:own_row + 128, :], W.ckv[:], reads=[W.Tckv], writes=[Tout])
                          sp.dma(o_kp[own_row:own_row + 128, :], W.kpe[:], reads=[W.Tkpe], writes=[Tout])

                  gdone = {}

                  TQT, TUT = T("QT", dram=True), T("UT", dram=True)
                  zpad = sb(stP1, "zpad", [128, 4, 32], F32)
                  Tzpad = T("zpad")
                  dve.op(lambda e: e.memset(zpad[:], 0.0), writes=[Tzpad])
                  sp.dma(UT[:, :, 0:32], zpad[:], reads=[Tzpad], writes=[TUT])

                  def qu_to_scratch_g(W, cs, Tcs, ci):
                      yield from qglu_g(W, cs, Tcs, W.qst[:], W.Tqst, [(0, 128, (W.ust, W.Tust, 0))])
                      sp.dma(QT[:, :, ci * 128:(ci + 1) * 128], W.qst[:], reads=[W.Tqst], writes=[TQT])
                      sp.dma(UT[:, :, 32 + ci * 128:32 + (ci + 1) * 128], W.ust[:], reads=[W.Tust], writes=[TUT])

                  def p1_tile_g(lt, W):
                      if isinstance(lt, tuple):
                          t = lt[1]
                          yield from front_end_g(W, x_halo[t * 128:(t + 1) * 128, :])
                          yield from qu_to_scratch_g(W, c_ropeh[:, t, :], Tc, t * (RT + 1))
                          return
                      gi = lt // KS
                      sbuf, slot = gi % 2, lt % KS
                      sp.dma(W.rk[:], rope_k[:, lt, :], writes=[W.Trk])
                      yield from front_end_g(W, x_all[lt * 128:(lt + 1) * 128, :])
                      t, rem = divmod(lt, G)
                      own = rem < RT
                      yield from ckv_from_x_g(W, own_row=(t * RT + rem) * 128 if own else None, o_ck=o_ckv, o_kp=o_kpe)
                      yield from kv_from_ckv_g(W, W.ckv[:], W.Tckv, W.kpe[:], W.Tkpe, W.rk, W.Trk, (sbuf, slot),
                                               have_bf16=not own)
                      if own:
                          yield from qu_to_scratch_g(W, W.rk, W.Trk, t * (RT + 1) + 1 + rem)
                      gdone[gi] = gdone.get(gi, 0) + 1
                      if gdone[gi] == KS:
                          lt0 = gi * KS
                          sp.dma(KT[:, :, lt0 * 128:(lt0 + KS) * 128].rearrange("h d t -> d h t"), kst[sbuf][:],
                                 reads=[Tkst[sbuf]], writes=[TKT])
                          sp.dma(VV[:, :, lt0:lt0 + KS, :].rearrange("h p s c -> p h s c"), vst[sbuf][:],
                                 reads=[Tvst[sbuf]], writes=[TVV])
                  run_ways(list(range(NT)) + [("h", t) for t in range(NSLOT)], p1_tile_g)

                  ckpt("p1")
                  NG0 = NT // KS
                  PG = (PT + KS - 1) // KS
                  sdone = {}

                  def p1s_tile_g(ep, W):
                      e_, p = ep
                      gi = NG0 + e_ * PG + p // KS
                      sbuf, slot = gi % 2, p % KS
                      r0 = e_ * PAST + p * 128
                      sp.dma(W.cin[:], cckv[r0:r0 + 128, :], writes=[W.Tcin])
                      sp.dma(W.kin[:], ckpe[r0:r0 + 128, :], writes=[W.Tkin])
                      sp.dma(W.rk[:], rope_sp[:, p, :], writes=[W.Trk])
                      yield from kv_from_ckv_g(W, W.cin[:], W.Tcin, W.kin[:], W.Tkin, W.rk, W.Trk, (sbuf, slot))
                      sdone[gi] = sdone.get(gi, 0) + 1
                      p0 = (p // KS) * KS
                      ns = min(KS, PT - p0)
                      if sdone[gi] == ns:
                          sp.dma(KTs[e_, :, :, p0 * 128:(p0 + ns) * 128].rearrange("h d t -> d h t"),
                                 kst[sbuf][:, :, 0:ns * 128], reads=[Tkst[sbuf]], writes=[TKTs])
                          sp.dma(VVs[e_, :, :, p0:p0 + ns, :].rearrange("h p s c -> p h s c"),
                                 vst[sbuf][:, :, 0:ns, :], reads=[Tvst[sbuf]], writes=[TVVs])
                  run_ways([(e_, p) for e_ in range(2) for p in range(PT)], p1s_tile_g)
                  sbuf = (NG0 + 2 * PG) % 2

                  def p1n_g(_, W):
                      yield from front_end_g(W, x_s[:, :])
                      yield from ckv_from_x_g(W, own_row=0, o_ck=o_ckvs, o_kp=o_kpes)
                      yield from kv_from_ckv_g(W, W.ckv[:], W.Tckv, W.kpe[:], W.Tkpe, c_ropesn, Tc, (sbuf, 0))
                  run_ways([0], p1n_g)
                  for e_ in range(2):
                      sp.dma(KTs[e_, :, :, PT * 128:PT * 128 + 64].rearrange("h d t -> d h t"),
                             kst[sbuf][:, :, e_ * 64:(e_ + 1) * 64], reads=[Tkst[sbuf]], writes=[TKTs])
                      sp.dma(VVs[e_, :, 0:64, PT:PT + 1, :].rearrange("h p s c -> p h s c"),
                             vst[sbuf][e_ * 64:(e_ + 1) * 64, :, 0:1, :], reads=[Tvst[sbuf]], writes=[TVVs])

                  tk.barrier()
                  del ways[NWAYS:]
              ckpt("p1s")
              qTas = [sb(stA, f"qTa{i}", [128, H, QG * 128], BF16) for i in range(2)]
              TqTas = [T(f"qTa{i}") for i in range(2)]
              for i in range(2):
                  for h in range(H):
                      sp.dma(qTas[i][96:128, h, :], qmask[:, :], writes=[TqTas[i]])
              qTa, TqTa = qTas[0], TqTas[0]
              cTgs = [sb(stA, f"cTg{i}", [128, 4, QG * 128], BF16) for i in range(2)]
              TcTgs = [T(f"cTg{i}") for i in range(2)]
              cTg, TcTg = cTgs[0], TcTgs[0]
              attTs = [sb(stA, f"attT{i}", [64, H, QG * 128], BF16) for i in range(2)]
              TattTs = [T(f"attT{i}") for i in range(2)]
              attT, TattT = attTs[0], TattTs[0]
              for i in range(2):
                  pool.op(lambda e, i=i: e.memset(attTs[i][:], 0.0), writes=[TattTs[i]])
              UW = CK - 1 + QG * 128
              uTs = [sb(stA, f"uT{i}", [128, 4, UW], F32) for i in range(2)]
              TuTs = [T(f"uT{i}") for i in range(2)]
              uT, TuT = uTs[0], TuTs[0]
              acc = sb(stA, "acc", [128, 4, QG * 128], F32)
              Tacc = T("acc")
              sqc = sb(stA, "sqc", [128, 4, QG * 128], BF16)
              Tsqc = T("sqc")
              rsc = sb(stA, "rsc", [128, QG * 128], F32)
              Trsc = T("rsc")
              cpre, Tcpre = acc, Tacc
              NKB = 3
              kb = [sb(stA, f"kb{i}", [HD, KC * 128], BF16) for i in range(NKB)]
              Tkb = [T(f"kb{i}") for i in range(NKB)]
              vb = [sb(stA, f"vb{i}", [128, KC, 128], BF16) for i in range(NKB)]
              Tvb = [T(f"vb{i}") for i in range(NKB)]
              kd = [sb(stA, f"kd{i}", [128, QG * 128], BF16) for i in range(2)]
              Tkd = [T(f"kd{i}") for i in range(2)]
              for i in range(2):
                  sp.dma(kd[i][96:128, :], khot[:, :], writes=[Tkd[i]])
              vd = [sb(stA, f"vd{i}", [128, QG, 128], BF16) for i in range(2)]
              Tvd = [T(f"vd{i}") for i in range(2)]
              pb = [sb(stA, f"pb{i}", [128, 1024], BF16) for i in range(2)]
              Tpb = [T(f"pb{i}") for i in range(2)]
              rcp = sb(stA, "rcp", [64, 512], F32)
              Trcp = T("rcp")
              rcs = sb(stA, "rcs", [128, 512], F32)
              Trcs = T("rcs")
              xr = [sb(stA, f"xr{i}", [128, D], F32) for i in range(2)]
              Txr = [T(f"xr{i}") for i in range(2)]
              x1o, Tx1o = xr, Txr
              cvo = sb(stA, "cvo", [32, CC], F32)
              Tcvo = T("cvo")
              cnt = dict(kb=0, pb=0, x1=0, po=0, kd=0)

              def conv_module_g(uT_, TuT_, cT_, TcT_, ucol, ncol, ccol, bank=0):
                  PB_, TPB_ = pM[bank], TpM[bank]
                  for c in range(4):
                      dve.op(lambda e, c=c: e.tensor_scalar(out=acc[:, c, 0:ncol], in0=uT_[:, c, ucol - 30:ucol - 30 + ncol],
                                                            scalar1=c_cw[:, c, 0:1], scalar2=c_cb[:, c:c + 1],
                                                            op0=ALU.mult, op1=ALU.add),
                             reads=[TuT_, Tc], writes=[Tacc])
                      yield
                      for k in range(1, CK):
                          dve.op(lambda e, c=c, k=k: e.scalar_tensor_tensor(
                              out=acc[:, c, 0:ncol], in0=uT_[:, c, ucol - 30 + k:ucol - 30 + k + ncol],
                              scalar=c_cw[:, c, k:k + 1], in1=acc[:, c, 0:ncol], op0=ALU.mult, op1=ALU.add),
                              reads=[TuT_, Tc, Tacc], writes=[Tacc])
                          yield
                  act.op(lambda e: e.activation(out=sqc[:, :, 0:ncol], in_=acc[:, :, 0:ncol], func=AF.Square,
                                                scale=1.0 / math.sqrt(CC)), reads=[Tacc], writes=[Tsqc])
                  yield
                  for c in range(4):
                      pe.op(lambda e, c=c: e.matmul(PB_[:, 0:ncol], lhsT=c_ones[:], rhs=sqc[:, c, 0:ncol],
                                                    start=(c == 0), stop=(c == 3)), reads=[Tsqc, Tc], writes=[TPB_])
                  dve.op(lambda e: e.tensor_scalar(out=rsc[:, 0:ncol], in0=PB_[:, 0:ncol], scalar1=EPS, scalar2=None,
                                                   op0=ALU.add), reads=[TPB_], writes=[Trsc])
                  yield
                  act.op(lambda e: e.activation(out=rsc[:, 0:ncol], in_=rsc[:, 0:ncol], func=AF.Sqrt),
                         reads=[Trsc], writes=[Trsc])
                  yield
                  dve.op(lambda e: e.reciprocal(out=rsc[:, 0:ncol], in_=rsc[:, 0:ncol]), reads=[Trsc], writes=[Trsc])
                  yield
                  for c in range(4):
                      dve.op(lambda e, c=c: e.scalar_tensor_tensor(out=cpre[:, c, 0:ncol], in0=acc[:, c, 0:ncol],
                                                                   scalar=c_cg[:, c:c + 1], in1=rsc[:, 0:ncol],
                                                                   op0=ALU.mult, op1=ALU.mult),
                             reads=[Tacc, Trsc, Tc], writes=[Tcpre])
                      yield
                  act.op(lambda e: e.activation(out=cT_[:, :, ccol:ccol + ncol], in_=cpre[:, :, 0:ncol], func=AF.Silu),
                         reads=[Tcpre], writes=[TcT_])
                  yield

              def conv_module(ucol, ncol, ccol):
                  for _ in conv_module_g(uT, TuT, cTg, TcTg, ucol, ncol, ccol, bank=2):
                      pass

              def attention(ncols, segs, kt_src, vv_src, Tk, Tv, qc0=0, ksz_last=128, side=None):
                  items = []
                  for h in range(H):
                      po_i = cnt["po"] % 2
                      cnt["po"] += 1
                      PO, TPO = pO[po_i], TpO[po_i]
                      first = True
                      nseg = len(segs)
                      for si, (kind, t0, ntl, bcol, ksz) in enumerate(segs):
                          last_seg = si == nseg - 1
                          if kind == "d":
                              di = cnt["kd"] % 2
                              cnt["kd"] += 1
                              KD, TKD, VD, TVD = kd[di], Tkd[di], vd[di], Tvd[di]
                              loads = [(KD[0:HD, 0:ntl * 128], kt_src(h, t0, ntl), Tk, TKD),
                                       (VD[:, 0:ntl, :], vv_src(h, t0, ntl), Tv, TVD)]
                              chunks = [(t0, ntl, KD, TKD, VD, TVD, 128, loads)]
                          else:
                              chunks = []
                              for c0 in range(t0, t0 + ntl, KC):
                                  cn = min(KC, t0 + ntl - c0)
                                  bi = cnt["kb"] % NKB
                                  cnt["kb"] += 1
                                  KB, TKB, VB, TVB = kb[bi], Tkb[bi], vb[bi], Tvb[bi]
                                  lastc = (c0 + cn == t0 + ntl)
                                  kz_l = ksz if lastc else 128
                                  loads = [(KB[0:HD, 0:(cn - 1) * 128 + kz_l], kt_src(h, c0, cn, kz_l), Tk, TKB)]
                                  if kz_l == 128:
                                      loads.append((VB[:, 0:cn, :], vv_src(h, c0, cn), Tv, TVB))
                                  else:
                                      if cn > 1:
                                          loads.append((VB[:, 0:cn - 1, :], vv_src(h, c0, cn - 1), Tv, TVB))
                                      loads.append((VB[0:kz_l, cn - 1:cn, :], vv_src(h, c0 + cn - 1, 1, kz_l), Tv, TVB))
                                  chunks.append((c0, cn, KB, TKB, VB, TVB, HD, loads))
                          for ci_, (c0, cn, KB, TKB, VB, TVB, KR, loads) in enumerate(chunks):
                              for j in range(cn):
                                  lastt = (c0 + j == t0 + ntl - 1)
                                  kz = ksz if lastt else 128
                                  cs0 = j * 128 if kind == "d" else 0
                                  items.append(dict(h=h, PO=PO, TPO=TPO, KB=KB, TKB=TKB, VB=VB, TVB=TVB, KR=KR, j=j, kz=kz,
                                                    cs0=cs0, bcol=bcol, first=first, last=(last_seg and lastt),
                                                    loads=(loads if j == 0 else None), key=(h, si, ci_, kind)))
                                  first = False
                  units = []
                  i = 0
                  while i < len(items):
                      a = items[i]
                      if (i + 1 < len(items) and a["key"][3] == "n" and items[i + 1]["key"] == a["key"]
                              and a["kz"] == 128 and items[i + 1]["kz"] == 128):
                          units.append([a, items[i + 1]])
                          i += 2
                      else:
                          units.append([a])
                          i += 1

                  def emit_qk(u):
                      pi = cnt["pb"] % 2
                      cnt["pb"] += 1
                      PSp = pSS[:, pi * 1024:(pi + 1) * 1024].rearrange("p (i c) -> p i c", i=2)
                      PBp = pb[pi][:, :].rearrange("p (i c) -> p i c", i=2)
                      for i, it in enumerate(u):
                          if it["loads"]:
                              for (dst, src, Tsrc, Tdst) in it["loads"]:
                                  sp.dma(dst, src, reads=[Tsrc], writes=[Tdst])
                          it["PS"], it["PB"], it["TPS"], it["TPB"] = PSp, PBp, TpSS[pi], Tpb[pi]
                          pe.op(lambda e, it=it, i=i: e.matmul(
                              PSp[0:it["kz"], i, it["cs0"]:ncols],
                              lhsT=it["KB"][0:it["KR"], it["j"] * 128:it["j"] * 128 + it["kz"]],
                              rhs=qTa[0:it["KR"], it["h"], qc0 + it["cs0"]:qc0 + ncols], start=True, stop=True),
                              reads=[it["TKB"], TqTa], writes=[TpSS[pi]])

                  def emit_rest(u):
                      a = u[0]
                      kz, cs0, n = a["kz"], a["cs0"], len(u)
                      PSp, PBp, TPS, TPB = a["PS"], a["PB"], a["TPS"], a["TPB"]
                      bias_ap = c_zero[0:kz, 0:1] if a["bcol"] is None else c_bias[0:kz, a["bcol"]:a["bcol"] + 1]
                      act.op(lambda e: e.activation(out=PBp[0:kz, 0:n, cs0:ncols], in_=PSp[0:kz, 0:n, cs0:ncols],
                                                    func=AF.Exp, bias=bias_ap, scale=SCALE),
                             reads=[TPS, Tc], writes=[TPB])
                      for i, it in enumerate(u):
                          pe.op(lambda e, it=it, i=i: e.matmul(it["PO"][:, cs0:ncols], lhsT=it["VB"][0:kz, it["j"], :],
                                                               rhs=PBp[0:kz, i, cs0:ncols], start=it["first"], stop=it["last"]),
                                reads=[it["TVB"], TPB], writes=[it["TPO"]])
                          if it["last"]:
                              PO, TPO, h = it["PO"], it["TPO"], it["h"]
                              dve.op(lambda e, PO=PO: e.tensor_scalar(out=rcs[64:128, 0:ncols], in0=PO[64:128, 0:ncols],
                                                                      scalar1=1e-30, scalar2=None, op0=ALU.add),
                                     reads=[TPO], writes=[Trcs])
                              dve.op(lambda e: e.reciprocal(out=rcs[64:128, 0:ncols], in_=rcs[64:128, 0:ncols]),
                                     reads=[Trcs], writes=[Trcs])
                              dve.op(lambda e: e.tensor_copy(out=rcp[0:64, 0:ncols], in_=rcs[64:128, 0:ncols]),
                                     reads=[Trcs], writes=[Trcp])
                              dve.op(lambda e, PO=PO, h=h: e.tensor_tensor(out=attT[:, h, qc0:qc0 + ncols], in0=PO[0:64, 0:ncols],
                                                                           in1=rcp[0:64, 0:ncols], op=ALU.mult),
                                     reads=[TPO, Trcp], writes=[TattT])
                  nu = len(units)
                  if nu:
                      emit_qk(units[0])
                  for i in range(nu):
                      if i + 1 < nu:
                          emit_qk(units[i + 1])
                      emit_rest(units[i])
                      if side is not None:
                          next(side, None)
                  if side is not None:
                      for _ in side:
                          pass

              def attention_halo(segs, side=None):
                  tiles = []
                  for (kind, t0, ntl, bcol, ksz) in segs:
                      for c0 in range(t0, t0 + ntl, 2):
                          cn = min(2, t0 + ntl - c0)
                          for j in range(cn):
                              tiles.append((c0, cn, j, bcol))
                  nt_ = len(tiles)
                  st_ = {}

                  def emit_qk(n):
                      c0, cn, j, bcol = tiles[n]
                      if j == 0:
                          bi = cnt["kb"] % NKB
                          cnt["kb"] += 1
                          KB3 = kb[bi][0:HD, 0:H * 256].rearrange("p (h t) -> p h t", h=H)
                          VB4 = vb[bi][:, 0:H * 2, :].rearrange("p (h s) c -> p h s c", h=H)
                          sp.dma(KB3[:, :, 0:cn * 128], KT[:, :, c0 * 128:(c0 + cn) * 128].rearrange("h d t -> d h t"),
                                 reads=[TKT], writes=[Tkb[bi]])
                          sp.dma(VB4[:, :, 0:cn, :], VV[:, :, c0:c0 + cn, :].rearrange("h p s c -> p h s c"),
                                 reads=[TVV], writes=[Tvb[bi]])
                          st_["cur"] = (KB3, VB4, Tkb[bi], Tvb[bi])
                      KB3, VB4, TKB, TVB = st_["cur"]
                      pi = cnt["pb"] % 2
                      cnt["pb"] += 1
                      PSp = pSS[:, pi * 1024:(pi + 1) * 1024]
                      for h in range(H):
                          pe.op(lambda e, h=h: e.matmul(PSp[:, h * 64:(h + 1) * 64], lhsT=KB3[:, h, j * 128:(j + 1) * 128],
                                                        rhs=qTa[0:HD, h, 64:128], start=True, stop=True,
                                                        skip_group_check=True),
                                reads=[TKB, TqTa], writes=[TpSS[pi]])
                      st_[n] = (PSp, pb[pi], TpSS[pi], Tpb[pi], VB4, TVB, j, bcol)

                  def emit_rest(n):
                      PSp, PB, TPS, TPB, VB4, TVB, j, bcol = st_.pop(n)
                      bias_ap = c_zero[:, 0:1] if bcol is None else c_bias[:, bcol:bcol + 1]
                      act.op(lambda e: e.activation(out=PB[:, 0:512], in_=PSp[:, 0:512], func=AF.Exp, bias=bias_ap, scale=SCALE),
                             reads=[TPS, Tc], writes=[TPB])
                      for h in range(H):
                          pe.op(lambda e, h=h: e.matmul(pO[0][:, h * 64:(h + 1) * 64], lhsT=VB4[:, h, j, :],
                                                        rhs=PB[:, h * 64:(h + 1) * 64],
                                                        start=(n == 0 and h == 0), stop=(n == nt_ - 1),
                                                        skip_group_check=True),
                                reads=[TVB, TPB], writes=[TpO[0]])
                  if nt_:
                      emit_qk(0)
                  for n in range(nt_):
                      if n + 1 < nt_:
                          emit_qk(n + 1)
                      emit_rest(n)
                      if side is not None:
                          next(side, None)
                          next(side, None)
                  if side is not None:
                      for _ in side:
                          pass
                  dve.op(lambda e: e.tensor_scalar(out=rcs[64:128, 0:512], in0=pO[0][64:128, :], scalar1=1e-30,
                                                   scalar2=None, op0=ALU.add), reads=[TpO[0]], writes=[Trcs])
                  dve.op(lambda e: e.reciprocal(out=rcs[64:128, 0:512], in_=rcs[64:128, 0:512]), reads=[Trcs], writes=[Trcs])
                  dve.op(lambda e: e.tensor_copy(out=rcp[0:64, 0:512], in_=rcs[64:128, 0:512]), reads=[Trcs], writes=[Trcp])
                  dve.op(lambda e: e.tensor_tensor(
                      out=attT[:, :, 64:128], in0=pO[0][0:64, :].rearrange("p (h t) -> p h t", h=H),
                      in1=rcp[0:64, 0:512].rearrange("p (h t) -> p h t", h=H), op=ALU.mult),
                      reads=[TpO[0], Trcp], writes=[TattT])

              def out_proj_g(aT_, TaT_, cT_, TcT_, src_rows, col, x1_row, bank=0):
                  bi = cnt["x1"] % 2
                  cnt["x1"] += 1
                  PB_, TPB_ = pM[bank], TpM[bank]
                  sp.dma(xr[bi][:], src_rows, writes=[Txr[bi]])
                  for nb in range(2):
                      for h in range(H):
                          pe.op(lambda e, nb=nb, h=h: e.matmul(PB_[:, :], lhsT=aT_[:, h, col:col + 128],
                                                               rhs=wA_oa[:, h, nb * 512:(nb + 1) * 512],
                                                               start=(h == 0), stop=False),
                                reads=[TaT_, Tw], writes=[TPB_])
                      for c in range(4):
                          pe.op(lambda e, nb=nb, c=c: e.matmul(PB_[:, :], lhsT=cT_[:, c, col:col + 128],
                                                               rhs=wA_oc[:, c, nb * 512:(nb + 1) * 512],
                                                               start=False, stop=(c == 3)),
                                reads=[TcT_, Tw], writes=[TPB_])
                      dve.op(lambda e, nb=nb: e.tensor_tensor(out=x1o[bi][:, nb * 512:(nb + 1) * 512], in0=PB_[:, :],
                                                              in1=xr[bi][:, nb * 512:(nb + 1) * 512], op=ALU.add),
                             reads=[TPB_, Txr[bi]], writes=[Tx1o[bi]])
                      yield
                  sp.dma(X1[x1_row * 128:(x1_row + 1) * 128, :], x1o[bi][:], reads=[Tx1o[bi]], writes=[TX1])
                  yield

              def out_proj(src_rows, col, x1_row):
                  for _ in out_proj_g(attT, TattT, cTg, TcTg, src_rows, col, x1_row, bank=0):
                      pass

              def emit_conv_state(ub, Tub, col0, dst):
                  for c in range(4):
                      pe.op(lambda e, c=c: e.transpose(out=pM[2][0:32, c * 128:(c + 1) * 128], in_=ub[:, c, col0:col0 + 32],
                                                       identity=c_idf[:]), reads=[Tub, Tc], writes=[TpM[2]])
                  dve.op(lambda e: e.tensor_copy(out=cvo[:], in_=pM[2][0:32, :]), reads=[TpM[2]], writes=[Tcvo])
                  sp.dma(dst, cvo[:], reads=[Tcvo], writes=[Tout])

              def kt_p(h, t0, n, ksz=128):
                  return KT[h, :, t0 * 128:(t0 + n - 1) * 128 + ksz]

              def vv_p(h, t0, n, ksz=128):
                  return VV[h, 0:ksz, t0:t0 + n, :]


              groups = []
              for t in range(NSLOT):
                  groups.append((t, None))
                  for g in range(RT // QG):
                      groups.append((t, g))

              def load_group(n):
                  t, g = groups[n]
                  b = n % 2
                  ci0 = t * (RT + 1) + (0 if g is None else 1 + g * QG)
                  ntl = 1 if g is None else QG
                  sp.dma(qTas[b][0:HD, :, 0:ntl * 128], QT[:, :, ci0 * 128:(ci0 + ntl) * 128], reads=[TQT], writes=[TqTas[b]])
                  sp.dma(uTs[b][:, :, 0:CK - 1 + ntl * 128],
                         UT[:, :, 32 + ci0 * 128 - (CK - 1):32 + (ci0 + ntl) * 128], reads=[TUT], writes=[TuTs[b]])
              def conv_of(n):
                  ncol_ = 128 if groups[n][1] is None else QG * 128
                  return conv_module_g(uTs[n % 2], TuTs[n % 2], cTgs[n % 2], TcTgs[n % 2], CK - 1, ncol_, 0, bank=0)

              def outproj_of(n):
                  t, g = groups[n]
                  b = n % 2
                  if g is None:
                      yield from out_proj_g(attTs[b], TattTs[b], cTgs[b], TcTgs[b], x_halo[t * 128:(t + 1) * 128, :], 0,
                                            t * (RT + 1), bank=0)
                  else:
                      for i in range(QG):
                          lt = t * G + g * QG + i
                          yield from out_proj_g(attTs[b], TattTs[b], cTgs[b], TcTgs[b], x_all[lt * 128:(lt + 1) * 128, :],
                                                i * 128, t * (RT + 1) + 1 + g * QG + i, bank=0)

              def chain(*gens):
                  for g_ in gens:
                      if g_ is not None:
                          yield from g_
              load_group(0)
              for _ in conv_of(0):
                  pass
              for n, (t, g) in enumerate(groups):
                  base = t * G
                  qTa, TqTa, uT, TuT = qTas[n % 2], TqTas[n % 2], uTs[n % 2], TuTs[n % 2]
                  attT, TattT = attTs[n % 2], TattTs[n % 2]
                  nxt = None
                  if n + 1 < len(groups):
                      load_group(n + 1)
                      nxt = conv_of(n + 1)
                  side = chain(outproj_of(n - 1) if n > 0 else None, nxt)
                  if g is None:
                      segs = []
                      if t > 0:
                          segs.append(("n", 0, base, None, 128))
                      for r in range(1, NCPB):
                          segs.append(("n", base + r * RT, RT, r, 128))
                      attention_halo(segs, side=side)
                  else:
                      i0 = g * QG
                      segs = []
                      if base + i0 > 0:
                          segs.append(("n", 0, base + i0, None, 128))
                      segs.append(("d", base + i0, QG, None, 128))
                      for r in range(1, NCPB):
                          segs.append(("n", base + r * RT, RT, r, 128))
                      attention(QG * 128, segs, kt_p, vv_p, TKT, TVV, side=side)
                      if t == NSLOT - 1 and g == RT // QG - 1:
                          emit_conv_state(uT, TuT, UW - 32, o_conv[:, :])
              for _ in outproj_of(len(groups) - 1):
                  pass
              attT, TattT = attTs[0], TattTs[0]
              qTa, TqTa, uT, TuT = qTas[0], TqTas[0], uTs[0], TuTs[0]
              cTg, TcTg = cTgs[0], TcTgs[0]

              ckpt("p2")
              uS = [sb(stA, f"uS{i}", [128, 4, CK - 1 + 64], F32) for i in range(2)]
              TuS = [T(f"uS{i}") for i in range(2)]
              for e_ in range(2):
                  sp.dma(uS[e_][:, :, 0:CK - 1], sconvT[:, e_, :, :], writes=[TuS[e_]])
              run_ways([0], lambda _, W: tile_front_A_g(W, x_s[:, :], c_ropesn, Tc, 0,
                                                        [(0, 64, (uS[0], TuS[0], CK - 1)), (64, 64, (uS[1], TuS[1], CK - 1))]))
              for e_ in range(2):
                  dve.op(lambda e, e_=e_: e.tensor_copy(out=uT[:, :, 0:CK - 1 + 64], in_=uS[e_][:, :, :]),
                         reads=[TuS[e_]], writes=[TuT])
                  conv_module(CK - 1, 64, e_ * 64)
                  emit_conv_state(uS[e_], TuS[e_], CK - 1 + 64 - 32, o_convs[e_, :, :])

                  def kt_s(h, t0, n, ksz=128, e_=e_):
                      return KTs[e_, h, :, t0 * 128:(t0 + n - 1) * 128 + ksz]

                  def vv_s(h, t0, n, ksz=128, e_=e_):
                      return VVs[e_, h, 0:ksz, t0:t0 + n, :]
                  attention(64, [("n", 0, PT + 1, None, 64)], kt_s, vv_s, TKTs, TVVs, qc0=e_ * 64)
              out_proj(x_s[:, :], 0, NX1 - 1)
              tk.barrier()
              ckpt("p2s")

          with ExitStack() as stB:
              wB_up = sb(stB, "wB_up", [128, 8, 2 * DFF], BF16)
              wB_dn = sb(stB, "wB_dn", [128, NFC, D], BF16)
              with ExitStack() as stW:
                  ssB = make_stg(stW, "B")
                  load_weight(ssB, wB_up, w_up, 8, 2 * DFF, c_gffn, name="up")
                  load_weight(ssB, wB_dn, w_down, NFC, D, None, name="dn")
                  tk.barrier()
              QB = QG
              NB_ = QB * 128
              xtB = sb(stB, "xtB", [128, QB, D], F32)
              TxtB = [T(f"xtB{j}") for j in range(QB)]
              msB = sb(stB, "msB", [128, QB], F32)
              TmsB = T("msB")
              xnB = sb(stB, "xnB", [128, D], BF16)
              TxnB = T("xnB")
              xnTB = sb(stB, "xnTB", [128, 8, NB_], BF16)
              TxnTB = T("xnTB")
              hT = sb(stB, "hT", [128, NFC, NB_], BF16)
              ThT = T("hT")
              AW = NB_ + 8
              aTc = [sb(stB, f"aTc{i}", [128, AW], F32) for i in range(2)]
              TaTc = [T(f"aTc{i}") for i in range(2)]
              accB = [sb(stB, f"accB{i}", [128, AW], F32) for i in range(2)]
              TaccB = [T(f"accB{i}") for i in range(2)]
              yo = [sb(stB, f"yo{i}", [128, D], F32) for i in range(2)]
              Tyo = [T(f"yo{i}") for i in range(2)]
              arow = [sb(stB, f"arow{i}", [128, 512], F32) for i in range(2)]
              Tarow = [T(f"arow{i}") for i in range(2)]
              hist = sb(stB, "hist", [128, NFC, 2], F32)
              Thist = T("hist")
              hnew = sb(stB, "hnew", [128, NFC, 2], F32)
              Thnew = T("hnew")
              Toutb = T("outsB", dram=True)
              Abank = [(pM[0], TpM[0]), (pS[3], TpS[3])]
              Gbank = [(pS[0], TpS[0]), (pS[1], TpS[1])]
              Dbank = [(pO[0], TpO[0]), (pO[1], TpO[1])]
              cb_ = dict(ab=0, gb=0, db=0, yo=0, ar=0)

              def ffn_batch(x1_rows, subs, outs, a_rows_dst=None):
                  nt = len(x1_rows)
                  N = nt * 128
                  halo = outs is None
                  for j, r in enumerate(x1_rows):
                      sp.dma(xtB[:, j, :], X1[r * 128:(r + 1) * 128, :], writes=[TxtB[j]])
                      act.op(lambda e, j=j: e.activation(out=xnB[:], in_=xtB[:, j, :], func=AF.Square, scale=1.0 / math.sqrt(D),
                                                         accum_out=msB[:, j:j + 1]), reads=[TxtB[j]], writes=[TxnB, TmsB])
                  rstd_from_msq(None, (msB[:, 0:nt], TmsB), nt)
                  for j in range(nt):
                      dve.op(lambda e, j=j: e.tensor_scalar(out=xnB[:], in0=xtB[:, j, :], scalar1=msB[:, j:j + 1], scalar2=None,
                                                            op0=ALU.mult), reads=[TxtB[j], TmsB], writes=[TxnB])
                      for k in range(8):
                          pe.op(lambda e, k=k: e.transpose(out=pT[:, k * 128:(k + 1) * 128], in_=xnB[:, k * 128:(k + 1) * 128],
                                                           identity=c_idb[:]), reads=[TxnB, Tc], writes=[TpT])
                      act.op(lambda e, j=j: e.activation(out=xnTB[:, :, j * 128:(j + 1) * 128],
                                                         in_=pT[:, :].rearrange("p (k t) -> p k t", k=8), func=AF.Copy),
                             reads=[TpT], writes=[TxnTB])
                  W_ = N + 2 * len(subs)
                  state = {}

                  def stage1(c):
                      ai = cb_["ab"] % 2
                      cb_["ab"] += 1
                      PA, TPA = Abank[ai]
                      AT, TAT, AC, TAC = aTc[ai], TaTc[ai], accB[ai], TaccB[ai]
                      for k in range(8):
                          pe.op(lambda e, k=k: e.matmul(PA[:, 0:N], lhsT=wB_up[:, k, c * 128:(c + 1) * 128], rhs=xnTB[:, k, 0:N],
                                                        start=(k == 0), stop=(k == 7)), reads=[TxnTB, Tw], writes=[TPA])
                      if not halo:
                          gi = cb_["gb"] % 2
                          cb_["gb"] += 1
                          PG, TPG = Gbank[gi]
                          col = DFF + c * 128
                          for k in range(8):
                              pe.op(lambda e, k=k: e.matmul(PG[:, 0:N], lhsT=wB_up[:, k, col:col + 128], rhs=xnTB[:, k, 0:N],
                                                            start=(k == 0), stop=(k == 7)), reads=[TxnTB, Tw], writes=[TPG])
                      else:
                          PG = TPG = None
                      for i, (tok0, ntok, hap, Th) in enumerate(subs):
                          w0 = tok0 + 2 * i
                          act.op(lambda e, w0=w0, tok0=tok0, ntok=ntok: e.activation(
                              out=AT[:, w0 + 2:w0 + 2 + ntok], in_=PA[:, tok0:tok0 + ntok], func=AF.Copy),
                              reads=[TPA], writes=[TAT])
                          if not halo:
                              act.op(lambda e, w0=w0, tok0=tok0, ntok=ntok: e.activation(
                                  out=AC[:, w0:w0 + ntok], in_=PA[:, tok0:tok0 + ntok], func=AF.Identity,
                                  scale=c_fw[:, c, 2:3], bias=c_fb[:, c:c + 1]), reads=[TPA, Tc], writes=[TAC])
                              dve.op(lambda e, w0=w0, hap=hap: e.tensor_copy(out=AT[:, w0:w0 + 2], in_=hap[:, c, :]),
                                     reads=[Th], writes=[TAT])
                          if i == len(subs) - 1:
                              dve.op(lambda e, w0=w0, ntok=ntok: e.tensor_copy(out=hnew[:, c, :], in_=AT[:, w0 + ntok:w0 + ntok + 2]),
                                     reads=[TAT], writes=[Thnew])
                      state[c] = (AT, TAT, AC, TAC, PG, TPG)

                  def stage2(c):
                      AT, TAT, AC, TAC, PG, TPG = state.pop(c)
                      si = cb_["ab"] % 2
                      SL, TSL = AC, TAC
                      dve.op(lambda e: e.scalar_tensor_tensor(out=AC[:, 0:W_ - 2], in0=AT[:, 0:W_ - 2], scalar=c_fw[:, c, 0:1],
                                                              in1=AC[:, 0:W_ - 2], op0=ALU.mult, op1=ALU.add),
                             reads=[TAT, TAC, Tc], writes=[TAC])
                      dve.op(lambda e: e.scalar_tensor_tensor(out=AC[:, 0:W_ - 2], in0=AT[:, 1:W_ - 1], scalar=c_fw[:, c, 1:2],
                                                              in1=AC[:, 0:W_ - 2], op0=ALU.mult, op1=ALU.add),
                             reads=[TAT, TAC, Tc], writes=[TAC])
                      act.op(lambda e: e.activation(out=SL[:, 0:W_ - 2], in_=AC[:, 0:W_ - 2], func=AF.Silu),
                             reads=[TAC], writes=[TSL])
                      for i, (tok0, ntok, hap, Th) in enumerate(subs):
                          w0 = tok0 + 2 * i
                          dve.op(lambda e, w0=w0, tok0=tok0, ntok=ntok: e.tensor_tensor(
                              out=hT[:, c, tok0:tok0 + ntok], in0=PG[:, tok0:tok0 + ntok], in1=SL[:, w0:w0 + ntok], op=ALU.mult),
                              reads=[TPG, TSL], writes=[ThT])

                  if len(subs) > 1:
                      for i in range(2):
                          dve.op(lambda e, i=i: e.memset(accB[i][:], 0.0), writes=[TaccB[i]])
                  for c in range(NFC + 1):
                      if c < NFC:
                          stage1(c)
                      if c >= 1 and not halo:
                          stage2(c - 1)
                  dve.op(lambda e: e.tensor_copy(out=hist[:], in_=hnew[:]), reads=[Thnew], writes=[Thist])
                  if halo:
                      return
                  if a_rows_dst is not None:
                      jl = nt - 1
                      for n0 in range(0, DFF, 512):
                          nw = min(512, DFF - n0)
                          PA, TPA = pS[2], TpS[2]
                          ri = cb_["ar"] % 2
                          cb_["ar"] += 1
                          for k in range(8):
                              pe.op(lambda e, k=k, n0=n0, nw=nw: e.matmul(
                                  PA[:, 0:nw], lhsT=xnTB[:, k, jl * 128:(jl + 1) * 128], rhs=wB_up[:, k, n0:n0 + nw],
                                  start=(k == 0), stop=(k == 7)), reads=[TxnTB, Tw], writes=[TPA])
                          dve.op(lambda e, nw=nw, ri=ri: e.tensor_copy(out=arow[ri][:, 0:nw], in_=PA[:, 0:nw]),
                                 reads=[TPA], writes=[Tarow[ri]])
                          for (r0, dst) in a_rows_dst:
                              sp.dma(dst[:, n0:n0 + nw], arow[ri][r0:r0 + 32, 0:nw], reads=[Tarow[ri]], writes=[Toutb])
                  for j in range(nt):
                      yi = cb_["yo"] % 2
                      cb_["yo"] += 1
                      for nb in range(2):
                          PD, TPD = Dbank[cb_["db"] % 2]
                          cb_["db"] += 1
                          for c in range(NFC):
                              pe.op(lambda e, nb=nb, c=c, j=j, PD=PD: e.matmul(
                                  PD[:, :], lhsT=hT[:, c, j * 128:(j + 1) * 128], rhs=wB_dn[:, c, nb * 512:(nb + 1) * 512],
                                  start=(c == 0), stop=(c == NFC - 1)), reads=[ThT, Tw], writes=[TPD])
                          dve.op(lambda e, nb=nb, j=j, PD=PD, yi=yi: e.tensor_tensor(
                              out=yo[yi][:, nb * 512:(nb + 1) * 512], in0=PD[:, :], in1=xtB[:, j, nb * 512:(nb + 1) * 512],
                              op=ALU.add), reads=[TPD, TxtB[j]], writes=[Tyo[yi]])
                      sp.dma(outs[j], yo[yi][:], reads=[Tyo[yi]], writes=[Toutb])

              for t in range(NSLOT):
                  r0 = t * (RT + 1)
                  ffn_batch([r0], [(0, 128, None, None)], None)
                  dve.op(lambda e, t=t: e.tensor_scalar(out=hist[:], in0=hist[:], scalar1=c_hflag[:, t:t + 1],
                                                        scalar2=None, op0=ALU.mult), reads=[Thist, Tc], writes=[Thist])
                  for i0 in range(0, RT, QB):
                      last = (t == NSLOT - 1 and i0 + QB == RT)
                      ffn_batch([r0 + 1 + i0 + i for i in range(QB)], [(0, NB_, hist, Thist)],
                                [o_y[(t * RT + i0 + i) * 128:(t * RT + i0 + i + 1) * 128, :] for i in range(QB)],
                                a_rows_dst=[(96, o_ffn)] if last else None)
              hs = [sb(stB, f"hs{i}", [128, NFC, 2], F32) for i in range(2)]
              Ths = [T(f"hs{i}") for i in range(2)]
              for e_ in range(2):
                  sp.dma(hs[e_][:], sffnT[:, e_, :, :], writes=[Ths[e_]])
              ffn_batch([NX1 - 1], [(0, 64, hs[0], Ths[0]), (64, 64, hs[1], Ths[1])], [o_ys[:, :]],
                        a_rows_dst=[(32, o_ffns[0]), (96, o_ffns[1])])
              tk.barrier()
          print(f"[build] sems={tk.nsem} waits={tk.nwaits} insts={tk.ninst}", flush=True)
    except _Stop:
        pass
    return nc


def _rope_tab(pos):
    inv = (1.0 / (10000.0 ** (np.arange(0, RD, 2, dtype=np.float32) / np.float32(RD)))).astype(np.float32)
    ang = pos.astype(np.float32)[:, None] * inv[None, :]
    return np.concatenate([np.cos(ang.astype(np.float64)), np.sin(ang.astype(np.float64))], axis=1).astype(np.float32)


def _run(inputs, cfg):
    SEQ, PAST, RT = cfg["SEQ"], cfg["PAST"], cfg["RT"]
    NT = SEQ // 128
    G = NCPB * RT
    NSLOT = NT // G
    QG = min(4, RT)
    PT = PAST // 128
    f32 = np.float32
    bf = ml_dtypes.bfloat16
    g = {k: np.asarray(v) for k, v in inputs.items()}
    xp, xs = g["x_prompt"], g["x_sample"]
    B = xp.shape[0]
    assert B * NCPB == 8 and xs.shape[0] == 16

    def chunked(v, n):
        return np.ascontiguousarray(v.reshape(n, 128).T).astype(f32)

    def bc(v):
        return np.ascontiguousarray(np.broadcast_to(v[None, :], (128, v.shape[0]))).astype(f32)
    common = {
        "w_in": g["w_in"][0], "w_uq": g["w_uq"][0], "w_ukv": g["w_ukv"][0], "w_out": g["w_out"][0],
        "w_up": g["w_up"][0], "w_down": g["w_down"][0],
        "g_attn": chunked(g["attn_norm"][0], 8), "g_q": chunked(g["q_norm"][0], 3),
        "g_ffn": chunked(g["ffn_norm"][0], 8), "g_kv": bc(g["kv_norm"][0]),
        "g_hq": bc(g["qk_norm_q"][0]), "g_hk": bc(g["qk_norm_k"][0]),
        "cw": np.ascontiguousarray(g["conv_w"][0].T.reshape(4, 128, CK).transpose(1, 0, 2)),
        "cb": chunked(g["conv_b"][0], 4), "cg": chunked(g["conv_norm"][0], 4),
        "fw": np.ascontiguousarray(g["ffn_conv_w"][0].T.reshape(NFC, 128, 3).transpose(1, 0, 2)),
        "fb": chunked(g["ffn_conv_b"][0], NFC),
        "identb": np.eye(128, dtype=f32).astype(bf), "identf": np.eye(128, dtype=f32),
        "onesb": np.ones((128, 128), f32).astype(bf),
    }
    nch = QG * 2
    kh = np.zeros((32, QG * 128), f32)
    qm = np.zeros((32, QG * 128), f32)
    for c in range(nch):
        kh[c, c * 64:(c + 1) * 64] = 1.0
        qm[c, :c * 64] = NEG
    common["khot"] = kh.astype(bf)
    common["qmask"] = qm.astype(bf)
    common["rope_sp"] = np.ascontiguousarray(
        _rope_tab(np.arange(max(PT, 1) * 128)).reshape(max(PT, 1), 128, 32).transpose(1, 0, 2))
    common["rope_sn"] = _rope_tab(PAST + (np.arange(128) % 64))

    in_maps, metas = [], []
    for core in range(8):
        b, j = divmod(core, NCPB)
        others = [r for r in range(NCPB) if r != j]
        order = [j] + others
        gt = np.array([t * G + order[r] * RT + i for t in range(NSLOT) for r in range(NCPB) for i in range(RT)])
        tok = (gt[:, None] * 128 + np.arange(128)[None, :]).reshape(-1)
        m = dict(common)
        m["x_all"] = np.ascontiguousarray(xp[b][tok])
        xh = np.zeros((NSLOT, 128, D), f32)
        hf = np.zeros((128, NSLOT), f32)
        rh = np.zeros((NSLOT, 128, 32), f32)
        for t in range(NSLOT):
            ht = t * G + j * RT - 1
            if ht >= 0:
                xh[t] = xp[b, ht * 128:(ht + 1) * 128]
                hf[:, t] = 1.0
                rh[t] = _rope_tab(ht * 128 + np.arange(128))
        m["x_halo"] = xh.reshape(NSLOT * 128, D)
        m["hflag"] = hf
        m["rope_h"] = np.ascontiguousarray(rh.transpose(1, 0, 2))
        m["rope_k"] = np.ascontiguousarray(_rope_tab(tok).reshape(NT, 128, 32).transpose(1, 0, 2))
        bt = np.zeros((128, NCPB), f32)
        for r in range(1, NCPB):
            bt[:, r] = 0.0 if others[r - 1] < j else NEG
        m["biast"] = bt
        e0 = 2 * core
        m["x_s"] = np.ascontiguousarray(xs[e0:e0 + 2].reshape(128, D))
        m["cckv"] = np.ascontiguousarray(g["cache_ckv"][0, e0:e0 + 2].reshape(2 * PAST, KVL))
        m["ckpe"] = np.ascontiguousarray(g["cache_kpe"][0, e0:e0 + 2].reshape(2 * PAST, RD))
        sc = g["state_conv"][0, e0:e0 + 2]
        m["sconvT"] = np.ascontiguousarray(sc.transpose(2, 0, 1).reshape(4, 128, 2, CK - 1).transpose(1, 2, 0, 3))
        sf = g["state_ffn_conv"][0, e0:e0 + 2]
        m["sffnT"] = np.ascontiguousarray(sf.transpose(2, 0, 1).reshape(NFC, 128, 2, 2).transpose(1, 2, 0, 3))
        in_maps.append(m)
        own_tok = (np.array([t * G + j * RT + i for t in range(NSLOT) for i in range(RT)])[:, None] * 128
                   + np.arange(128)[None, :]).reshape(-1)
        metas.append((b, j, own_tok))

    nc = build(cfg)
    res = run_bass_kernel_spmd(nc, in_maps, core_ids=list(range(8)))
    R = res.results

    y_p = np.zeros((B, SEQ, D), f32)
    ckv_p = np.zeros((1, B, SEQ, KVL), f32)
    kpe_p = np.zeros((1, B, SEQ, RD), f32)
    conv_p = np.zeros((1, B, CK - 1, CC), f32)
    ffn_p = np.zeros((1, B, 2, DFF), f32)
    y_s = np.zeros((16, 64, D), f32)
    ckv_s = np.zeros((1, 16, 64, KVL), f32)
    kpe_s = np.zeros((1, 16, 64, RD), f32)
    conv_s = np.zeros((1, 16, CK - 1, CC), f32)
    ffn_s = np.zeros((1, 16, 2, DFF), f32)
    for core in range(8):
        b, j, own_tok = metas[core]
        r = R[core]
        y_p[b, own_tok] = r["o_y"]
        ckv_p[0, b, own_tok] = r["o_ckv"]
        kpe_p[0, b, own_tok] = r["o_kpe"]
        if j == NCPB - 1:
            conv_p[0, b] = r["o_conv"][2:32]
            ffn_p[0, b] = r["o_ffn"][30:32]
        e0 = 2 * core
        y_s[e0:e0 + 2] = r["o_ys"].reshape(2, 64, D)
        ckv_s[0, e0:e0 + 2] = r["o_ckvs"].reshape(2, 64, KVL)
        kpe_s[0, e0:e0 + 2] = r["o_kpes"].reshape(2, 64, RD)
        conv_s[0, e0:e0 + 2] = r["o_convs"][:, 2:32]
        ffn_s[0, e0:e0 + 2] = r["o_ffns"][:, 30:32]
    return (y_p, y_s, ckv_p, kpe_p, conv_p, ffn_p, ckv_s, kpe_s, conv_s, ffn_s)


def kernel(**inputs):
    return _run(inputs, CFG_FULL)
```

```python
import math
from contextlib import ExitStack

import numpy as np
import ml_dtypes

import concourse.bass as bass
import concourse.mybir as mybir
from concourse.bass_utils import run_bass_kernel_spmd

F32 = mybir.dt.float32
BF16 = mybir.dt.bfloat16
ALU = mybir.AluOpType
AF = mybir.ActivationFunctionType
AX = mybir.AxisListType

D = 1024
QL, KVL, RD, CC = 384, 256, 32, 512
H, HD, NOPE, VD = 8, 96, 64, 64
INW = QL + KVL + RD + 2 * CC
DFF = 2816
NFC = DFF // 128
CK = 31
EPS = 1e-6
SCALE = HD ** -0.5
NEG = -30000.0
NCPB = 4
KC = 16

CFG_FULL = dict(SEQ=16384, PAST=2048, RT=8)


class T:
    __slots__ = ("name", "w", "r", "dsem", "dcnt", "excl")

    def __init__(self, name, excl=False):
        self.name = name
        self.excl = excl
        self.w = None
        self.r = {}
        self.dsem = None
        self.dcnt = 0


class Eng:
    ROT = 30000

    def __init__(self, trk, eng, name):
        self.trk, self.eng, self.name = trk, eng, name
        self.sem = trk.new_sem(name)
        self.cnt = 0
        self.seen = {}

    def _wait(self, sem, val):
        if self.seen.get(sem, 0) >= val:
            return
        self.eng.wait_ge(sem, val)
        self.seen[sem] = val
        self.trk.nwaits += 1

    def _deps(self, reads, writes):
        need = {}

        def add(p, same_ok):
            if p is None:
                return
            sem, val = p
            if sem is self.sem and same_ok and self.name == "pe":
                return
            if need.get(sem, 0) < val:
                need[sem] = val
        for t in reads:
            add(t.w, False)
        for t in writes:
            add(t.w, True)
            for sem, val in t.r.items():
                add((sem, val), True)
        for sem, val in need.items():
            self._wait(sem, val)

    def op(self, fn, reads=(), writes=()):
        ex = [t for t in reads if t.excl and t not in writes]
        if ex:
            reads = [t for t in reads if not t.excl or t in writes]
            writes = list(writes) + ex
        self._deps(reads, writes)
        if self.cnt >= self.ROT:
            self.sem = self.trk.new_sem(self.name)
            self.cnt = 0
        inst = fn(self.eng)
        self.cnt += 1
        inst.then_inc(self.sem, 1)
        self.trk.ninst += 1
        for t in reads:
            if t.r.get(self.sem, 0) < self.cnt:
                t.r[self.sem] = self.cnt
        for t in writes:
            t.w = (self.sem, self.cnt)
            t.r = {}
        return inst

    def dma(self, out, in_, reads=(), writes=()):
        self._deps(reads, writes)
        tw = writes[0]
        if tw.dsem is None:
            tw.dsem = self.trk.new_sem("d_" + tw.name)
            self.trk.dts.append(tw)
        inst = self.eng.dma_start(out=out, in_=in_)
        inst.then_inc(tw.dsem, 16)
        tw.dcnt += 16
        self.trk.ninst += 1
        for t in reads:
            if t.r.get(tw.dsem, 0) < tw.dcnt:
                t.r[tw.dsem] = tw.dcnt
        tw.w = (tw.dsem, tw.dcnt)
        tw.r = {}
        return inst

    def wait_for(self, t):
        if t.w is not None:
            self._wait(*t.w)


class Tracker:
    def __init__(self, nc, stack):
        self.nc, self.stack = nc, stack
        self.nsem = 0
        self.nwaits = 0
        self.ninst = 0
        self.dts = []
        self.pe = Eng(self, nc.tensor, "pe")
        self.act = Eng(self, nc.scalar, "act")
        self.dve = Eng(self, nc.vector, "dve")
        self.pool = Eng(self, nc.gpsimd, "pool")
        self.sp = Eng(self, nc.sync, "sp")
        self.engs = [self.pe, self.act, self.dve, self.pool, self.sp]

    def new_sem(self, name):
        self.nsem += 1
        return self.stack.enter_context(self.nc.semaphore(f"s{self.nsem}_{name}"))

    def barrier(self):
        pts = [(e.sem, e.cnt) for e in self.engs if e.cnt > 0]
        pts += [(t.dsem, t.dcnt) for t in self.dts if t.dcnt > 0]
        for e in self.engs:
            for sem, val in pts:
                e._wait(sem, val)


def build(cfg):
    SEQ, PAST, RT = cfg["SEQ"], cfg["PAST"], cfg["RT"]
    NT = SEQ // 128
    G = NCPB * RT
    NSLOT = NT // G
    assert NSLOT * G == NT
    QG = min(4, RT)
    assert RT % QG == 0
    NOWN = NSLOT * RT
    PT = PAST // 128
    NX1 = NSLOT * (RT + 1) + 1

    nc = bass.Bass("TRN2", target_bir_lowering=False)

    def din(name, shape, dt=F32):
        return nc.dram_tensor(name, list(shape), dt, kind="ExternalInput").ap()

    def dout(name, shape, dt=F32):
        return nc.dram_tensor(name, list(shape), dt, kind="ExternalOutput").ap()

    def dscr(name, shape, dt):
        return nc.dram_tensor(name, list(shape), dt, kind="Internal").ap()

    x_all = din("x_all", [NT * 128, D])
    x_halo = din("x_halo", [NSLOT * 128, D])
    x_s = din("x_s", [128, D])
    cckv = din("cckv", [2 * PAST, KVL])
    ckpe = din("ckpe", [2 * PAST, RD])
    sconvT = din("sconvT", [128, 2, 4, CK - 1])
    sffnT = din("sffnT", [128, 2, NFC, 2])
    rope_k = din("rope_k", [128, NT, 32])
    rope_h = din("rope_h", [128, NSLOT, 32])
    rope_sp = din("rope_sp", [128, max(PT, 1), 32])
    rope_sn = din("rope_sn", [128, 32])
    hflag = din("hflag", [128, NSLOT])
    biast = din("biast", [128, NCPB])
    w_in = din("w_in", [D, INW])
    w_uq = din("w_uq", [QL, H * HD])
    w_ukv = din("w_ukv", [KVL, H * 128])
    w_out = din("w_out", [D, D])
    w_up = din("w_up", [D, 2 * DFF])
    w_down = din("w_down", [DFF, D])
    g_attn = din("g_attn", [128, 8])
    g_q = din("g_q", [128, 3])
    g_ffn = din("g_ffn", [128, 8])
    g_kv = din("g_kv", [128, KVL])
    g_hq = din("g_hq", [128, HD])
    g_hk = din("g_hk", [128, HD])
    cw = din("cw", [128, 4, CK])
    cb = din("cb", [128, 4])
    cg = din("cg", [128, 4])
    fw = din("fw", [128, NFC, 3])
    fb = din("fb", [128, NFC])
    identb = din("identb", [128, 128], BF16)
    identf = din("identf", [128, 128])
    onesb = din("onesb", [128, 128], BF16)
    khot = din("khot", [32, QG * 128], BF16)
    qmask = din("qmask", [32, QG * 128], BF16)

    o_y = dout("o_y", [NOWN * 128, D])
    o_ckv = dout("o_ckv", [NOWN * 128, KVL])
    o_kpe = dout("o_kpe", [NOWN * 128, RD])
    o_conv = dout("o_conv", [32, CC])
    o_ffn = dout("o_ffn", [32, DFF])
    o_ys = dout("o_ys", [128, D])
    o_ckvs = dout("o_ckvs", [128, KVL])
    o_kpes = dout("o_kpes", [128, RD])
    o_convs = dout("o_convs", [2, 32, CC])
    o_ffns = dout("o_ffns", [2, 32, DFF])

    KT = dscr("KT", [H, HD, NT * 128], BF16)
    VV = dscr("VV", [H, 128, NT, 128], BF16)
    KTs = dscr("KTs", [2, H, HD, (PT + 1) * 128], BF16)
    VVs = dscr("VVs", [2, H, 128, PT + 1, 128], BF16)
    X1 = dscr("X1", [NX1 * 128, D], F32)
    NCI = NSLOT * (RT + 1)
    QT = dscr("QT", [HD, H, NCI * 128], BF16)
    UT = dscr("UT", [128, 4, 32 + NCI * 128], F32)

    class _Stop(Exception):
        pass

    def ckpt(name):
        if cfg.get("STOP") == name:
            tk.barrier()
            raise _Stop()
    try:
      with ExitStack() as top:
          tk = Tracker(nc, top)
          pe, act, dve, pool, sp = tk.pe, tk.act, tk.dve, tk.pool, tk.sp

          def sb(st, name, shape, dt):
              return st.enter_context(nc.sbuf_tensor(name, list(shape), dt))

          def ps(st, name, shape, dt):
              return st.enter_context(nc.psum_tensor(name, list(shape), dt))

          pSS = ps(top, "pSS", [128, 2048], F32)
          pS = [pSS[:, i * 512:(i + 1) * 512] for i in range(4)]
          TpS = [T(f"pS{i}", excl=True) for i in range(4)]
          TpSS = [T(f"pSS{i}", excl=True) for i in range(2)]
          pOO = ps(top, "pOO", [128, 1024], F32)
          pO = [pOO[:, i * 512:(i + 1) * 512] for i in range(2)]
          TpO = [T(f"pO{i}", excl=True) for i in range(2)]
          pM0 = ps(top, "pM0", [128, 512], F32)
          pM = [pM0[:, :], pO[0], pO[1]]
          TpM = [T("pM0", excl=True), TpO[0], TpO[1]]
          pT = ps(top, "pT", [128, 1024], BF16)
          TpT = T("pT", excl=True)

          cst = {}
          Tc = T("consts")

          def cload(name, src, shape, dt=F32):
              t = sb(top, "c_" + name, shape, dt)
              sp.dma(t[:], src, writes=[Tc])
              cst[name] = t
              return t
          c_idb = cload("idb", identb[:, :], [128, 128], BF16)
          c_idf = cload("idf", identf[:, :], [128, 128])
          c_ones = cload("ones", onesb[:, :], [128, 128], BF16)
          c_gkv = cload("gkv", g_kv[:, :], [128, KVL])
          c_ghq = cload("ghq", g_hq[:, :], [128, HD])
          c_ghk = cload("ghk", g_hk[:, :], [128, HD])
          c_cw = cload("cw", cw[:, :, :], [128, 4, CK])
          c_cb = cload("cb", cb[:, :], [128, 4])
          c_cg = cload("cg", cg[:, :], [128, 4])
          c_fw = cload("fw", fw[:, :, :], [128, NFC, 3])
          c_fb = cload("fb", fb[:, :], [128, NFC])
          c_hflag = cload("hflag", hflag[:, :], [128, NSLOT])
          c_bias = cload("bias", biast[:, :], [128, NCPB])
          c_gattn = cload("gattn", g_attn[:, :], [128, 8])
          c_gq = cload("gq", g_q[:, :], [128, 3])
          c_gffn = cload("gffn", g_ffn[:, :], [128, 8])
          c_ropeh = cload("ropeh", rope_h[:, :, :], [128, NSLOT, 32])
          c_ropesn = cload("ropesn", rope_sn[:, :], [128, 32])
          c_zero = sb(top, "c_zero", [128, 1], F32)
          dve.op(lambda e: e.memset(c_zero[:], 0.0), writes=[Tc])
          c_eps = sb(top, "c_eps", [128, 1], F32)
          dve.op(lambda e: e.memset(c_eps[:], EPS), writes=[Tc])

          WCH = 2048
          NSTG = 4

          def make_stg(st, tag):
              return dict(stg=[sb(st, f"stg_{tag}{i}", [128, WCH], F32) for i in range(NSTG)],
                          T=[T(f"stg_{tag}{i}") for i in range(NSTG)], n=0)

          def load_weight(ss, dst, src2d, nk, ncols, gain, kp=128, name="w"):
              CH = WCH
              stg, Tst = ss["stg"], ss["T"]
              for k in range(nk):
                  for c0 in range(0, ncols, CH):
                      cwid = min(CH, ncols - c0)
                      b = ss["n"] % NSTG
                      ss["n"] += 1
                      n = ss["n"]
                      Tw = T("wchunk")
                      sp.dma(stg[b][0:kp, 0:cwid], src2d[k * kp:(k + 1) * kp, c0:c0 + cwid], writes=[Tst[b]])
                      if n % 2:
                          if gain is not None:
                              dve.op(lambda e, b=b, k=k, c0=c0, cwid=cwid: e.tensor_scalar(
                                  out=dst[0:kp, k, c0:c0 + cwid], in0=stg[b][0:kp, 0:cwid],
                                  scalar1=gain[0:kp, k:k + 1], scalar2=None, op0=ALU.mult),
                                  reads=[Tst[b], Tc], writes=[Tw])
                          else:
                              dve.op(lambda e, b=b, k=k, c0=c0, cwid=cwid: e.tensor_copy(
                                  out=dst[0:kp, k, c0:c0 + cwid], in_=stg[b][0:kp, 0:cwid]),
                                  reads=[Tst[b]], writes=[Tw])
                      else:
                          if gain is not None:
                              act.op(lambda e, b=b, k=k, c0=c0, cwid=cwid: e.activation(
                                  out=dst[0:kp, k, c0:c0 + cwid], in_=stg[b][0:kp, 0:cwid], func=AF.Copy,
                                  scale=gain[0:kp, k:k + 1]), reads=[Tst[b], Tc], writes=[Tw])
                          else:
                              act.op(lambda e, b=b, k=k, c0=c0, cwid=cwid: e.activation(
                                  out=dst[0:kp, k, c0:c0 + cwid], in_=stg[b][0:kp, 0:cwid], func=AF.Copy),
                                  reads=[Tst[b]], writes=[Tw])

          Tw = T("weights")

          def rstd_from_msq(st_bufs, msq, n):
              ap, Tm = msq
              act.op(lambda e: e.activation(out=ap, in_=ap, func=AF.Sqrt, bias=c_eps[:, 0:1]), reads=[Tm, Tc], writes=[Tm])
              dve.op(lambda e: e.reciprocal(out=ap, in_=ap), reads=[Tm], writes=[Tm])

          class TileBufs:
              def __init__(self, st, tag, nx=2):
                  self.xt = [sb(st, f"xt{tag}{i}", [128, D], F32) for i in range(nx)]
                  self.Txt = [T(f"xt{tag}{i}") for i in range(nx)]
                  self.junk = sb(st, f"junk{tag}", [128, D], BF16)
                  self.Tjunk = T("junk" + tag)
                  self.st = sb(st, f"stat{tag}", [128, 8], F32)
                  self.Tst = [T(f"stat{tag}{i}") for i in range(8)]
                  self.xn = sb(st, f"xn{tag}", [128, D], BF16)
                  self.Txn = T("xn" + tag)
                  self.xnT = sb(st, f"xnT{tag}", [128, 8, 128], BF16)
                  self.TxnT = T("xnT" + tag)
                  self.n = 0

          def front_end(tb, src_rows, w_reads=()):
              b = tb.n % len(tb.xt)
              tb.n += 1
              xt, Txt = tb.xt[b], tb.Txt[b]
              sp.dma(xt[:], src_rows, reads=list(w_reads), writes=[Txt])
              ms, Tms = tb.st[:, 0:1], tb.Tst[0]
              act.op(lambda e: e.activation(out=tb.junk[:], in_=xt[:], func=AF.Square, scale=1.0 / math.sqrt(D),
                                            accum_out=ms), reads=[Txt], writes=[tb.Tjunk, Tms])
              rstd_from_msq(None, (ms, Tms), 1)
              dve.op(lambda e: e.tensor_scalar(out=tb.xn[:], in0=xt[:], scalar1=ms, scalar2=None, op0=ALU.mult),
                     reads=[Txt, Tms], writes=[tb.Txn])
              for k in range(8):
                  pe.op(lambda e, k=k: e.transpose(out=pT[:, k * 128:(k + 1) * 128], in_=tb.xn[:, k * 128:(k + 1) * 128],
                                                   identity=c_idb[:]), reads=[tb.Txn, Tc], writes=[TpT])
              act.op(lambda e: e.activation(out=tb.xnT[:].rearrange("p k t -> p (k t)"), in_=pT[:], func=AF.Copy),
                     reads=[TpT], writes=[tb.TxnT])
              return b

          class HeadBufs:
              def __init__(self, st, tag):
                  self.raw = sb(st, f"hraw{tag}", [128, H, HD], F32)
                  self.Traw = T("hraw" + tag)
                  self.sq = sb(st, f"hsq{tag}", [128, H, HD], F32)
                  self.Tsq = T("hsq" + tag)
                  self.rs = sb(st, f"hrs{tag}", [128, H], F32)
                  self.Trs = T("hrs" + tag)
                  self.t1, self.Tt1 = self.sq, self.Tsq
                  self.ra = sb(st, f"hra{tag}", [128, H, 16], F32)
                  self.rb = sb(st, f"hrb{tag}", [128, H, 16], F32)
                  self.Tra, self.Trb = T("hra" + tag), T("hrb" + tag)
                  self.fin = sb(st, f"hfin{tag}", [128, H, HD], BF16)
                  self.Tfin = T("hfin" + tag)

          def head_norm_rope(hb, gain, cs, Tcs):
              raw, sq, rs, t1, fin = hb.raw, hb.sq, hb.rs, hb.t1, hb.fin
              act.op(lambda e: e.activation(out=sq[:], in_=raw[:], func=AF.Square, scale=1.0 / math.sqrt(HD)),
                     reads=[hb.Traw], writes=[hb.Tsq])
              dve.op(lambda e: e.tensor_reduce(out=rs[:], in_=sq[:], axis=AX.X, op=ALU.add),
                     reads=[hb.Tsq], writes=[hb.Trs])
              rstd_from_msq(None, (rs[:], hb.Trs), H)
              dve.op(lambda e: e.tensor_tensor(out=t1[:], in0=raw[:], in1=rs[:].unsqueeze(2).to_broadcast([128, H, HD]),
                                               op=ALU.mult), reads=[hb.Traw, hb.Trs], writes=[hb.Tt1])
              pool.op(lambda e: e.tensor_tensor(out=t1[:], in0=t1[:], in1=gain[:].unsqueeze(1).to_broadcast([128, H, HD]),
                                                op=ALU.mult), reads=[hb.Tt1, Tc], writes=[hb.Tt1])
              cosb = cs[:, 0:16].unsqueeze(1).to_broadcast([128, H, 16])
              sinb = cs[:, 16:32].unsqueeze(1).to_broadcast([128, H, 16])
              p1, p2 = t1[:, :, 64:80], t1[:, :, 80:96]
              act.op(lambda e: e.activation(out=fin[:, :, 0:64], in_=t1[:, :, 0:64], func=AF.Copy),
                     reads=[hb.Tt1], writes=[hb.Tfin])
              dve.op(lambda e: e.tensor_tensor(out=hb.ra[:], in0=p1, in1=cosb, op=ALU.mult),
                     reads=[hb.Tt1, Tcs], writes=[hb.Tra])
              dve.op(lambda e: e.tensor_tensor(out=hb.rb[:], in0=p2, in1=sinb, op=ALU.mult),
                     reads=[hb.Tt1, Tcs], writes=[hb.Trb])
              dve.op(lambda e: e.tensor_tensor(out=fin[:, :, 64:80], in0=hb.ra[:], in1=hb.rb[:], op=ALU.subtract),
                     reads=[hb.Tra, hb.Trb], writes=[hb.Tfin])
              dve.op(lambda e: e.tensor_tensor(out=hb.ra[:], in0=p2, in1=cosb, op=ALU.mult),
                     reads=[hb.Tt1, Tcs], writes=[hb.Tra])
              dve.op(lambda e: e.tensor_tensor(out=hb.rb[:], in0=p1, in1=sinb, op=ALU.mult),
                     reads=[hb.Tt1, Tcs], writes=[hb.Trb])
              dve.op(lambda e: e.tensor_tensor(out=fin[:, :, 80:96], in0=hb.ra[:], in1=hb.rb[:], op=ALU.add),
                     reads=[hb.Tra, hb.Trb], writes=[hb.Tfin])

          NWAYS = 1
          NWAYS_P1 = 4

          def run_ways(tasks, make_gen, nways=None):
              nways = len(ways) if nways is None else nways
              it = iter(tasks)
              free = list(range(nways))
              active = []
              more = True
              while True:
                  while free and more:
                      try:
                          tsk = next(it)
                      except StopIteration:
                          more = False
                          break
                      w = free.pop(0)
                      active.append((make_gen(tsk, ways[w]), w))
                  if not active:
                      break
                  for gw in list(active):
                      try:
                          next(gw[0])
                      except StopIteration:
                          active.remove(gw)
                          free.append(gw[1])

          def rstd_g(ap, Tm):
              act.op(lambda e: e.activation(out=ap, in_=ap, func=AF.Sqrt, bias=c_eps[:, 0:1]), reads=[Tm, Tc], writes=[Tm])
              yield
              dve.op(lambda e: e.reciprocal(out=ap, in_=ap), reads=[Tm], writes=[Tm])
              yield

          pS2b = pS[2].bitcast(BF16)
          Tbanks = [(pT[:, :], TpT), (pS2b, TpS[2])]
          Pbanks = [(pO[1], TpO[1]), (pO[0], TpO[0])]
          Kbanks = [((pM[0], pM[1]), (TpM[0], TpM[1])), ((pS[0], pS[1]), (TpS[0], TpS[1]))]

          class Way:
              def __init__(self, st, w):
                  tag = f"W{w}"
                  self.w = w
                  self.xt = sb(st, "xt" + tag, [128, D], F32)
                  self.Txt = T("xt" + tag)
                  self.st = sb(st, "stat" + tag, [128, 8], F32)
                  self.Tst = [T(f"stat{tag}{i}") for i in range(8)]
                  self.xn = sb(st, "xn" + tag, [128, D], BF16)
                  self.Txn = T("xn" + tag)
                  self.junk, self.Tjunk = self.xn, self.Txn
                  self.xnT = sb(st, "xnT" + tag, [128, 8, 128], BF16)
                  self.TxnT = T("xnT" + tag)
                  self.hb = HeadBufs(st, tag)
                  self.rk = sb(st, "rk" + tag, [128, 32], F32)
                  self.Trk = T("rk" + tag)
                  self.cqb = sb(st, "cqb" + tag, [128, QL], BF16)
                  self.Tcqb = T("cqb" + tag)
                  self.cqf = self.hb.sq[:].rearrange("p h d -> p (h d)")[:, 0:QL]
                  self.Tcqf = self.hb.Tsq
                  self.cqT = sb(st, "cqT" + tag, [128, 3, 128], BF16)
                  self.TcqT = T("cqT" + tag)
                  self.sig = sb(st, "sig" + tag, [128, 4, 128], F32)
                  self.Tsig = T("sig" + tag)
                  self.pT, self.TpT = Tbanks[w % 2]
                  self.pP, self.TpP = Pbanks[w % 2]
                  self.pK, self.TpK = Kbanks[w % 2]

              def alloc_p1(self, st):
                  tag = f"W{self.w}"
                  self.ckv = sb(st, "ckv" + tag, [128, KVL], F32)
                  self.Tckv = T("ckv" + tag)
                  self.kpe = sb(st, "kpe" + tag, [128, RD], F32)
                  self.Tkpe = T("kpe" + tag)
                  self.ckvb = sb(st, "ckvb" + tag, [128, KVL], BF16)
                  self.Tckvb = T("ckvb" + tag)
                  self.ckvT = sb(st, "ckvT" + tag, [128, 2, 128], BF16)
                  self.TckvT = T("ckvT" + tag)
                  self.qst = sb(st, "qst" + tag, [HD, H, 128], BF16)
                  self.Tqst = T("qst" + tag)
                  self.ust, self.Tust = self.sig, self.Tsig
                  self.prj = sb(st, "prj" + tag, [128, KVL + RD], F32)
                  self.Tprj = T("prj" + tag)
                  self.cin = sb(st, "cin" + tag, [128, KVL], F32)
                  self.Tcin = T("cin" + tag)
                  self.kin = sb(st, "kin" + tag, [128, RD], F32)
                  self.Tkin = T("kin" + tag)

          def front_end_g(W, src_rows):
              sp.dma(W.xt[:], src_rows, writes=[W.Txt])
              ms, Tms = W.st[:, 0:1], W.Tst[0]
              act.op(lambda e: e.activation(out=W.junk[:], in_=W.xt[:], func=AF.Square, scale=1.0 / math.sqrt(D),
                                            accum_out=ms), reads=[W.Txt], writes=[W.Tjunk, Tms])
              yield
              yield from rstd_g(ms, Tms)
              dve.op(lambda e: e.tensor_scalar(out=W.xn[:], in0=W.xt[:], scalar1=ms, scalar2=None, op0=ALU.mult),
                     reads=[W.Txt, Tms], writes=[W.Txn])
              yield
              for k in range(8):
                  pe.op(lambda e, k=k: e.transpose(out=W.pT[:, k * 128:(k + 1) * 128], in_=W.xn[:, k * 128:(k + 1) * 128],
                                                   identity=c_idb[:]), reads=[W.Txn, Tc], writes=[W.TpT])
              act.op(lambda e: e.activation(out=W.xnT[:].rearrange("p k t -> p (k t)"), in_=W.pT, func=AF.Copy),
                     reads=[W.TpT], writes=[W.TxnT])
              yield

          def head_norm_rope_g(hb, gain, cs, Tcs):
              raw, sq, rs, t1, fin = hb.raw, hb.sq, hb.rs, hb.t1, hb.fin
              act.op(lambda e: e.activation(out=sq[:], in_=raw[:], func=AF.Square, scale=1.0 / math.sqrt(HD)),
                     reads=[hb.Traw], writes=[hb.Tsq])
              yield
              dve.op(lambda e: e.tensor_reduce(out=rs[:], in_=sq[:], axis=AX.X, op=ALU.add),
                     reads=[hb.Tsq], writes=[hb.Trs])
              yield
              yield from rstd_g(rs[:], hb.Trs)
              dve.op(lambda e: e.tensor_tensor(out=t1[:], in0=raw[:], in1=rs[:].unsqueeze(2).to_broadcast([128, H, HD]),
                                               op=ALU.mult), reads=[hb.Traw, hb.Trs], writes=[hb.Tt1])
              yield
              pool.op(lambda e: e.tensor_tensor(out=t1[:], in0=t1[:], in1=gain[:].unsqueeze(1).to_broadcast([128, H, HD]),
                                                op=ALU.mult), reads=[hb.Tt1, Tc], writes=[hb.Tt1])
              yield
              cosb = cs[:, 0:16].unsqueeze(1).to_broadcast([128, H, 16])
              sinb = cs[:, 16:32].unsqueeze(1).to_broadcast([128, H, 16])
              p1, p2 = t1[:, :, 64:80], t1[:, :, 80:96]
              act.op(lambda e: e.activation(out=fin[:, :, 0:64], in_=t1[:, :, 0:64], func=AF.Copy),
                     reads=[hb.Tt1], writes=[hb.Tfin])
              dve.op(lambda e: e.tensor_tensor(out=hb.ra[:], in0=p1, in1=cosb, op=ALU.mult),
                     reads=[hb.Tt1, Tcs], writes=[hb.Tra])
              dve.op(lambda e: e.tensor_tensor(out=hb.rb[:], in0=p2, in1=sinb, op=ALU.mult),
                     reads=[hb.Tt1, Tcs], writes=[hb.Trb])
              yield
              dve.op(lambda e: e.tensor_tensor(out=fin[:, :, 64:80], in0=hb.ra[:], in1=hb.rb[:], op=ALU.subtract),
                     reads=[hb.Tra, hb.Trb], writes=[hb.Tfin])
              yield
              dve.op(lambda e: e.tensor_tensor(out=hb.ra[:], in0=p2, in1=cosb, op=ALU.mult),
                     reads=[hb.Tt1, Tcs], writes=[hb.Tra])
              dve.op(lambda e: e.tensor_tensor(out=hb.rb[:], in0=p1, in1=sinb, op=ALU.mult),
                     reads=[hb.Tt1, Tcs], writes=[hb.Trb])
              yield
              dve.op(lambda e: e.tensor_tensor(out=fin[:, :, 80:96], in0=hb.ra[:], in1=hb.rb[:], op=ALU.add),
                     reads=[hb.Tra, hb.Trb], writes=[hb.Tfin])
              yield

          with ExitStack() as stA:
              wA_in = sb(stA, "wA_in", [128, 8, INW], BF16)
              wA_uq = sb(stA, "wA_uq", [128, 3, H * HD], BF16)
              wA_ukv = sb(stA, "wA_ukv", [128, 2, H * 128], BF16)
              wA_oa = sb(stA, "wA_oa", [64, 8, D], BF16)
              wA_oc = sb(stA, "wA_oc", [128, 4, D], BF16)
              with ExitStack() as stW:
                  ssA = make_stg(stW, "A")
                  load_weight(ssA, wA_in, w_in, 8, INW, c_gattn, name="in")
                  load_weight(ssA, wA_uq, w_uq, 3, H * HD, c_gq, name="uq")
                  load_weight(ssA, wA_ukv, w_ukv, 2, H * 128, None, name="ukv")
                  load_weight(ssA, wA_oa, w_out, 8, D, None, kp=64, name="oa")
                  load_weight(ssA, wA_oc, w_out[512:1024, :], 4, D, None, name="oc")
                  tk.barrier()
              ckpt("w")

              def tile_front_A_g(W, src_rows, cs, Tcs, qcol, ntok_groups):
                  yield from front_end_g(W, src_rows)
                  yield from qglu_g(W, cs, Tcs, qTa[0:HD, :, qcol:qcol + 128], TqTa, ntok_groups)

              def qglu_g(W, cs, Tcs, qdst, Tqdst, ntok_groups):
                  hb = W.hb
                  for k in range(8):
                      pe.op(lambda e, k=k: e.matmul(W.pP[:, 0:QL], lhsT=W.xnT[:, k, :], rhs=wA_in[:, k, 0:QL],
                                                    start=(k == 0), stop=(k == 7)), reads=[W.TxnT, Tw], writes=[W.TpP])
                  dve.op(lambda e: e.tensor_copy(out=W.cqf, in_=W.pP[:, 0:QL]), reads=[W.TpP], writes=[W.Tcqf])
                  yield
                  ms, Tms = W.st[:, 2:3], W.Tst[2]
                  act.op(lambda e: e.activation(out=W.junk[:, 0:QL], in_=W.cqf, func=AF.Square,
                                                scale=1.0 / math.sqrt(QL), accum_out=ms),
                         reads=[W.Tcqf], writes=[W.Tjunk, Tms])
                  yield
                  yield from rstd_g(ms, Tms)
                  dve.op(lambda e: e.tensor_scalar(out=W.cqb[:], in0=W.cqf, scalar1=ms, scalar2=None, op0=ALU.mult),
                         reads=[W.Tcqf, Tms], writes=[W.Tcqb])
                  yield
                  for k in range(3):
                      pe.op(lambda e, k=k: e.transpose(out=W.pT[:, k * 128:(k + 1) * 128], in_=W.cqb[:, k * 128:(k + 1) * 128],
                                                       identity=c_idb[:]), reads=[W.Tcqb, Tc], writes=[W.TpT])
                  dve.op(lambda e: e.tensor_copy(out=W.cqT[:].rearrange("p k t -> p (k t)"), in_=W.pT[:, 0:384]),
                         reads=[W.TpT], writes=[W.TcqT])
                  yield
                  for nb, (c0, cw_) in enumerate(((0, 512), (512, 256))):
                      for k in range(3):
                          pe.op(lambda e, nb=nb, k=k, c0=c0, cw_=cw_: e.matmul(
                              W.pK[nb][:, 0:cw_], lhsT=W.cqT[:, k, :], rhs=wA_uq[:, k, c0:c0 + cw_],
                              start=(k == 0), stop=(k == 2)), reads=[W.TcqT, Tw], writes=[W.TpK[nb]])
                  rawf = hb.raw[:].rearrange("p h d -> p (h d)")
                  act.op(lambda e: e.activation(out=rawf[:, 0:512], in_=W.pK[0][:, 0:512], func=AF.Copy),
                         reads=[W.TpK[0]], writes=[hb.Traw])
                  dve.op(lambda e: e.tensor_copy(out=rawf[:, 512:768], in_=W.pK[1][:, 0:256]),
                         reads=[W.TpK[1]], writes=[hb.Traw])
                  yield
                  yield from head_norm_rope_g(hb, c_ghq, cs, Tcs)
                  for h in range(H):
                      pe.op(lambda e, h=h: e.transpose(out=W.pT[0:HD, h * 128:(h + 1) * 128], in_=hb.fin[:, h, :],
                                                       identity=c_idb[:]), reads=[hb.Tfin, Tc], writes=[W.TpT])
                  act.op(lambda e: e.activation(out=qdst,
                                                in_=W.pT[0:HD, :].rearrange("p (h t) -> p h t", h=H), func=AF.Copy),
                         reads=[W.TpT], writes=[Tqdst])
                  yield
                  for half in (1, 0):
                      for c in range(4):
                          col = QL + KVL + RD + half * CC + c * 128
                          for k in range(8):
                              pe.op(lambda e, half=half, c=c, k=k, col=col: e.matmul(
                                  W.pK[half][:, c * 128:(c + 1) * 128], lhsT=wA_in[:, k, col:col + 128], rhs=W.xnT[:, k, :],
                                  start=(k == 0), stop=(k == 7)), reads=[W.TxnT, Tw], writes=[W.TpK[half]])
                      if half == 1:
                          act.op(lambda e: e.activation(out=W.sig[:].rearrange("p c t -> p (c t)"), in_=W.pK[1][:, :],
                                                        func=AF.Sigmoid), reads=[W.TpK[1]], writes=[W.Tsig])
                  for (tok0, ntok, ucol_) in ntok_groups:
                      tgt = ucol_ if isinstance(ucol_, tuple) else (uT, TuT, ucol_)
                      ub, Tub, uc = tgt
                      dve.op(lambda e, tok0=tok0, ntok=ntok, ub=ub, uc=uc: e.tensor_tensor(
                          out=ub[:, :, uc:uc + ntok],
                          in0=W.pK[0][:, :].rearrange("p (c t) -> p c t", c=4)[:, :, tok0:tok0 + ntok],
                          in1=W.sig[:, :, tok0:tok0 + ntok], op=ALU.mult), reads=[W.TpK[0], W.Tsig], writes=[Tub])
                  yield

              ways = [Way(stA, w) for w in range(NWAYS)]
              TKT, TVV = T("KT"), T("VV")
              TKTs, TVVs = T("KTs"), T("VVs")
              TX1 = T("X1")
              Tout = T("outs")
              with ExitStack() as stP1:
                  for w in range(NWAYS, NWAYS_P1):
                      ways.append(Way(stP1, w))
                  for W_ in ways:
                      W_.alloc_p1(stP1)
                  KS = 4
                  kst = [sb(stP1, f"kst{i}", [HD, H, KS * 128], BF16) for i in range(2)]
                  Tkst = [T(f"kst{i}") for i in range(2)]
                  vst = [sb(stP1, f"vst{i}", [128, H, KS, 128], BF16) for i in range(2)]
                  Tvst = [T(f"vst{i}") for i in range(2)]
                  for i in range(2):
                      pool.op(lambda e, i=i: e.memset(vst[i][:], 1.0), writes=[Tvst[i]])

                  def kv_from_ckv_g(W, ckv_ap, Tck, kpe_ap, Tkp, cs, Tcs, stage_slot, have_bf16=False):
                      sbuf, slot = stage_slot
                      hb = W.hb
                      if not have_bf16:
                          act.op(lambda e: e.activation(out=W.ckvb[:], in_=ckv_ap, func=AF.Copy), reads=[Tck], writes=[W.Tckvb])
                          yield
                      for k in range(2):
                          pe.op(lambda e, k=k: e.transpose(out=W.pT[:, k * 128:(k + 1) * 128],
                                                           in_=W.ckvb[:, k * 128:(k + 1) * 128], identity=c_idb[:]),
                                reads=[W.Tckvb, Tc], writes=[W.TpT])
                      dve.op(lambda e: e.tensor_copy(out=W.ckvT[:].rearrange("p k t -> p (k t)"), in_=W.pT[:, 0:256]),
                             reads=[W.TpT], writes=[W.TckvT])
                      yield
                      for nb in range(2):
                          for k in range(2):
                              pe.op(lambda e, nb=nb, k=k: e.matmul(W.pK[nb][:, :], lhsT=W.ckvT[:, k, :],
                                                                   rhs=wA_ukv[:, k, nb * 512:(nb + 1) * 512],
                                                                   start=(k == 0), stop=(k == 1)),
                                    reads=[W.TckvT, Tw], writes=[W.TpK[nb]])
                      for nb in range(2):
                          src = W.pK[nb][:, :].rearrange("p (h c) -> p h c", h=4)
                          act.op(lambda e, nb=nb, src=src: e.activation(out=hb.raw[:, nb * 4:(nb + 1) * 4, 0:64],
                                                                        in_=src[:, :, 0:64], func=AF.Copy),
                                 reads=[W.TpK[nb]], writes=[hb.Traw])
                          dve.op(lambda e, nb=nb, src=src: e.tensor_copy(out=vst[sbuf][:, nb * 4:(nb + 1) * 4, slot, 0:64],
                                                                         in_=src[:, :, 64:128]),
                                 reads=[W.TpK[nb]], writes=[Tvst[sbuf]])
                      yield
                      pool.op(lambda e: e.tensor_copy(out=hb.raw[:, :, 64:96],
                                                      in_=kpe_ap.unsqueeze(1).to_broadcast([128, H, RD])),
                              reads=[Tkp], writes=[hb.Traw])
                      yield
                      yield from head_norm_rope_g(hb, c_ghk, cs, Tcs)
                      for h in range(H):
                          pe.op(lambda e, h=h: e.transpose(out=W.pT[0:HD, h * 128:(h + 1) * 128], in_=hb.fin[:, h, :],
                                                           identity=c_idb[:]), reads=[hb.Tfin, Tc], writes=[W.TpT])
                      act.op(lambda e: e.activation(out=kst[sbuf][:, :, slot * 128:(slot + 1) * 128],
                                                    in_=W.pT[0:HD, :].rearrange("p (h t) -> p h t", h=H), func=AF.Copy),
                             reads=[W.TpT], writes=[Tkst[sbuf]])
                      yield

                  def ckv_from_x_g(W, own_row=None, o_ck=None, o_kp=None):
                      for k in range(8):
                          pe.op(lambda e, k=k: e.matmul(W.pP[:, 0:KVL + RD], lhsT=W.xnT[:, k, :],
                                                        rhs=wA_in[:, k, QL:QL + KVL + RD], start=(k == 0), stop=(k == 7)),
                                reads=[W.TxnT, Tw], writes=[W.TpP])
                      dve.op(lambda e: e.tensor_copy(out=W.prj[:], in_=W.pP[:, 0:KVL + RD]), reads=[W.TpP], writes=[W.Tprj])
                      yield
                      ms, Tms = W.st[:, 1:2], W.Tst[1]
                      act.op(lambda e: e.activation(out=W.junk[:, 0:KVL], in_=W.prj[:, 0:KVL], func=AF.Square,
                                                    scale=1.0 / math.sqrt(KVL), accum_out=ms),
                             reads=[W.Tprj], writes=[W.Tjunk, Tms])
                      pool.op(lambda e: e.tensor_copy(out=W.kpe[:], in_=W.prj[:, KVL:KVL + RD]),
                              reads=[W.Tprj], writes=[W.Tkpe])
                      yield
                      yield from rstd_g(ms, Tms)
                      if own_row is None:
                          dve.op(lambda e: e.scalar_tensor_tensor(out=W.ckvb[:], in0=W.prj[:, 0:KVL], scalar=ms, in1=c_gkv[:],
                                                                  op0=ALU.mult, op1=ALU.mult),
                                 reads=[W.Tprj, Tms, Tc], writes=[W.Tckvb])
                          yield
                          return
                      dve.op(lambda e: e.scalar_tensor_tensor(out=W.ckv[:], in0=W.prj[:, 0:KVL], scalar=ms, in1=c_gkv[:],
                                                              op0=ALU.mult, op1=ALU.mult),
                             reads=[W.Tprj, Tms, Tc], writes=[W.Tckv])
                      yield
                      if own_row is not None:
                          sp.dma(o_ck[own_row:own_row + 128, :], W.ckv[:], reads=[W.Tckv], writes=[Tout])
                          sp.dma(o_kp[own_row:own_row + 128, :], W.kpe[:], reads=[W.Tkpe], writes=[Tout])

                  gdone = {}

                  TQT, TUT = T("QT"), T("UT")
                  zpad = sb(stP1, "zpad", [128, 4, 32], F32)
                  Tzpad = T("zpad")
                  dve.op(lambda e: e.memset(zpad[:], 0.0), writes=[Tzpad])
                  sp.dma(UT[:, :, 0:32], zpad[:], reads=[Tzpad], writes=[TUT])

                  def qu_to_scratch_g(W, cs, Tcs, ci):
                      yield from qglu_g(W, cs, Tcs, W.qst[:], W.Tqst, [(0, 128, (W.ust, W.Tust, 0))])
                      sp.dma(QT[:, :, ci * 128:(ci + 1) * 128], W.qst[:], reads=[W.Tqst], writes=[TQT])
                      sp.dma(UT[:, :, 32 + ci * 128:32 + (ci + 1) * 128], W.ust[:], reads=[W.Tust], writes=[TUT])

                  def p1_tile_g(lt, W):
                      if isinstance(lt, tuple):
                          t = lt[1]
                          yield from front_end_g(W, x_halo[t * 128:(t + 1) * 128, :])
                          yield from qu_to_scratch_g(W, c_ropeh[:, t, :], Tc, t * (RT + 1))
                          return
                      gi = lt // KS
                      sbuf, slot = gi % 2, lt % KS
                      sp.dma(W.rk[:], rope_k[:, lt, :], writes=[W.Trk])
                      yield from front_end_g(W, x_all[lt * 128:(lt + 1) * 128, :])
                      t, rem = divmod(lt, G)
                      own = rem < RT
                      yield from ckv_from_x_g(W, own_row=(t * RT + rem) * 128 if own else None, o_ck=o_ckv, o_kp=o_kpe)
                      yield from kv_from_ckv_g(W, W.ckv[:], W.Tckv, W.kpe[:], W.Tkpe, W.rk, W.Trk, (sbuf, slot),
                                               have_bf16=not own)
                      if own:
                          yield from qu_to_scratch_g(W, W.rk, W.Trk, t * (RT + 1) + 1 + rem)
                      gdone[gi] = gdone.get(gi, 0) + 1
                      if gdone[gi] == KS:
                          lt0 = gi * KS
                          sp.dma(KT[:, :, lt0 * 128:(lt0 + KS) * 128].rearrange("h d t -> d h t"), kst[sbuf][:],
                                 reads=[Tkst[sbuf]], writes=[TKT])
                          sp.dma(VV[:, :, lt0:lt0 + KS, :].rearrange("h p s c -> p h s c"), vst[sbuf][:],
                                 reads=[Tvst[sbuf]], writes=[TVV])
                  run_ways(list(range(NT)) + [("h", t) for t in range(NSLOT)], p1_tile_g)

                  ckpt("p1")
                  NG0 = NT // KS
                  PG = (PT + KS - 1) // KS
                  sdone = {}

                  def p1s_tile_g(ep, W):
                      e_, p = ep
                      gi = NG0 + e_ * PG + p // KS
                      sbuf, slot = gi % 2, p % KS
                      r0 = e_ * PAST + p * 128
                      sp.dma(W.cin[:], cckv[r0:r0 + 128, :], writes=[W.Tcin])
                      sp.dma(W.kin[:], ckpe[r0:r0 + 128, :], writes=[W.Tkin])
                      sp.dma(W.rk[:], rope_sp[:, p, :], writes=[W.Trk])
                      yield from kv_from_ckv_g(W, W.cin[:], W.Tcin, W.kin[:], W.Tkin, W.rk, W.Trk, (sbuf, slot))
                      sdone[gi] = sdone.get(gi, 0) + 1
                      p0 = (p // KS) * KS
                      ns = min(KS, PT - p0)
                      if sdone[gi] == ns:
                          sp.dma(KTs[e_, :, :, p0 * 128:(p0 + ns) * 128].rearrange("h d t -> d h t"),
                                 kst[sbuf][:, :, 0:ns * 128], reads=[Tkst[sbuf]], writes=[TKTs])
                          sp.dma(VVs[e_, :, :, p0:p0 + ns, :].rearrange("h p s c -> p h s c"),
                                 vst[sbuf][:, :, 0:ns, :], reads=[Tvst[sbuf]], writes=[TVVs])
                  run_ways([(e_, p) for e_ in range(2) for p in range(PT)], p1s_tile_g)
                  sbuf = (NG0 + 2 * PG) % 2

                  def p1n_g(_, W):
                      yield from front_end_g(W, x_s[:, :])
                      yield from ckv_from_x_g(W, own_row=0, o_ck=o_ckvs, o_kp=o_kpes)
                      yield from kv_from_ckv_g(W, W.ckv[:], W.Tckv, W.kpe[:], W.Tkpe, c_ropesn, Tc, (sbuf, 0))
                  run_ways([0], p1n_g)
                  for e_ in range(2):
                      sp.dma(KTs[e_, :, :, PT * 128:PT * 128 + 64].rearrange("h d t -> d h t"),
                             kst[sbuf][:, :, e_ * 64:(e_ + 1) * 64], reads=[Tkst[sbuf]], writes=[TKTs])
                      sp.dma(VVs[e_, :, 0:64, PT:PT + 1, :].rearrange("h p s c -> p h s c"),
                             vst[sbuf][e_ * 64:(e_ + 1) * 64, :, 0:1, :], reads=[Tvst[sbuf]], writes=[TVVs])

                  tk.barrier()
                  del ways[NWAYS:]
              ckpt("p1s")
              qTas = [sb(stA, f"qTa{i}", [128, H, QG * 128], BF16) for i in range(2)]
              TqTas = [T(f"qTa{i}") for i in range(2)]
              for i in range(2):
                  for h in range(H):
                      sp.dma(qTas[i][96:128, h, :], qmask[:, :], writes=[TqTas[i]])
              qTa, TqTa = qTas[0], TqTas[0]
              cTgs = [sb(stA, f"cTg{i}", [128, 4, QG * 128], BF16) for i in range(2)]
              TcTgs = [T(f"cTg{i}") for i in range(2)]
              cTg, TcTg = cTgs[0], TcTgs[0]
              attTs = [sb(stA, f"attT{i}", [64, H, QG * 128], BF16) for i in range(2)]
              TattTs = [T(f"attT{i}") for i in range(2)]
              attT, TattT = attTs[0], TattTs[0]
              for i in range(2):
                  pool.op(lambda e, i=i: e.memset(attTs[i][:], 0.0), writes=[TattTs[i]])
              UW = CK - 1 + QG * 128
              uTs = [sb(stA, f"uT{i}", [128, 4, UW], F32) for i in range(2)]
              TuTs = [T(f"uT{i}") for i in range(2)]
              uT, TuT = uTs[0], TuTs[0]
              acc = sb(stA, "acc", [128, 4, QG * 128], F32)
              Tacc = T("acc")
              sqc = sb(stA, "sqc", [128, 4, QG * 128], BF16)
              Tsqc = T("sqc")
              rsc = sb(stA, "rsc", [128, QG * 128], F32)
              Trsc = T("rsc")
              cpre, Tcpre = acc, Tacc
              NKB = 3
              kb = [sb(stA, f"kb{i}", [HD, KC * 128], BF16) for i in range(NKB)]
              Tkb = [T(f"kb{i}") for i in range(NKB)]
              vb = [sb(stA, f"vb{i}", [128, KC, 128], BF16) for i in range(NKB)]
              Tvb = [T(f"vb{i}") for i in range(NKB)]
              kd = [sb(stA, f"kd{i}", [128, QG * 128], BF16) for i in range(2)]
              Tkd = [T(f"kd{i}") for i in range(2)]
              for i in range(2):
                  sp.dma(kd[i][96:128, :], khot[:, :], writes=[Tkd[i]])
              vd = [sb(stA, f"vd{i}", [128, QG, 128], BF16) for i in range(2)]
              Tvd = [T(f"vd{i}") for i in range(2)]
              pb = [sb(stA, f"pb{i}", [128, 1024], BF16) for i in range(2)]
              Tpb = [T(f"pb{i}") for i in range(2)]
              rcp = sb(stA, "rcp", [64, 512], F32)
              Trcp = T("rcp")
              rcs = sb(stA, "rcs", [128, 512], F32)
              Trcs = T("rcs")
              xr = [sb(stA, f"xr{i}", [128, D], F32) for i in range(2)]
              Txr = [T(f"xr{i}") for i in range(2)]
              x1o, Tx1o = xr, Txr
              cvo = sb(stA, "cvo", [32, CC], F32)
              Tcvo = T("cvo")
              cnt = dict(kb=0, pb=0, x1=0, po=0, kd=0)

              def conv_module_g(uT_, TuT_, cT_, TcT_, ucol, ncol, ccol, bank=0):
                  PB_, TPB_ = pM[bank], TpM[bank]
                  for c in range(4):
                      dve.op(lambda e, c=c: e.tensor_scalar(out=acc[:, c, 0:ncol], in0=uT_[:, c, ucol - 30:ucol - 30 + ncol],
                                                            scalar1=c_cw[:, c, 0:1], scalar2=c_cb[:, c:c + 1],
                                                            op0=ALU.mult, op1=ALU.add),
                             reads=[TuT_, Tc], writes=[Tacc])
                      yield
                      for k in range(1, CK):
                          dve.op(lambda e, c=c, k=k: e.scalar_tensor_tensor(
                              out=acc[:, c, 0:ncol], in0=uT_[:, c, ucol - 30 + k:ucol - 30 + k + ncol],
                              scalar=c_cw[:, c, k:k + 1], in1=acc[:, c, 0:ncol], op0=ALU.mult, op1=ALU.add),
                              reads=[TuT_, Tc, Tacc], writes=[Tacc])
                          yield
                  act.op(lambda e: e.activation(out=sqc[:, :, 0:ncol], in_=acc[:, :, 0:ncol], func=AF.Square,
                                                scale=1.0 / math.sqrt(CC)), reads=[Tacc], writes=[Tsqc])
                  yield
                  for c in range(4):
                      pe.op(lambda e, c=c: e.matmul(PB_[:, 0:ncol], lhsT=c_ones[:], rhs=sqc[:, c, 0:ncol],
                                                    start=(c == 0), stop=(c == 3)), reads=[Tsqc, Tc], writes=[TPB_])
                  dve.op(lambda e: e.tensor_scalar(out=rsc[:, 0:ncol], in0=PB_[:, 0:ncol], scalar1=EPS, scalar2=None,
                                                   op0=ALU.add), reads=[TPB_], writes=[Trsc])
                  yield
                  act.op(lambda e: e.activation(out=rsc[:, 0:ncol], in_=rsc[:, 0:ncol], func=AF.Sqrt),
                         reads=[Trsc], writes=[Trsc])
                  yield
                  dve.op(lambda e: e.reciprocal(out=rsc[:, 0:ncol], in_=rsc[:, 0:ncol]), reads=[Trsc], writes=[Trsc])
                  yield
                  for c in range(4):
                      dve.op(lambda e, c=c: e.scalar_tensor_tensor(out=cpre[:, c, 0:ncol], in0=acc[:, c, 0:ncol],
                                                                   scalar=c_cg[:, c:c + 1], in1=rsc[:, 0:ncol],
                                                                   op0=ALU.mult, op1=ALU.mult),
                             reads=[Tacc, Trsc, Tc], writes=[Tcpre])
                      yield
                  act.op(lambda e: e.activation(out=cT_[:, :, ccol:ccol + ncol], in_=cpre[:, :, 0:ncol], func=AF.Silu),
                         reads=[Tcpre], writes=[TcT_])
                  yield

              def conv_module(ucol, ncol, ccol):
                  for _ in conv_module_g(uT, TuT, cTg, TcTg, ucol, ncol, ccol, bank=2):
                      pass

              def attention(ncols, segs, kt_src, vv_src, Tk, Tv, qc0=0, ksz_last=128, side=None):
                  items = []
                  for h in range(H):
                      po_i = cnt["po"] % 2
                      cnt["po"] += 1
                      PO, TPO = pO[po_i], TpO[po_i]
                      first = True
                      nseg = len(segs)
                      for si, (kind, t0, ntl, bcol, ksz) in enumerate(segs):
                          last_seg = si == nseg - 1
                          if kind == "d":
                              di = cnt["kd"] % 2
                              cnt["kd"] += 1
                              KD, TKD, VD, TVD = kd[di], Tkd[di], vd[di], Tvd[di]
                              loads = [(KD[0:HD, 0:ntl * 128], kt_src(h, t0, ntl), Tk, TKD),
                                       (VD[:, 0:ntl, :], vv_src(h, t0, ntl), Tv, TVD)]
                              chunks = [(t0, ntl, KD, TKD, VD, TVD, 128, loads)]
                          else:
                              chunks = []
                              for c0 in range(t0, t0 + ntl, KC):
                                  cn = min(KC, t0 + ntl - c0)
                                  bi = cnt["kb"] % NKB
                                  cnt["kb"] += 1
                                  KB, TKB, VB, TVB = kb[bi], Tkb[bi], vb[bi], Tvb[bi]
                                  lastc = (c0 + cn == t0 + ntl)
                                  kz_l = ksz if lastc else 128
                                  loads = [(KB[0:HD, 0:(cn - 1) * 128 + kz_l], kt_src(h, c0, cn, kz_l), Tk, TKB)]
                                  if kz_l == 128:
                                      loads.append((VB[:, 0:cn, :], vv_src(h, c0, cn), Tv, TVB))
                                  else:
                                      if cn > 1:
                                          loads.append((VB[:, 0:cn - 1, :], vv_src(h, c0, cn - 1), Tv, TVB))
                                      loads.append((VB[0:kz_l, cn - 1:cn, :], vv_src(h, c0 + cn - 1, 1, kz_l), Tv, TVB))
                                  chunks.append((c0, cn, KB, TKB, VB, TVB, HD, loads))
                          for ci_, (c0, cn, KB, TKB, VB, TVB, KR, loads) in enumerate(chunks):
                              for j in range(cn):
                                  lastt = (c0 + j == t0 + ntl - 1)
                                  kz = ksz if lastt else 128
                                  cs0 = j * 128 if kind == "d" else 0
                                  items.append(dict(h=h, PO=PO, TPO=TPO, KB=KB, TKB=TKB, VB=VB, TVB=TVB, KR=KR, j=j, kz=kz,
                                                    cs0=cs0, bcol=bcol, first=first, last=(last_seg and lastt),
                                                    loads=(loads if j == 0 else None), key=(h, si, ci_, kind)))
                                  first = False
                  units = []
                  i = 0
                  while i < len(items):
                      a = items[i]
                      if (i + 1 < len(items) and a["key"][3] == "n" and items[i + 1]["key"] == a["key"]
                              and a["kz"] == 128 and items[i + 1]["kz"] == 128):
                          units.append([a, items[i + 1]])
                          i += 2
                      else:
                          units.append([a])
                          i += 1

                  def emit_qk(u):
                      pi = cnt["pb"] % 2
                      cnt["pb"] += 1
                      PSp = pSS[:, pi * 1024:(pi + 1) * 1024].rearrange("p (i c) -> p i c", i=2)
                      PBp = pb[pi][:, :].rearrange("p (i c) -> p i c", i=2)
                      for i, it in enumerate(u):
                          if it["loads"]:
                              for (dst, src, Tsrc, Tdst) in it["loads"]:
                                  sp.dma(dst, src, reads=[Tsrc], writes=[Tdst])
                          it["PS"], it["PB"], it["TPS"], it["TPB"] = PSp, PBp, TpSS[pi], Tpb[pi]
                          pe.op(lambda e, it=it, i=i: e.matmul(
                              PSp[0:it["kz"], i, it["cs0"]:ncols],
                              lhsT=it["KB"][0:it["KR"], it["j"] * 128:it["j"] * 128 + it["kz"]],
                              rhs=qTa[0:it["KR"], it["h"], qc0 + it["cs0"]:qc0 + ncols], start=True, stop=True),
                              reads=[it["TKB"], TqTa], writes=[TpSS[pi]])

                  def emit_rest(u):
                      a = u[0]
                      kz, cs0, n = a["kz"], a["cs0"], len(u)
                      PSp, PBp, TPS, TPB = a["PS"], a["PB"], a["TPS"], a["TPB"]
                      bias_ap = c_zero[0:kz, 0:1] if a["bcol"] is None else c_bias[0:kz, a["bcol"]:a["bcol"] + 1]
                      act.op(lambda e: e.activation(out=PBp[0:kz, 0:n, cs0:ncols], in_=PSp[0:kz, 0:n, cs0:ncols],
                                                    func=AF.Exp, bias=bias_ap, scale=SCALE),
                             reads=[TPS, Tc], writes=[TPB])
                      for i, it in enumerate(u):
                          pe.op(lambda e, it=it, i=i: e.matmul(it["PO"][:, cs0:ncols], lhsT=it["VB"][0:kz, it["j"], :],
                                                               rhs=PBp[0:kz, i, cs0:ncols], start=it["first"], stop=it["last"]),
                                reads=[it["TVB"], TPB], writes=[it["TPO"]])
                          if it["last"]:
                              PO, TPO, h = it["PO"], it["TPO"], it["h"]
                              dve.op(lambda e, PO=PO: e.tensor_scalar(out=rcs[64:128, 0:ncols], in0=PO[64:128, 0:ncols],
                                                                      scalar1=1e-30, scalar2=None, op0=ALU.add),
                                     reads=[TPO], writes=[Trcs])
                              dve.op(lambda e: e.reciprocal(out=rcs[64:128, 0:ncols], in_=rcs[64:128, 0:ncols]),
                                     reads=[Trcs], writes=[Trcs])
                              dve.op(lambda e: e.tensor_copy(out=rcp[0:64, 0:ncols], in_=rcs[64:128, 0:ncols]),
                                     reads=[Trcs], writes=[Trcp])
                              dve.op(lambda e, PO=PO, h=h: e.tensor_tensor(out=attT[:, h, qc0:qc0 + ncols], in0=PO[0:64, 0:ncols],
                                                                           in1=rcp[0:64, 0:ncols], op=ALU.mult),
                                     reads=[TPO, Trcp], writes=[TattT])
                  nu = len(units)
                  if nu:
                      emit_qk(units[0])
                  for i in range(nu):
                      if i + 1 < nu:
                          emit_qk(units[i + 1])
                      emit_rest(units[i])
                      if side is not None:
                          next(side, None)
                  if side is not None:
                      for _ in side:
                          pass

              def attention_halo(segs, side=None):
                  tiles = []
                  for (kind, t0, ntl, bcol, ksz) in segs:
                      for c0 in range(t0, t0 + ntl, 2):
                          cn = min(2, t0 + ntl - c0)
                          for j in range(cn):
                              tiles.append((c0, cn, j, bcol))
                  nt_ = len(tiles)
                  st_ = {}

                  def emit_qk(n):
                      c0, cn, j, bcol = tiles[n]
                      if j == 0:
                          bi = cnt["kb"] % NKB
                          cnt["kb"] += 1
                          KB3 = kb[bi][0:HD, 0:H * 256].rearrange("p (h t) -> p h t", h=H)
                          VB4 = vb[bi][:, 0:H * 2, :].rearrange("p (h s) c -> p h s c", h=H)
                          sp.dma(KB3[:, :, 0:cn * 128], KT[:, :, c0 * 128:(c0 + cn) * 128].rearrange("h d t -> d h t"),
                                 reads=[TKT], writes=[Tkb[bi]])
                          sp.dma(VB4[:, :, 0:cn, :], VV[:, :, c0:c0 + cn, :].rearrange("h p s c -> p h s c"),
                                 reads=[TVV], writes=[Tvb[bi]])
                          st_["cur"] = (KB3, VB4, Tkb[bi], Tvb[bi])
                      KB3, VB4, TKB, TVB = st_["cur"]
                      pi = cnt["pb"] % 2
                      cnt["pb"] += 1
                      PSp = pSS[:, pi * 1024:(pi + 1) * 1024]
                      for h in range(H):
                          pe.op(lambda e, h=h: e.matmul(PSp[:, h * 64:(h + 1) * 64], lhsT=KB3[:, h, j * 128:(j + 1) * 128],
                                                        rhs=qTa[0:HD, h, 64:128], start=True, stop=True,
                                                        skip_group_check=True),
                                reads=[TKB, TqTa], writes=[TpSS[pi]])
                      st_[n] = (PSp, pb[pi], TpSS[pi], Tpb[pi], VB4, TVB, j, bcol)

                  def emit_rest(n):
                      PSp, PB, TPS, TPB, VB4, TVB, j, bcol = st_.pop(n)
                      bias_ap = c_zero[:, 0:1] if bcol is None else c_bias[:, bcol:bcol + 1]
                      act.op(lambda e: e.activation(out=PB[:, 0:512], in_=PSp[:, 0:512], func=AF.Exp, bias=bias_ap, scale=SCALE),
                             reads=[TPS, Tc], writes=[TPB])
                      for h in range(H):
                          pe.op(lambda e, h=h: e.matmul(pO[0][:, h * 64:(h + 1) * 64], lhsT=VB4[:, h, j, :],
                                                        rhs=PB[:, h * 64:(h + 1) * 64],
                                                        start=(n == 0 and h == 0), stop=(n == nt_ - 1),
                                                        skip_group_check=True),
                                reads=[TVB, TPB], writes=[TpO[0]])
                  if nt_:
                      emit_qk(0)
                  for n in range(nt_):
                      if n + 1 < nt_:
                          emit_qk(n + 1)
                      emit_rest(n)
                      if side is not None:
                          next(side, None)
                          next(side, None)
                  if side is not None:
                      for _ in side:
                          pass
                  dve.op(lambda e: e.tensor_scalar(out=rcs[64:128, 0:512], in0=pO[0][64:128, :], scalar1=1e-30,
                                                   scalar2=None, op0=ALU.add), reads=[TpO[0]], writes=[Trcs])
                  dve.op(lambda e: e.reciprocal(out=rcs[64:128, 0:512], in_=rcs[64:128, 0:512]), reads=[Trcs], writes=[Trcs])
                  dve.op(lambda e: e.tensor_copy(out=rcp[0:64, 0:512], in_=rcs[64:128, 0:512]), reads=[Trcs], writes=[Trcp])
                  dve.op(lambda e: e.tensor_tensor(
                      out=attT[:, :, 64:128], in0=pO[0][0:64, :].rearrange("p (h t) -> p h t", h=H),
                      in1=rcp[0:64, 0:512].rearrange("p (h t) -> p h t", h=H), op=ALU.mult),
                      reads=[TpO[0], Trcp], writes=[TattT])

              def out_proj_g(aT_, TaT_, cT_, TcT_, src_rows, col, x1_row, bank=0):
                  bi = cnt["x1"] % 2
                  cnt["x1"] += 1
                  PB_, TPB_ = pM[bank], TpM[bank]
                  sp.dma(xr[bi][:], src_rows, writes=[Txr[bi]])
                  for nb in range(2):
                      for h in range(H):
                          pe.op(lambda e, nb=nb, h=h: e.matmul(PB_[:, :], lhsT=aT_[:, h, col:col + 128],
                                                               rhs=wA_oa[:, h, nb * 512:(nb + 1) * 512],
                                                               start=(h == 0), stop=False),
                                reads=[TaT_, Tw], writes=[TPB_])
                      for c in range(4):
                          pe.op(lambda e, nb=nb, c=c: e.matmul(PB_[:, :], lhsT=cT_[:, c, col:col + 128],
                                                               rhs=wA_oc[:, c, nb * 512:(nb + 1) * 512],
                                                               start=False, stop=(c == 3)),
                                reads=[TcT_, Tw], writes=[TPB_])
                      dve.op(lambda e, nb=nb: e.tensor_tensor(out=x1o[bi][:, nb * 512:(nb + 1) * 512], in0=PB_[:, :],
                                                              in1=xr[bi][:, nb * 512:(nb + 1) * 512], op=ALU.add),
                             reads=[TPB_, Txr[bi]], writes=[Tx1o[bi]])
                      yield
                  sp.dma(X1[x1_row * 128:(x1_row + 1) * 128, :], x1o[bi][:], reads=[Tx1o[bi]], writes=[TX1])
                  yield

              def out_proj(src_rows, col, x1_row):
                  for _ in out_proj_g(attT, TattT, cTg, TcTg, src_rows, col, x1_row, bank=0):
                      pass

              def emit_conv_state(ub, Tub, col0, dst):
                  for c in range(4):
                      pe.op(lambda e, c=c: e.transpose(out=pM[2][0:32, c * 128:(c + 1) * 128], in_=ub[:, c, col0:col0 + 32],
                                                       identity=c_idf[:]), reads=[Tub, Tc], writes=[TpM[2]])
                  dve.op(lambda e: e.tensor_copy(out=cvo[:], in_=pM[2][0:32, :]), reads=[TpM[2]], writes=[Tcvo])
                  sp.dma(dst, cvo[:], reads=[Tcvo], writes=[Tout])

              def kt_p(h, t0, n, ksz=128):
                  return KT[h, :, t0 * 128:(t0 + n - 1) * 128 + ksz]

              def vv_p(h, t0, n, ksz=128):
                  return VV[h, 0:ksz, t0:t0 + n, :]


              groups = []
              for t in range(NSLOT):
                  groups.append((t, None))
                  for g in range(RT // QG):
                      groups.append((t, g))

              def load_group(n):
                  t, g = groups[n]
                  b = n % 2
                  ci0 = t * (RT + 1) + (0 if g is None else 1 + g * QG)
                  ntl = 1 if g is None else QG
                  sp.dma(qTas[b][0:HD, :, 0:ntl * 128], QT[:, :, ci0 * 128:(ci0 + ntl) * 128], reads=[TQT], writes=[TqTas[b]])
                  sp.dma(uTs[b][:, :, 0:CK - 1 + ntl * 128],
                         UT[:, :, 32 + ci0 * 128 - (CK - 1):32 + (ci0 + ntl) * 128], reads=[TUT], writes=[TuTs[b]])
              def conv_of(n):
                  ncol_ = 128 if groups[n][1] is None else QG * 128
                  return conv_module_g(uTs[n % 2], TuTs[n % 2], cTgs[n % 2], TcTgs[n % 2], CK - 1, ncol_, 0, bank=0)

              def outproj_of(n):
                  t, g = groups[n]
                  b = n % 2
                  if g is None:
                      yield from out_proj_g(attTs[b], TattTs[b], cTgs[b], TcTgs[b], x_halo[t * 128:(t + 1) * 128, :], 0,
                                            t * (RT + 1), bank=0)
                  else:
                      for i in range(QG):
                          lt = t * G + g * QG + i
                          yield from out_proj_g(attTs[b], TattTs[b], cTgs[b], TcTgs[b], x_all[lt * 128:(lt + 1) * 128, :],
                                                i * 128, t * (RT + 1) + 1 + g * QG + i, bank=0)

              def chain(*gens):
                  for g_ in gens:
                      if g_ is not None:
                          yield from g_
              load_group(0)
              for _ in conv_of(0):
                  pass
              for n, (t, g) in enumerate(groups):
                  base = t * G
                  qTa, TqTa, uT, TuT = qTas[n % 2], TqTas[n % 2], uTs[n % 2], TuTs[n % 2]
                  attT, TattT = attTs[n % 2], TattTs[n % 2]
                  nxt = None
                  if n + 1 < len(groups):
                      load_group(n + 1)
                      nxt = conv_of(n + 1)
                  side = chain(outproj_of(n - 1) if n > 0 else None, nxt)
                  if g is None:
                      segs = []
                      if t > 0:
                          segs.append(("n", 0, base, None, 128))
                      for r in range(1, NCPB):
                          segs.append(("n", base + r * RT, RT, r, 128))
                      attention_halo(segs, side=side)
                  else:
                      i0 = g * QG
                      segs = []
                      if base + i0 > 0:
                          segs.append(("n", 0, base + i0, None, 128))
                      segs.append(("d", base + i0, QG, None, 128))
                      for r in range(1, NCPB):
                          segs.append(("n", base + r * RT, RT, r, 128))
                      attention(QG * 128, segs, kt_p, vv_p, TKT, TVV, side=side)
                      if t == NSLOT - 1 and g == RT // QG - 1:
                          emit_conv_state(uT, TuT, UW - 32, o_conv[:, :])
              for _ in outproj_of(len(groups) - 1):
                  pass
              attT, TattT = attTs[0], TattTs[0]
              qTa, TqTa, uT, TuT = qTas[0], TqTas[0], uTs[0], TuTs[0]
              cTg, TcTg = cTgs[0], TcTgs[0]

              ckpt("p2")
              uS = [sb(stA, f"uS{i}", [128, 4, CK - 1 + 64], F32) for i in range(2)]
              TuS = [T(f"uS{i}") for i in range(2)]
              for e_ in range(2):
                  sp.dma(uS[e_][:, :, 0:CK - 1], sconvT[:, e_, :, :], writes=[TuS[e_]])
              run_ways([0], lambda _, W: tile_front_A_g(W, x_s[:, :], c_ropesn, Tc, 0,
                                                        [(0, 64, (uS[0], TuS[0], CK - 1)), (64, 64, (uS[1], TuS[1], CK - 1))]))
              for e_ in range(2):
                  dve.op(lambda e, e_=e_: e.tensor_copy(out=uT[:, :, 0:CK - 1 + 64], in_=uS[e_][:, :, :]),
                         reads=[TuS[e_]], writes=[TuT])
                  conv_module(CK - 1, 64, e_ * 64)
                  emit_conv_state(uS[e_], TuS[e_], CK - 1 + 64 - 32, o_convs[e_, :, :])

                  def kt_s(h, t0, n, ksz=128, e_=e_):
                      return KTs[e_, h, :, t0 * 128:(t0 + n - 1) * 128 + ksz]

                  def vv_s(h, t0, n, ksz=128, e_=e_):
                      return VVs[e_, h, 0:ksz, t0:t0 + n, :]
                  attention(64, [("n", 0, PT + 1, None, 64)], kt_s, vv_s, TKTs, TVVs, qc0=e_ * 64)
              out_proj(x_s[:, :], 0, NX1 - 1)
              tk.barrier()
              ckpt("p2s")

          with ExitStack() as stB:
              wB_up = sb(stB, "wB_up", [128, 8, 2 * DFF], BF16)
              wB_dn = sb(stB, "wB_dn", [128, NFC, D], BF16)
              with ExitStack() as stW:
                  ssB = make_stg(stW, "B")
                  load_weight(ssB, wB_up, w_up, 8, 2 * DFF, c_gffn, name="up")
                  load_weight(ssB, wB_dn, w_down, NFC, D, None, name="dn")
                  tk.barrier()
              QB = QG
              NB_ = QB * 128
              xtB = sb(stB, "xtB", [128, QB, D], F32)
              TxtB = [T(f"xtB{j}") for j in range(QB)]
              msB = sb(stB, "msB", [128, QB], F32)
              TmsB = T("msB")
              xnB = sb(stB, "xnB", [128, D], BF16)
              TxnB = T("xnB")
              xnTB = sb(stB, "xnTB", [128, 8, NB_], BF16)
              TxnTB = T("xnTB")
              hT = sb(stB, "hT", [128, NFC, NB_], BF16)
              ThT = T("hT")
              AW = NB_ + 8
              aTc = [sb(stB, f"aTc{i}", [128, AW], F32) for i in range(2)]
              TaTc = [T(f"aTc{i}") for i in range(2)]
              accB = [sb(stB, f"accB{i}", [128, AW], F32) for i in range(2)]
              TaccB = [T(f"accB{i}") for i in range(2)]
              yo = [sb(stB, f"yo{i}", [128, D], F32) for i in range(2)]
              Tyo = [T(f"yo{i}") for i in range(2)]
              arow = [sb(stB, f"arow{i}", [128, 512], F32) for i in range(2)]
              Tarow = [T(f"arow{i}") for i in range(2)]
              hist = sb(stB, "hist", [128, NFC, 2], F32)
              Thist = T("hist")
              hnew = sb(stB, "hnew", [128, NFC, 2], F32)
              Thnew = T("hnew")
              Toutb = T("outsB")
              Abank = [(pM[0], TpM[0]), (pS[3], TpS[3])]
              Gbank = [(pS[0], TpS[0]), (pS[1], TpS[1])]
              Dbank = [(pO[0], TpO[0]), (pO[1], TpO[1])]
              cb_ = dict(ab=0, gb=0, db=0, yo=0, ar=0)

              def ffn_batch(x1_rows, subs, outs, a_rows_dst=None):
                  nt = len(x1_rows)
                  N = nt * 128
                  halo = outs is None
                  for j, r in enumerate(x1_rows):
                      sp.dma(xtB[:, j, :], X1[r * 128:(r + 1) * 128, :], writes=[TxtB[j]])
                      act.op(lambda e, j=j: e.activation(out=xnB[:], in_=xtB[:, j, :], func=AF.Square, scale=1.0 / math.sqrt(D),
                                                         accum_out=msB[:, j:j + 1]), reads=[TxtB[j]], writes=[TxnB, TmsB])
                  rstd_from_msq(None, (msB[:, 0:nt], TmsB), nt)
                  for j in range(nt):
                      dve.op(lambda e, j=j: e.tensor_scalar(out=xnB[:], in0=xtB[:, j, :], scalar1=msB[:, j:j + 1], scalar2=None,
                                                            op0=ALU.mult), reads=[TxtB[j], TmsB], writes=[TxnB])
                      for k in range(8):
                          pe.op(lambda e, k=k: e.transpose(out=pT[:, k * 128:(k + 1) * 128], in_=xnB[:, k * 128:(k + 1) * 128],
                                                           identity=c_idb[:]), reads=[TxnB, Tc], writes=[TpT])
                      act.op(lambda e, j=j: e.activation(out=xnTB[:, :, j * 128:(j + 1) * 128],
                                                         in_=pT[:, :].rearrange("p (k t) -> p k t", k=8), func=AF.Copy),
                             reads=[TpT], writes=[TxnTB])
                  W_ = N + 2 * len(subs)
                  state = {}

                  def stage1(c):
                      ai = cb_["ab"] % 2
                      cb_["ab"] += 1
                      PA, TPA = Abank[ai]
                      AT, TAT, AC, TAC = aTc[ai], TaTc[ai], accB[ai], TaccB[ai]
                      for k in range(8):
                          pe.op(lambda e, k=k: e.matmul(PA[:, 0:N], lhsT=wB_up[:, k, c * 128:(c + 1) * 128], rhs=xnTB[:, k, 0:N],
                                                        start=(k == 0), stop=(k == 7)), reads=[TxnTB, Tw], writes=[TPA])
                      if not halo:
                          gi = cb_["gb"] % 2
                          cb_["gb"] += 1
                          PG, TPG = Gbank[gi]
                          col = DFF + c * 128
                          for k in range(8):
                              pe.op(lambda e, k=k: e.matmul(PG[:, 0:N], lhsT=wB_up[:, k, col:col + 128], rhs=xnTB[:, k, 0:N],
                                                            start=(k == 0), stop=(k == 7)), reads=[TxnTB, Tw], writes=[TPG])
                      else:
                          PG = TPG = None
                      for i, (tok0, ntok, hap, Th) in enumerate(subs):
                          w0 = tok0 + 2 * i
                          act.op(lambda e, w0=w0, tok0=tok0, ntok=ntok: e.activation(
                              out=AT[:, w0 + 2:w0 + 2 + ntok], in_=PA[:, tok0:tok0 + ntok], func=AF.Copy),
                              reads=[TPA], writes=[TAT])
                          if not halo:
                              act.op(lambda e, w0=w0, tok0=tok0, ntok=ntok: e.activation(
                                  out=AC[:, w0:w0 + ntok], in_=PA[:, tok0:tok0 + ntok], func=AF.Identity,
                                  scale=c_fw[:, c, 2:3], bias=c_fb[:, c:c + 1]), reads=[TPA, Tc], writes=[TAC])
                              dve.op(lambda e, w0=w0, hap=hap: e.tensor_copy(out=AT[:, w0:w0 + 2], in_=hap[:, c, :]),
                                     reads=[Th], writes=[TAT])
                          if i == len(subs) - 1:
                              dve.op(lambda e, w0=w0, ntok=ntok: e.tensor_copy(out=hnew[:, c, :], in_=AT[:, w0 + ntok:w0 + ntok + 2]),
                                     reads=[TAT], writes=[Thnew])
                      state[c] = (AT, TAT, AC, TAC, PG, TPG)

                  def stage2(c):
                      AT, TAT, AC, TAC, PG, TPG = state.pop(c)
                      si = cb_["ab"] % 2
                      SL, TSL = AC, TAC
                      dve.op(lambda e: e.scalar_tensor_tensor(out=AC[:, 0:W_ - 2], in0=AT[:, 0:W_ - 2], scalar=c_fw[:, c, 0:1],
                                                              in1=AC[:, 0:W_ - 2], op0=ALU.mult, op1=ALU.add),
                             reads=[TAT, TAC, Tc], writes=[TAC])
                      dve.op(lambda e: e.scalar_tensor_tensor(out=AC[:, 0:W_ - 2], in0=AT[:, 1:W_ - 1], scalar=c_fw[:, c, 1:2],
                                                              in1=AC[:, 0:W_ - 2], op0=ALU.mult, op1=ALU.add),
                             reads=[TAT, TAC, Tc], writes=[TAC])
                      act.op(lambda e: e.activation(out=SL[:, 0:W_ - 2], in_=AC[:, 0:W_ - 2], func=AF.Silu),
                             reads=[TAC], writes=[TSL])
                      for i, (tok0, ntok, hap, Th) in enumerate(subs):
                          w0 = tok0 + 2 * i
                          dve.op(lambda e, w0=w0, tok0=tok0, ntok=ntok: e.tensor_tensor(
                              out=hT[:, c, tok0:tok0 + ntok], in0=PG[:, tok0:tok0 + ntok], in1=SL[:, w0:w0 + ntok], op=ALU.mult),
                              reads=[TPG, TSL], writes=[ThT])

                  if len(subs) > 1:
                      for i in range(2):
                          dve.op(lambda e, i=i: e.memset(accB[i][:], 0.0), writes=[TaccB[i]])
                  for c in range(NFC + 1):
                      if c < NFC:
                          stage1(c)
                      if c >= 1 and not halo:
                          stage2(c - 1)
                  dve.op(lambda e: e.tensor_copy(out=hist[:], in_=hnew[:]), reads=[Thnew], writes=[Thist])
                  if halo:
                      return
                  if a_rows_dst is not None:
                      jl = nt - 1
                      for n0 in range(0, DFF, 512):
                          nw = min(512, DFF - n0)
                          PA, TPA = pS[2], TpS[2]
                          ri = cb_["ar"] % 2
                          cb_["ar"] += 1
                          for k in range(8):
                              pe.op(lambda e, k=k, n0=n0, nw=nw: e.matmul(
                                  PA[:, 0:nw], lhsT=xnTB[:, k, jl * 128:(jl + 1) * 128], rhs=wB_up[:, k, n0:n0 + nw],
                                  start=(k == 0), stop=(k == 7)), reads=[TxnTB, Tw], writes=[TPA])
                          dve.op(lambda e, nw=nw, ri=ri: e.tensor_copy(out=arow[ri][:, 0:nw], in_=PA[:, 0:nw]),
                                 reads=[TPA], writes=[Tarow[ri]])
                          for (r0, dst) in a_rows_dst:
                              sp.dma(dst[:, n0:n0 + nw], arow[ri][r0:r0 + 32, 0:nw], reads=[Tarow[ri]], writes=[Toutb])
                  for j in range(nt):
                      yi = cb_["yo"] % 2
                      cb_["yo"] += 1
                      for nb in range(2):
                          PD, TPD = Dbank[cb_["db"] % 2]
                          cb_["db"] += 1
                          for c in range(NFC):
                              pe.op(lambda e, nb=nb, c=c, j=j, PD=PD: e.matmul(
                                  PD[:, :], lhsT=hT[:, c, j * 128:(j + 1) * 128], rhs=wB_dn[:, c, nb * 512:(nb + 1) * 512],
                                  start=(c == 0), stop=(c == NFC - 1)), reads=[ThT, Tw], writes=[TPD])
                          dve.op(lambda e, nb=nb, j=j, PD=PD, yi=yi: e.tensor_tensor(
                              out=yo[yi][:, nb * 512:(nb + 1) * 512], in0=PD[:, :], in1=xtB[:, j, nb * 512:(nb + 1) * 512],
                              op=ALU.add), reads=[TPD, TxtB[j]], writes=[Tyo[yi]])
                      sp.dma(outs[j], yo[yi][:], reads=[Tyo[yi]], writes=[Toutb])

              for t in range(NSLOT):
                  r0 = t * (RT + 1)
                  ffn_batch([r0], [(0, 128, None, None)], None)
                  dve.op(lambda e, t=t: e.tensor_scalar(out=hist[:], in0=hist[:], scalar1=c_hflag[:, t:t + 1],
                                                        scalar2=None, op0=ALU.mult), reads=[Thist, Tc], writes=[Thist])
                  for i0 in range(0, RT, QB):
                      last = (t == NSLOT - 1 and i0 + QB == RT)
                      ffn_batch([r0 + 1 + i0 + i for i in range(QB)], [(0, NB_, hist, Thist)],
                                [o_y[(t * RT + i0 + i) * 128:(t * RT + i0 + i + 1) * 128, :] for i in range(QB)],
                                a_rows_dst=[(96, o_ffn)] if last else None)
              hs = [sb(stB, f"hs{i}", [128, NFC, 2], F32) for i in range(2)]
              Ths = [T(f"hs{i}") for i in range(2)]
              for e_ in range(2):
                  sp.dma(hs[e_][:], sffnT[:, e_, :, :], writes=[Ths[e_]])
              ffn_batch([NX1 - 1], [(0, 64, hs[0], Ths[0]), (64, 64, hs[1], Ths[1])], [o_ys[:, :]],
                        a_rows_dst=[(32, o_ffns[0]), (96, o_ffns[1])])
              tk.barrier()
          print(f"[build] sems={tk.nsem} waits={tk.nwaits} insts={tk.ninst}", flush=True)
    except _Stop:
        pass
    return nc


def _rope_tab(pos):
    inv = (1.0 / (10000.0 ** (np.arange(0, RD, 2, dtype=np.float32) / np.float32(RD)))).astype(np.float32)
    ang = pos.astype(np.float32)[:, None] * inv[None, :]
    return np.concatenate([np.cos(ang.astype(np.float64)), np.sin(ang.astype(np.float64))], axis=1).astype(np.float32)


def _run(inputs, cfg):
    SEQ, PAST, RT = cfg["SEQ"], cfg["PAST"], cfg["RT"]
    NT = SEQ // 128
    G = NCPB * RT
    NSLOT = NT // G
    QG = min(4, RT)
    PT = PAST // 128
    f32 = np.float32
    bf = ml_dtypes.bfloat16
    g = {k: np.asarray(v) for k, v in inputs.items()}
    xp, xs = g["x_prompt"], g["x_sample"]
    B = xp.shape[0]
    assert B * NCPB == 8 and xs.shape[0] == 16

    def chunked(v, n):
        return np.ascontiguousarray(v.reshape(n, 128).T).astype(f32)

    def bc(v):
        return np.ascontiguousarray(np.broadcast_to(v[None, :], (128, v.shape[0]))).astype(f32)
    common = {
        "w_in": g["w_in"][0], "w_uq": g["w_uq"][0], "w_ukv": g["w_ukv"][0], "w_out": g["w_out"][0],
        "w_up": g["w_up"][0], "w_down": g["w_down"][0],
        "g_attn": chunked(g["attn_norm"][0], 8), "g_q": chunked(g["q_norm"][0], 3),
        "g_ffn": chunked(g["ffn_norm"][0], 8), "g_kv": bc(g["kv_norm"][0]),
        "g_hq": bc(g["qk_norm_q"][0]), "g_hk": bc(g["qk_norm_k"][0]),
        "cw": np.ascontiguousarray(g["conv_w"][0].T.reshape(4, 128, CK).transpose(1, 0, 2)),
        "cb": chunked(g["conv_b"][0], 4), "cg": chunked(g["conv_norm"][0], 4),
        "fw": np.ascontiguousarray(g["ffn_conv_w"][0].T.reshape(NFC, 128, 3).transpose(1, 0, 2)),
        "fb": chunked(g["ffn_conv_b"][0], NFC),
        "identb": np.eye(128, dtype=f32).astype(bf), "identf": np.eye(128, dtype=f32),
        "onesb": np.ones((128, 128), f32).astype(bf),
    }
    nch = QG * 2
    kh = np.zeros((32, QG * 128), f32)
    qm = np.zeros((32, QG * 128), f32)
    for c in range(nch):
        kh[c, c * 64:(c + 1) * 64] = 1.0
        qm[c, :c * 64] = NEG
    common["khot"] = kh.astype(bf)
    common["qmask"] = qm.astype(bf)
    common["rope_sp"] = np.ascontiguousarray(
        _rope_tab(np.arange(max(PT, 1) * 128)).reshape(max(PT, 1), 128, 32).transpose(1, 0, 2))
    common["rope_sn"] = _rope_tab(PAST + (np.arange(128) % 64))

    in_maps, metas = [], []
    for core in range(8):
        b, j = divmod(core, NCPB)
        others = [r for r in range(NCPB) if r != j]
        order = [j] + others
        gt = np.array([t * G + order[r] * RT + i for t in range(NSLOT) for r in range(NCPB) for i in range(RT)])
        tok = (gt[:, None] * 128 + np.arange(128)[None, :]).reshape(-1)
        m = dict(common)
        m["x_all"] = np.ascontiguousarray(xp[b][tok])
        xh = np.zeros((NSLOT, 128, D), f32)
        hf = np.zeros((128, NSLOT), f32)
        rh = np.zeros((NSLOT, 128, 32), f32)
        for t in range(NSLOT):
            ht = t * G + j * RT - 1
            if ht >= 0:
                xh[t] = xp[b, ht * 128:(ht + 1) * 128]
                hf[:, t] = 1.0
                rh[t] = _rope_tab(ht * 128 + np.arange(128))
        m["x_halo"] = xh.reshape(NSLOT * 128, D)
        m["hflag"] = hf
        m["rope_h"] = np.ascontiguousarray(rh.transpose(1, 0, 2))
        m["rope_k"] = np.ascontiguousarray(_rope_tab(tok).reshape(NT, 128, 32).transpose(1, 0, 2))
        bt = np.zeros((128, NCPB), f32)
        for r in range(1, NCPB):
            bt[:, r] = 0.0 if others[r - 1] < j else NEG
        m["biast"] = bt
        e0 = 2 * core
        m["x_s"] = np.ascontiguousarray(xs[e0:e0 + 2].reshape(128, D))
        m["cckv"] = np.ascontiguousarray(g["cache_ckv"][0, e0:e0 + 2].reshape(2 * PAST, KVL))
        m["ckpe"] = np.ascontiguousarray(g["cache_kpe"][0, e0:e0 + 2].reshape(2 * PAST, RD))
        sc = g["state_conv"][0, e0:e0 + 2]
        m["sconvT"] = np.ascontiguousarray(sc.transpose(2, 0, 1).reshape(4, 128, 2, CK - 1).transpose(1, 2, 0, 3))
        sf = g["state_ffn_conv"][0, e0:e0 + 2]
        m["sffnT"] = np.ascontiguousarray(sf.transpose(2, 0, 1).reshape(NFC, 128, 2, 2).transpose(1, 2, 0, 3))
        in_maps.append(m)
        own_tok = (np.array([t * G + j * RT + i for t in range(NSLOT) for i in range(RT)])[:, None] * 128
                   + np.arange(128)[None, :]).reshape(-1)
        metas.append((b, j, own_tok))

    nc = build(cfg)
    res = run_bass_kernel_spmd(nc, in_maps, core_ids=list(range(8)))
    R = res.results

    y_p = np.zeros((B, SEQ, D), f32)
    ckv_p = np.zeros((1, B, SEQ, KVL), f32)
    kpe_p = np.zeros((1, B, SEQ, RD), f32)
    conv_p = np.zeros((1, B, CK - 1, CC), f32)
    ffn_p = np.zeros((1, B, 2, DFF), f32)
    y_s = np.zeros((16, 64, D), f32)
    ckv_s = np.zeros((1, 16, 64, KVL), f32)
    kpe_s = np.zeros((1, 16, 64, RD), f32)
    conv_s = np.zeros((1, 16, CK - 1, CC), f32)
    ffn_s = np.zeros((1, 16, 2, DFF), f32)
    for core in range(8):
        b, j, own_tok = metas[core]
        r = R[core]
        y_p[b, own_tok] = r["o_y"]
        ckv_p[0, b, own_tok] = r["o_ckv"]
        kpe_p[0, b, own_tok] = r["o_kpe"]
        if j == NCPB - 1:
            conv_p[0, b] = r["o_conv"][2:32]
            ffn_p[0, b] = r["o_ffn"][30:32]
        e0 = 2 * core
        y_s[e0:e0 + 2] = r["o_ys"].reshape(2, 64, D)
        ckv_s[0, e0:e0 + 2] = r["o_ckvs"].reshape(2, 64, KVL)
        kpe_s[0, e0:e0 + 2] = r["o_kpes"].reshape(2, 64, RD)
        conv_s[0, e0:e0 + 2] = r["o_convs"][:, 2:32]
        ffn_s[0, e0:e0 + 2] = r["o_ffns"][:, 30:32]
    return (y_p, y_s, ckv_p, kpe_p, conv_p, ffn_p, ckv_s, kpe_s, conv_s, ffn_s)


def kernel(**inputs):
    return _run(inputs, CFG_FULL)
```
